# Optimizing a Trainium2 kernel written in Bass

```python
import math
import jax
import jax.numpy as jnp
from jax import lax
import numpy as np


D_MODEL = 2048
BATCH = 2
SEQ = 4096
DEPTH = 2

CHUNK = 64
N_EVEN = (DEPTH + 1) // 2
N_ODD = DEPTH // 2
D_FF = 5632
EPS = 1e-6
S5_WIDTH = 1024
S5_GROUP = 16
S5_GROUPS = S5_WIDTH // S5_GROUP
S5_STATE = 64
GDN_HEADS = 8
GDN_DK = 128
GDN_DV = 128
GDN_CONV = 4
GDN_QKV = GDN_HEADS * (2 * GDN_DK + GDN_DV)
MLSTM_HEADS = 4
MLSTM_DK = 128
MLSTM_DV = 256
RET_HEADS = 4
RET_DK = 128
RET_DV = 256
ROPE_BASE = 10000.0
EVEN_IN = S5_WIDTH + GDN_HEADS * (2 * GDN_DK + 2 * GDN_DV) + 2 * GDN_HEADS
EVEN_MIX = S5_WIDTH + GDN_HEADS * GDN_DV
ODD_IN = MLSTM_HEADS * (2 * MLSTM_DK + 2 * MLSTM_DV) + 2 * MLSTM_HEADS + RET_HEADS * (2 * RET_DK + 2 * RET_DV)
ODD_MIX = MLSTM_HEADS * MLSTM_DV + RET_HEADS * RET_DV

kernel_name = 'chunk_causal_hybrid_s5_deltanet_mlstm_retention'


def rms_norm(x, g):
    xf = x.astype(jnp.float32)
    y = xf * lax.rsqrt(jnp.mean(xf * xf, axis=-1, keepdims=True) + EPS)
    return (y * g.astype(jnp.float32)).astype(x.dtype)


def head_group_norm(y, g):
    mu = jnp.mean(y, axis=-1, keepdims=True)
    var = jnp.mean(jnp.square(y - mu), axis=-1, keepdims=True)
    return (y - mu) * lax.rsqrt(var + EPS) * g.astype(jnp.float32)


def l2_normalize(x):
    return x * lax.rsqrt(jnp.sum(x * x, axis=-1, keepdims=True) + EPS)


def swiglu(x, w1, w3, w2):
    return (jax.nn.silu(x @ w1) * (x @ w3)) @ w2


def split_cols(t, sizes):
    idx = np.cumsum(sizes)[:-1].tolist()
    return jnp.split(t, idx, axis=-1)


def chunk_heads(t):
    b, l, h = t.shape[:3]
    t = t.reshape((b, l // CHUNK, CHUNK, h) + t.shape[3:])
    return t.transpose((1, 0, 3, 2) + tuple(range(4, t.ndim)))


def unchunk_heads(t):
    n, b, h, c = t.shape[:4]
    t = t.transpose((1, 0, 3, 2) + tuple(range(4, t.ndim)))
    return t.reshape((b, n * c, h) + t.shape[4:])


def causal_depthwise_conv(x, w):
    k, c = w.shape
    return lax.conv_general_dilated(x, w[:, None, :].astype(x.dtype), window_strides=(1,), padding=[(k - 1, 0)], dimension_numbers=('NWC', 'WIO', 'NWC'), feature_group_count=c)


def apply_rotary(x, positions):
    half = x.shape[-1] // 2
    inv_freq = ROPE_BASE ** (-jnp.arange(half, dtype=jnp.float32) / half)
    ang = positions.astype(jnp.float32)[:, :, None, None] * inv_freq
    cos, sin = jnp.cos(ang), jnp.sin(ang)
    x1, x2 = x[..., :half], x[..., half:]
    return jnp.concatenate([x1 * cos - x2 * sin, x2 * cos + x1 * sin], axis=-1)


def _complex_linear_combine(e1, e2):
    a1r, a1i, b1r, b1i = e1
    a2r, a2i, b2r, b2i = e2
    return (a2r * a1r - a2i * a1i, a2r * a1i + a2i * a1r, a2r * b1r - a2i * b1i + b2r, a2r * b1i + a2i * b1r + b2i)


def s5_mixer(u, a_re, a_im, log_step, b_re, b_im, c_re, c_im, d_skip, w_glu, b_glu):
    bsz, seq, _ = u.shape
    f32 = jnp.float32
    uf = u.astype(f32).reshape(bsz, seq, S5_GROUPS, S5_GROUP)
    step = jnp.exp(log_step.astype(f32))[:, None]
    ar, ai = a_re.astype(f32), a_im.astype(f32)
    mag = jnp.exp(ar * step)
    lr, li = mag * jnp.cos(ai * step), mag * jnp.sin(ai * step)
    den = ar * ar + ai * ai
    fr = ((lr - 1.0) * ar + li * ai) / den
    fi = (li * ar - (lr - 1.0) * ai) / den
    br, bi = b_re.astype(f32), b_im.astype(f32)
    bbr = fr[..., None] * br - fi[..., None] * bi
    bbi = fr[..., None] * bi + fi[..., None] * br
    bu_r = jnp.einsum('blgh,gph->lbgp', uf, bbr)
    bu_i = jnp.einsum('blgh,gph->lbgp', uf, bbi)
    lam_r = jnp.broadcast_to(lr[None, None], (seq, 1) + lr.shape)
    lam_i = jnp.broadcast_to(li[None, None], (seq, 1) + li.shape)
    _, _, xr, xi = lax.associative_scan(_complex_linear_combine, (lam_r, lam_i, bu_r, bu_i), axis=0)
    y = (jnp.einsum('lbgp,ghp->blgh', xr, c_re.astype(f32)) - jnp.einsum('lbgp,ghp->blgh', xi, c_im.astype(f32)) + d_skip.astype(f32) * uf)
    y = jax.nn.gelu(y.reshape(bsz, seq, S5_WIDTH))
    return y * jax.nn.sigmoid(y @ w_glu.astype(f32) + b_glu.astype(f32))


def gated_deltanet(q, k, v, z, beta_pre, a_pre, conv_w, a_log, dt_bias, norm_g):
    bsz, seq, _ = q.shape
    f32 = jnp.float32
    qkv = jax.nn.silu(causal_depthwise_conv(jnp.concatenate([q, k, v], axis=-1), conv_w)).astype(f32)
    q, k, v = split_cols(qkv, [GDN_HEADS * GDN_DK, GDN_HEADS * GDN_DK, GDN_HEADS * GDN_DV])
    q = l2_normalize(q.reshape(bsz, seq, GDN_HEADS, GDN_DK)) * GDN_DK ** -0.5
    k = l2_normalize(k.reshape(bsz, seq, GDN_HEADS, GDN_DK))
    v = v.reshape(bsz, seq, GDN_HEADS, GDN_DV)
    beta = jax.nn.sigmoid(beta_pre.astype(f32))
    g = -jnp.exp(a_log.astype(f32)) * jax.nn.softplus(a_pre.astype(f32) + dt_bias.astype(f32))
    qc, kc, vc = chunk_heads(q), chunk_heads(k), chunk_heads(v)
    bc = chunk_heads(beta)
    gcum = jnp.cumsum(chunk_heads(g), axis=-1)
    tri = jnp.tril(jnp.ones((CHUNK, CHUNK), dtype=bool))
    strict = jnp.tril(jnp.ones((CHUNK, CHUNK), dtype=bool), -1)
    decay = jnp.exp(jnp.where(tri, gcum[..., :, None] - gcum[..., None, :], -jnp.inf))
    kb = kc * bc[..., None]
    lower = jnp.where(strict, jnp.einsum('nbhcd,nbhsd->nbhcs', kb, kc) * decay, 0.0)
    unit = lower + jnp.eye(CHUNK, dtype=f32)
    rhs = jnp.concatenate([vc * bc[..., None], kb * jnp.exp(gcum)[..., None]], axis=-1)
    sol = lax.linalg.triangular_solve(unit, rhs, left_side=True, lower=True, unit_diagonal=True)
    u_c, w_c = sol[..., :GDN_DV], sol[..., GDN_DV:]
    qk = jnp.where(tri, jnp.einsum('nbhcd,nbhsd->nbhcs', qc, kc) * decay, 0.0)
    q_dec = qc * jnp.exp(gcum)[..., None]
    k_dec = kc * jnp.exp(gcum[..., -1:] - gcum)[..., None]
    g_last = jnp.exp(gcum[..., -1])

    def step(state, xs):
        u_i, w_i, qk_i, qd_i, kd_i, gl_i = xs
        v_new = u_i - jnp.einsum('bhcd,bhde->bhce', w_i, state)
        out = jnp.einsum('bhcd,bhde->bhce', qd_i, state) + jnp.einsum('bhcs,bhse->bhce', qk_i, v_new)
        state = state * gl_i[..., None, None] + jnp.einsum('bhcd,bhce->bhde', kd_i, v_new)
        return state, out

    s0 = jnp.zeros((bsz, GDN_HEADS, GDN_DK, GDN_DV), f32)
    _, o = lax.scan(step, s0, (u_c, w_c, qk, q_dec, k_dec, g_last))
    o = unchunk_heads(o)
    zf = z.astype(f32).reshape(bsz, seq, GDN_HEADS, GDN_DV)
    o = rms_norm(o, norm_g) * jax.nn.silu(zf)
    return o.reshape(bsz, seq, GDN_HEADS * GDN_DV)


def mlstm(q, k, v, o_pre, i_pre, f_pre, gate_bias, norm_g):
    bsz, seq, _ = q.shape
    f32 = jnp.float32
    nh = MLSTM_HEADS
    qc = chunk_heads(q.astype(f32).reshape(bsz, seq, nh, MLSTM_DK) * MLSTM_DK ** -0.5)
    kc = chunk_heads(k.astype(f32).reshape(bsz, seq, nh, MLSTM_DK))
    vc = chunk_heads(v.astype(f32).reshape(bsz, seq, nh, MLSTM_DV))
    ig = chunk_heads(i_pre.astype(f32) + gate_bias[0].astype(f32))
    logf = chunk_heads(jax.nn.log_sigmoid(f_pre.astype(f32) + gate_bias[1].astype(f32)))
    bcum = jnp.cumsum(logf, axis=-1)
    b_last = bcum[..., -1]
    tri = jnp.tril(jnp.ones((CHUNK, CHUNK), dtype=bool))
    intra_log = jnp.where(tri, bcum[..., :, None] - bcum[..., None, :] + ig[..., None, :], -jnp.inf)
    intra_max = jnp.max(intra_log, axis=-1)
    upd_log = b_last[..., None] - bcum + ig
    upd_max = jnp.max(upd_log, axis=-1)
    qk = jnp.einsum('nbhcd,nbhsd->nbhcs', qc, kc)

    def step(carry, xs):
        c_mat, n_vec, m = carry
        q_i, k_i, v_i, qk_i, b_i, bl_i, il_i, im_i, ul_i, um_i = xs
        inter_log = b_i + m[..., None]
        m_s = jnp.maximum(inter_log, im_i)
        inter_w = jnp.exp(inter_log - m_s)
        s = qk_i * jnp.exp(il_i - m_s[..., None])
        num = inter_w[..., None] * jnp.einsum('bhcd,bhde->bhce', q_i, c_mat) + jnp.einsum('bhcs,bhse->bhce', s, v_i)
        den = inter_w * jnp.einsum('bhcd,bhd->bhc', q_i, n_vec) + jnp.sum(s, axis=-1)
        h = num / jnp.maximum(jnp.abs(den), jnp.exp(-m_s))[..., None]
        m_new = jnp.maximum(bl_i + m, um_i)
        carry_w = jnp.exp(bl_i + m - m_new)
        kw = k_i * jnp.exp(ul_i - m_new[..., None])[..., None]
        c_mat = carry_w[..., None, None] * c_mat + jnp.einsum('bhcd,bhce->bhde', kw, v_i)
        n_vec = carry_w[..., None] * n_vec + jnp.sum(kw, axis=-2)
        return (c_mat, n_vec, m_new), h

    init = (jnp.zeros((bsz, nh, MLSTM_DK, MLSTM_DV), f32), jnp.zeros((bsz, nh, MLSTM_DK), f32), jnp.zeros((bsz, nh), f32))
    _, h = lax.scan(step, init, (qc, kc, vc, qk, bcum, b_last, intra_log, intra_max, upd_log, upd_max))
    h = unchunk_heads(h)
    o = jax.nn.sigmoid(o_pre.astype(f32).reshape(bsz, seq, nh, MLSTM_DV))
    return (o * rms_norm(h, norm_g)).reshape(bsz, seq, nh * MLSTM_DV)


def retention(q, k, v, g_pre, positions, norm_g):
    bsz, seq, _ = q.shape
    f32 = jnp.float32
    nh = RET_HEADS
    qh = apply_rotary(q.astype(f32).reshape(bsz, seq, nh, RET_DK), positions) * RET_DK ** -0.5
    kh = apply_rotary(k.astype(f32).reshape(bsz, seq, nh, RET_DK), positions)
    vh = v.astype(f32).reshape(bsz, seq, nh, RET_DV)
    log_gamma = jnp.log1p(-jnp.exp2(-5.0 - jnp.arange(nh, dtype=f32)))
    idx = jnp.arange(CHUNK, dtype=f32)
    tri = jnp.tril(jnp.ones((CHUNK, CHUNK), dtype=bool))
    diff = jnp.where(tri, idx[:, None] - idx[None, :], 0.0)
    dmat = jnp.where(tri, jnp.exp(diff * log_gamma[:, None, None]), 0.0)
    xi = jnp.exp((idx + 1.0) * log_gamma[:, None])
    zeta = jnp.exp((CHUNK - 1.0 - idx) * log_gamma[:, None])
    gamma_c = jnp.exp(CHUNK * log_gamma)
    qc, kc, vc = chunk_heads(qh), chunk_heads(kh), chunk_heads(vh)
    intra = jnp.einsum('nbhcs,nbhse->nbhce', jnp.einsum('nbhcd,nbhsd->nbhcs', qc, kc) * dmat, vc)
    kz = kc * zeta[:, :, None]

    def step(state, xs):
        q_i, kz_i, v_i = xs
        out = jnp.einsum('bhcd,bhde->bhce', q_i, state) * xi[:, :, None]
        state = state * gamma_c[:, None, None] + jnp.einsum('bhcd,bhce->bhde', kz_i, v_i)
        return state, out

    s0 = jnp.zeros((bsz, nh, RET_DK, RET_DV), f32)
    _, inter = lax.scan(step, s0, (qc, kz, vc))
    y = unchunk_heads(intra + inter)
    y = head_group_norm(y, norm_g) * jax.nn.silu(g_pre.astype(f32).reshape(bsz, seq, nh, RET_DV))
    return y.reshape(bsz, seq, nh * RET_DV)


def even_mixer(h, w_in, w_out, a_re, a_im, log_step, b_re, b_im, c_re, c_im, d_skip, w_glu, b_glu, conv_w, a_log, dt_bias, gdn_g):
    zin = h @ w_in
    u, q, k, v, z, beta_pre, a_pre = split_cols(zin, [S5_WIDTH, GDN_HEADS * GDN_DK, GDN_HEADS * GDN_DK, GDN_HEADS * GDN_DV, GDN_HEADS * GDN_DV, GDN_HEADS, GDN_HEADS])
    ya = s5_mixer(u, a_re, a_im, log_step, b_re, b_im, c_re, c_im, d_skip, w_glu, b_glu)
    yb = gated_deltanet(q, k, v, z, beta_pre, a_pre, conv_w, a_log, dt_bias, gdn_g)
    return jnp.concatenate([ya, yb], axis=-1).astype(h.dtype) @ w_out


def odd_mixer(h, positions, w_in, w_out, gate_bias, mlstm_g, ret_g):
    zin = h @ w_in
    cq, ck, cv, co, ci, cf, rq, rk, rv, rg = split_cols(zin, [MLSTM_HEADS * MLSTM_DK, MLSTM_HEADS * MLSTM_DK, MLSTM_HEADS * MLSTM_DV, MLSTM_HEADS * MLSTM_DV, MLSTM_HEADS, MLSTM_HEADS, RET_HEADS * RET_DK, RET_HEADS * RET_DK, RET_HEADS * RET_DV, RET_HEADS * RET_DV])
    yc = mlstm(cq, ck, cv, co, ci, cf, gate_bias, mlstm_g)
    yd = retention(rq, rk, rv, rg, positions, ret_g)
    return jnp.concatenate([yc, yd], axis=-1).astype(h.dtype) @ w_out


def setup_inputs(seed: int = 0) -> dict:
    key = jax.random.key(seed)
    ks = jax.random.split(key, 32)
    f32 = jnp.float32

    def nrm(i, shape, scale):
        return jax.random.normal(ks[i], shape, f32) * scale

    x = nrm(0, (BATCH, SEQ, D_MODEL), 1.0)
    offset = jax.random.randint(ks[1], (BATCH, 1), 0, 64, dtype=jnp.int32) * CHUNK
    positions = offset + jnp.arange(SEQ, dtype=jnp.int32)[None, :]
    ffn_norm = 1.0 + nrm(2, (DEPTH, 2, D_MODEL), 0.02)
    ffn_w1 = nrm(3, (DEPTH, 2, D_MODEL, D_FF), D_MODEL ** -0.5)
    ffn_w3 = nrm(4, (DEPTH, 2, D_MODEL, D_FF), D_MODEL ** -0.5)
    ffn_w2 = nrm(5, (DEPTH, 2, D_FF, D_MODEL), D_FF ** -0.5)
    mix_norm = 1.0 + nrm(6, (DEPTH, D_MODEL), 0.02)
    even_w_in = nrm(7, (N_EVEN, D_MODEL, EVEN_IN), D_MODEL ** -0.5)
    even_w_out = nrm(8, (N_EVEN, EVEN_MIX, D_MODEL), EVEN_MIX ** -0.5)
    n_idx = jnp.arange(S5_STATE, dtype=f32)
    s5_a_re = -0.5 + nrm(9, (N_EVEN, S5_GROUPS, S5_STATE), 0.01)
    s5_a_im = math.pi * n_idx + nrm(10, (N_EVEN, S5_GROUPS, S5_STATE), 0.01)
    s5_log_step = jax.random.uniform(ks[11], (N_EVEN, S5_GROUPS), f32, math.log(1e-3), math.log(1e-1))
    s5_b_re = nrm(12, (N_EVEN, S5_GROUPS, S5_STATE, S5_GROUP), (2 * S5_GROUP) ** -0.5)
    s5_b_im = nrm(13, (N_EVEN, S5_GROUPS, S5_STATE, S5_GROUP), (2 * S5_GROUP) ** -0.5)
    s5_c_re = nrm(14, (N_EVEN, S5_GROUPS, S5_GROUP, S5_STATE), S5_STATE ** -0.5)
    s5_c_im = nrm(15, (N_EVEN, S5_GROUPS, S5_GROUP, S5_STATE), S5_STATE ** -0.5)
    s5_d = nrm(16, (N_EVEN, S5_GROUPS, S5_GROUP), 1.0)
    s5_w_glu = nrm(17, (N_EVEN, S5_WIDTH, S5_WIDTH), S5_WIDTH ** -0.5)
    s5_b_glu = nrm(18, (N_EVEN, S5_WIDTH), 0.02)
    gdn_conv_w = nrm(19, (N_EVEN, GDN_CONV, GDN_QKV), GDN_CONV ** -0.5)
    gdn_a_log = jnp.log(jax.random.uniform(ks[20], (N_EVEN, GDN_HEADS), f32, 1.0, 16.0))
    dt = jnp.exp(jax.random.uniform(ks[21], (N_EVEN, GDN_HEADS), f32, math.log(1e-3), math.log(1e-1)))
    gdn_dt_bias = dt + jnp.log(-jnp.expm1(-dt))
    gdn_norm = 1.0 + nrm(22, (N_EVEN, GDN_DV), 0.02)
    odd_w_in = nrm(23, (N_ODD, D_MODEL, ODD_IN), D_MODEL ** -0.5)
    odd_w_out = nrm(24, (N_ODD, ODD_MIX, D_MODEL), ODD_MIX ** -0.5)
    i_bias = -1.0 + nrm(25, (N_ODD, MLSTM_HEADS), 0.1)
    f_bias = jnp.linspace(3.0, 6.0, MLSTM_HEADS, dtype=f32)[None, :] + nrm(26, (N_ODD, MLSTM_HEADS), 0.1)
    mlstm_gate_bias = jnp.stack([i_bias, f_bias], axis=1)
    mlstm_norm = 1.0 + nrm(27, (N_ODD, MLSTM_DV), 0.02)
    ret_norm = 1.0 + nrm(28, (N_ODD, RET_DV), 0.02)
    final_norm = 1.0 + nrm(29, (D_MODEL,), 0.02)
    return {'x': x, 'positions': positions, 'ffn_norm': ffn_norm, 'ffn_w1': ffn_w1, 'ffn_w3': ffn_w3, 'ffn_w2': ffn_w2, 'mix_norm': mix_norm, 'even_w_in': even_w_in, 'even_w_out': even_w_out, 's5_a_re': s5_a_re, 's5_a_im': s5_a_im, 's5_log_step': s5_log_step, 's5_b_re': s5_b_re, 's5_b_im': s5_b_im, 's5_c_re': s5_c_re, 's5_c_im': s5_c_im, 's5_d': s5_d, 's5_w_glu': s5_w_glu, 's5_b_glu': s5_b_glu, 'gdn_conv_w': gdn_conv_w, 'gdn_a_log': gdn_a_log, 'gdn_dt_bias': gdn_dt_bias, 'gdn_norm': gdn_norm, 'odd_w_in': odd_w_in, 'odd_w_out': odd_w_out, 'mlstm_gate_bias': mlstm_gate_bias, 'mlstm_norm': mlstm_norm, 'ret_norm': ret_norm, 'final_norm': final_norm}


def reference(x, positions, ffn_norm, ffn_w1, ffn_w3, ffn_w2, mix_norm, even_w_in, even_w_out, s5_a_re, s5_a_im, s5_log_step, s5_b_re, s5_b_im, s5_c_re, s5_c_im, s5_d, s5_w_glu, s5_b_glu, gdn_conv_w, gdn_a_log, gdn_dt_bias, gdn_norm, odd_w_in, odd_w_out, mlstm_gate_bias, mlstm_norm, ret_norm, final_norm):
    for layer in range(DEPTH):
        x = x + 0.5 * swiglu(rms_norm(x, ffn_norm[layer, 0]), ffn_w1[layer, 0], ffn_w3[layer, 0], ffn_w2[layer, 0])
        h = rms_norm(x, mix_norm[layer])
        j = layer // 2
        if layer % 2 == 0:
            x = x + even_mixer(h, even_w_in[j], even_w_out[j], s5_a_re[j], s5_a_im[j], s5_log_step[j], s5_b_re[j], s5_b_im[j], s5_c_re[j], s5_c_im[j], s5_d[j], s5_w_glu[j], s5_b_glu[j], gdn_conv_w[j], gdn_a_log[j], gdn_dt_bias[j], gdn_norm[j])
        else:
            x = x + odd_mixer(h, positions, odd_w_in[j], odd_w_out[j], mlstm_gate_bias[j], mlstm_norm[j], ret_norm[j])
        x = x + 0.5 * swiglu(rms_norm(x, ffn_norm[layer, 1]), ffn_w1[layer, 1], ffn_w3[layer, 1], ffn_w2[layer, 1])
    return rms_norm(x, final_norm)
```

```python
import contextlib
import numpy as np
import concourse.bass as bass
import concourse.mybir as mybir
from concourse.bass_utils import run_bass_kernel_spmd

F32 = mybir.dt.float32
BF16 = mybir.dt.bfloat16
I32 = mybir.dt.int32
AF = mybir.ActivationFunctionType
ALU = mybir.AluOpType
AX = mybir.AxisListType


class Dep:
    __slots__ = ("w", "r")

    def __init__(self):
        self.w = None
        self.r = {}


class View:
    __slots__ = ("ap", "deps")

    def __init__(self, ap, deps):
        self.ap = ap
        self.deps = deps


class Buf:
    def __init__(self, t):
        self.t = t
        self.base = Dep()
        self.parts = {}

    def dep(self, key=None):
        if key is None:
            return self.base
        d = self.parts.get(key)
        if d is None:
            d = self.parts[key] = Dep()
            d.w = self.base.w
            d.r = dict(self.base.r)
        return d

    def v(self, ap=None, key=None):
        if ap is None:
            ap = self.t[:]
        if key is None:
            return View(ap, [self.base] + list(self.parts.values()))
        if isinstance(key, list):
            return View(ap, [self.dep(k) for k in key])
        return View(ap, [self.dep(key)])

    def all(self, ap=None):
        if ap is None:
            ap = self.t[:]
        return View(ap, [self.base] + list(self.parts.values()))


class Eng:
    def __init__(self, name, handle, sem, self_sync):
        self.name = name
        self.h = handle
        self.sem = sem
        self.count = 0
        self.known = {}
        self.self_sync = self_sync
        self.own = {id(sem)}


SEM_ROT = 1800


class Ctx:
    def __init__(self, n_dma_sems=14):
        self.nc = bass.Bass("TRN2", target_bir_lowering=False)
        nc = self.nc
        self.es = contextlib.ExitStack()
        self.sems = {}
        self.E = {}
        for name, h, ss in (("pe", nc.tensor, False), ("act", nc.scalar, True), ("dve", nc.vector, True),
                            ("pool", nc.gpsimd, True), ("sp", nc.sync, True)):
            s = self.es.enter_context(nc.semaphore("s_" + name))
            self.E[name] = Eng(name, h, s, ss)
            self.sems[id(s)] = s
        self.rings = {}
        for q in ("sp", "pool"):
            ring = []
            for i in range(n_dma_sems):
                s = self.es.enter_context(nc.semaphore("d_%s%d" % (q, i)))
                self.sems[id(s)] = s
                ring.append([s, 0])
            self.rings[q] = [ring, 0]
        self.out_stamps = []
        self.cc_stamps = []
        self.cc_groups = {}
        self.cc_count = 0
        self.sem_es = self.es
        self.n_wait = 0
        self.n_ins = 0

    def sb(self, name, shape, dtype=F32):
        self.nalloc = getattr(self, "nalloc", 0) + 1
        return Buf(self.es.enter_context(self.nc.sbuf_tensor("%s_a%d" % (name, self.nalloc), list(shape), dtype)))

    def ps(self, name, shape, dtype=F32):
        return Buf(self.es.enter_context(self.nc.psum_tensor(name, list(shape), dtype)))

    def dram(self, name, shape, dtype=F32, kind="Internal"):
        if kind == "Internal":
            t = self.nc.dram_tensor(name, list(shape), dtype)
        else:
            t = self.nc.dram_tensor(name, list(shape), dtype, kind=kind)
        b = Buf(t.ap())
        b.is_out = kind == "ExternalOutput"
        return b

    def _waits(self, E, reads, writes, extra=()):
        need = {}

        def add(st):
            if st is None:
                return
            sid, val = st
            if need.get(sid, 0) < val:
                need[sid] = val

        for v in reads:
            for d in v.deps:
                add(d.w)
        for v in writes:
            for d in v.deps:
                add(d.w)
                for sid, val in d.r.items():
                    add((sid, val))
        for st in extra:
            add(st)
        for sid, val in need.items():
            if sid in E.own and not E.self_sync:
                continue
            if E.known.get(sid, 0) < val:
                E.h.wait_ge(self.sems[sid], val)
                E.known[sid] = val
                self.n_wait += 1

    def _commit(self, stamp, reads, writes):
        sid, val = stamp
        for v in reads:
            for d in v.deps:
                if d.r.get(sid, 0) < val:
                    d.r[sid] = val
        for v in writes:
            for d in v.deps:
                d.w = stamp
                d.r = {}

    def emit(self, eng, fn, writes, reads):
        E = self.E[eng]
        self._waits(E, reads, writes)
        if E.count >= SEM_ROT:
            ns = self.sem_es.enter_context(self.nc.semaphore("s_%s_%d" % (eng, self.n_ins)))
            self.sems[id(ns)] = ns
            E.prev = (id(E.sem), E.count)
            E.sem = ns
            E.count = 0
            E.own.add(id(ns))
        ins = fn(E.h)
        E.count += 1
        ins.then_inc(E.sem, 1)
        self.n_ins += 1
        self._commit((id(E.sem), E.count), reads, writes)
        return ins

    def barrier(self):
        stamps = []
        for e in self.E.values():
            if e.count:
                stamps.append((id(e.sem), e.count))
            elif getattr(e, "prev", None):
                stamps.append(e.prev)
        for ring, _ in self.rings.values():
            for sem, total in ring:
                if total:
                    stamps.append((id(sem), total))
        for name, e in self.E.items():
            for sid, val in stamps:
                if sid in e.own and sid != id(e.sem):
                    continue
                if sid == id(e.sem) and not e.self_sync:
                    continue
                if e.known.get(sid, 0) < val:
                    e.h.wait_ge(self.sems[sid], val)
                    e.known[sid] = val
                    self.n_wait += 1

    @contextlib.contextmanager
    def scope(self):
        outer = self.es
        inner = contextlib.ExitStack()
        self.es = inner
        try:
            yield
        finally:
            self.barrier()
            self.es = outer
            inner.close()

    def allgather(self, snd_ap, rcv_ap, snd_views, rcv_views, groups, grp="g", grp_total=None):
        E = self.E["pool"]
        self._waits(E, snd_views, rcv_views)
        st = self.cc_groups.get(grp)
        if st is None:
            s = self.sem_es.enter_context(self.nc.semaphore("cc_" + grp))
            self.sems[id(s)] = s
            st = self.cc_groups[grp] = [s, 0, grp_total]
        ins = E.h.collective_compute("AllGather", ALU.bypass, replica_groups=groups, ins=[snd_ap.opt()], outs=[rcv_ap.opt()])
        ins.then_inc(st[0], 1)
        st[1] += 1
        assert st[1] <= st[2]
        self.cc_count += 1
        self.n_ins += 1
        stamp = (id(st[0]), st[2])
        self._commit(stamp, snd_views, rcv_views)
        return stamp

    def gather(self, out, G, g_ap, idx, blocks=None):
        q = "pool"
        E = self.E[q]
        ringinfo = self.rings[q]
        ring, i = ringinfo
        slot = ring[i]
        ringinfo[1] = (i + 1) % len(ring)
        sem, total = slot
        extra = [(id(sem), total)] if total else []
        gv = G.v()
        self._waits(E, [gv, idx], [out], extra)
        ins = E.h.indirect_dma_start(out=out.ap, out_offset=None, in_=g_ap, in_offset=bass.IndirectOffsetOnAxis(ap=idx.ap, axis=0))
        slot[1] = total + 16
        ins.then_inc(sem, 16)
        self.n_ins += 1
        stamp = (id(sem), total + 16)
        self._commit(stamp, [gv, idx], [out])
        return stamp

    def dma(self, q, out, in_, **kw):
        E = self.E[q]
        ringinfo = self.rings[q]
        ring, idx = ringinfo
        slot = ring[idx]
        ringinfo[1] = (idx + 1) % len(ring)
        sem, total = slot
        extra = [(id(sem), total)] if total else []
        self._waits(E, [in_], [out], extra)
        ins = E.h.dma_start(out=out.ap, in_=in_.ap, **kw)
        slot[1] = total + 16
        ins.then_inc(sem, 16)
        self.n_ins += 1
        stamp = (id(sem), total + 16)
        self._commit(stamp, [in_], [out])
        return stamp

    def finish(self, stamps, eng="sp"):
        E = self.E[eng]
        self._waits(E, [], [], extra=stamps)

    def matmul(self, out, lhsT, rhs, start=True, stop=True, **kw):
        return self.emit("pe", lambda e: e.matmul(out.ap, lhsT=lhsT.ap, rhs=rhs.ap, start=start, stop=stop, **kw),
                         [out], [lhsT, rhs] + ([] if start else [out]))

    def transpose(self, out, in_, ident):
        return self.emit("pe", lambda e: e.transpose(out.ap, in_.ap, ident.ap), [out], [in_, ident])

    def act(self, out, in_, func, bias=None, scale=1.0, accum_out=None, eng="act"):
        reads = [in_]
        kw = {}
        if isinstance(bias, View):
            reads.append(bias)
            kw["bias"] = bias.ap
        elif bias is not None:
            kw["bias"] = bias
        if isinstance(scale, View):
            reads.append(scale)
            kw["scale"] = scale.ap
        else:
            kw["scale"] = scale
        writes = [out]
        if accum_out is not None:
            writes.append(accum_out)
            kw["accum_out"] = accum_out.ap
        return self.emit(eng, lambda e: e.activation(out=out.ap, in_=in_.ap, func=func, **kw), writes, reads)

    def tt(self, out, in0, in1, op, eng="dve"):
        return self.emit(eng, lambda e: e.tensor_tensor(out=out.ap, in0=in0.ap, in1=in1.ap, op=op), [out], [in0, in1])

    def ts(self, out, in0, s1, s2=None, op0=ALU.mult, op1=None, accum_out=None, eng="dve"):
        reads = [in0]
        a1 = s1
        if isinstance(s1, View):
            reads.append(s1)
            a1 = s1.ap
        a2 = s2
        if isinstance(s2, View):
            reads.append(s2)
            a2 = s2.ap
        kw = {}
        if op1 is not None:
            kw["op1"] = op1
        writes = [out]
        if accum_out is not None:
            writes.append(accum_out)
            kw["accum_out"] = accum_out.ap
        return self.emit(eng, lambda e: e.tensor_scalar(out=out.ap, in0=in0.ap, scalar1=a1, scalar2=a2, op0=op0, **kw),
                         writes, reads)

    def stt(self, out, in0, scalar, in1, op0, op1, eng="dve"):
        reads = [in0, in1]
        a = scalar
        if isinstance(scalar, View):
            reads.append(scalar)
            a = scalar.ap
        return self.emit(eng, lambda e: e.scalar_tensor_tensor(out=out.ap, in0=in0.ap, scalar=a, in1=in1.ap, op0=op0, op1=op1),
                         [out], reads)

    def copy(self, out, in_, eng="dve"):
        if eng == "act":
            return self.emit("act", lambda e: e.copy(out=out.ap, in_=in_.ap), [out], [in_])
        return self.emit(eng, lambda e: e.tensor_copy(out=out.ap, in_=in_.ap), [out], [in_])

    def memset(self, out, val, eng="dve"):
        return self.emit(eng, lambda e: e.memset(out.ap, val), [out], [])

    def scan(self, out, d0, d1, initial, op0, op1, eng="dve"):
        reads = [d0, d1]
        a = initial
        if isinstance(initial, View):
            reads.append(initial)
            a = initial.ap
        return self.emit(eng, lambda e: e.tensor_tensor_scan(out=out.ap, data0=d0.ap, data1=d1.ap, initial=a, op0=op0, op1=op1),
                         [out], reads)

    def reduce(self, out, in_, op, axis=AX.X, eng="dve", **kw):
        return self.emit(eng, lambda e: e.tensor_reduce(out=out.ap, in_=in_.ap, axis=axis, op=op, **kw), [out], [in_])

    def recip(self, out, in_):
        return self.emit("dve", lambda e: e.reciprocal(out=out.ap, in_=in_.ap), [out], [in_])

    def close(self):
        self.es.close()


NT = 1024
D = 2048
DFF = 5632
EPS = 1e-6


class Tok:
    def __init__(self, c, n_norms, yf=False):
        self.c = c
        self.xT = c.sb("xT", [128, 16, NT], F32)
        self.hT = c.sb("hT", [128, 16, NT], BF16)
        self.gT = c.sb("gT", [128, 12, NT], BF16)
        self.rstd = c.sb("rstd", [128, NT], F32)
        self.sq = [c.sb("sq%d" % i, [128, NT], BF16) for i in range(2)]
        self.wa = [c.sb("wa%d" % i, [128, 16, 256], BF16) for i in range(2)]
        self.wb = [c.sb("wb%d" % i, [128, 16, 256], BF16) for i in range(2)]
        self.w2 = [c.sb("w2_%d" % i, [128, 12, 256], BF16) for i in range(2)]
        self.tmp = [c.sb("tmp%d" % i, [128, 512], F32) for i in range(2)]
        self.stage = [c.sb("stg%d" % i, [128, NT], F32) for i in range(2)]
        self.ones = c.sb("ones", [128, 128], BF16)
        self.norms = c.sb("norms", [128, n_norms, 16], F32)
        self.pb = [c.ps("pb%d" % i, [128, 512], F32) for i in range(8)]
        self.cnt = {"wa": 0, "wb": 0, "w2": 0, "tmp": 0, "sq": 0, "stage": 0}
        c.memset(self.ones.v(), 1.0)
        self.tasks = []
        self.out_stamps = []

    def rot(self, kind):
        lst = getattr(self, kind)
        i = self.cnt[kind]
        self.cnt[kind] = i + 1
        return lst[i % len(lst)]

    def add_task(self, load, compute):
        self.tasks.append((load, compute))

    def flush(self):
        ts = self.tasks
        self.tasks = []
        if not ts:
            return
        state = [None] * len(ts)
        state[0] = ts[0][0]()
        for i in range(len(ts)):
            if i + 1 < len(ts):
                state[i + 1] = ts[i + 1][0]()
            ts[i][1](state[i])

    def load_x(self, xT_d):
        c = self.c
        v = xT_d.t.rearrange("(c p) t -> p c t", p=128)
        for k in range(16):
            c.dma("sp", self.xT.v(self.xT.t[:, k, :], key=k), View(v[:, k, :], [xT_d.dep()]))

    def load_norms(self, norms_d):
        self.c.dma("sp", self.norms.v(), norms_d.v())

    def rmsnorm(self, ni, to_bf=True, out_fn=None):
        c = self.c
        xT = self.xT
        pss = [self.pb[0], self.pb[1]]
        for k in range(16):
            s = self.rot("sq")
            c.act(s.v(), xT.v(xT.t[:, k, :], key=k), AF.Square)
            for th in range(2):
                c.matmul(pss[th].v(), self.ones.v(), s.v(s.t[:, th * 512:(th + 1) * 512]), start=(k == 0), stop=(k == 15))
        for th in range(2):
            sl = slice(th * 512, (th + 1) * 512)
            c.act(self.rstd.v(self.rstd.t[:, sl], key=th), pss[th].v(), AF.Ln, bias=EPS, scale=1.0 / D)
            c.act(self.rstd.v(self.rstd.t[:, sl], key=th), self.rstd.v(self.rstd.t[:, sl], key=th), AF.Exp, scale=-0.5)
        for k in range(16):
            if to_bf:
                out = self.hT.v(self.hT.t[:, k, :], key=k)
            else:
                out = out_fn(k)
            c.stt(out, xT.v(xT.t[:, k, :], key=k), self.norms.v(self.norms.t[:, ni, k:k + 1]),
                  self.rstd.v(self.rstd.t[:], key=[0, 1]), ALU.mult, ALU.mult)
            if not to_bf:
                out_fn(k, done=True)

    def wview(self, w_d, c0, ncols, kchunks=16):
        v = w_d.t.rearrange("(c p) f -> p c f", p=128)
        return View(v[:, 0:kchunks, c0:c0 + ncols], [w_d.dep()])

    def ffn(self, ni, w1_d, w3_d, w2_d, splits=(6, 6, 5, 5)):
        c = self.c
        self.rmsnorm(ni)
        b0 = 0
        par = [0]
        for nb in splits:
            nj = nb * 2
            for b in range(b0, b0 + nb):
                def load(b=b):
                    ta = self.rot("wa")
                    tb = self.rot("wb")
                    c.dma("pool", ta.v(), self.wview(w1_d, b * 256, 256))
                    c.dma("pool", tb.v(), self.wview(w3_d, b * 256, 256))
                    return ta, tb

                def compute(st, b=b, b0=b0):
                    ta, tb = st
                    for jj in range(2):
                        jloc = (b - b0) * 2 + jj
                        p = par[0]
                        par[0] ^= 1
                        for th in range(2):
                            pa = self.pb[p * 4 + th * 2]
                            pbk = self.pb[p * 4 + th * 2 + 1]
                            for (wt, pp) in ((ta, pa), (tb, pbk)):
                                for k in range(16):
                                    c.matmul(pp.v(), wt.v(wt.t[:, k, jj * 128:(jj + 1) * 128]),
                                             self.hT.v(self.hT.t[:, k, th * 512:(th + 1) * 512], key=k),
                                             start=(k == 0), stop=(k == 15))
                            t = self.rot("tmp")
                            c.act(t.v(), pa.v(), AF.Silu)
                            c.tt(self.gT.v(self.gT.t[:, jloc, th * 512:(th + 1) * 512], key=(jloc, th)), t.v(), pbk.v(), ALU.mult)
                self.add_task(load, compute)
            for ib in range(8):
                def load(ib=ib, b0=b0, nj=nj):
                    t2 = self.rot("w2")
                    v = w2_d.t.rearrange("(j p) d -> p j d", p=128)
                    c.dma("pool", t2.v(t2.t[:, 0:nj, :]), View(v[:, b0 * 2:b0 * 2 + nj, ib * 256:(ib + 1) * 256], [w2_d.dep()]))
                    return t2

                def compute(t2, ib=ib, nj=nj):
                    for dd in range(2):
                        i = ib * 2 + dd
                        for th in range(2):
                            pp = self.pb[(i % 2) * 4 + th]
                            for j in range(nj):
                                c.matmul(pp.v(), t2.v(t2.t[:, j, dd * 128:(dd + 1) * 128]),
                                         self.gT.v(self.gT.t[:, j, th * 512:(th + 1) * 512], key=(j, th)),
                                         start=(j == 0), stop=(j == nj - 1))
                            xv = self.xT.v(self.xT.t[:, i, th * 512:(th + 1) * 512], key=i)
                            c.stt(xv, pp.v(), 0.5, xv, ALU.mult, ALU.add)
                self.add_task(load, compute)
            b0 += nb
        self.flush()

    def proj_out(self, ni, w_d, N, out_d):
        c = self.c
        self.rmsnorm(ni)
        nblk = (N + 255) // 256
        for b in range(nblk):
            ncols = min(256, N - b * 256)

            def load(b=b, ncols=ncols):
                ta = self.rot("wa")
                c.dma("pool", ta.v(ta.t[:, :, 0:ncols]), self.wview(w_d, b * 256, ncols))
                return ta

            def compute(ta, b=b, ncols=ncols):
                for jj in range((ncols + 127) // 128):
                    m = min(128, ncols - jj * 128)
                    n0 = b * 256 + jj * 128
                    sg = self.rot("stage")
                    for th in range(2):
                        pp = self.pb[(jj % 2) * 2 + th]
                        for k in range(16):
                            c.matmul(pp.v(pp.t[0:m, :]), ta.v(ta.t[:, k, jj * 128:jj * 128 + m]),
                                     self.hT.v(self.hT.t[:, k, th * 512:(th + 1) * 512], key=k),
                                     start=(k == 0), stop=(k == 15))
                        c.copy(sg.v(sg.t[0:m, th * 512:(th + 1) * 512]), pp.v(pp.t[0:m, :]), eng="act")
                    st = c.dma("sp", View(out_d.t[n0:n0 + m, :], [Dep()]), sg.v(sg.t[0:m, :]))
                    self.out_stamps.append(st)
            self.add_task(load, compute)
        self.flush()

    def store_x(self, out_d):
        c = self.c
        v = out_d.t.rearrange("(c p) t -> p c t", p=128)
        for k in range(16):
            st = c.dma("sp", View(v[:, k, :], [Dep()]), self.xT.v(self.xT.t[:, k, :], key=k))
            self.out_stamps.append(st)

    def final_norm(self, ni, out_d):
        c = self.c
        v = out_d.t.rearrange("(c p) t -> p c t", p=128)
        cur = {}

        def out_fn(k, done=False):
            if not done:
                cur[k] = self.rot("stage")
                return cur[k].v()
            st = c.dma("sp", View(v[:, k, :], [Dep()]), cur[k].v())
            self.out_stamps.append(st)
        self.rmsnorm(ni, to_bf=False, out_fn=out_fn)

    def mix_out(self, yT_d, w_out_d, even=None):
        c = self.c
        yv = yT_d.t.rearrange("(c p) t -> p c t", p=128)
        k0 = 0
        if even is not None:
            k0 = 8
        for k in range(k0, 16):
            c.dma("pool", self.hT.v(self.hT.t[:, k, :], key=k), View(yv[:, k, :], [yT_d.dep()]))
        if even is not None:
            self.s5_glu(yv, yT_d, even)
        src = []
        for k in range(16):
            if even is not None and k < 8:
                src.append((self.gT, k))
            else:
                src.append((self.hT, k))
        for b in range(8):
            def load(b=b):
                ta = self.rot("wa")
                c.dma("pool", ta.v(), self.wview(w_out_d, b * 256, 256))
                return ta

            def compute(ta, b=b):
                for jj in range(2):
                    i = b * 2 + jj
                    for th in range(2):
                        pp = self.pb[(i % 2) * 2 + th]
                        for k in range(16):
                            buf, kk = src[k]
                            key = kk if buf is self.hT else [(kk, 0), (kk, 1)]
                            c.matmul(pp.v(), ta.v(ta.t[:, k, jj * 128:(jj + 1) * 128]),
                                     buf.v(buf.t[:, kk, th * 512:(th + 1) * 512], key=key),
                                     start=(k == 0), stop=(k == 15))
                        xv = self.xT.v(self.xT.t[:, i, th * 512:(th + 1) * 512], key=i)
                        c.stt(xv, pp.v(), 1.0, xv, ALU.mult, ALU.add)
            self.add_task(load, compute)
        self.flush()

    def s5_glu(self, yv, yT_d, even):
        c = self.c
        yf = even["yf"]
        t1 = even["t1"]
        bglu = even["bglu"]
        wglu_d = even["wglu"]
        for th in range(2):
            sl = slice(th * 512, (th + 1) * 512)
            for k in range(8):
                yk = yf.v(yf.t[:, k, :], key=k)
                c.dma("sp", yk, View(yv[:, k, sl], [yT_d.dep()]))
                c.tt(t1.v(), yk, yk, ALU.mult)
                c.ts(t1.v(), t1.v(), 0.044715, 1.0, ALU.mult, ALU.add)
                c.tt(t1.v(), t1.v(), yk, ALU.mult)
                c.act(t1.v(), t1.v(), AF.Sigmoid, scale=2.0 * 0.7978845608028654)
                c.tt(yk, yk, t1.v(), ALU.mult)
                c.copy(self.hT.v(self.hT.t[:, k, sl], key=k), yk, eng="act")
            for b in range(4):
                ta = self.rot("wa")
                c.dma("pool", ta.v(ta.t[:, 0:8, :]), self.wview(wglu_d, b * 256, 256, kchunks=8))
                for jj in range(2):
                    n = b * 2 + jj
                    pp = self.pb[4 + (n % 2)]
                    for k in range(8):
                        c.matmul(pp.v(), ta.v(ta.t[:, k, jj * 128:(jj + 1) * 128]), self.hT.v(self.hT.t[:, k, sl], key=k),
                                 start=(k == 0), stop=(k == 7))
                    t = self.rot("tmp")
                    c.act(t.v(), pp.v(), AF.Sigmoid, bias=bglu.v(bglu.t[:, n:n + 1]))
                    c.tt(self.gT.v(self.gT.t[:, n, sl], key=(n, th)), t.v(), yf.v(yf.t[:, n, :], key=n), ALU.mult)

import math

NEG = -1.0e30


class MixBase:
    def __init__(self, c, L):
        self.c = c
        self.L = L
        self.n = L // 64
        self.nseg = L // 512
        self.ones = c.sb("ones", [128, 128], F32)
        self.ident = c.sb("ident", [128, 128], F32)
        self.negmask = c.sb("negmask", [64, 64], F32)
        self.zeros = c.sb("zeros", [128, 64], F32)
        c.memset(self.ones.v(), 1.0)
        c.memset(self.zeros.v(), 0.0)
        c.emit("pool", lambda e: e.affine_select(out=self.ident.t[:], in_=self.ones.t[:], pattern=[[1, 128]],
                                                 compare_op=ALU.is_equal, fill=0.0, base=0, channel_multiplier=-1),
               [self.ident.v()], [self.ones.v()])
        c.emit("pool", lambda e: e.affine_select(out=self.negmask.t[:], in_=self.zeros.t[0:64, :], pattern=[[1, 64]],
                                                 compare_op=ALU.is_ge, fill=NEG, base=0, channel_multiplier=-1),
               [self.negmask.v()], [self.zeros.v()])
        self.pb = [c.ps("pb%d" % i, [128, 512], F32) for i in range(8)]
        self.out_stamps = []
        self.uid = 0

    def dbg(self, nm, buf, shape):
        d = self.c.dram("dbg_" + nm, list(shape), kind="ExternalOutput")
        st = self.c.dma("sp", View(d.t, [Dep()]), buf.all())
        self.out_stamps.append(st)

    def name(self, s):
        self.uid += 1
        return "%s_%d" % (s, self.uid)

    def mk_(self, s, shape, dt=F32):
        return self.c.sb(self.name(s), shape, dt)

    def rep_chunk(self, colN, out, bank):
        c, n = self.c, self.n
        t = self.mk_("rep", [n, 128])
        c.ts(t.v(), self.ones.v(self.ones.t[0:n, :]), colN, None, ALU.mult)
        pp = self.pb[bank]
        c.matmul(pp.v(pp.t[:, 0:n]), t.v(), self.ident.v(self.ident.t[0:n, 0:n]))
        c.copy(out, pp.v(pp.t[:, 0:n]), eng="act")

    def to_col(self, XN, out, bank):
        c, n = self.c, self.n
        pp = self.pb[bank]
        c.transpose(pp.v(pp.t[0:64, 0:n]), XN, self.ident.v(self.ident.t[0:n, 0:n]))
        c.copy(out, pp.v(pp.t[0:64, 0:n]), eng="act")

    def to_row_seg(self, XN_buf, seg, out, bank, E):
        c, n = self.c, self.n
        j0 = seg * 8
        c.tt(E.v(), self.ident.v(self.ident.t[0:n, j0:j0 + 8].unsqueeze(2).to_broadcast([n, 8, 64])),
             XN_buf.v(XN_buf.t[:, :].unsqueeze(1).to_broadcast([n, 8, 64])), ALU.mult)
        pp = self.pb[bank]
        c.matmul(pp.v(), self.ones.v(self.ones.t[0:n, :]), E.v(E.t[:].rearrange("p a b -> p (a b)")))
        c.copy(out, pp.v(), eng="act")

    def chunk_loop(self, seg, QtT, ST, V, Kc, decay, S, dvx, O, pre_hook=None, post_chunk=None):
        c = self.c
        for j in range(8):
            g = seg * 8 + j
            cur = S["bufs"][S["i"] % 2]
            nxt = S["bufs"][(S["i"] + 1) % 2]
            S["i"] += 1
            Vj = V.v(V.t[:, j, 0:dvx], key=j)
            if pre_hook is not None:
                Vj = pre_hook(j, g, cur)
            po = self.pb[1 + (g % 2)]
            pov = po.v(po.t[0:64, 0:dvx])
            c.matmul(pov, QtT.v(QtT.t[:, j * 64:(j + 1) * 64]), cur.v(cur.t[:, 0:dvx]), start=True, stop=False)
            c.matmul(pov, ST.v(ST.t[:, j, :], key=j), Vj, start=False, stop=True)
            c.copy(O.v(O.t[:, j, 0:dvx], key=j), pov, eng="act")
            pu = self.pb[3 + (g % 2)]
            puv = pu.v(pu.t[:, 0:dvx])
            c.matmul(puv, Kc.v(Kc.t[:, j, :], key=j), Vj)
            c.stt(nxt.v(nxt.t[:, 0:dvx]), cur.v(cur.t[:, 0:dvx]), decay(g), puv, ALU.mult, ALU.add)
            if post_chunk is not None:
                post_chunk()

    def scores(self, KT, QT, DT, ST, bank=0):
        c = self.c
        pp = self.pb[bank]
        for j in range(8):
            c.matmul(pp.v(pp.t[0:64, j * 64:(j + 1) * 64], key=j), KT.v(KT.t[:, j * 64:(j + 1) * 64]), QT.v(QT.t[:, j * 64:(j + 1) * 64]))
        c.tt(ST.all(), pp.all(pp.t[0:64, :].rearrange("p (a b) -> p a b", a=8)), DT.all(), ALU.mult)

    def sincos_tables(self, ang, sin_out, cos_out, tmp_f, tmp_i):
        c = self.c
        for (off, out) in ((0.5, sin_out), (0.75, cos_out)):
            c.ts(tmp_f, ang, 1.0 / (2 * math.pi), off, ALU.mult, ALU.add)
            c.copy(tmp_i, tmp_f)
            c.copy(out, tmp_i)
            c.tt(tmp_f, tmp_f, out, ALU.subtract)
            c.act(out, tmp_f, AF.Sign)
            c.ts(tmp_f, tmp_f, 2 * math.pi, None, ALU.mult)
            c.stt(tmp_f, out, -math.pi, tmp_f, ALU.mult, ALU.add)
            c.ts(tmp_f, tmp_f, math.pi, -math.pi, ALU.min, ALU.max)
            c.act(out, tmp_f, AF.Sin)


def dview(d, ap):
    return View(ap, [d.dep()])


class Odd(MixBase):
    def __init__(self, c, L):
        super().__init__(c, L)
        n = self.n
        D = c.dram
        self.i = dict(
            m_qT=D("m_qT", [128, L], kind="ExternalInput"), m_kT=D("m_kT", [128, L], kind="ExternalInput"),
            m_k=D("m_k", [L, 128], kind="ExternalInput"), m_v=D("m_v", [L, 256], kind="ExternalInput"),
            m_o=D("m_o", [L, 256], kind="ExternalInput"), m_gN=D("m_gN", [n, 2, 64], kind="ExternalInput"),
            m_par=D("m_par", [128, 2], kind="ExternalInput"), m_ng=D("m_ng", [64, 256], kind="ExternalInput"),
            r_qT=D("r_qT", [128, L], kind="ExternalInput"), r_qTs=D("r_qTs", [128, L], kind="ExternalInput"),
            r_kT=D("r_kT", [128, L], kind="ExternalInput"), r_kTs=D("r_kTs", [128, L], kind="ExternalInput"),
            r_v=D("r_v", [L, 256], kind="ExternalInput"), r_g=D("r_g", [L, 256], kind="ExternalInput"),
            pos=D("pos", [128, L], I32, kind="ExternalInput"), hidx=D("hidx", [128, 1], kind="ExternalInput"),
            r_ng=D("r_ng", [64, 256], kind="ExternalInput"),
        )
        self.yc = D("yc", [L, 256], kind="ExternalOutput")
        self.yd = D("yd", [L, 256], kind="ExternalOutput")

    def tokview(self, d, seg, w):
        return dview(d, d.t[seg * 512:(seg + 1) * 512, :].rearrange("(a p) d -> p a d", p=64)[:, :, 0:w])

    def mlstm(self):
        c, n, L = self.c, self.n, self.L
        I = self.i
        T = self.mk_
        par = T("mpar", [128, 2])
        c.dma("sp", par.v(), I["m_par"].v())
        gN = T("gN", [n, 2, 64])
        c.dma("sp", gN.v(), I["m_gN"].v())
        ng = T("mng", [64, 256])
        c.dma("sp", ng.v(), I["m_ng"].v())
        nbf = T("nbf", [128, 1])
        c.ts(nbf.v(), par.v(par.t[:, 1:2]), -1.0, None, ALU.mult)
        ig = T("ig", [n, 64]); sp = T("sp", [n, 64]); csp = T("csp", [n, 64]); bcum = T("bcum", [n, 64])
        colB = T("colB", [n, 64]); cmx = T("cmx", [n, 64]); imax = T("imax", [n, 64]); ul = T("ul", [n, 64])
        umax = T("umax", [n, 1]); blast = T("blast", [n, 1])
        c.ts(ig.v(), gN.v(gN.t[:, 0, :]), par.v(par.t[0:n, 0:1]), None, ALU.add)
        c.act(sp.v(), gN.v(gN.t[:, 1, :]), AF.Exp, bias=nbf.v(nbf.t[0:n, :]), scale=-1.0)
        c.act(sp.v(), sp.v(), AF.Ln, bias=1.0)
        c.scan(csp.v(), self.ones.v(self.ones.t[0:n, 0:64]), sp.v(), 0.0, ALU.mult, ALU.add)
        c.ts(bcum.v(), csp.v(), -1.0, None, ALU.mult)
        c.tt(colB.v(), ig.v(), csp.v(), ALU.add)
        c.scan(cmx.v(), self.zeros.v(self.zeros.t[0:n, :]), colB.v(), NEG, ALU.add, ALU.max)
        c.tt(imax.v(), bcum.v(), cmx.v(), ALU.add)
        c.copy(blast.v(), bcum.v(bcum.t[:, 63:64]))
        c.ts(ul.v(), colB.v(), blast.v(), None, ALU.add)
        c.reduce(umax.v(), ul.v(), ALU.max)
        Rb = T("Rb", [128, n]); Ru = T("Ru", [128, n]); Mn = T("Mn", [128, n]); Mo = T("Mo", [128, n]); Rc = T("Rc", [128, n])
        self.rep_chunk(blast.v(), Rb.v(), 5)
        self.rep_chunk(umax.v(), Ru.v(), 6)
        c.scan(Mn.v(), Rb.v(), Ru.v(), 0.0, ALU.add, ALU.max)
        c.memset(Mo.v(), 0.0)
        if n > 1:
            c.copy(Mo.v(Mo.t[:, 1:n]), Mn.v(Mn.t[:, 0:n - 1]))
        c.tt(Rc.v(), Rb.v(), Mo.v(), ALU.add)
        c.tt(Rc.v(), Rc.v(), Mn.v(), ALU.subtract)
        c.act(Rc.v(), Rc.v(), AF.Exp)
        bcumC = T("bcumC", [64, n]); colBC = T("colBC", [64, n]); imaxC = T("imaxC", [64, n]); ulC = T("ulC", [64, n])
        self.to_col(bcum.v(), bcumC.v(), 5)
        self.to_col(colB.v(), colBC.v(), 6)
        self.to_col(imax.v(), imaxC.v(), 5)
        self.to_col(ul.v(), ulC.v(), 6)
        msC = T("msC", [64, n]); enC = T("enC", [64, n]); kwsC = T("kwsC", [64, n])
        c.tt(msC.v(), bcumC.v(), Mo.v(Mo.t[0:64, :]), ALU.add)
        c.tt(msC.v(), msC.v(), imaxC.v(), ALU.max)
        c.act(enC.v(), msC.v(), AF.Exp, scale=-1.0)
        c.tt(kwsC.v(), ulC.v(), Mn.v(Mn.t[0:64, :]), ALU.subtract)
        c.act(kwsC.v(), kwsC.v(), AF.Exp)
        if getattr(self, "debug", False):
            self.dbg("bcum", bcum, [n, 64]); self.dbg("Mn", Mn, [128, n]); self.dbg("Rc", Rc, [128, n])
            self.dbg("msC", msC, [64, n]); self.dbg("kwsC", kwsC, [64, n]); self.dbg("imax", imax, [n, 64])
        E = T("E", [n, 8, 64])
        bR = T("bR", [128, 512]); iR = T("iR", [128, 512]); msR = T("msR", [128, 512])
        QT = T("mQT", [128, 512]); QtT = T("mQtT", [128, 512]); KT = T("mKT", [128, 512])
        Kt = T("mKt", [64, 8, 128]); Kc = T("mKc", [64, 8, 128]); V = T("mV", [64, 8, 258]); O = T("mO", [64, 8, 258])
        G = T("mG", [64, 8, 256]); DT = T("mDT", [64, 8, 64]); ST = T("mST", [64, 8, 64])
        H = T("mH", [64, 8, 256]); s8 = T("ms8", [64, 8]); s8b = T("ms8b", [64, 8]); sqm = T("msq", [64, 8, 256])
        S = {"bufs": [T("mS0", [128, 258]), T("mS1", [128, 258])], "i": 0}
        c.memset(S["bufs"][0].v(), 0.0)
        for seg in range(self.nseg):
            sl = slice(seg * 512, (seg + 1) * 512)
            cs = slice(seg * 8, seg * 8 + 8)
            c.dma("sp", QT.v(), dview(I["m_qT"], I["m_qT"].t[:, sl]))
            c.dma("sp", KT.v(), dview(I["m_kT"], I["m_kT"].t[:, sl]))
            c.dma("sp", Kt.v(), self.tokview(I["m_k"], seg, 128))
            c.dma("sp", V.v(V.t[:, :, 0:256]), self.tokview(I["m_v"], seg, 256))
            c.memset(V.v(V.t[:, :, 256:257]), 1.0)
            c.memset(V.v(V.t[:, :, 257:258]), 0.0)
            c.dma("sp", G.v(), self.tokview(I["m_o"], seg, 256))
            self.to_row_seg(bcum, seg, bR.v(), 5, E)
            self.to_row_seg(imax, seg, iR.v(), 6, E)
            mo_b = Mo.v(Mo.t[:, cs].unsqueeze(2).to_broadcast([128, 8, 64]))
            b3 = bR.v(bR.t[:].rearrange("p (a b) -> p a b", a=8))
            ms3 = msR.v(msR.t[:].rearrange("p (a b) -> p a b", a=8))
            i3 = iR.v(iR.t[:].rearrange("p (a b) -> p a b", a=8))
            c.tt(i3, i3, i3, ALU.max) if False else None
            c.tt(ms3, b3, mo_b, ALU.add)
            c.tt(iR.v(), msR.v(), iR.v(), ALU.max)
            c.tt(msR.v(), msR.v(), iR.v(), ALU.subtract)
            c.act(msR.v(), msR.v(), AF.Exp)
            c.tt(bR.v(), bR.v(), iR.v(), ALU.subtract)
            c.ts(QT.v(), QT.v(), 128 ** -0.5, None, ALU.mult)
            c.tt(QtT.v(), QT.v(), msR.v(), ALU.mult)
            c.tt(DT.v(), bR.v(bR.t[0:64, :].rearrange("p (a b) -> p a b", a=8)),
                 self.negmask.v(self.negmask.t[:, :].unsqueeze(1).to_broadcast([64, 8, 64])), ALU.add)
            for j in range(8):
                c.act(DT.v(DT.t[:, j, :]), DT.v(DT.t[:, j, :]), AF.Exp, bias=colBC.v(colBC.t[:, seg * 8 + j:seg * 8 + j + 1]))
            self.scores(KT, QT, DT, ST)
            c.tt(Kc.v(), Kt.v(), kwsC.v(kwsC.t[:, cs].unsqueeze(2).to_broadcast([64, 8, 128])), ALU.mult)
            if getattr(self, "debug", False) and seg == 0:
                self.dbg("rowA", bR, [128, 512]); self.dbg("iw", msR, [128, 512]); self.dbg("DT", DT, [64, 8, 64]); self.dbg("ST", ST, [64, 8, 64])
            self.chunk_loop(seg, QtT, ST, V, Kc, lambda g: Rc.v(Rc.t[:, g:g + 1]), S, 258, O)
            if getattr(self, "debug", False) and seg == 0:
                self.dbg("O", O, [64, 8, 258])
            c.ts(s8.v(), O.all(O.t[:, :, 256]), -1.0, None, ALU.mult)
            c.tt(s8.v(), s8.v(), O.all(O.t[:, :, 256]), ALU.max)
            c.tt(s8.v(), s8.v(), enC.v(enC.t[:, cs]), ALU.max)
            c.recip(s8.v(), s8.v())
            c.tt(H.v(), O.all(O.t[:, :, 0:256]), s8.v(s8.t[:, :].unsqueeze(2).to_broadcast([64, 8, 256])), ALU.mult)
            self.post_rms(H, s8, s8b, ng, G, AF.Sigmoid, 256, False, sqm)
            st = c.dma("sp", View(self.yc.t[sl, :].rearrange("(a p) d -> p a d", p=64), [Dep()]), H.v())
            self.out_stamps.append(st)

    def post_rms(self, H, s8, s8b, ng, G, gate_fn, dv, center, sq):
        c = self.c
        if center:
            c.reduce(s8.v(), H.v(), ALU.add)
            c.ts(s8.v(), s8.v(), -1.0 / dv, None, ALU.mult)
            c.tt(H.v(), H.v(), s8.v(s8.t[:, :].unsqueeze(2).to_broadcast([64, 8, dv])), ALU.add)
        c.tt(sq.v(), H.v(), H.v(), ALU.mult)
        c.reduce(s8b.v(), sq.v(), ALU.add)
        c.act(s8b.v(), s8b.v(), AF.Ln, bias=1e-6, scale=1.0 / dv)
        c.act(s8b.v(), s8b.v(), AF.Exp, scale=-0.5)
        c.tt(H.v(), H.v(), s8b.v(s8b.t[:, :].unsqueeze(2).to_broadcast([64, 8, dv])), ALU.mult)
        c.tt(H.v(), H.v(), ng.v(ng.t[:, :].unsqueeze(1).to_broadcast([64, 8, dv])), ALU.mult)
        if gate_fn == "silu":
            c.act(sq.v(), G.v(), AF.Silu)
        else:
            c.act(sq.v(), G.v(), gate_fn)
        c.tt(H.v(), H.v(), sq.v(), ALU.mult)

    def retention(self):
        c, n, L = self.c, self.n, self.L
        I = self.i
        T = self.mk_
        hid = T("hid", [128, 1]); lg = T("lg", [128, 1])
        c.dma("sp", hid.v(), I["hidx"].v())
        ng = T("rng", [64, 256])
        c.dma("sp", ng.v(), I["r_ng"].v())
        c.act(lg.v(), hid.v(), AF.Exp, scale=-math.log(2.0), bias=None)
        c.ts(lg.v(), lg.v(), 2.0 ** -5, None, ALU.mult)
        c.act(lg.v(), lg.v(), AF.Ln, scale=-1.0, bias=1.0)
        io_cs = T("io_cs", [64, 64], I32); f_cs = T("f_cs", [64, 64])
        c.emit("pool", lambda e: e.iota(io_cs.t[:], pattern=[[1, 64]], base=0, channel_multiplier=-1), [io_cs.v()], [])
        c.copy(f_cs.v(), io_cs.v())
        DT1 = T("DT1", [64, 64])
        c.act(DT1.v(), f_cs.v(), AF.Exp, scale=lg.v(lg.t[0:64, :]))
        c.emit("pool", lambda e: e.affine_select(out=DT1.t[:], in_=DT1.t[:], pattern=[[1, 64]], compare_op=ALU.is_ge,
                                                 fill=0.0, base=0, channel_multiplier=-1), [DT1.v()], [DT1.v()])
        DT = T("rDT", [64, 8, 64])
        c.copy(DT.v(), DT1.v(DT1.t[:, :].unsqueeze(1).to_broadcast([64, 8, 64])))
        io_r = T("io_r", [128, 64], I32); xi = T("xi", [128, 64])
        c.emit("pool", lambda e: e.iota(io_r.t[:], pattern=[[1, 64]], base=1, channel_multiplier=0), [io_r.v()], [])
        c.copy(xi.v(), io_r.v())
        c.act(xi.v(), xi.v(), AF.Exp, scale=lg.v())
        c.ts(xi.v(), xi.v(), 128 ** -0.5, None, ALU.mult)
        io_p = T("io_p", [128, 1], I32); zeta = T("zeta", [128, 1]); gc = T("gc", [128, 1]); invf = T("invf", [128, 1])
        c.emit("pool", lambda e: e.iota(io_p.t[0:64, :], pattern=[[0, 1]], base=63, channel_multiplier=-1), [io_p.v()], [])
        c.copy(zeta.v(zeta.t[0:64, :]), io_p.v(io_p.t[0:64, :]))
        c.act(zeta.v(zeta.t[0:64, :]), zeta.v(zeta.t[0:64, :]), AF.Exp, scale=lg.v(lg.t[0:64, :]))
        c.act(gc.v(), lg.v(), AF.Exp, scale=64.0)
        for h0 in (0, 64):
            c.emit("pool", lambda e, h0=h0: e.iota(io_p.t[h0:h0 + 64, :], pattern=[[0, 1]], base=0, channel_multiplier=1), [io_p.v()], [])
        c.copy(invf.v(), io_p.v())
        c.act(invf.v(), invf.v(), AF.Exp, scale=-math.log(10000.0) / 64.0)
        sgn = T("sgn", [128, 1])
        c.memset(sgn.v(sgn.t[0:64, :]), -1.0)
        c.memset(sgn.v(sgn.t[64:128, :]), 1.0)
        posi = T("posi", [128, 512], I32); ang = T("ang", [128, 512]); sn = T("sn", [128, 512]); cs_ = T("cs", [128, 512])
        tf = T("tf", [128, 512]); ti = T("ti", [128, 512], I32)
        QT = T("rQT", [128, 512]); QTs = T("rQTs", [128, 512]); KT = T("rKT", [128, 512]); KTs = T("rKTs", [128, 512])
        QtT = T("rQtT", [128, 512]); Kc = T("rKc", [64, 8, 128]); V = T("rV", [64, 8, 256]); O = T("rO", [64, 8, 256])
        G = T("rG", [64, 8, 256]); ST = T("rST", [64, 8, 64]); s8 = T("rs8", [64, 8]); s8b = T("rs8b", [64, 8]); sqr = T("rsq", [64, 8, 256])
        S = {"bufs": [T("rS0", [128, 256]), T("rS1", [128, 256])], "i": 0}
        c.memset(S["bufs"][0].v(), 0.0)
        for seg in range(self.nseg):
            sl = slice(seg * 512, (seg + 1) * 512)
            c.dma("sp", posi.v(), dview(I["pos"], I["pos"].t[:, sl]))
            for (tl, nm) in ((QT, "r_qT"), (QTs, "r_qTs"), (KT, "r_kT"), (KTs, "r_kTs")):
                c.dma("sp", tl.v(), dview(I[nm], I[nm].t[:, sl]))
            c.dma("sp", V.v(), self.tokview(I["r_v"], seg, 256))
            c.dma("sp", G.v(), self.tokview(I["r_g"], seg, 256))
            c.copy(ang.v(), posi.v())
            c.ts(ang.v(), ang.v(), invf.v(), None, ALU.mult)
            self.sincos_tables(ang.v(), sn.v(), cs_.v(), tf.v(), ti.v())
            c.ts(sn.v(), sn.v(), sgn.v(), None, ALU.mult)
            for (A, As) in ((QT, QTs), (KT, KTs)):
                c.tt(A.v(), A.v(), cs_.v(), ALU.mult)
                c.tt(As.v(), As.v(), sn.v(), ALU.mult)
                c.tt(A.v(), A.v(), As.v(), ALU.add)
            c.tt(QtT.v(QtT.t[:].rearrange("p (a b) -> p a b", a=8)), QT.v(QT.t[:].rearrange("p (a b) -> p a b", a=8)),
                 xi.v(xi.t[:, :].unsqueeze(1).to_broadcast([128, 8, 64])), ALU.mult)
            c.ts(QT.v(), QT.v(), 128 ** -0.5, None, ALU.mult)
            self.scores(KT, QT, DT, ST)
            for j in range(8):
                pp = self.pb[5 + (j % 2)]
                c.transpose(pp.v(pp.t[0:64, 0:128]), KT.v(KT.t[:, j * 64:(j + 1) * 64]), self.ident.v())
                c.ts(Kc.v(Kc.t[:, j, :], key=j), pp.v(pp.t[0:64, 0:128]), zeta.v(zeta.t[0:64, :]), None, ALU.mult)
            self.chunk_loop(seg, QtT, ST, V, Kc, lambda g: gc.v(), S, 256, O)
            self.post_rms(O, s8, s8b, ng, G, "silu", 256, True, sqr)
            st = c.dma("sp", View(self.yd.t[sl, :].rearrange("(a p) d -> p a d", p=64), [Dep()]), O.all())
            self.out_stamps.append(st)


class Even(MixBase):
    def __init__(self, c, L, which="both"):
        super().__init__(c, L)
        n = self.n
        D = c.dram
        self.i = {}
        if which in ("both", "g"):
            self.i.update(
                g_qT=D("g_qT", [2, 128, L], kind="ExternalInput"), g_kT=D("g_kT", [2, 128, L], kind="ExternalInput"),
                g_vT=D("g_vT", [2, 128, L], kind="ExternalInput"), g_z=D("g_z", [2, L, 128], kind="ExternalInput"),
                g_cw=D("g_cw", [2, 128, 3, 4], kind="ExternalInput"), g_gN=D("g_gN", [2, n, 2, 64], kind="ExternalInput"),
                g_par=D("g_par", [2, 128, 2], kind="ExternalInput"), g_ng=D("g_ng", [64, 128], kind="ExternalInput"))
            self.yb = D("yb", [2, L, 128], kind="ExternalOutput")
        if which in ("both", "s"):
            self.i.update(
                s_uT=D("s_uT", [256, L], kind="ExternalInput"), s_par=D("s_par", [8, 128, 3], kind="ExternalInput"),
                s_B=D("s_B", [8, 128, 2, 16], kind="ExternalInput"), s_C=D("s_C", [8, 128, 2, 16], kind="ExternalInput"),
                s_d=D("s_d", [8, 32, 1], kind="ExternalInput"))
            self.ya = D("yaT", [256, L], kind="ExternalOutput")
        self.strict01 = c.sb("strict01", [64, 64], F32)
        c.emit("pool", lambda e: e.affine_select(out=self.strict01.t[:], in_=self.ones.t[0:64, 0:64], pattern=[[1, 64]],
                                                 compare_op=ALU.is_gt, fill=0.0, base=0, channel_multiplier=-1),
               [self.strict01.v()], [self.ones.v()])

    def gdn_alloc(self):
        T = self.mk_
        n = self.n
        t = {}
        for nm in ("cw",):
            t[nm] = T(nm, [128, 3, 4])
        t["par"] = T("gpar", [128, 2]); t["gN"] = T("ggN", [n, 2, 64]); t["ng"] = T("gng", [64, 128])
        for nm in ("beta", "sp", "gg", "gcum"):
            t[nm] = T("g" + nm, [n, 64])
        t["ea"] = T("gea", [128, 1]); t["glast"] = T("gglast", [n, 1])
        for nm in ("Rgl", "Rdec"):
            t[nm] = T("g" + nm, [128, n])
        for nm in ("gcumC", "ngcumC", "betaC", "egC", "bgC", "kdC"):
            t[nm] = T("g" + nm, [64, n])
        t["E"] = T("gE", [n, 8, 64])
        for nm in ("Xq", "Xk", "Xv"):
            t[nm] = T("g" + nm, [128, 515])
        for nm in ("QT", "KT", "VT", "KbT", "QdT", "tmp", "gR", "bR", "WT"):
            t[nm] = T("g" + nm, [128, 512])
        for nm in ("DT", "DTsn", "ST", "X8", "Z0", "Z1", "Y0", "Y1", "P"):
            t[nm] = T("g" + nm, [64, 8, 64])
        for nm in ("Kbg", "Kd", "Vb", "U", "Vn", "O", "G", "sq"):
            t[nm] = T("g" + nm, [64, 8, 128])
        t["s8"] = T("gs8", [64, 8]); t["s8b"] = T("gs8b", [64, 8])
        t["S"] = [T("gS0", [128, 128]), T("gS1", [128, 128])]
        return t

    def gdn_head(self, hh, t):
        c, n, L = self.c, self.n, self.L
        I = self.i

        def dv(nm, ap):
            return dview(I[nm], ap)
        c.dma("sp", t["cw"].v(), dv("g_cw", I["g_cw"].t[hh]))
        c.dma("sp", t["par"].v(), dv("g_par", I["g_par"].t[hh]))
        c.dma("sp", t["gN"].v(), dv("g_gN", I["g_gN"].t[hh]))
        c.dma("sp", t["ng"].v(), I["g_ng"].v())
        par, gN = t["par"], t["gN"]
        beta, sp, gg, gcum = t["beta"], t["sp"], t["gg"], t["gcum"]
        c.act(beta.v(), gN.v(gN.t[:, 0, :]), AF.Sigmoid)
        c.act(sp.v(), gN.v(gN.t[:, 1, :]), AF.Exp, bias=par.v(par.t[0:n, 1:2]))
        c.act(sp.v(), sp.v(), AF.Ln, bias=1.0)
        c.act(t["ea"].v(), par.v(par.t[:, 0:1]), AF.Exp)
        c.ts(gg.v(), sp.v(), t["ea"].v(t["ea"].t[0:n, :]), -1.0, ALU.mult, ALU.mult)
        c.scan(gcum.v(), self.ones.v(self.ones.t[0:n, 0:64]), gg.v(), 0.0, ALU.mult, ALU.add)
        c.copy(t["glast"].v(), gcum.v(gcum.t[:, 63:64]))
        self.rep_chunk(t["glast"].v(), t["Rgl"].v(), 5)
        c.act(t["Rdec"].v(), t["Rgl"].v(), AF.Exp)
        self.to_col(gcum.v(), t["gcumC"].v(), 5)
        self.to_col(beta.v(), t["betaC"].v(), 6)
        c.ts(t["ngcumC"].v(), t["gcumC"].v(), -1.0, None, ALU.mult)
        c.act(t["egC"].v(), t["gcumC"].v(), AF.Exp)
        c.tt(t["bgC"].v(), t["betaC"].v(), t["egC"].v(), ALU.mult)
        c.tt(t["kdC"].v(), t["Rgl"].v(t["Rgl"].t[0:64, :]), t["gcumC"].v(), ALU.subtract)
        c.act(t["kdC"].v(), t["kdC"].v(), AF.Exp)
        S = {"bufs": t["S"], "i": 0}
        c.memset(t["S"][0].v(), 0.0)
        QT, KT, VT, KbT, QdT, tmp, gR, bR, WT = (t[k] for k in ("QT", "KT", "VT", "KbT", "QdT", "tmp", "gR", "bR", "WT"))
        DT, DTsn, ST, X8, P = (t[k] for k in ("DT", "DTsn", "ST", "X8", "P"))
        Kbg, Kd, Vb, U, Vn, O, G = (t[k] for k in ("Kbg", "Kd", "Vb", "U", "Vn", "O", "G"))
        for seg in range(self.nseg):
            sl = slice(seg * 512, (seg + 1) * 512)
            cs = slice(seg * 8, seg * 8 + 8)
            for ci, (X, src, dst) in enumerate(((t["Xq"], "g_qT", QT), (t["Xk"], "g_kT", KT), (t["Xv"], "g_vT", VT))):
                if seg == 0:
                    c.memset(X.v(X.t[:, 0:3]), 0.0)
                    c.dma("sp", X.v(X.t[:, 3:515]), dv(src, I[src].t[hh, :, 0:512]))
                else:
                    c.dma("sp", X.v(), dv(src, I[src].t[hh, :, seg * 512 - 3:seg * 512 + 512]))
                cw = t["cw"]
                c.ts(dst.v(), X.v(X.t[:, 0:512]), cw.v(cw.t[:, ci, 0:1]), None, ALU.mult)
                for j in range(1, 4):
                    c.stt(dst.v(), X.v(X.t[:, j:j + 512]), cw.v(cw.t[:, ci, j:j + 1]), dst.v(), ALU.mult, ALU.add)
                c.act(dst.v(), dst.v(), AF.Silu)
                if ci < 2:
                    c.tt(tmp.v(), dst.v(), dst.v(), ALU.mult)
                    pp = self.pb[5 + ci]
                    c.matmul(pp.v(), self.ones.v(), tmp.v())
                    c.act(tmp.v(), pp.v(), AF.Ln, bias=1e-6)
                    c.act(tmp.v(), tmp.v(), AF.Exp, scale=-0.5)
                    if ci == 0:
                        c.stt(dst.v(), dst.v(), 128 ** -0.5, tmp.v(), ALU.mult, ALU.mult)
                    else:
                        c.tt(dst.v(), dst.v(), tmp.v(), ALU.mult)
            c.dma("sp", G.v(), dv("g_z", I["g_z"].t[hh, sl, :].rearrange("(a p) d -> p a d", p=64)))
            self.to_row_seg(gcum, seg, gR.v(), 5, t["E"])
            self.to_row_seg(beta, seg, bR.v(), 6, t["E"])
            c.tt(KbT.v(), KT.v(), bR.v(), ALU.mult)
            c.act(tmp.v(), gR.v(), AF.Exp)
            c.tt(QdT.v(), QT.v(), tmp.v(), ALU.mult)
            c.tt(DT.v(), gR.v(gR.t[0:64, :].rearrange("p (a b) -> p a b", a=8)),
                 self.negmask.v(self.negmask.t[:, :].unsqueeze(1).to_broadcast([64, 8, 64])), ALU.add)
            for j in range(8):
                g = seg * 8 + j
                c.act(DT.v(DT.t[:, j, :]), DT.v(DT.t[:, j, :]), AF.Exp, bias=t["ngcumC"].v(t["ngcumC"].t[:, g:g + 1]))
            c.stt(DTsn.v(), DT.v(), -1.0, self.strict01.v(self.strict01.t[:, :].unsqueeze(1).to_broadcast([64, 8, 64])), ALU.mult, ALU.mult)
            self.scores(KT, QT, DT, ST)
            self.scores(KT, KbT, DTsn, X8)
            for j in range(8):
                g = seg * 8 + j
                pp = self.pb[5 + (j % 2)]
                c.transpose(pp.v(pp.t[0:64, 0:128]), KT.v(KT.t[:, j * 64:(j + 1) * 64]), self.ident.v())
                c.transpose(pp.v(pp.t[0:64, 128:256]), VT.v(VT.t[:, j * 64:(j + 1) * 64]), self.ident.v())
                c.ts(Kbg.v(Kbg.t[:, j, :], key=j), pp.v(pp.t[0:64, 0:128]), t["bgC"].v(t["bgC"].t[:, g:g + 1]), None, ALU.mult)
                c.ts(Kd.v(Kd.t[:, j, :], key=j), pp.v(pp.t[0:64, 0:128]), t["kdC"].v(t["kdC"].t[:, g:g + 1]), None, ALU.mult)
                c.ts(Vb.v(Vb.t[:, j, :], key=j), pp.v(pp.t[0:64, 128:256]), t["betaC"].v(t["betaC"].t[:, g:g + 1]), None, ALU.mult)
            Y = [X8, t["Y0"], t["Y1"]]
            Z = [t["Z0"], t["Z1"]]
            pz = self.pb[7]
            for j in range(8):
                c.transpose(pz.v(pz.t[0:64, j * 64:(j + 1) * 64]), X8.v(X8.t[:, j, :]), self.ident.v(self.ident.t[0:64, 0:64]))
            c.copy(Z[0].v(), pz.v(pz.t[0:64, :].rearrange("p (a b) -> p a b", a=8)), eng="act")
            c.tt(P.v(), X8.v(), self.ident.v(self.ident.t[0:64, 0:64].unsqueeze(1).to_broadcast([64, 8, 64])), ALU.add)
            ycur, zcur = X8, Z[0]
            for lvl in range(5):
                ynew = t["Y0"] if lvl % 2 == 0 else t["Y1"]
                znew = Z[(lvl + 1) % 2]
                pa, pbk, pc = self.pb[5], self.pb[6], self.pb[7]
                for j in range(8):
                    c.matmul(pa.v(pa.t[0:64, j * 64:(j + 1) * 64]), zcur.v(zcur.t[:, j, :]), ycur.v(ycur.t[:, j, :]))
                    c.matmul(pbk.v(pbk.t[0:64, j * 64:(j + 1) * 64]), ycur.v(ycur.t[:, j, :]), zcur.v(zcur.t[:, j, :]))
                c.copy(ynew.v(), pa.v(pa.t[0:64, :].rearrange("p (a b) -> p a b", a=8)), eng="act")
                c.copy(znew.v(), pbk.v(pbk.t[0:64, :].rearrange("p (a b) -> p a b", a=8)))
                for j in range(8):
                    c.matmul(pc.v(pc.t[0:64, j * 64:(j + 1) * 64]), znew.v(znew.t[:, j, :]), P.v(P.t[:, j, :]))
                c.tt(P.v(), P.v(), pc.v(pc.t[0:64, :].rearrange("p (a b) -> p a b", a=8)), ALU.add)
                ycur, zcur = ynew, znew
            pw = self.pb[5]
            for j in range(8):
                c.matmul(pw.v(pw.t[:, j * 64:(j + 1) * 64]), Kbg.v(Kbg.t[:, j, :], key=j), P.v(P.t[:, j, :]))
            c.copy(WT.v(), pw.v(), eng="act")
            for half in range(2):
                pu = self.pb[6 + half]
                for jj in range(4):
                    j = half * 4 + jj
                    c.matmul(pu.v(pu.t[0:64, jj * 128:(jj + 1) * 128]), P.v(P.t[:, j, :]), Vb.v(Vb.t[:, j, :], key=j))
                c.copy(U.v(U.t[:, half * 4:half * 4 + 4, :]), pu.v(pu.t[0:64, :].rearrange("p (a b) -> p a b", a=4)))

            def pre_hook(j, g, cur):
                p0 = self.pb[0]
                c.matmul(p0.v(p0.t[0:64, 0:128]), WT.v(WT.t[:, j * 64:(j + 1) * 64]), cur.v())
                c.tt(Vn.v(Vn.t[:, j, :], key=j), U.v(U.t[:, j, :]), p0.v(p0.t[0:64, 0:128]), ALU.subtract)
                return Vn.v(Vn.t[:, j, :], key=j)
            self.chunk_loop(seg, QdT, ST, Vn, Kd, lambda g: t["Rdec"].v(t["Rdec"].t[:, g:g + 1]), S, 128, O, pre_hook=pre_hook)
            self.post_rms(O, t["s8"], t["s8b"], t["ng"], G, "silu", 128, False, t["sq"])
            st = c.dma("sp", View(self.yb.t[hh, sl, :].rearrange("(a p) d -> p a d", p=64), [Dep()]), O.v())
            self.out_stamps.append(st)

    def gdn(self):
        t = self.gdn_alloc()
        for hh in range(2):
            self.gdn_head(hh, t)

    def s5(self):
        c, L = self.c, self.L
        I = self.i
        T = self.mk_
        TC = 512
        nch = L // TC
        par = T("spar", [128, 3]); Bm = T("sB", [128, 2, 16]); Cm = T("sC", [128, 2, 16]); dsk = T("sd", [32, 1])
        step = T("sstep", [128, 1]); rr = T("srr", [128, 1]); th = T("sth", [128, 1])
        sn1 = T("ssn1", [128, 1]); cs1 = T("scs1", [128, 1]); f1 = T("sf1", [128, 1]); i1 = T("si1", [128, 1], I32)
        lr = T("slr", [128, 1]); li = T("sli", [128, 1]); den = T("sden", [128, 1]); fr = T("sfr", [128, 1]); fi = T("sfi", [128, 1])
        a1 = T("sa1", [128, 1]); a2 = T("sa2", [128, 1])
        bb = T("sbb", [128, 2, 16]); b1 = T("sb1", [128, 16]); b2 = T("sb2", [128, 16])
        BBr = T("sBBr", [128, 32]); BBi = T("sBBi", [128, 32]); CCr = T("sCCr", [128, 32]); nCCi = T("snCCi", [128, 32])
        BrT = T("sBrT", [32, 128]); BiT = T("sBiT", [32, 128])
        io = T("sio", [128, TC], I32); tp1 = T("stp1", [128, TC])
        ang = T("sang", [128, TC]); sn = T("ssn", [128, TC]); cs_ = T("scs", [128, TC]); tf = T("stf", [128, TC]); ti = T("sti", [128, TC], I32)
        Rt = T("sRt", [128, TC]); t1 = T("st1", [128, TC]); t2 = T("st2", [128, TC]); br = T("sbr", [128, TC]); bi = T("sbi", [128, TC])
        zr = T("szr", [128, TC]); zi = T("szi", [128, TC])
        xr = [T("sxr0", [128, TC]), T("sxr1", [128, TC])]; xi = [T("sxi0", [128, TC]), T("sxi1", [128, TC])]
        uT = T("suT", [32, TC]); ystg = [T("sy0", [32, TC]), T("sy1", [32, TC])]
        x0 = T("sx0", [128, 1])
        c.memset(x0.v(), 0.0)
        c.emit("pool", lambda e: e.iota(io.t[:], pattern=[[1, TC]], base=1, channel_multiplier=0), [io.v()], [])
        c.copy(tp1.v(), io.v())
        cnt = 0
        for tl in range(8):
            c.dma("sp", par.v(), dview(I["s_par"], I["s_par"].t[tl]))
            c.dma("sp", Bm.v(), dview(I["s_B"], I["s_B"].t[tl]))
            c.dma("sp", Cm.v(), dview(I["s_C"], I["s_C"].t[tl]))
            c.dma("sp", dsk.v(), dview(I["s_d"], I["s_d"].t[tl]))
            ar, ai = par.v(par.t[:, 0:1]), par.v(par.t[:, 1:2])
            c.act(step.v(), par.v(par.t[:, 2:3]), AF.Exp)
            c.tt(rr.v(), ar, step.v(), ALU.mult)
            c.act(rr.v(), rr.v(), AF.Exp)
            c.tt(th.v(), ai, step.v(), ALU.mult)
            self.sincos_tables(th.v(), sn1.v(), cs1.v(), f1.v(), i1.v())
            c.tt(lr.v(), rr.v(), cs1.v(), ALU.mult)
            c.tt(li.v(), rr.v(), sn1.v(), ALU.mult)
            c.tt(den.v(), ar, ar, ALU.mult)
            c.tt(a1.v(), ai, ai, ALU.mult)
            c.tt(den.v(), den.v(), a1.v(), ALU.add)
            c.recip(den.v(), den.v())
            c.ts(a1.v(), lr.v(), -1.0, None, ALU.add)
            c.tt(fr.v(), a1.v(), ar, ALU.mult)
            c.tt(a2.v(), li.v(), ai, ALU.mult)
            c.tt(fr.v(), fr.v(), a2.v(), ALU.add)
            c.tt(fr.v(), fr.v(), den.v(), ALU.mult)
            c.tt(fi.v(), li.v(), ar, ALU.mult)
            c.tt(a2.v(), a1.v(), ai, ALU.mult)
            c.tt(fi.v(), fi.v(), a2.v(), ALU.subtract)
            c.tt(fi.v(), fi.v(), den.v(), ALU.mult)
            c.ts(b1.v(), Bm.v(Bm.t[:, 0, :]), fr.v(), None, ALU.mult)
            c.ts(b2.v(), Bm.v(Bm.t[:, 1, :]), fi.v(), None, ALU.mult)
            c.tt(bb.v(bb.t[:, 0, :]), b1.v(), b2.v(), ALU.subtract)
            c.ts(b1.v(), Bm.v(Bm.t[:, 1, :]), fr.v(), None, ALU.mult)
            c.ts(b2.v(), Bm.v(Bm.t[:, 0, :]), fi.v(), None, ALU.mult)
            c.tt(bb.v(bb.t[:, 1, :]), b1.v(), b2.v(), ALU.add)
            for (dstb, src, k, sc) in ((BBr, bb, 0, 1.0), (BBi, bb, 1, 1.0), (CCr, Cm, 0, 1.0), (nCCi, Cm, 1, -1.0)):
                c.memset(dstb.v(), 0.0)
                for gi in range(2):
                    c.ts(dstb.v(dstb.t[gi * 64:(gi + 1) * 64, gi * 16:(gi + 1) * 16]), src.v(src.t[gi * 64:(gi + 1) * 64, k, :]), sc, None, ALU.mult)
            for (dstT, srcb, bank) in ((BrT, BBr, 5), (BiT, BBi, 6)):
                pp = self.pb[bank]
                c.transpose(pp.v(pp.t[0:32, 0:128]), srcb.v(), self.ident.v())
                c.copy(dstT.v(), pp.v(pp.t[0:32, 0:128]), eng="act")
            c.ts(ang.v(), tp1.v(), th.v(), None, ALU.mult)
            self.sincos_tables(ang.v(), sn.v(), cs_.v(), tf.v(), ti.v())
            c.ts(Rt.v(), self.ones.v(self.ones.t[:, 0:1].to_broadcast([128, TC])), rr.v(), None, ALU.mult)
            prev_r, prev_i = x0.v(), x0.v()
            for ch in range(nch):
                sl = slice(ch * TC, (ch + 1) * TC)
                c.dma("sp", uT.v(), dview(I["s_uT"], I["s_uT"].t[tl * 32:(tl + 1) * 32, sl]))
                pr, pi = self.pb[1 + (cnt % 2) * 2], self.pb[2 + (cnt % 2) * 2]
                c.matmul(pr.v(), BrT.v(), uT.v())
                c.matmul(pi.v(), BiT.v(), uT.v())
                c.tt(t1.v(), pr.v(), cs_.v(), ALU.mult)
                c.tt(t2.v(), pi.v(), sn.v(), ALU.mult)
                c.tt(br.v(), t1.v(), t2.v(), ALU.add)
                c.tt(t1.v(), pi.v(), cs_.v(), ALU.mult)
                c.tt(t2.v(), pr.v(), sn.v(), ALU.mult)
                c.tt(bi.v(), t1.v(), t2.v(), ALU.subtract)
                c.scan(zr.v(), Rt.v(), br.v(), prev_r, ALU.mult, ALU.add)
                c.scan(zi.v(), Rt.v(), bi.v(), prev_i, ALU.mult, ALU.add)
                xrc, xic = xr[cnt % 2], xi[cnt % 2]
                c.tt(t1.v(), zr.v(), cs_.v(), ALU.mult)
                c.tt(t2.v(), zi.v(), sn.v(), ALU.mult)
                c.tt(xrc.v(), t1.v(), t2.v(), ALU.subtract)
                c.tt(t1.v(), zi.v(), cs_.v(), ALU.mult)
                c.tt(t2.v(), zr.v(), sn.v(), ALU.mult)
                c.tt(xic.v(), t1.v(), t2.v(), ALU.add)
                prev_r, prev_i = xrc.v(xrc.t[:, TC - 1:TC]), xic.v(xic.t[:, TC - 1:TC])
                py = self.pb[5 + (cnt % 2)]
                c.matmul(py.v(py.t[0:32, :]), CCr.v(), xrc.v(), start=True, stop=False)
                c.matmul(py.v(py.t[0:32, :]), nCCi.v(), xic.v(), start=False, stop=True)
                ys = ystg[cnt % 2]
                c.stt(ys.v(), uT.v(), dsk.v(), py.v(py.t[0:32, :]), ALU.mult, ALU.add)
                st = c.dma("sp", View(self.ya.t[tl * 32:(tl + 1) * 32, sl], [Dep()]), ys.v())
                self.out_stamps.append(st)
                cnt += 1

Even.post_rms = Odd.post_rms


INTERLEAVE = False


class FusedMix(MixBase):
    def __init__(self, c, L, consts):
        self.c = c
        self.L = L
        self.n = L // 64
        self.nseg = L // 512
        for k, v in consts.items():
            setattr(self, k, v)
        self.out_stamps = []
        self.uid = consts["uidbox"]

    def name(self, s):
        self.uid[0] += 1
        return "%s_%d" % (s, self.uid[0])

    def mk(self, s, shape, dt=F32):
        return self.c.sb(self.name(s), shape, dt)

    def setup_io(self, G, nrows, idx, SA, SB, GA, GB, groups, tag):
        self.G, self.nrows, self.idx = G, nrows, idx
        self.SA, self.SB, self.GA, self.GB, self.groups = SA, SB, GA, GB, groups
        self.skeys = {}
        self.gtag = tag
        self.Grow = G.t.rearrange("k a r t -> (k a r) t")
        self.Gblk = G.t.rearrange("k a r (j b) -> (k a r j) b", b=64)

    def ld(self, dst, col, parts=128, blocks=None):
        self.c.gather(dst, self.G, self.Grow, self.idx.v(self.idx.t[0:parts, col:col + 1]), blocks)

    def ld_gN(self, dst, col, blocks=None):
        self.c.gather(dst, self.G, self.Gblk, self.idx.v(self.idx.t[0:self.n, col:col + 1]), blocks)

    def st(self, src, seg, row0, rows):
        part = "A" if row0 < 256 else "B"
        S = self.SA if part == "A" else self.SB
        r0 = row0 % 256
        key = (seg, r0)
        self.skeys.setdefault((part, seg // 2), []).append(key)
        return self.c.dma("sp", S.v(S.t[seg // 2, seg % 2, r0:r0 + rows, :], key=key), src)

    def send(self, part, pr):
        S, G = (self.SA, self.GA) if part == "A" else (self.SB, self.GB)
        self.c.allgather(S.t[pr].rearrange("a r t -> (a r) t"), G.t[pr].rearrange("s a r t -> (s a r) t"),
                         [S.v(S.t[pr], key=self.skeys[(part, pr)])], [G.v(G.t[pr], key=pr)], self.groups,
                         grp=self.gtag + part, grp_total=4)

    def step(self, gen, k):
        if gen is not None and INTERLEAVE:
            for _ in range(k):
                next(gen, None)

    def tok_from_feat(self, FT, out, w0=0, scale=None):
        c = self.c
        for half in range(2):
            pp = self.pb[5 + half]
            for jj in range(4):
                j = half * 4 + jj
                c.transpose(pp.v(pp.t[0:64, jj * 128:(jj + 1) * 128]), FT.v(FT.t[:, j * 64:(j + 1) * 64]), self.ident.v())
            c.copy(out.v(out.t[:, half * 4:half * 4 + 4, w0:w0 + 128]), pp.v(pp.t[0:64, :].rearrange("p (a b) -> p a b", a=4)), eng="act")

    def feat_from_tok(self, TK, w0, out, bank):
        c = self.c
        pp = self.pb[bank]
        for j in range(8):
            c.transpose(pp.v(pp.t[:, j * 64:(j + 1) * 64]), TK.v(TK.t[:, j, w0:w0 + 128]), self.ident.v(self.ident.t[0:64, 0:64]))
        c.copy(out.v(), pp.v(), eng="act")


class OddF(FusedMix):
    COLS_PER_SEG = 14

    def mlstm(self, ext):
        c, n, L = self.c, self.n, self.L
        T = self.mk
        par = T("mpar", [128, 2])
        c.dma("sp", par.v(), ext["m_par"].v())
        gN = T("gN", [n, 2, 64])
        gcol = self.nseg * self.COLS_PER_SEG
        self.ld_gN(gN.v(gN.t[:, 0, :]), gcol, [12])
        self.ld_gN(gN.v(gN.t[:, 1, :]), gcol + 1, [12])
        ng = T("mng", [64, 256])
        c.dma("sp", ng.v(), ext["m_ng"].v())
        nbf = T("nbf", [128, 1])
        c.ts(nbf.v(), par.v(par.t[:, 1:2]), -1.0, None, ALU.mult)
        ig = T("ig", [n, 64]); sp = T("sp", [n, 64]); csp = T("csp", [n, 64]); bcum = T("bcum", [n, 64])
        colB = T("colB", [n, 64]); cmx = T("cmx", [n, 64]); imax = T("imax", [n, 64]); ul = T("ul", [n, 64])
        umax = T("umax", [n, 1]); blast = T("blast", [n, 1])
        c.ts(ig.v(), gN.v(gN.t[:, 0, :]), par.v(par.t[0:n, 0:1]), None, ALU.add)
        c.act(sp.v(), gN.v(gN.t[:, 1, :]), AF.Exp, bias=nbf.v(nbf.t[0:n, :]), scale=-1.0)
        c.act(sp.v(), sp.v(), AF.Ln, bias=1.0)
        c.scan(csp.v(), self.ones.v(self.ones.t[0:n, 0:64]), sp.v(), 0.0, ALU.mult, ALU.add)
        c.ts(bcum.v(), csp.v(), -1.0, None, ALU.mult)
        c.tt(colB.v(), ig.v(), csp.v(), ALU.add)
        c.scan(cmx.v(), self.zeros.v(self.zeros.t[0:n, :]), colB.v(), NEG, ALU.add, ALU.max)
        c.tt(imax.v(), bcum.v(), cmx.v(), ALU.add)
        c.copy(blast.v(), bcum.v(bcum.t[:, 63:64]))
        c.ts(ul.v(), colB.v(), blast.v(), None, ALU.add)
        c.reduce(umax.v(), ul.v(), ALU.max)
        Rb = T("Rb", [128, n]); Ru = T("Ru", [128, n]); Mn = T("Mn", [128, n]); Mo = T("Mo", [128, n]); Rc = T("Rc", [128, n])
        self.rep_chunk(blast.v(), Rb.v(), 5)
        self.rep_chunk(umax.v(), Ru.v(), 6)
        c.scan(Mn.v(), Rb.v(), Ru.v(), 0.0, ALU.add, ALU.max)
        c.memset(Mo.v(), 0.0)
        c.copy(Mo.v(Mo.t[:, 1:n]), Mn.v(Mn.t[:, 0:n - 1]))
        c.tt(Rc.v(), Rb.v(), Mo.v(), ALU.add)
        c.tt(Rc.v(), Rc.v(), Mn.v(), ALU.subtract)
        c.act(Rc.v(), Rc.v(), AF.Exp)
        bcumC = T("bcumC", [64, n]); colBC = T("colBC", [64, n]); imaxC = T("imaxC", [64, n]); ulC = T("ulC", [64, n])
        self.to_col(bcum.v(), bcumC.v(), 5)
        self.to_col(colB.v(), colBC.v(), 6)
        self.to_col(imax.v(), imaxC.v(), 5)
        self.to_col(ul.v(), ulC.v(), 6)
        msC = T("msC", [64, n]); enC = T("enC", [64, n]); kwsC = T("kwsC", [64, n])
        c.tt(msC.v(), bcumC.v(), Mo.v(Mo.t[0:64, :]), ALU.add)
        c.tt(msC.v(), msC.v(), imaxC.v(), ALU.max)
        c.act(enC.v(), msC.v(), AF.Exp, scale=-1.0)
        c.tt(kwsC.v(), ulC.v(), Mn.v(Mn.t[0:64, :]), ALU.subtract)
        c.act(kwsC.v(), kwsC.v(), AF.Exp)
        E = T("E", [n, 8, 64])
        bR = T("bR", [128, 512]); iR = T("iR", [128, 512]); msR = T("msR", [128, 512])
        QT = T("mQT", [128, 512]); QtT = T("mQtT", [128, 512]); KT = T("mKT", [128, 512]); FT = [T("mFT0", [128, 512]), T("mFT1", [128, 512])]
        Kt = T("mKt", [64, 8, 128]); Kc = T("mKc", [64, 8, 128]); V = T("mV", [64, 8, 258]); O = T("mO", [64, 8, 258])
        G = T("mG", [64, 8, 256]); DT = T("mDT", [64, 8, 64]); ST = T("mST", [64, 8, 64])
        H = T("mH", [64, 8, 256]); s8 = T("ms8", [64, 8]); s8b = T("ms8b", [64, 8]); sqm = T("msq", [64, 8, 256])
        OF = [T("mOF0", [128, 512]), T("mOF1", [128, 512])]
        S = {"bufs": [T("mS0", [128, 258]), T("mS1", [128, 258])], "i": 0}
        c.memset(S["bufs"][0].v(), 0.0)
        for seg in range(self.nseg):
            cs = slice(seg * 8, seg * 8 + 8)
            c0 = seg * self.COLS_PER_SEG
            self.ld(QT.v(), c0 + 0, blocks=[0, 1])
            self.ld(KT.v(), c0 + 1, blocks=[2, 3])
            self.tok_from_feat(KT, Kt)
            for hb in range(2):
                f = FT[hb]
                self.ld(f.v(), c0 + 2 + hb, blocks=[4, 5, 6, 7])
                self.tok_from_feat(f, V, w0=hb * 128)
            c.memset(V.v(V.t[:, :, 256:257]), 1.0)
            c.memset(V.v(V.t[:, :, 257:258]), 0.0)
            for hb in range(2):
                f = FT[hb]
                self.ld(f.v(), c0 + 4 + hb, blocks=[8, 9, 10, 11])
                self.tok_from_feat(f, G, w0=hb * 128)
            self.to_row_seg(bcum, seg, bR.v(), 5, E)
            self.to_row_seg(imax, seg, iR.v(), 6, E)
            mo_b = Mo.v(Mo.t[:, cs].unsqueeze(2).to_broadcast([128, 8, 64]))
            b3 = bR.v(bR.t[:].rearrange("p (a b) -> p a b", a=8))
            ms3 = msR.v(msR.t[:].rearrange("p (a b) -> p a b", a=8))
            c.tt(ms3, b3, mo_b, ALU.add)
            c.tt(iR.v(), msR.v(), iR.v(), ALU.max)
            c.tt(msR.v(), msR.v(), iR.v(), ALU.subtract)
            c.act(msR.v(), msR.v(), AF.Exp)
            c.tt(bR.v(), bR.v(), iR.v(), ALU.subtract)
            c.ts(QT.v(), QT.v(), 128 ** -0.5, None, ALU.mult)
            c.tt(QtT.v(), QT.v(), msR.v(), ALU.mult)
            c.tt(DT.v(), bR.v(bR.t[0:64, :].rearrange("p (a b) -> p a b", a=8)),
                 self.negmask.v(self.negmask.t[:, :].unsqueeze(1).to_broadcast([64, 8, 64])), ALU.add)
            for j in range(8):
                c.act(DT.v(DT.t[:, j, :]), DT.v(DT.t[:, j, :]), AF.Exp, bias=colBC.v(colBC.t[:, seg * 8 + j:seg * 8 + j + 1]))
            self.scores(KT, QT, DT, ST)
            c.tt(Kc.v(), Kt.v(), kwsC.v(kwsC.t[:, cs].unsqueeze(2).to_broadcast([64, 8, 128])), ALU.mult)
            self.chunk_loop(seg, QtT, ST, V, Kc, lambda g: Rc.v(Rc.t[:, g:g + 1]), S, 258, O)
            c.ts(s8.v(), O.v(O.t[:, :, 256]), -1.0, None, ALU.mult)
            c.tt(s8.v(), s8.v(), O.v(O.t[:, :, 256]), ALU.max)
            c.tt(s8.v(), s8.v(), enC.v(enC.t[:, cs]), ALU.max)
            c.recip(s8.v(), s8.v())
            c.tt(H.v(), O.v(O.t[:, :, 0:256]), s8.v(s8.t[:, :].unsqueeze(2).to_broadcast([64, 8, 256])), ALU.mult)
            self.post_rms(H, s8, s8b, ng, G, AF.Sigmoid, 256, False, sqm)
            for hb in range(2):
                self.feat_from_tok(H, hb * 128, OF[hb], 5 + hb)
                self.out_stamps.append(self.st(OF[hb].v(), seg, hb * 128, 128))
            if seg % 2 == 1:
                self.send("A", seg // 2)

    def retention(self, ext):
        c, n, L = self.c, self.n, self.L
        T = self.mk
        hid = T("hid", [128, 1]); lg = T("lg", [128, 1])
        c.dma("sp", hid.v(), ext["hidx"].v())
        ng = T("rng", [64, 256])
        c.dma("sp", ng.v(), ext["r_ng"].v())
        c.act(lg.v(), hid.v(), AF.Exp, scale=-math.log(2.0))
        c.ts(lg.v(), lg.v(), 2.0 ** -5, None, ALU.mult)
        c.act(lg.v(), lg.v(), AF.Ln, scale=-1.0, bias=1.0)
        io_cs = T("io_cs", [64, 64], I32); f_cs = T("f_cs", [64, 64])
        c.emit("pool", lambda e: e.iota(io_cs.t[:], pattern=[[1, 64]], base=0, channel_multiplier=-1), [io_cs.v()], [])
        c.copy(f_cs.v(), io_cs.v())
        DT1 = T("DT1", [64, 64])
        c.act(DT1.v(), f_cs.v(), AF.Exp, scale=lg.v(lg.t[0:64, :]))
        c.emit("pool", lambda e: e.affine_select(out=DT1.t[:], in_=DT1.t[:], pattern=[[1, 64]], compare_op=ALU.is_ge,
                                                 fill=0.0, base=0, channel_multiplier=-1), [DT1.v()], [DT1.v()])
        DT = T("rDT", [64, 8, 64])
        c.copy(DT.v(), DT1.v(DT1.t[:, :].unsqueeze(1).to_broadcast([64, 8, 64])))
        io_r = T("io_r", [128, 64], I32); xi = T("xi", [128, 64])
        c.emit("pool", lambda e: e.iota(io_r.t[:], pattern=[[1, 64]], base=1, channel_multiplier=0), [io_r.v()], [])
        c.copy(xi.v(), io_r.v())
        c.act(xi.v(), xi.v(), AF.Exp, scale=lg.v())
        c.ts(xi.v(), xi.v(), 128 ** -0.5, None, ALU.mult)
        io_p = T("io_p", [128, 1], I32); zeta = T("zeta", [128, 1]); gc = T("gc", [128, 1]); invf = T("invf", [128, 1])
        c.emit("pool", lambda e: e.iota(io_p.t[0:64, :], pattern=[[0, 1]], base=63, channel_multiplier=-1), [io_p.v()], [])
        c.copy(zeta.v(zeta.t[0:64, :]), io_p.v(io_p.t[0:64, :]))
        c.act(zeta.v(zeta.t[0:64, :]), zeta.v(zeta.t[0:64, :]), AF.Exp, scale=lg.v(lg.t[0:64, :]))
        c.act(gc.v(), lg.v(), AF.Exp, scale=64.0)
        for h0 in (0, 64):
            c.emit("pool", lambda e, h0=h0: e.iota(io_p.t[h0:h0 + 64, :], pattern=[[0, 1]], base=0, channel_multiplier=1), [io_p.v()], [])
        c.copy(invf.v(), io_p.v())
        c.act(invf.v(), invf.v(), AF.Exp, scale=-math.log(10000.0) / 64.0)
        sgn = T("sgn", [128, 1])
        c.memset(sgn.v(sgn.t[0:64, :]), -1.0)
        c.memset(sgn.v(sgn.t[64:128, :]), 1.0)
        posi = T("posi", [128, 512], I32); ang = T("ang", [128, 512]); sn = T("sn", [128, 512]); cs_ = T("cs", [128, 512])
        tf = T("tf", [128, 512]); ti = T("ti", [128, 512], I32)
        QT = T("rQT", [128, 512]); QTs = T("rQTs", [128, 512]); KT = T("rKT", [128, 512]); KTs = T("rKTs", [128, 512])
        QtT = T("rQtT", [128, 512]); Kc = T("rKc", [64, 8, 128]); V = T("rV", [64, 8, 256]); O = T("rO", [64, 8, 256])
        G = T("rG", [64, 8, 256]); ST = T("rST", [64, 8, 64]); s8 = T("rs8", [64, 8]); s8b = T("rs8b", [64, 8]); sqr = T("rsq", [64, 8, 256])
        FT = [T("rFT0", [128, 512]), T("rFT1", [128, 512])]
        OF = [T("rOF0", [128, 512]), T("rOF1", [128, 512])]
        S = {"bufs": [T("rS0", [128, 256]), T("rS1", [128, 256])], "i": 0}
        c.memset(S["bufs"][0].v(), 0.0)
        pos_d = ext["pos"]
        for seg in range(self.nseg):
            sl = slice(seg * 512, (seg + 1) * 512)
            c0 = seg * self.COLS_PER_SEG + 6
            c.dma("sp", posi.v(), dview(pos_d, pos_d.t[:, sl]))
            for i, tl in enumerate((QT, QTs, KT, KTs)):
                self.ld(tl.v(), c0 + i)
            for hb in range(2):
                self.ld(FT[hb].v(), c0 + 4 + hb)
                self.tok_from_feat(FT[hb], V, w0=hb * 128)
            for hb in range(2):
                self.ld(FT[hb].v(), c0 + 6 + hb)
                self.tok_from_feat(FT[hb], G, w0=hb * 128)
            c.copy(ang.v(), posi.v())
            c.ts(ang.v(), ang.v(), invf.v(), None, ALU.mult)
            self.sincos_tables(ang.v(), sn.v(), cs_.v(), tf.v(), ti.v())
            c.ts(sn.v(), sn.v(), sgn.v(), None, ALU.mult)
            for (A, As) in ((QT, QTs), (KT, KTs)):
                c.tt(A.v(), A.v(), cs_.v(), ALU.mult)
                c.tt(As.v(), As.v(), sn.v(), ALU.mult)
                c.tt(A.v(), A.v(), As.v(), ALU.add)
            c.tt(QtT.v(QtT.t[:].rearrange("p (a b) -> p a b", a=8)), QT.v(QT.t[:].rearrange("p (a b) -> p a b", a=8)),
                 xi.v(xi.t[:, :].unsqueeze(1).to_broadcast([128, 8, 64])), ALU.mult)
            c.ts(QT.v(), QT.v(), 128 ** -0.5, None, ALU.mult)
            self.scores(KT, QT, DT, ST)
            for j in range(8):
                pp = self.pb[5 + (j % 2)]
                c.transpose(pp.v(pp.t[0:64, 0:128]), KT.v(KT.t[:, j * 64:(j + 1) * 64]), self.ident.v())
                c.ts(Kc.v(Kc.t[:, j, :], key=j), pp.v(pp.t[0:64, 0:128]), zeta.v(zeta.t[0:64, :]), None, ALU.mult)
            self.chunk_loop(seg, QtT, ST, V, Kc, lambda g: gc.v(), S, 256, O)
            self.post_rms(O, s8, s8b, ng, G, "silu", 256, True, sqr)
            for hb in range(2):
                self.feat_from_tok(O, hb * 128, OF[hb], 5 + hb)
                self.out_stamps.append(self.st(OF[hb].v(), seg, 256 + hb * 128, 128))
            if seg % 2 == 1:
                self.send("B", seg // 2)


OddF.post_rms = Odd.post_rms


class EvenF(FusedMix):
    GCOLS = 8

    def gdn_alloc(self):
        T = self.mk
        n = self.n
        t = {}
        t["cw"] = T("cw", [128, 3, 4])
        t["par"] = T("gpar", [128, 2]); t["gN"] = T("ggN", [n, 2, 64]); t["ng"] = T("gng", [64, 128])
        for nm in ("beta", "sp", "gg", "gcum"):
            t[nm] = T("g" + nm, [n, 64])
        t["ea"] = T("gea", [128, 1]); t["glast"] = T("gglast", [n, 1])
        for nm in ("Rgl", "Rdec"):
            t[nm] = T("g" + nm, [128, n])
        for nm in ("gcumC", "ngcumC", "betaC", "egC", "bgC", "kdC"):
            t[nm] = T("g" + nm, [64, n])
        t["E"] = T("gE", [n, 8, 64])
        for nm in ("Xq", "Xk", "Xv"):
            t[nm] = T("g" + nm, [128, 515])
        for nm in ("QT", "KT", "VT", "KbT", "QdT", "tmp", "gR", "bR", "WT", "ZT", "OF"):
            t[nm] = T("g" + nm, [128, 512])
        for nm in ("DT", "DTsn", "ST", "X8", "Z0", "Z1", "Y0", "Y1", "P"):
            t[nm] = T("g" + nm, [64, 8, 64])
        for nm in ("Kbg", "Kd", "Vb", "U", "Vn", "O", "G", "sq"):
            t[nm] = T("g" + nm, [64, 8, 128])
        t["s8"] = T("gs8", [64, 8]); t["s8b"] = T("gs8b", [64, 8])
        t["S"] = [T("gS0", [128, 128]), T("gS1", [128, 128])]
        return t

    def gdn_head(self, hh, t, ext, gen=None):
        c, n, L = self.c, self.n, self.L
        c.dma("sp", t["cw"].v(), dview(ext["g_cw"], ext["g_cw"].t[hh]))
        c.dma("sp", t["par"].v(), dview(ext["g_par"], ext["g_par"].t[hh]))
        c.dma("sp", t["ng"].v(), ext["g_ng"].v())
        gcol = self.nseg * self.GCOLS + hh * 2
        par, gN = t["par"], t["gN"]
        self.ld_gN(gN.v(gN.t[:, 0, :]), gcol, [20])
        self.ld_gN(gN.v(gN.t[:, 1, :]), gcol + 1, [20])
        beta, sp, gg, gcum = t["beta"], t["sp"], t["gg"], t["gcum"]
        c.act(beta.v(), gN.v(gN.t[:, 0, :]), AF.Sigmoid)
        c.act(sp.v(), gN.v(gN.t[:, 1, :]), AF.Exp, bias=par.v(par.t[0:n, 1:2]))
        c.act(sp.v(), sp.v(), AF.Ln, bias=1.0)
        c.act(t["ea"].v(), par.v(par.t[:, 0:1]), AF.Exp)
        c.ts(gg.v(), sp.v(), t["ea"].v(t["ea"].t[0:n, :]), -1.0, ALU.mult, ALU.mult)
        c.scan(gcum.v(), self.ones.v(self.ones.t[0:n, 0:64]), gg.v(), 0.0, ALU.mult, ALU.add)
        c.copy(t["glast"].v(), gcum.v(gcum.t[:, 63:64]))
        self.rep_chunk(t["glast"].v(), t["Rgl"].v(), 5)
        c.act(t["Rdec"].v(), t["Rgl"].v(), AF.Exp)
        self.to_col(gcum.v(), t["gcumC"].v(), 5)
        self.to_col(beta.v(), t["betaC"].v(), 6)
        c.ts(t["ngcumC"].v(), t["gcumC"].v(), -1.0, None, ALU.mult)
        c.act(t["egC"].v(), t["gcumC"].v(), AF.Exp)
        c.tt(t["bgC"].v(), t["betaC"].v(), t["egC"].v(), ALU.mult)
        c.tt(t["kdC"].v(), t["Rgl"].v(t["Rgl"].t[0:64, :]), t["gcumC"].v(), ALU.subtract)
        c.act(t["kdC"].v(), t["kdC"].v(), AF.Exp)
        S = {"bufs": t["S"], "i": 0}
        c.memset(t["S"][0].v(), 0.0)
        QT, KT, VT, KbT, QdT, tmp, gR, bR, WT = (t[k] for k in ("QT", "KT", "VT", "KbT", "QdT", "tmp", "gR", "bR", "WT"))
        DT, DTsn, ST, X8, P = (t[k] for k in ("DT", "DTsn", "ST", "X8", "P"))
        Kbg, Kd, Vb, U, Vn, O, G = (t[k] for k in ("Kbg", "Kd", "Vb", "U", "Vn", "O", "G"))
        for seg in range(self.nseg):
            c0 = seg * self.GCOLS + hh * 4
            for ci, (X, dst) in enumerate(((t["Xq"], QT), (t["Xk"], KT), (t["Xv"], VT))):
                if seg == 0:
                    c.memset(X.v(X.t[:, 0:3]), 0.0)
                else:
                    c.copy(X.v(X.t[:, 0:3]), X.v(X.t[:, 512:515]))
                self.ld(X.v(X.t[:, 3:515]), c0 + ci, blocks=[4 + 4 * ci + i_ for i_ in range(4)])
                cw = t["cw"]
                c.ts(dst.v(), X.v(X.t[:, 0:512]), cw.v(cw.t[:, ci, 0:1]), None, ALU.mult)
                for j in range(1, 4):
                    c.stt(dst.v(), X.v(X.t[:, j:j + 512]), cw.v(cw.t[:, ci, j:j + 1]), dst.v(), ALU.mult, ALU.add)
                c.act(dst.v(), dst.v(), AF.Silu)
                if ci < 2:
                    c.tt(tmp.v(), dst.v(), dst.v(), ALU.mult)
                    pp = self.pb[5 + ci]
                    c.matmul(pp.v(), self.ones.v(), tmp.v())
                    c.act(tmp.v(), pp.v(), AF.Ln, bias=1e-6)
                    c.act(tmp.v(), tmp.v(), AF.Exp, scale=-0.5)
                    if ci == 0:
                        c.stt(dst.v(), dst.v(), 128 ** -0.5, tmp.v(), ALU.mult, ALU.mult)
                    else:
                        c.tt(dst.v(), dst.v(), tmp.v(), ALU.mult)
            self.ld(t["ZT"].v(), c0 + 3, blocks=[16, 17, 18, 19])
            self.tok_from_feat(t["ZT"], G)
            self.to_row_seg(gcum, seg, gR.v(), 5, t["E"])
            self.to_row_seg(beta, seg, bR.v(), 6, t["E"])
            c.tt(KbT.v(), KT.v(), bR.v(), ALU.mult)
            c.act(tmp.v(), gR.v(), AF.Exp)
            c.tt(QdT.v(), QT.v(), tmp.v(), ALU.mult)
            c.tt(DT.v(), gR.v(gR.t[0:64, :].rearrange("p (a b) -> p a b", a=8)),
                 self.negmask.v(self.negmask.t[:, :].unsqueeze(1).to_broadcast([64, 8, 64])), ALU.add)
            for j in range(8):
                g = seg * 8 + j
                c.act(DT.v(DT.t[:, j, :]), DT.v(DT.t[:, j, :]), AF.Exp, bias=t["ngcumC"].v(t["ngcumC"].t[:, g:g + 1]))
            c.stt(DTsn.v(), DT.v(), -1.0, self.strict01.v(self.strict01.t[:, :].unsqueeze(1).to_broadcast([64, 8, 64])), ALU.mult, ALU.mult)
            self.scores(KT, QT, DT, ST)
            self.scores(KT, KbT, DTsn, X8)
            for j in range(8):
                g = seg * 8 + j
                pp = self.pb[5 + (j % 2)]
                c.transpose(pp.v(pp.t[0:64, 0:128]), KT.v(KT.t[:, j * 64:(j + 1) * 64]), self.ident.v())
                c.transpose(pp.v(pp.t[0:64, 128:256]), VT.v(VT.t[:, j * 64:(j + 1) * 64]), self.ident.v())
                c.ts(Kbg.v(Kbg.t[:, j, :], key=j), pp.v(pp.t[0:64, 0:128]), t["bgC"].v(t["bgC"].t[:, g:g + 1]), None, ALU.mult)
                c.ts(Kd.v(Kd.t[:, j, :], key=j), pp.v(pp.t[0:64, 0:128]), t["kdC"].v(t["kdC"].t[:, g:g + 1]), None, ALU.mult)
                c.ts(Vb.v(Vb.t[:, j, :], key=j), pp.v(pp.t[0:64, 128:256]), t["betaC"].v(t["betaC"].t[:, g:g + 1]), None, ALU.mult)
            Z = [t["Z0"], t["Z1"]]
            pz = self.pb[7]
            for j in range(8):
                c.transpose(pz.v(pz.t[0:64, j * 64:(j + 1) * 64]), X8.v(X8.t[:, j, :]), self.ident.v(self.ident.t[0:64, 0:64]))
            c.copy(Z[0].v(), pz.v(pz.t[0:64, :].rearrange("p (a b) -> p a b", a=8)), eng="act")
            c.tt(P.v(), X8.v(), self.ident.v(self.ident.t[0:64, 0:64].unsqueeze(1).to_broadcast([64, 8, 64])), ALU.add)
            ycur, zcur = X8, Z[0]
            for lvl in range(5):
                ynew = t["Y0"] if lvl % 2 == 0 else t["Y1"]
                znew = Z[(lvl + 1) % 2]
                pa, pbk, pc = self.pb[5], self.pb[6], self.pb[7]
                for j in range(8):
                    c.matmul(pa.v(pa.t[0:64, j * 64:(j + 1) * 64]), zcur.v(zcur.t[:, j, :]), ycur.v(ycur.t[:, j, :]))
                    c.matmul(pbk.v(pbk.t[0:64, j * 64:(j + 1) * 64]), ycur.v(ycur.t[:, j, :]), zcur.v(zcur.t[:, j, :]))
                c.copy(ynew.v(), pa.v(pa.t[0:64, :].rearrange("p (a b) -> p a b", a=8)), eng="act")
                c.copy(znew.v(), pbk.v(pbk.t[0:64, :].rearrange("p (a b) -> p a b", a=8)))
                for j in range(8):
                    c.matmul(pc.v(pc.t[0:64, j * 64:(j + 1) * 64]), znew.v(znew.t[:, j, :]), P.v(P.t[:, j, :]))
                c.tt(P.v(), P.v(), pc.v(pc.t[0:64, :].rearrange("p (a b) -> p a b", a=8)), ALU.add)
                ycur, zcur = ynew, znew
            self.step(gen, 6)
            pw = self.pb[5]
            for j in range(8):
                c.matmul(pw.v(pw.t[:, j * 64:(j + 1) * 64]), Kbg.v(Kbg.t[:, j, :], key=j), P.v(P.t[:, j, :]))
            c.copy(WT.v(), pw.v(), eng="act")
            for half in range(2):
                pu = self.pb[6 + half]
                for jj in range(4):
                    j = half * 4 + jj
                    c.matmul(pu.v(pu.t[0:64, jj * 128:(jj + 1) * 128]), P.v(P.t[:, j, :]), Vb.v(Vb.t[:, j, :], key=j))
                c.copy(U.v(U.t[:, half * 4:half * 4 + 4, :]), pu.v(pu.t[0:64, :].rearrange("p (a b) -> p a b", a=4)))

            def pre_hook(j, g, cur):
                p0 = self.pb[0]
                c.matmul(p0.v(p0.t[0:64, 0:128]), WT.v(WT.t[:, j * 64:(j + 1) * 64]), cur.v())
                c.tt(Vn.v(Vn.t[:, j, :], key=j), U.v(U.t[:, j, :]), p0.v(p0.t[0:64, 0:128]), ALU.subtract)
                return Vn.v(Vn.t[:, j, :], key=j)
            self.chunk_loop(seg, QdT, ST, Vn, Kd, lambda g: t["Rdec"].v(t["Rdec"].t[:, g:g + 1]), S, 128, O, pre_hook=pre_hook,
                            post_chunk=lambda: self.step(gen, 4))
            self.post_rms(O, t["s8"], t["s8b"], t["ng"], G, "silu", 128, False, t["sq"])
            self.feat_from_tok(O, 0, t["OF"], 5)
            self.out_stamps.append(self.st(t["OF"].v(), seg, 256 + hh * 128, 128))
            if hh == 1 and seg % 2 == 1:
                self.send("B", seg // 2)
            self.step(gen, 6)

    def gdn(self, ext, with_s5=True):
        t = self.gdn_alloc()
        gen = self.s5(ext) if with_s5 else None
        for hh in range(2):
            self.gdn_head(hh, t, ext, gen)
        if gen is not None:
            for _ in gen:
                pass

    def s5(self, ext):
        c, L = self.c, self.L
        T = self.mk
        TC = 512
        nch = L // TC
        scol = self.nseg * self.GCOLS + 4
        par = T("spar", [128, 3]); Bm = T("sB", [128, 2, 16]); Cm = T("sC", [128, 2, 16]); dsk = T("sd", [32, 1])
        step = T("sstep", [128, 1]); rr = T("srr", [128, 1]); th = T("sth", [128, 1])
        sn1 = T("ssn1", [128, 1]); cs1 = T("scs1", [128, 1]); f1 = T("sf1", [128, 1]); i1 = T("si1", [128, 1], I32)
        lr = T("slr", [128, 1]); li = T("sli", [128, 1]); den = T("sden", [128, 1]); fr = T("sfr", [128, 1]); fi = T("sfi", [128, 1])
        a1 = T("sa1", [128, 1]); a2 = T("sa2", [128, 1])
        bb = T("sbb", [128, 2, 16]); b1 = T("sb1", [128, 16]); b2 = T("sb2", [128, 16])
        BBr = T("sBBr", [128, 32]); BBi = T("sBBi", [128, 32]); CCr = T("sCCr", [128, 32]); nCCi = T("snCCi", [128, 32])
        BrT = T("sBrT", [32, 128]); BiT = T("sBiT", [32, 128])
        io = T("sio", [128, TC], I32); tp1 = T("stp1", [128, TC])
        ang = T("sang", [128, TC]); sn = T("ssn", [128, TC]); cs_ = T("scs", [128, TC]); tf = T("stf", [128, TC]); ti = T("sti", [128, TC], I32)
        Rt = T("sRt", [128, TC]); t1 = T("st1", [128, TC]); t2 = T("st2", [128, TC]); br = T("sbr", [128, TC]); bi = T("sbi", [128, TC])
        zr = T("szr", [128, TC]); zi = T("szi", [128, TC])
        xr = [T("sxr0", [128, TC]), T("sxr1", [128, TC])]; xi = [T("sxi0", [128, TC]), T("sxi1", [128, TC])]
        uT = [T("suT0", [32, TC]), T("suT1", [32, TC])]; ystg = [T("sy0", [32, TC]), T("sy1", [32, TC])]
        x0 = T("sx0", [128, 1])
        c.memset(x0.v(), 0.0)
        c.emit("pool", lambda e: e.iota(io.t[:], pattern=[[1, TC]], base=1, channel_multiplier=0), [io.v()], [])
        c.copy(tp1.v(), io.v())
        cnt = 0
        for tl in range(8):
            c.dma("sp", par.v(), dview(ext["s_par"], ext["s_par"].t[tl]))
            c.dma("sp", Bm.v(), dview(ext["s_B"], ext["s_B"].t[tl]))
            c.dma("sp", Cm.v(), dview(ext["s_C"], ext["s_C"].t[tl]))
            c.dma("sp", dsk.v(), dview(ext["s_d"], ext["s_d"].t[tl]))
            ar, ai = par.v(par.t[:, 0:1]), par.v(par.t[:, 1:2])
            c.act(step.v(), par.v(par.t[:, 2:3]), AF.Exp)
            c.tt(rr.v(), ar, step.v(), ALU.mult)
            c.act(rr.v(), rr.v(), AF.Exp)
            c.tt(th.v(), ai, step.v(), ALU.mult)
            self.sincos_tables(th.v(), sn1.v(), cs1.v(), f1.v(), i1.v())
            yield
            c.tt(lr.v(), rr.v(), cs1.v(), ALU.mult)
            c.tt(li.v(), rr.v(), sn1.v(), ALU.mult)
            c.tt(den.v(), ar, ar, ALU.mult)
            c.tt(a1.v(), ai, ai, ALU.mult)
            c.tt(den.v(), den.v(), a1.v(), ALU.add)
            c.recip(den.v(), den.v())
            c.ts(a1.v(), lr.v(), -1.0, None, ALU.add)
            c.tt(fr.v(), a1.v(), ar, ALU.mult)
            c.tt(a2.v(), li.v(), ai, ALU.mult)
            c.tt(fr.v(), fr.v(), a2.v(), ALU.add)
            c.tt(fr.v(), fr.v(), den.v(), ALU.mult)
            c.tt(fi.v(), li.v(), ar, ALU.mult)
            c.tt(a2.v(), a1.v(), ai, ALU.mult)
            c.tt(fi.v(), fi.v(), a2.v(), ALU.subtract)
            c.tt(fi.v(), fi.v(), den.v(), ALU.mult)
            yield
            c.ts(b1.v(), Bm.v(Bm.t[:, 0, :]), fr.v(), None, ALU.mult)
            c.ts(b2.v(), Bm.v(Bm.t[:, 1, :]), fi.v(), None, ALU.mult)
            c.tt(bb.v(bb.t[:, 0, :]), b1.v(), b2.v(), ALU.subtract)
            c.ts(b1.v(), Bm.v(Bm.t[:, 1, :]), fr.v(), None, ALU.mult)
            c.ts(b2.v(), Bm.v(Bm.t[:, 0, :]), fi.v(), None, ALU.mult)
            c.tt(bb.v(bb.t[:, 1, :]), b1.v(), b2.v(), ALU.add)
            for (dstb, src, k, sc) in ((BBr, bb, 0, 1.0), (BBi, bb, 1, 1.0), (CCr, Cm, 0, 1.0), (nCCi, Cm, 1, -1.0)):
                c.memset(dstb.v(), 0.0)
                for gi in range(2):
                    c.ts(dstb.v(dstb.t[gi * 64:(gi + 1) * 64, gi * 16:(gi + 1) * 16]), src.v(src.t[gi * 64:(gi + 1) * 64, k, :]), sc, None, ALU.mult)
            for (dstT, srcb, bank) in ((BrT, BBr, 5), (BiT, BBi, 6)):
                pp = self.pb[bank]
                c.transpose(pp.v(pp.t[0:32, 0:128]), srcb.v(), self.ident.v())
                c.copy(dstT.v(), pp.v(pp.t[0:32, 0:128]), eng="act")
            yield
            c.ts(ang.v(), tp1.v(), th.v(), None, ALU.mult)
            self.sincos_tables(ang.v(), sn.v(), cs_.v(), tf.v(), ti.v())
            yield
            c.ts(Rt.v(), self.ones.v(self.ones.t[:, 0:1].to_broadcast([128, TC])), rr.v(), None, ALU.mult)
            prev_r, prev_i = x0.v(), x0.v()
            for ch in range(nch):
                u = uT[cnt % 2]
                self.ld(u.v(), scol + tl * self.nseg + ch, parts=32, blocks=[0, 1, 2, 3])
                pr, pi = self.pb[1 + (cnt % 2) * 2], self.pb[2 + (cnt % 2) * 2]
                c.matmul(pr.v(), BrT.v(), u.v())
                c.matmul(pi.v(), BiT.v(), u.v())
                yield
                c.tt(t1.v(), pr.v(), cs_.v(), ALU.mult)
                c.tt(t2.v(), pi.v(), sn.v(), ALU.mult)
                yield
                c.tt(br.v(), t1.v(), t2.v(), ALU.add)
                c.tt(t1.v(), pi.v(), cs_.v(), ALU.mult)
                yield
                c.tt(t2.v(), pr.v(), sn.v(), ALU.mult)
                c.tt(bi.v(), t1.v(), t2.v(), ALU.subtract)
                yield
                c.scan(zr.v(), Rt.v(), br.v(), prev_r, ALU.mult, ALU.add)
                yield
                c.scan(zi.v(), Rt.v(), bi.v(), prev_i, ALU.mult, ALU.add)
                yield
                xrc, xic = xr[cnt % 2], xi[cnt % 2]
                c.tt(t1.v(), zr.v(), cs_.v(), ALU.mult)
                c.tt(t2.v(), zi.v(), sn.v(), ALU.mult)
                yield
                c.tt(xrc.v(), t1.v(), t2.v(), ALU.subtract)
                c.tt(t1.v(), zi.v(), cs_.v(), ALU.mult)
                yield
                c.tt(t2.v(), zr.v(), sn.v(), ALU.mult)
                c.tt(xic.v(), t1.v(), t2.v(), ALU.add)
                yield
                prev_r, prev_i = xrc.v(xrc.t[:, TC - 1:TC]), xic.v(xic.t[:, TC - 1:TC])
                py = self.pb[5 + (cnt % 2)]
                c.matmul(py.v(py.t[0:32, :]), CCr.v(), xrc.v(), start=True, stop=False)
                c.matmul(py.v(py.t[0:32, :]), nCCi.v(), xic.v(), start=False, stop=True)
                ys = ystg[cnt % 2]
                c.stt(ys.v(), u.v(), dsk.v(), py.v(py.t[0:32, :]), ALU.mult, ALU.add)
                self.out_stamps.append(self.st(ys.v(), ch, tl * 32, 32))
                cnt += 1
                if tl == 7 and ch % 2 == 1:
                    self.send("A", ch // 2)
                yield


EvenF.post_rms = Odd.post_rms


GROUPS = [[0, 1, 2, 3], [4, 5, 6, 7]]
EVEN_IN = 5136
ODD_IN = 6152
L = 4096
NE = 8 * 8 + 4 + 64
NO = 8 * 14 + 2


class TokF(Tok):
    def __init__(self, c, P, yf=False):
        self.c = c
        self.xT = P["xT"]
        self.norms = P["norms"]
        self.ones = P["onesb"]
        self.pb = P["pb"]
        self.hT = c.sb("hT", [128, 16, NT], BF16)
        self.gT = c.sb("gT", [128, 12, NT], BF16)
        self.rstd = c.sb("rstd", [128, NT], F32)
        self.sq = [c.sb("sq%d" % i, [128, NT], BF16) for i in range(2)]
        self.wa = [c.sb("wa%d" % i, [128, 16, 256], BF16) for i in range(2)]
        self.wb = [c.sb("wb%d" % i, [128, 16, 256], BF16) for i in range(2)]
        self.w2 = [c.sb("w2_%d" % i, [128, 12, 256], BF16) for i in range(2)]
        self.tmp = [c.sb("tmp%d" % i, [128, 512], F32) for i in range(2)]
        self.stage = [c.sb("stg%d" % i, [128, NT], F32) for i in range(2)]
        self.cnt = {"wa": 0, "wb": 0, "w2": 0, "tmp": 0, "sq": 0, "stage": 0}
        self.tasks = []
        self.out_stamps = []
        if yf:
            self.yf = c.sb("yf", [128, 8, 512], F32)
            self.t1 = c.sb("t1g", [128, 512], F32)

    def proj_send(self, ni, w_d, N, S, G, grp_of, order=None):
        c = self.c
        self.rmsnorm(ni)
        nblk = (N + 255) // 256
        for b in (order if order is not None else range(nblk)):
            ncols = min(256, N - b * 256)

            def load(b=b, ncols=ncols):
                ta = self.rot("wa")
                c.dma("pool", ta.v(ta.t[:, :, 0:ncols]), self.wview(w_d, b * 256, ncols))
                return ta

            def compute(ta, b=b, ncols=ncols):
                for jj in range((ncols + 127) // 128):
                    m = min(128, ncols - jj * 128)
                    n0 = b * 256 + jj * 128
                    sg = self.rot("stage")
                    for th in range(2):
                        pp = self.pb[(jj % 2) * 2 + th]
                        for k in range(16):
                            c.matmul(pp.v(pp.t[0:m, :]), ta.v(ta.t[:, k, jj * 128:jj * 128 + m]),
                                     self.hT.v(self.hT.t[:, k, th * 512:(th + 1) * 512], key=k),
                                     start=(k == 0), stop=(k == 15))
                        c.copy(sg.v(sg.t[0:m, th * 512:(th + 1) * 512]), pp.v(pp.t[0:m, :]), eng="act")
                        c.dma("sp", S.v(S.t[b, th, jj * 128:jj * 128 + m, :], key=(b, th, jj)), sg.v(sg.t[0:m, th * 512:(th + 1) * 512]))
                keys = [(b, th, jj) for th in range(2) for jj in range((ncols + 127) // 128)]
                gname, gtot = grp_of(b)
                c.allgather(S.t[b].rearrange("a r t -> (a r) t"), G.t[b].rearrange("a r t -> (a r) t"),
                            [S.v(S.t[b], key=keys)], [G.v(G.t[b], key=b)], GROUPS, grp=gname, grp_total=gtot)
            self.add_task(load, compute)
        self.flush()

    def mix_gather(self, GA, GB, idx, w_out_d, even=None):
        c = self.c
        k0 = 8 if even is not None else 0
        for k in range(k0, 16):
            G = GA if k < 8 else GB
            Grow = G.t.rearrange("p s a r t -> (p s a r) t")
            sg = self.rot("stage")
            for half in range(2):
                c.gather(sg.v(sg.t[:, half * 512:(half + 1) * 512]), G, Grow, idx.v(idx.t[:, k * 2 + half:k * 2 + half + 1]))
            c.copy(self.hT.v(self.hT.t[:, k, :], key=k), sg.v(), eng=("act" if k % 2 else "dve"))
        if even is not None:
            self.s5_glu_g(GA, GA.t.rearrange("p s a r t -> (p s a r) t"), idx, even)
        src = []
        for k in range(16):
            if even is not None and k < 8:
                src.append((self.gT, k))
            else:
                src.append((self.hT, k))
        for b in range(8):
            def load(b=b):
                ta = self.rot("wa")
                c.dma("pool", ta.v(), self.wview(w_out_d, b * 256, 256))
                return ta

            def compute(ta, b=b):
                for jj in range(2):
                    i = b * 2 + jj
                    for th in range(2):
                        pp = self.pb[(i % 2) * 2 + th]
                        for k in range(16):
                            buf, kk = src[k]
                            key = kk if buf is self.hT else [(kk, 0), (kk, 1)]
                            c.matmul(pp.v(), ta.v(ta.t[:, k, jj * 128:(jj + 1) * 128]),
                                     buf.v(buf.t[:, kk, th * 512:(th + 1) * 512], key=key),
                                     start=(k == 0), stop=(k == 15))
                        xv = self.xT.v(self.xT.t[:, i, th * 512:(th + 1) * 512], key=i)
                        c.stt(xv, pp.v(), 1.0, xv, ALU.mult, ALU.add)
            self.add_task(load, compute)
        self.flush()

    def s5_glu_g(self, G, Grow, idx, even):
        c = self.c
        yf, t1 = self.yf, self.t1
        bglu = even["bglu"]
        wglu_d = even["wglu"]
        for th in range(2):
            sl = slice(th * 512, (th + 1) * 512)
            for k in range(8):
                yk = yf.v(yf.t[:, k, :], key=k)
                c.gather(yk, G, Grow, idx.v(idx.t[:, k * 2 + th:k * 2 + th + 1]))
                c.tt(t1.v(), yk, yk, ALU.mult)
                c.ts(t1.v(), t1.v(), 0.044715, 1.0, ALU.mult, ALU.add)
                c.tt(t1.v(), t1.v(), yk, ALU.mult)
                c.act(t1.v(), t1.v(), AF.Sigmoid, scale=2.0 * 0.7978845608028654)
                c.tt(yk, yk, t1.v(), ALU.mult)
                c.copy(self.hT.v(self.hT.t[:, k, sl], key=k), yk, eng="act")
            for b in range(4):
                ta = self.rot("wa")
                c.dma("pool", ta.v(ta.t[:, 0:8, :]), self.wview(wglu_d, b * 256, 256, kchunks=8))
                for jj in range(2):
                    n = b * 2 + jj
                    pp = self.pb[4 + (n % 2)]
                    for k in range(8):
                        c.matmul(pp.v(), ta.v(ta.t[:, k, jj * 128:(jj + 1) * 128]), self.hT.v(self.hT.t[:, k, sl], key=k),
                                 start=(k == 0), stop=(k == 7))
                    t = self.rot("tmp")
                    c.act(t.v(), pp.v(), AF.Sigmoid, bias=bglu.v(bglu.t[:, n:n + 1]))
                    c.tt(self.gT.v(self.gT.t[:, n, sl], key=(n, th)), t.v(), yf.v(yf.t[:, n, :], key=n), ALU.mult)


def build(debug=False, upto=9):
    c = Ctx()
    X = lambda name, shape, dt=F32: c.dram(name, shape, dt, kind="ExternalInput")
    xin = X("xT_in", [D, NT])
    norms_d = X("norms_in", [128, 7, 16])
    W = {}
    for tag in "abcd":
        if tag == "a" or (tag in "bc" and upto >= 3) or (tag == "d" and upto >= 5):
            W[tag] = (X("w1_" + tag, [D, DFF]), X("w3_" + tag, [D, DFF]), X("w2_" + tag, [DFF, D]))
    w_in_e = X("w_in_e", [D, EVEN_IN])
    bglu_d = X("bglu_in", [128, 8])
    if upto >= 3:
        w_out_e = X("w_out_e", [D, D]); w_in_o = X("w_in_o", [D, ODD_IN]); w_glu = X("w_glu", [1024, 1024])
    if upto >= 5:
        w_out_o = X("w_out_o", [D, D])
    idxE_d = X("idxE", [128, NE], I32); idxO_d = X("idxO", [128, NO], I32)
    idxM0_d = X("idxM0", [128, 32], I32); idxM1_d = X("idxM1", [128, 32], I32)
    ext = dict(
        g_cw=X("g_cw", [2, 128, 3, 4]), g_par=X("g_par", [2, 128, 2]), g_ng=X("g_ng", [64, 128]),
        s_par=X("s_par", [8, 128, 3]), s_B=X("s_B", [8, 128, 2, 16]), s_C=X("s_C", [8, 128, 2, 16]), s_d=X("s_d", [8, 32, 1]),
        m_par=X("m_par", [128, 2]), m_ng=X("m_ng", [64, 256]), pos=X("pos", [128, L], I32), hidx=X("hidx", [128, 1]),
        r_ng=X("r_ng", [64, 256]))
    outT = c.dram("outT", [D, NT], F32, kind="ExternalOutput")
    NB0 = (EVEN_IN + 255) // 256
    NB2 = (ODD_IN + 255) // 256
    S0 = c.dram("S0", [NB0, 2, 256, 512]); G0 = c.dram("G0", [NB0, 8, 256, 512])
    S1a = c.dram("S1a", [4, 2, 256, 512]); G1a = c.dram("G1a", [4, 4, 2, 256, 512])
    S1b = c.dram("S1b", [4, 2, 256, 512]); G1b = c.dram("G1b", [4, 4, 2, 256, 512])
    S2 = c.dram("S2", [NB2, 2, 256, 512]); G2 = c.dram("G2", [NB2, 8, 256, 512])
    S3a = c.dram("S3a", [4, 2, 256, 512]); G3a = c.dram("G3a", [4, 4, 2, 256, 512])
    S3b = c.dram("S3b", [4, 2, 256, 512]); G3b = c.dram("G3b", [4, 4, 2, 256, 512])
    P = dict(xT=c.sb("xT", [128, 16, NT], F32), norms=c.sb("norms", [128, 7, 16], F32), onesb=c.sb("onesb", [128, 128], BF16),
             pb=[c.ps("pb%d" % i, [128, 512], F32) for i in range(8)])
    ones = c.sb("ones", [128, 128], F32); ident = c.sb("ident", [128, 128], F32); negmask = c.sb("negmask", [64, 64], F32)
    zeros = c.sb("zeros", [128, 64], F32); strict01 = c.sb("strict01", [64, 64], F32)
    idxE = c.sb("idxE_t", [128, NE], I32); idxO = c.sb("idxO_t", [128, NO], I32)
    idxM0 = c.sb("idxM0_t", [128, 32], I32); idxM1 = c.sb("idxM1_t", [128, 32], I32)
    bglu = c.sb("bglu", [128, 8], F32)
    c.memset(P["onesb"].v(), 1.0)
    c.memset(ones.v(), 1.0)
    c.memset(zeros.v(), 0.0)
    c.emit("pool", lambda e: e.affine_select(out=ident.t[:], in_=ones.t[:], pattern=[[1, 128]], compare_op=ALU.is_equal,
                                             fill=0.0, base=0, channel_multiplier=-1), [ident.v()], [ones.v()])
    c.emit("pool", lambda e: e.affine_select(out=negmask.t[:], in_=zeros.t[0:64, :], pattern=[[1, 64]], compare_op=ALU.is_ge,
                                             fill=NEG, base=0, channel_multiplier=-1), [negmask.v()], [zeros.v()])
    c.emit("pool", lambda e: e.affine_select(out=strict01.t[:], in_=ones.t[0:64, 0:64], pattern=[[1, 64]], compare_op=ALU.is_gt,
                                             fill=0.0, base=0, channel_multiplier=-1), [strict01.v()], [ones.v()])
    for t, d in ((idxE, idxE_d), (idxO, idxO_d), (idxM0, idxM0_d), (idxM1, idxM1_d), (bglu, bglu_d), (P["norms"], norms_d)):
        c.dma("sp", t.v(), d.v())
    consts = dict(ones=ones, ident=ident, negmask=negmask, zeros=zeros, strict01=strict01, pb=P["pb"], uidbox=[0])
    dbg_stamps = []

    def dump(name, buf):
        if debug:
            shape = list(buf.t.shape)
            d = c.dram("dbg_" + name, shape, F32, kind="ExternalOutput")
            dbg_stamps.append(c.dma("sp", View(d.t, [Dep()]), buf.v()))

    def ag(S, G):
        for seg in range(8):
            c.allgather(S.t[seg], G.t[seg].rearrange("a r t -> (a r) t"), [S.v(S.t[seg])], [G.v(G.t[seg], key=seg)], GROUPS)

    with c.scope():
        T = TokF(c, P)
        T.load_x(xin)
        T.ffn(0, *W["a"])
        T.proj_send(4, w_in_e, EVEN_IN, S0, G0, lambda b: ("e1", 17) if b >= 4 else ("e2", 4), order=[20] + list(range(4, 20)) + [0, 1, 2, 3])
    dump("S0", S0)
    if upto >= 2:
        with c.scope():
            M = EvenF(c, L, consts)
            M.setup_io(G0, EVEN_IN, idxE, S1a, S1b, G1a, G1b, GROUPS, "me")
            M.gdn(ext, with_s5=True)
    if upto >= 3:
        with c.scope():
            T = TokF(c, P, yf=True)
            T.mix_gather(G1a, G1b, idxM0, w_out_e, even=dict(bglu=bglu, wglu=w_glu))
            T.ffn(1, *W["b"])
            T.ffn(2, *W["c"])
            T.proj_send(5, w_in_o, ODD_IN, S2, G2, lambda b: ("o1", 13) if b <= 12 else ("o2", 12))
            if debug:
                xd = c.dram("dbg_x4T", [D, NT], F32, kind="ExternalOutput")
                T.store_x(xd)
                dbg_stamps.extend(T.out_stamps)
        dump("S2", S2)
    if upto >= 4:
        with c.scope():
            M = OddF(c, L, consts)
            M.setup_io(G2, ODD_IN, idxO, S3a, S3b, G3a, G3b, GROUPS, "mo")
            M.mlstm(ext)
        with c.scope():
            M = OddF(c, L, consts)
            M.setup_io(G2, ODD_IN, idxO, S3a, S3b, G3a, G3b, GROUPS, "mo")
            M.retention(ext)
    with c.scope():
        T = TokF(c, P)
        if upto >= 5:
            T.mix_gather(G3a, G3b, idxM1, w_out_o)
            T.ffn(3, *W["d"])
        T.final_norm(6, outT)
        fin = T.out_stamps
    c.finish(fin + dbg_stamps)
    c.close()
    return c


def _nl(g):
    return np.ascontiguousarray(np.asarray(g, np.float32).reshape(16, 128).T)


def host_inputs(inp):
    A = np.asarray
    f32 = np.float32
    C_ = np.ascontiguousarray
    x = A(inp["x"]).astype(f32, copy=False).reshape(8, NT, D)
    fn, mn, fin = A(inp["ffn_norm"]), A(inp["mix_norm"]), A(inp["final_norm"])
    norms = C_(np.stack([_nl(fn[0, 0]), _nl(fn[0, 1]), _nl(fn[1, 0]), _nl(fn[1, 1]), _nl(mn[0]), _nl(mn[1]), _nl(fin)], axis=1))
    W1, W3, W2 = A(inp["ffn_w1"]), A(inp["ffn_w3"]), A(inp["ffn_w2"])
    shared = {"norms_in": norms, "w_in_e": A(inp["even_w_in"])[0], "w_out_e": A(inp["even_w_out"])[0],
              "w_in_o": A(inp["odd_w_in"])[0], "w_out_o": A(inp["odd_w_out"])[0], "w_glu": A(inp["s5_w_glu"])[0],
              "bglu_in": C_(A(inp["s5_b_glu"])[0].astype(f32).reshape(8, 128).T),
              "g_ng": C_(np.broadcast_to(A(inp["gdn_norm"])[0][None], (64, 128)).astype(f32)),
              "m_ng": C_(np.broadcast_to(A(inp["mlstm_norm"])[0][None], (64, 256)).astype(f32)),
              "r_ng": C_(np.broadcast_to(A(inp["ret_norm"])[0][None], (64, 256)).astype(f32))}
    for tag, (l, i) in zip("abcd", ((0, 0), (0, 1), (1, 0), (1, 1))):
        shared["w1_" + tag] = W1[l, i]; shared["w3_" + tag] = W3[l, i]; shared["w2_" + tag] = W2[l, i]
    cw = A(inp["gdn_conv_w"])[0]
    P5 = {k: A(inp["s5_" + k])[0] for k in ("a_re", "a_im", "log_step", "b_re", "b_im", "c_re", "c_im", "d")}
    pos = A(inp["positions"]).astype(np.int32)
    gb = A(inp["mlstm_gate_bias"])[0]
    p = np.arange(128)

    def fr(seg, r):
        r = np.asarray(r)
        return ((r // 256) * 8 + seg) * 256 + r % 256
    maps = []
    for core in range(8):
        b, r = core // 4, core % 4
        m = dict(shared)
        m["xT_in"] = C_(x[core].T)
        hg = r
        heads = [2 * hg, 2 * hg + 1]
        m["g_cw"] = C_(np.stack([np.stack([cw[:, off + h * 128: off + (h + 1) * 128].T for off in (0, 1024, 2048)], axis=1) for h in heads]).astype(f32))
        m["g_par"] = C_(np.stack([np.broadcast_to(np.array([A(inp["gdn_a_log"])[0, h], A(inp["gdn_dt_bias"])[0, h]], f32)[None], (128, 2)) for h in heads]))
        s_par = np.zeros((8, 128, 3), f32); s_B = np.zeros((8, 128, 2, 16), f32); s_Cm = np.zeros((8, 128, 2, 16), f32); s_d = np.zeros((8, 32, 1), f32)
        for tl in range(8):
            for gi in range(2):
                g = hg * 16 + tl * 2 + gi
                sl = slice(gi * 64, (gi + 1) * 64)
                s_par[tl, sl, 0] = P5["a_re"][g]; s_par[tl, sl, 1] = P5["a_im"][g]; s_par[tl, sl, 2] = P5["log_step"][g]
                s_B[tl, sl, 0] = P5["b_re"][g]; s_B[tl, sl, 1] = P5["b_im"][g]
                s_Cm[tl, sl, 0] = P5["c_re"][g].T; s_Cm[tl, sl, 1] = P5["c_im"][g].T
                s_d[tl, gi * 16:(gi + 1) * 16, 0] = P5["d"][g]
        m.update(s_par=s_par, s_B=s_B, s_C=s_Cm, s_d=s_d)
        h = r
        m["m_par"] = C_(np.broadcast_to(gb[:, h][None, :], (128, 2)).astype(f32))
        m["pos"] = C_(np.broadcast_to(pos[b][None], (128, L)))
        m["hidx"] = np.full((128, 1), h, f32)
        iE = np.zeros((128, NE), np.int64)
        for seg in range(8):
            for hh in range(2):
                hd = 2 * hg + hh
                for ci, base in enumerate((1024, 2048, 3072, 4096)):
                    iE[:, seg * 8 + hh * 4 + ci] = fr(seg, base + hd * 128 + p)
        cidx = np.arange(64)
        for hh in range(2):
            hd = 2 * hg + hh
            for gi, rbase in enumerate((5120, 5128)):
                iE[:64, 64 + hh * 2 + gi] = fr(cidx // 8, rbase + hd) * 8 + cidx % 8
        for tl in range(8):
            for ch in range(8):
                iE[:32, 68 + tl * 8 + ch] = fr(ch, hg * 256 + tl * 32 + np.arange(32))
        m["idxE"] = C_(iE.astype(np.int32))
        iO = np.zeros((128, NO), np.int64)
        sw = (p + 64) % 128
        for seg in range(8):
            rows = [h * 128 + p, 512 + h * 128 + p, 1024 + h * 256 + p, 1024 + h * 256 + 128 + p, 2048 + h * 256 + p, 2048 + h * 256 + 128 + p,
                    3080 + h * 128 + p, 3080 + h * 128 + sw, 3592 + h * 128 + p, 3592 + h * 128 + sw,
                    4104 + h * 256 + p, 4104 + h * 256 + 128 + p, 5128 + h * 256 + p, 5128 + h * 256 + 128 + p]
            for k, rr in enumerate(rows):
                iO[:, seg * 14 + k] = fr(seg, rr)
        for gi, rbase in enumerate((3072, 3076)):
            iO[:64, 112 + gi] = fr(cidx // 8, rbase + h) * 8 + cidx % 8
        m["idxO"] = C_(iO.astype(np.int32))
        q = r
        iM0 = np.zeros((128, 32), np.int64)
        for k in range(16):
            f_ = (k % 8) * 128 + p
            for half in range(2):
                iM0[:, k * 2 + half] = ((q * 4 + f_ // 256) * 2 + half) * 256 + f_ % 256
        iM1 = iM0
        m["idxM0"] = C_(iM0.astype(np.int32)); m["idxM1"] = C_(iM1.astype(np.int32))
        maps.append(m)
    return maps


def kernel(**inputs):
    c = build(debug=False)
    maps = host_inputs(inputs)
    res = run_bass_kernel_spmd(c.nc, maps, core_ids=list(range(8)))
    out = np.stack([res.results[i]["outT"].T for i in range(8)]).reshape(2, L, D)
    return np.ascontiguousarray(out.astype(np.float32, copy=False))
```

```python
import contextlib
import numpy as np
import concourse.bass as bass
import concourse.mybir as mybir
from concourse.bass_utils import run_bass_kernel_spmd

F32 = mybir.dt.float32
BF16 = mybir.dt.bfloat16
I32 = mybir.dt.int32
AF = mybir.ActivationFunctionType
ALU = mybir.AluOpType
AX = mybir.AxisListType


class Dep:
    __slots__ = ("w", "r")

    def __init__(self):
        self.w = None
        self.r = {}


class View:
    __slots__ = ("ap", "deps")

    def __init__(self, ap, deps):
        self.ap = ap
        self.deps = deps


class Buf:
    def __init__(self, t):
        self.t = t
        self.base = Dep()
        self.parts = {}

    def dep(self, key=None):
        if key is None:
            return self.base
        d = self.parts.get(key)
        if d is None:
            d = self.parts[key] = Dep()
            d.w = self.base.w
            d.r = dict(self.base.r)
        return d

    def v(self, ap=None, key=None):
        if ap is None:
            ap = self.t[:]
        if key is None:
            return View(ap, [self.base] + list(self.parts.values()))
        if isinstance(key, list):
            return View(ap, [self.dep(k) for k in key])
        return View(ap, [self.dep(key)])

    def all(self, ap=None):
        if ap is None:
            ap = self.t[:]
        return View(ap, [self.base] + list(self.parts.values()))


class Eng:
    def __init__(self, name, handle, sem, self_sync):
        self.name = name
        self.h = handle
        self.sem = sem
        self.count = 0
        self.known = {}
        self.self_sync = self_sync
        self.own = {id(sem)}


SEM_ROT = 1800


class Ctx:
    def __init__(self, n_dma_sems=14):
        self.nc = bass.Bass("TRN2", target_bir_lowering=False)
        nc = self.nc
        self.es = contextlib.ExitStack()
        self.sems = {}
        self.E = {}
        for name, h, ss in (("pe", nc.tensor, False), ("act", nc.scalar, True), ("dve", nc.vector, True),
                            ("pool", nc.gpsimd, True), ("sp", nc.sync, True)):
            s = self.es.enter_context(nc.semaphore("s_" + name))
            self.E[name] = Eng(name, h, s, ss)
            self.sems[id(s)] = s
        self.rings = {}
        for q in ("sp", "pool"):
            ring = []
            for i in range(n_dma_sems):
                s = self.es.enter_context(nc.semaphore("d_%s%d" % (q, i)))
                self.sems[id(s)] = s
                ring.append([s, 0])
            self.rings[q] = [ring, 0]
        self.out_stamps = []
        self.cc_stamps = []
        self.cc_groups = {}
        self.cc_count = 0
        self.sem_es = self.es
        self.n_wait = 0
        self.n_ins = 0

    def sb(self, name, shape, dtype=F32):
        self.nalloc = getattr(self, "nalloc", 0) + 1
        return Buf(self.es.enter_context(self.nc.sbuf_tensor("%s_a%d" % (name, self.nalloc), list(shape), dtype)))

    def ps(self, name, shape, dtype=F32):
        return Buf(self.es.enter_context(self.nc.psum_tensor(name, list(shape), dtype)))

    def dram(self, name, shape, dtype=F32, kind="Internal"):
        if kind == "Internal":
            t = self.nc.dram_tensor(name, list(shape), dtype)
        else:
            t = self.nc.dram_tensor(name, list(shape), dtype, kind=kind)
        b = Buf(t.ap())
        b.is_out = kind == "ExternalOutput"
        return b

    def _waits(self, E, reads, writes, extra=()):
        need = {}

        def add(st):
            if st is None:
                return
            sid, val = st
            if need.get(sid, 0) < val:
                need[sid] = val

        for v in reads:
            for d in v.deps:
                add(d.w)
        for v in writes:
            for d in v.deps:
                add(d.w)
                for sid, val in d.r.items():
                    add((sid, val))
        for st in extra:
            add(st)
        for sid, val in need.items():
            if sid in E.own and not E.self_sync:
                continue
            if E.known.get(sid, 0) < val:
                E.h.wait_ge(self.sems[sid], val)
                E.known[sid] = val
                self.n_wait += 1

    def _commit(self, stamp, reads, writes):
        sid, val = stamp
        for v in reads:
            for d in v.deps:
                if d.r.get(sid, 0) < val:
                    d.r[sid] = val
        for v in writes:
            for d in v.deps:
                d.w = stamp
                d.r = {}

    def emit(self, eng, fn, writes, reads):
        E = self.E[eng]
        self._waits(E, reads, writes)
        if E.count >= SEM_ROT:
            ns = self.sem_es.enter_context(self.nc.semaphore("s_%s_%d" % (eng, self.n_ins)))
            self.sems[id(ns)] = ns
            E.prev = (id(E.sem), E.count)
            E.sem = ns
            E.count = 0
            E.own.add(id(ns))
        ins = fn(E.h)
        E.count += 1
        ins.then_inc(E.sem, 1)
        self.n_ins += 1
        self._commit((id(E.sem), E.count), reads, writes)
        return ins

    def barrier(self):
        stamps = []
        for e in self.E.values():
            if e.count:
                stamps.append((id(e.sem), e.count))
            elif getattr(e, "prev", None):
                stamps.append(e.prev)
        for ring, _ in self.rings.values():
            for sem, total in ring:
                if total:
                    stamps.append((id(sem), total))
        for name, e in self.E.items():
            for sid, val in stamps:
                if sid in e.own and sid != id(e.sem):
                    continue
                if sid == id(e.sem) and not e.self_sync:
                    continue
                if e.known.get(sid, 0) < val:
                    e.h.wait_ge(self.sems[sid], val)
                    e.known[sid] = val
                    self.n_wait += 1

    @contextlib.contextmanager
    def scope(self):
        outer = self.es
        inner = contextlib.ExitStack()
        self.es = inner
        try:
            yield
        finally:
            self.barrier()
            self.es = outer
            inner.close()

    def allgather(self, snd_ap, rcv_ap, snd_views, rcv_views, groups, grp="g", grp_total=None):
        E = self.E["pool"]
        self._waits(E, snd_views, rcv_views)
        st = self.cc_groups.get(grp)
        if st is None:
            s = self.sem_es.enter_context(self.nc.semaphore("cc_" + grp))
            self.sems[id(s)] = s
            st = self.cc_groups[grp] = [s, 0, grp_total]
        ins = E.h.collective_compute("AllGather", ALU.bypass, replica_groups=groups, ins=[snd_ap.opt()], outs=[rcv_ap.opt()])
        ins.then_inc(st[0], 1)
        st[1] += 1
        assert st[1] <= st[2]
        self.cc_count += 1
        self.n_ins += 1
        stamp = (id(st[0]), st[2])
        self._commit(stamp, snd_views, rcv_views)
        return stamp

    def gather(self, out, G, g_ap, idx, blocks=None):
        q = "pool"
        E = self.E[q]
        ringinfo = self.rings[q]
        ring, i = ringinfo
        slot = ring[i]
        ringinfo[1] = (i + 1) % len(ring)
        sem, total = slot
        extra = [(id(sem), total)] if total else []
        gv = G.v() if blocks is None else G.v(key=list(blocks))
        self._waits(E, [gv, idx], [out], extra)
        ins = E.h.indirect_dma_start(out=out.ap, out_offset=None, in_=g_ap, in_offset=bass.IndirectOffsetOnAxis(ap=idx.ap, axis=0))
        slot[1] = total + 16
        ins.then_inc(sem, 16)
        self.n_ins += 1
        stamp = (id(sem), total + 16)
        self._commit(stamp, [gv, idx], [out])
        return stamp

    def dma(self, q, out, in_, **kw):
        E = self.E[q]
        ringinfo = self.rings[q]
        ring, idx = ringinfo
        slot = ring[idx]
        ringinfo[1] = (idx + 1) % len(ring)
        sem, total = slot
        extra = [(id(sem), total)] if total else []
        self._waits(E, [in_], [out], extra)
        ins = E.h.dma_start(out=out.ap, in_=in_.ap, **kw)
        slot[1] = total + 16
        ins.then_inc(sem, 16)
        self.n_ins += 1
        stamp = (id(sem), total + 16)
        self._commit(stamp, [in_], [out])
        return stamp

    def finish(self, stamps, eng="sp"):
        E = self.E[eng]
        self._waits(E, [], [], extra=stamps)

    def matmul(self, out, lhsT, rhs, start=True, stop=True, **kw):
        return self.emit("pe", lambda e: e.matmul(out.ap, lhsT=lhsT.ap, rhs=rhs.ap, start=start, stop=stop, **kw),
                         [out], [lhsT, rhs] + ([] if start else [out]))

    def transpose(self, out, in_, ident):
        return self.emit("pe", lambda e: e.transpose(out.ap, in_.ap, ident.ap), [out], [in_, ident])

    def act(self, out, in_, func, bias=None, scale=1.0, accum_out=None, eng="act"):
        reads = [in_]
        kw = {}
        if isinstance(bias, View):
            reads.append(bias)
            kw["bias"] = bias.ap
        elif bias is not None:
            kw["bias"] = bias
        if isinstance(scale, View):
            reads.append(scale)
            kw["scale"] = scale.ap
        else:
            kw["scale"] = scale
        writes = [out]
        if accum_out is not None:
            writes.append(accum_out)
            kw["accum_out"] = accum_out.ap
        return self.emit(eng, lambda e: e.activation(out=out.ap, in_=in_.ap, func=func, **kw), writes, reads)

    def tt(self, out, in0, in1, op, eng="dve"):
        return self.emit(eng, lambda e: e.tensor_tensor(out=out.ap, in0=in0.ap, in1=in1.ap, op=op), [out], [in0, in1])

    def ts(self, out, in0, s1, s2=None, op0=ALU.mult, op1=None, accum_out=None, eng="dve"):
        reads = [in0]
        a1 = s1
        if isinstance(s1, View):
            reads.append(s1)
            a1 = s1.ap
        a2 = s2
        if isinstance(s2, View):
            reads.append(s2)
            a2 = s2.ap
        kw = {}
        if op1 is not None:
            kw["op1"] = op1
        writes = [out]
        if accum_out is not None:
            writes.append(accum_out)
            kw["accum_out"] = accum_out.ap
        return self.emit(eng, lambda e: e.tensor_scalar(out=out.ap, in0=in0.ap, scalar1=a1, scalar2=a2, op0=op0, **kw),
                         writes, reads)

    def stt(self, out, in0, scalar, in1, op0, op1, eng="dve"):
        reads = [in0, in1]
        a = scalar
        if isinstance(scalar, View):
            reads.append(scalar)
            a = scalar.ap
        return self.emit(eng, lambda e: e.scalar_tensor_tensor(out=out.ap, in0=in0.ap, scalar=a, in1=in1.ap, op0=op0, op1=op1),
                         [out], reads)

    def copy(self, out, in_, eng="dve"):
        if eng == "act":
            return self.emit("act", lambda e: e.copy(out=out.ap, in_=in_.ap), [out], [in_])
        return self.emit(eng, lambda e: e.tensor_copy(out=out.ap, in_=in_.ap), [out], [in_])

    def memset(self, out, val, eng="dve"):
        return self.emit(eng, lambda e: e.memset(out.ap, val), [out], [])

    def scan(self, out, d0, d1, initial, op0, op1, eng="dve"):
        reads = [d0, d1]
        a = initial
        if isinstance(initial, View):
            reads.append(initial)
            a = initial.ap
        return self.emit(eng, lambda e: e.tensor_tensor_scan(out=out.ap, data0=d0.ap, data1=d1.ap, initial=a, op0=op0, op1=op1),
                         [out], reads)

    def reduce(self, out, in_, op, axis=AX.X, eng="dve", **kw):
        return self.emit(eng, lambda e: e.tensor_reduce(out=out.ap, in_=in_.ap, axis=axis, op=op, **kw), [out], [in_])

    def recip(self, out, in_):
        return self.emit("dve", lambda e: e.reciprocal(out=out.ap, in_=in_.ap), [out], [in_])

    def close(self):
        self.es.close()


NT = 1024
D = 2048
DFF = 5632
EPS = 1e-6


class Tok:
    def __init__(self, c, n_norms, yf=False):
        self.c = c
        self.xT = c.sb("xT", [128, 16, NT], F32)
        self.hT = c.sb("hT", [128, 16, NT], BF16)
        self.gT = c.sb("gT", [128, 12, NT], BF16)
        self.rstd = c.sb("rstd", [128, NT], F32)
        self.sq = [c.sb("sq%d" % i, [128, NT], BF16) for i in range(2)]
        self.wa = [c.sb("wa%d" % i, [128, 16, 256], BF16) for i in range(2)]
        self.wb = [c.sb("wb%d" % i, [128, 16, 256], BF16) for i in range(2)]
        self.w2 = [c.sb("w2_%d" % i, [128, 12, 256], BF16) for i in range(2)]
        self.tmp = [c.sb("tmp%d" % i, [128, 512], F32) for i in range(2)]
        self.stage = [c.sb("stg%d" % i, [128, NT], F32) for i in range(2)]
        self.ones = c.sb("ones", [128, 128], BF16)
        self.norms = c.sb("norms", [128, n_norms, 16], F32)
        self.pb = [c.ps("pb%d" % i, [128, 512], F32) for i in range(8)]
        self.cnt = {"wa": 0, "wb": 0, "w2": 0, "tmp": 0, "sq": 0, "stage": 0}
        c.memset(self.ones.v(), 1.0)
        self.tasks = []
        self.out_stamps = []

    def rot(self, kind):
        lst = getattr(self, kind)
        i = self.cnt[kind]
        self.cnt[kind] = i + 1
        return lst[i % len(lst)]

    def add_task(self, load, compute):
        self.tasks.append((load, compute))

    def flush(self):
        ts = self.tasks
        self.tasks = []
        if not ts:
            return
        state = [None] * len(ts)
        state[0] = ts[0][0]()
        for i in range(len(ts)):
            if i + 1 < len(ts):
                state[i + 1] = ts[i + 1][0]()
            ts[i][1](state[i])

    def load_x(self, xT_d):
        c = self.c
        v = xT_d.t.rearrange("(c p) t -> p c t", p=128)
        for k in range(16):
            c.dma("sp", self.xT.v(self.xT.t[:, k, :], key=k), View(v[:, k, :], [xT_d.dep()]))

    def load_norms(self, norms_d):
        self.c.dma("sp", self.norms.v(), norms_d.v())

    def rmsnorm(self, ni, to_bf=True, out_fn=None):
        c = self.c
        xT = self.xT
        pss = [self.pb[0], self.pb[1]]
        for k in range(16):
            s = self.rot("sq")
            c.act(s.v(), xT.v(xT.t[:, k, :], key=k), AF.Square)
            for th in range(2):
                c.matmul(pss[th].v(), self.ones.v(), s.v(s.t[:, th * 512:(th + 1) * 512]), start=(k == 0), stop=(k == 15))
        for th in range(2):
            sl = slice(th * 512, (th + 1) * 512)
            c.act(self.rstd.v(self.rstd.t[:, sl], key=th), pss[th].v(), AF.Ln, bias=EPS, scale=1.0 / D)
            c.act(self.rstd.v(self.rstd.t[:, sl], key=th), self.rstd.v(self.rstd.t[:, sl], key=th), AF.Exp, scale=-0.5)
        for k in range(16):
            if to_bf:
                out = self.hT.v(self.hT.t[:, k, :], key=k)
            else:
                out = out_fn(k)
            c.stt(out, xT.v(xT.t[:, k, :], key=k), self.norms.v(self.norms.t[:, ni, k:k + 1]),
                  self.rstd.v(self.rstd.t[:], key=[0, 1]), ALU.mult, ALU.mult)
            if not to_bf:
                out_fn(k, done=True)

    def wview(self, w_d, c0, ncols, kchunks=16):
        v = w_d.t.rearrange("(c p) f -> p c f", p=128)
        return View(v[:, 0:kchunks, c0:c0 + ncols], [w_d.dep()])

    def ffn(self, ni, w1_d, w3_d, w2_d, splits=(6, 6, 5, 5)):
        c = self.c
        self.rmsnorm(ni)
        b0 = 0
        par = [0]
        for nb in splits:
            nj = nb * 2
            for b in range(b0, b0 + nb):
                def load(b=b):
                    ta = self.rot("wa")
                    tb = self.rot("wb")
                    c.dma("pool", ta.v(), self.wview(w1_d, b * 256, 256))
                    c.dma("pool", tb.v(), self.wview(w3_d, b * 256, 256))
                    return ta, tb

                def compute(st, b=b, b0=b0):
                    ta, tb = st
                    for jj in range(2):
                        jloc = (b - b0) * 2 + jj
                        p = par[0]
                        par[0] ^= 1
                        for th in range(2):
                            pa = self.pb[p * 4 + th * 2]
                            pbk = self.pb[p * 4 + th * 2 + 1]
                            for (wt, pp) in ((ta, pa), (tb, pbk)):
                                for k in range(16):
                                    c.matmul(pp.v(), wt.v(wt.t[:, k, jj * 128:(jj + 1) * 128]),
                                             self.hT.v(self.hT.t[:, k, th * 512:(th + 1) * 512], key=k),
                                             start=(k == 0), stop=(k == 15))
                            t = self.rot("tmp")
                            c.act(t.v(), pa.v(), AF.Silu)
                            c.tt(self.gT.v(self.gT.t[:, jloc, th * 512:(th + 1) * 512], key=(jloc, th)), t.v(), pbk.v(), ALU.mult)
                self.add_task(load, compute)
            for ib in range(8):
                def load(ib=ib, b0=b0, nj=nj):
                    t2 = self.rot("w2")
                    v = w2_d.t.rearrange("(j p) d -> p j d", p=128)
                    c.dma("pool", t2.v(t2.t[:, 0:nj, :]), View(v[:, b0 * 2:b0 * 2 + nj, ib * 256:(ib + 1) * 256], [w2_d.dep()]))
                    return t2

                def compute(t2, ib=ib, nj=nj):
                    for dd in range(2):
                        i = ib * 2 + dd
                        for th in range(2):
                            pp = self.pb[(i % 2) * 4 + th]
                            for j in range(nj):
                                c.matmul(pp.v(), t2.v(t2.t[:, j, dd * 128:(dd + 1) * 128]),
                                         self.gT.v(self.gT.t[:, j, th * 512:(th + 1) * 512], key=(j, th)),
                                         start=(j == 0), stop=(j == nj - 1))
                            xv = self.xT.v(self.xT.t[:, i, th * 512:(th + 1) * 512], key=i)
                            c.stt(xv, pp.v(), 0.5, xv, ALU.mult, ALU.add)
                self.add_task(load, compute)
            b0 += nb
        self.flush()

    def proj_out(self, ni, w_d, N, out_d):
        c = self.c
        self.rmsnorm(ni)
        nblk = (N + 255) // 256
        for b in range(nblk):
            ncols = min(256, N - b * 256)

            def load(b=b, ncols=ncols):
                ta = self.rot("wa")
                c.dma("pool", ta.v(ta.t[:, :, 0:ncols]), self.wview(w_d, b * 256, ncols))
                return ta

            def compute(ta, b=b, ncols=ncols):
                for jj in range((ncols + 127) // 128):
                    m = min(128, ncols - jj * 128)
                    n0 = b * 256 + jj * 128
                    sg = self.rot("stage")
                    for th in range(2):
                        pp = self.pb[(jj % 2) * 2 + th]
                        for k in range(16):
                            c.matmul(pp.v(pp.t[0:m, :]), ta.v(ta.t[:, k, jj * 128:jj * 128 + m]),
                                     self.hT.v(self.hT.t[:, k, th * 512:(th + 1) * 512], key=k),
                                     start=(k == 0), stop=(k == 15))
                        c.copy(sg.v(sg.t[0:m, th * 512:(th + 1) * 512]), pp.v(pp.t[0:m, :]), eng="act")
                    st = c.dma("sp", View(out_d.t[n0:n0 + m, :], [Dep()]), sg.v(sg.t[0:m, :]))
                    self.out_stamps.append(st)
            self.add_task(load, compute)
        self.flush()

    def store_x(self, out_d):
        c = self.c
        v = out_d.t.rearrange("(c p) t -> p c t", p=128)
        for k in range(16):
            st = c.dma("sp", View(v[:, k, :], [Dep()]), self.xT.v(self.xT.t[:, k, :], key=k))
            self.out_stamps.append(st)

    def final_norm(self, ni, out_d):
        c = self.c
        v = out_d.t.rearrange("(c p) t -> p c t", p=128)
        cur = {}

        def out_fn(k, done=False):
            if not done:
                cur[k] = self.rot("stage")
                return cur[k].v()
            st = c.dma("sp", View(v[:, k, :], [Dep()]), cur[k].v())
            self.out_stamps.append(st)
        self.rmsnorm(ni, to_bf=False, out_fn=out_fn)

    def mix_out(self, yT_d, w_out_d, even=None):
        c = self.c
        yv = yT_d.t.rearrange("(c p) t -> p c t", p=128)
        k0 = 0
        if even is not None:
            k0 = 8
        for k in range(k0, 16):
            c.dma("pool", self.hT.v(self.hT.t[:, k, :], key=k), View(yv[:, k, :], [yT_d.dep()]))
        if even is not None:
            self.s5_glu(yv, yT_d, even)
        src = []
        for k in range(16):
            if even is not None and k < 8:
                src.append((self.gT, k))
            else:
                src.append((self.hT, k))
        for b in range(8):
            def load(b=b):
                ta = self.rot("wa")
                c.dma("pool", ta.v(), self.wview(w_out_d, b * 256, 256))
                return ta

            def compute(ta, b=b):
                for jj in range(2):
                    i = b * 2 + jj
                    for th in range(2):
                        pp = self.pb[(i % 2) * 2 + th]
                        for k in range(16):
                            buf, kk = src[k]
                            key = kk if buf is self.hT else [(kk, 0), (kk, 1)]
                            c.matmul(pp.v(), ta.v(ta.t[:, k, jj * 128:(jj + 1) * 128]),
                                     buf.v(buf.t[:, kk, th * 512:(th + 1) * 512], key=key),
                                     start=(k == 0), stop=(k == 15))
                        xv = self.xT.v(self.xT.t[:, i, th * 512:(th + 1) * 512], key=i)
                        c.stt(xv, pp.v(), 1.0, xv, ALU.mult, ALU.add)
            self.add_task(load, compute)
        self.flush()

    def s5_glu(self, yv, yT_d, even):
        c = self.c
        yf = even["yf"]
        t1 = even["t1"]
        bglu = even["bglu"]
        wglu_d = even["wglu"]
        for th in range(2):
            sl = slice(th * 512, (th + 1) * 512)
            for k in range(8):
                yk = yf.v(yf.t[:, k, :], key=k)
                c.dma("sp", yk, View(yv[:, k, sl], [yT_d.dep()]))
                c.tt(t1.v(), yk, yk, ALU.mult)
                c.ts(t1.v(), t1.v(), 0.044715, 1.0, ALU.mult, ALU.add)
                c.tt(t1.v(), t1.v(), yk, ALU.mult)
                c.act(t1.v(), t1.v(), AF.Sigmoid, scale=2.0 * 0.7978845608028654)
                c.tt(yk, yk, t1.v(), ALU.mult)
                c.copy(self.hT.v(self.hT.t[:, k, sl], key=k), yk, eng="act")
            for b in range(4):
                ta = self.rot("wa")
                c.dma("pool", ta.v(ta.t[:, 0:8, :]), self.wview(wglu_d, b * 256, 256, kchunks=8))
                for jj in range(2):
                    n = b * 2 + jj
                    pp = self.pb[4 + (n % 2)]
                    for k in range(8):
                        c.matmul(pp.v(), ta.v(ta.t[:, k, jj * 128:(jj + 1) * 128]), self.hT.v(self.hT.t[:, k, sl], key=k),
                                 start=(k == 0), stop=(k == 7))
                    t = self.rot("tmp")
                    c.act(t.v(), pp.v(), AF.Sigmoid, bias=bglu.v(bglu.t[:, n:n + 1]))
                    c.tt(self.gT.v(self.gT.t[:, n, sl], key=(n, th)), t.v(), yf.v(yf.t[:, n, :], key=n), ALU.mult)

import math

NEG = -1.0e30


class MixBase:
    def __init__(self, c, L):
        self.c = c
        self.L = L
        self.n = L // 64
        self.nseg = L // 512
        self.ones = c.sb("ones", [128, 128], F32)
        self.ident = c.sb("ident", [128, 128], F32)
        self.negmask = c.sb("negmask", [64, 64], F32)
        self.zeros = c.sb("zeros", [128, 64], F32)
        c.memset(self.ones.v(), 1.0)
        c.memset(self.zeros.v(), 0.0)
        c.emit("pool", lambda e: e.affine_select(out=self.ident.t[:], in_=self.ones.t[:], pattern=[[1, 128]],
                                                 compare_op=ALU.is_equal, fill=0.0, base=0, channel_multiplier=-1),
               [self.ident.v()], [self.ones.v()])
        c.emit("pool", lambda e: e.affine_select(out=self.negmask.t[:], in_=self.zeros.t[0:64, :], pattern=[[1, 64]],
                                                 compare_op=ALU.is_ge, fill=NEG, base=0, channel_multiplier=-1),
               [self.negmask.v()], [self.zeros.v()])
        self.pb = [c.ps("pb%d" % i, [128, 512], F32) for i in range(8)]
        self.out_stamps = []
        self.uid = 0

    def dbg(self, nm, buf, shape):
        d = self.c.dram("dbg_" + nm, list(shape), kind="ExternalOutput")
        st = self.c.dma("sp", View(d.t, [Dep()]), buf.all())
        self.out_stamps.append(st)

    def name(self, s):
        self.uid += 1
        return "%s_%d" % (s, self.uid)

    def mk_(self, s, shape, dt=F32):
        return self.c.sb(self.name(s), shape, dt)

    def rep_chunk(self, colN, out, bank):
        c, n = self.c, self.n
        t = self.mk_("rep", [n, 128])
        c.ts(t.v(), self.ones.v(self.ones.t[0:n, :]), colN, None, ALU.mult)
        pp = self.pb[bank]
        c.matmul(pp.v(pp.t[:, 0:n]), t.v(), self.ident.v(self.ident.t[0:n, 0:n]))
        c.copy(out, pp.v(pp.t[:, 0:n]), eng="act")

    def to_col(self, XN, out, bank):
        c, n = self.c, self.n
        pp = self.pb[bank]
        c.transpose(pp.v(pp.t[0:64, 0:n]), XN, self.ident.v(self.ident.t[0:n, 0:n]))
        c.copy(out, pp.v(pp.t[0:64, 0:n]), eng="act")

    def to_row_seg(self, XN_buf, seg, out, bank, E):
        c, n = self.c, self.n
        j0 = seg * 8
        c.tt(E.v(), self.ident.v(self.ident.t[0:n, j0:j0 + 8].unsqueeze(2).to_broadcast([n, 8, 64])),
             XN_buf.v(XN_buf.t[:, :].unsqueeze(1).to_broadcast([n, 8, 64])), ALU.mult)
        pp = self.pb[bank]
        c.matmul(pp.v(), self.ones.v(self.ones.t[0:n, :]), E.v(E.t[:].rearrange("p a b -> p (a b)")))
        c.copy(out, pp.v(), eng="act")

    def chunk_loop(self, seg, QtT, ST, V, Kc, decay, S, dvx, O, pre_hook=None, post_chunk=None):
        c = self.c
        for j in range(8):
            g = seg * 8 + j
            cur = S["bufs"][S["i"] % 2]
            nxt = S["bufs"][(S["i"] + 1) % 2]
            S["i"] += 1
            Vj = V.v(V.t[:, j, 0:dvx], key=j)
            if pre_hook is not None:
                Vj = pre_hook(j, g, cur)
            po = self.pb[1 + (g % 2)]
            pov = po.v(po.t[0:64, 0:dvx])
            c.matmul(pov, QtT.v(QtT.t[:, j * 64:(j + 1) * 64]), cur.v(cur.t[:, 0:dvx]), start=True, stop=False)
            c.matmul(pov, ST.v(ST.t[:, j, :], key=j), Vj, start=False, stop=True)
            c.copy(O.v(O.t[:, j, 0:dvx], key=j), pov, eng="act")
            pu = self.pb[3 + (g % 2)]
            puv = pu.v(pu.t[:, 0:dvx])
            c.matmul(puv, Kc.v(Kc.t[:, j, :], key=j), Vj)
            c.stt(nxt.v(nxt.t[:, 0:dvx]), cur.v(cur.t[:, 0:dvx]), decay(g), puv, ALU.mult, ALU.add)
            if post_chunk is not None:
                post_chunk()

    def scores(self, KT, QT, DT, ST, bank=0):
        c = self.c
        pp = self.pb[bank]
        for j in range(8):
            c.matmul(pp.v(pp.t[0:64, j * 64:(j + 1) * 64], key=j), KT.v(KT.t[:, j * 64:(j + 1) * 64]), QT.v(QT.t[:, j * 64:(j + 1) * 64]))
        c.tt(ST.all(), pp.all(pp.t[0:64, :].rearrange("p (a b) -> p a b", a=8)), DT.all(), ALU.mult)

    def sincos_tables(self, ang, sin_out, cos_out, tmp_f, tmp_i):
        c = self.c
        for (off, out) in ((0.5, sin_out), (0.75, cos_out)):
            c.ts(tmp_f, ang, 1.0 / (2 * math.pi), off, ALU.mult, ALU.add)
            c.copy(tmp_i, tmp_f)
            c.copy(out, tmp_i)
            c.tt(tmp_f, tmp_f, out, ALU.subtract)
            c.act(out, tmp_f, AF.Sign)
            c.ts(tmp_f, tmp_f, 2 * math.pi, None, ALU.mult)
            c.stt(tmp_f, out, -math.pi, tmp_f, ALU.mult, ALU.add)
            c.ts(tmp_f, tmp_f, math.pi, -math.pi, ALU.min, ALU.max)
            c.act(out, tmp_f, AF.Sin)


def dview(d, ap):
    return View(ap, [d.dep()])


class Odd(MixBase):
    def __init__(self, c, L):
        super().__init__(c, L)
        n = self.n
        D = c.dram
        self.i = dict(
            m_qT=D("m_qT", [128, L], kind="ExternalInput"), m_kT=D("m_kT", [128, L], kind="ExternalInput"),
            m_k=D("m_k", [L, 128], kind="ExternalInput"), m_v=D("m_v", [L, 256], kind="ExternalInput"),
            m_o=D("m_o", [L, 256], kind="ExternalInput"), m_gN=D("m_gN", [n, 2, 64], kind="ExternalInput"),
            m_par=D("m_par", [128, 2], kind="ExternalInput"), m_ng=D("m_ng", [64, 256], kind="ExternalInput"),
            r_qT=D("r_qT", [128, L], kind="ExternalInput"), r_qTs=D("r_qTs", [128, L], kind="ExternalInput"),
            r_kT=D("r_kT", [128, L], kind="ExternalInput"), r_kTs=D("r_kTs", [128, L], kind="ExternalInput"),
            r_v=D("r_v", [L, 256], kind="ExternalInput"), r_g=D("r_g", [L, 256], kind="ExternalInput"),
            pos=D("pos", [128, L], I32, kind="ExternalInput"), hidx=D("hidx", [128, 1], kind="ExternalInput"),
            r_ng=D("r_ng", [64, 256], kind="ExternalInput"),
        )
        self.yc = D("yc", [L, 256], kind="ExternalOutput")
        self.yd = D("yd", [L, 256], kind="ExternalOutput")

    def tokview(self, d, seg, w):
        return dview(d, d.t[seg * 512:(seg + 1) * 512, :].rearrange("(a p) d -> p a d", p=64)[:, :, 0:w])

    def mlstm(self):
        c, n, L = self.c, self.n, self.L
        I = self.i
        T = self.mk_
        par = T("mpar", [128, 2])
        c.dma("sp", par.v(), I["m_par"].v())
        gN = T("gN", [n, 2, 64])
        c.dma("sp", gN.v(), I["m_gN"].v())
        ng = T("mng", [64, 256])
        c.dma("sp", ng.v(), I["m_ng"].v())
        nbf = T("nbf", [128, 1])
        c.ts(nbf.v(), par.v(par.t[:, 1:2]), -1.0, None, ALU.mult)
        ig = T("ig", [n, 64]); sp = T("sp", [n, 64]); csp = T("csp", [n, 64]); bcum = T("bcum", [n, 64])
        colB = T("colB", [n, 64]); cmx = T("cmx", [n, 64]); imax = T("imax", [n, 64]); ul = T("ul", [n, 64])
        umax = T("umax", [n, 1]); blast = T("blast", [n, 1])
        c.ts(ig.v(), gN.v(gN.t[:, 0, :]), par.v(par.t[0:n, 0:1]), None, ALU.add)
        c.act(sp.v(), gN.v(gN.t[:, 1, :]), AF.Exp, bias=nbf.v(nbf.t[0:n, :]), scale=-1.0)
        c.act(sp.v(), sp.v(), AF.Ln, bias=1.0)
        c.scan(csp.v(), self.ones.v(self.ones.t[0:n, 0:64]), sp.v(), 0.0, ALU.mult, ALU.add)
        c.ts(bcum.v(), csp.v(), -1.0, None, ALU.mult)
        c.tt(colB.v(), ig.v(), csp.v(), ALU.add)
        c.scan(cmx.v(), self.zeros.v(self.zeros.t[0:n, :]), colB.v(), NEG, ALU.add, ALU.max)
        c.tt(imax.v(), bcum.v(), cmx.v(), ALU.add)
        c.copy(blast.v(), bcum.v(bcum.t[:, 63:64]))
        c.ts(ul.v(), colB.v(), blast.v(), None, ALU.add)
        c.reduce(umax.v(), ul.v(), ALU.max)
        Rb = T("Rb", [128, n]); Ru = T("Ru", [128, n]); Mn = T("Mn", [128, n]); Mo = T("Mo", [128, n]); Rc = T("Rc", [128, n])
        self.rep_chunk(blast.v(), Rb.v(), 5)
        self.rep_chunk(umax.v(), Ru.v(), 6)
        c.scan(Mn.v(), Rb.v(), Ru.v(), 0.0, ALU.add, ALU.max)
        c.memset(Mo.v(), 0.0)
        if n > 1:
            c.copy(Mo.v(Mo.t[:, 1:n]), Mn.v(Mn.t[:, 0:n - 1]))
        c.tt(Rc.v(), Rb.v(), Mo.v(), ALU.add)
        c.tt(Rc.v(), Rc.v(), Mn.v(), ALU.subtract)
        c.act(Rc.v(), Rc.v(), AF.Exp)
        bcumC = T("bcumC", [64, n]); colBC = T("colBC", [64, n]); imaxC = T("imaxC", [64, n]); ulC = T("ulC", [64, n])
        self.to_col(bcum.v(), bcumC.v(), 5)
        self.to_col(colB.v(), colBC.v(), 6)
        self.to_col(imax.v(), imaxC.v(), 5)
        self.to_col(ul.v(), ulC.v(), 6)
        msC = T("msC", [64, n]); enC = T("enC", [64, n]); kwsC = T("kwsC", [64, n])
        c.tt(msC.v(), bcumC.v(), Mo.v(Mo.t[0:64, :]), ALU.add)
        c.tt(msC.v(), msC.v(), imaxC.v(), ALU.max)
        c.act(enC.v(), msC.v(), AF.Exp, scale=-1.0)
        c.tt(kwsC.v(), ulC.v(), Mn.v(Mn.t[0:64, :]), ALU.subtract)
        c.act(kwsC.v(), kwsC.v(), AF.Exp)
        if getattr(self, "debug", False):
            self.dbg("bcum", bcum, [n, 64]); self.dbg("Mn", Mn, [128, n]); self.dbg("Rc", Rc, [128, n])
            self.dbg("msC", msC, [64, n]); self.dbg("kwsC", kwsC, [64, n]); self.dbg("imax", imax, [n, 64])
        E = T("E", [n, 8, 64])
        bR = T("bR", [128, 512]); iR = T("iR", [128, 512]); msR = T("msR", [128, 512])
        QT = T("mQT", [128, 512]); QtT = T("mQtT", [128, 512]); KT = T("mKT", [128, 512])
        Kt = T("mKt", [64, 8, 128]); Kc = T("mKc", [64, 8, 128]); V = T("mV", [64, 8, 258]); O = T("mO", [64, 8, 258])
        G = T("mG", [64, 8, 256]); DT = T("mDT", [64, 8, 64]); ST = T("mST", [64, 8, 64])
        H = T("mH", [64, 8, 256]); s8 = T("ms8", [64, 8]); s8b = T("ms8b", [64, 8]); sqm = T("msq", [64, 8, 256])
        S = {"bufs": [T("mS0", [128, 258]), T("mS1", [128, 258])], "i": 0}
        c.memset(S["bufs"][0].v(), 0.0)
        for seg in range(self.nseg):
            sl = slice(seg * 512, (seg + 1) * 512)
            cs = slice(seg * 8, seg * 8 + 8)
            c.dma("sp", QT.v(), dview(I["m_qT"], I["m_qT"].t[:, sl]))
            c.dma("sp", KT.v(), dview(I["m_kT"], I["m_kT"].t[:, sl]))
            c.dma("sp", Kt.v(), self.tokview(I["m_k"], seg, 128))
            c.dma("sp", V.v(V.t[:, :, 0:256]), self.tokview(I["m_v"], seg, 256))
            c.memset(V.v(V.t[:, :, 256:257]), 1.0)
            c.memset(V.v(V.t[:, :, 257:258]), 0.0)
            c.dma("sp", G.v(), self.tokview(I["m_o"], seg, 256))
            self.to_row_seg(bcum, seg, bR.v(), 5, E)
            self.to_row_seg(imax, seg, iR.v(), 6, E)
            mo_b = Mo.v(Mo.t[:, cs].unsqueeze(2).to_broadcast([128, 8, 64]))
            b3 = bR.v(bR.t[:].rearrange("p (a b) -> p a b", a=8))
            ms3 = msR.v(msR.t[:].rearrange("p (a b) -> p a b", a=8))
            i3 = iR.v(iR.t[:].rearrange("p (a b) -> p a b", a=8))
            c.tt(i3, i3, i3, ALU.max) if False else None
            c.tt(ms3, b3, mo_b, ALU.add)
            c.tt(iR.v(), msR.v(), iR.v(), ALU.max)
            c.tt(msR.v(), msR.v(), iR.v(), ALU.subtract)
            c.act(msR.v(), msR.v(), AF.Exp)
            c.tt(bR.v(), bR.v(), iR.v(), ALU.subtract)
            c.ts(QT.v(), QT.v(), 128 ** -0.5, None, ALU.mult)
            c.tt(QtT.v(), QT.v(), msR.v(), ALU.mult)
            c.tt(DT.v(), bR.v(bR.t[0:64, :].rearrange("p (a b) -> p a b", a=8)),
                 self.negmask.v(self.negmask.t[:, :].unsqueeze(1).to_broadcast([64, 8, 64])), ALU.add)
            for j in range(8):
                c.act(DT.v(DT.t[:, j, :]), DT.v(DT.t[:, j, :]), AF.Exp, bias=colBC.v(colBC.t[:, seg * 8 + j:seg * 8 + j + 1]))
            self.scores(KT, QT, DT, ST)
            c.tt(Kc.v(), Kt.v(), kwsC.v(kwsC.t[:, cs].unsqueeze(2).to_broadcast([64, 8, 128])), ALU.mult)
            if getattr(self, "debug", False) and seg == 0:
                self.dbg("rowA", bR, [128, 512]); self.dbg("iw", msR, [128, 512]); self.dbg("DT", DT, [64, 8, 64]); self.dbg("ST", ST, [64, 8, 64])
            self.chunk_loop(seg, QtT, ST, V, Kc, lambda g: Rc.v(Rc.t[:, g:g + 1]), S, 258, O)
            if getattr(self, "debug", False) and seg == 0:
                self.dbg("O", O, [64, 8, 258])
            c.ts(s8.v(), O.all(O.t[:, :, 256]), -1.0, None, ALU.mult)
            c.tt(s8.v(), s8.v(), O.all(O.t[:, :, 256]), ALU.max)
            c.tt(s8.v(), s8.v(), enC.v(enC.t[:, cs]), ALU.max)
            c.recip(s8.v(), s8.v())
            c.tt(H.v(), O.all(O.t[:, :, 0:256]), s8.v(s8.t[:, :].unsqueeze(2).to_broadcast([64, 8, 256])), ALU.mult)
            self.post_rms(H, s8, s8b, ng, G, AF.Sigmoid, 256, False, sqm)
            st = c.dma("sp", View(self.yc.t[sl, :].rearrange("(a p) d -> p a d", p=64), [Dep()]), H.v())
            self.out_stamps.append(st)

    def post_rms(self, H, s8, s8b, ng, G, gate_fn, dv, center, sq):
        c = self.c
        if center:
            c.reduce(s8.v(), H.v(), ALU.add)
            c.ts(s8.v(), s8.v(), -1.0 / dv, None, ALU.mult)
            c.tt(H.v(), H.v(), s8.v(s8.t[:, :].unsqueeze(2).to_broadcast([64, 8, dv])), ALU.add)
        c.tt(sq.v(), H.v(), H.v(), ALU.mult)
        c.reduce(s8b.v(), sq.v(), ALU.add)
        c.act(s8b.v(), s8b.v(), AF.Ln, bias=1e-6, scale=1.0 / dv)
        c.act(s8b.v(), s8b.v(), AF.Exp, scale=-0.5)
        c.tt(H.v(), H.v(), s8b.v(s8b.t[:, :].unsqueeze(2).to_broadcast([64, 8, dv])), ALU.mult)
        c.tt(H.v(), H.v(), ng.v(ng.t[:, :].unsqueeze(1).to_broadcast([64, 8, dv])), ALU.mult)
        if gate_fn == "silu":
            c.act(sq.v(), G.v(), AF.Silu)
        else:
            c.act(sq.v(), G.v(), gate_fn)
        c.tt(H.v(), H.v(), sq.v(), ALU.mult)

    def retention(self):
        c, n, L = self.c, self.n, self.L
        I = self.i
        T = self.mk_
        hid = T("hid", [128, 1]); lg = T("lg", [128, 1])
        c.dma("sp", hid.v(), I["hidx"].v())
        ng = T("rng", [64, 256])
        c.dma("sp", ng.v(), I["r_ng"].v())
        c.act(lg.v(), hid.v(), AF.Exp, scale=-math.log(2.0), bias=None)
        c.ts(lg.v(), lg.v(), 2.0 ** -5, None, ALU.mult)
        c.act(lg.v(), lg.v(), AF.Ln, scale=-1.0, bias=1.0)
        io_cs = T("io_cs", [64, 64], I32); f_cs = T("f_cs", [64, 64])
        c.emit("pool", lambda e: e.iota(io_cs.t[:], pattern=[[1, 64]], base=0, channel_multiplier=-1), [io_cs.v()], [])
        c.copy(f_cs.v(), io_cs.v())
        DT1 = T("DT1", [64, 64])
        c.act(DT1.v(), f_cs.v(), AF.Exp, scale=lg.v(lg.t[0:64, :]))
        c.emit("pool", lambda e: e.affine_select(out=DT1.t[:], in_=DT1.t[:], pattern=[[1, 64]], compare_op=ALU.is_ge,
                                                 fill=0.0, base=0, channel_multiplier=-1), [DT1.v()], [DT1.v()])
        DT = T("rDT", [64, 8, 64])
        c.copy(DT.v(), DT1.v(DT1.t[:, :].unsqueeze(1).to_broadcast([64, 8, 64])))
        io_r = T("io_r", [128, 64], I32); xi = T("xi", [128, 64])
        c.emit("pool", lambda e: e.iota(io_r.t[:], pattern=[[1, 64]], base=1, channel_multiplier=0), [io_r.v()], [])
        c.copy(xi.v(), io_r.v())
        c.act(xi.v(), xi.v(), AF.Exp, scale=lg.v())
        c.ts(xi.v(), xi.v(), 128 ** -0.5, None, ALU.mult)
        io_p = T("io_p", [128, 1], I32); zeta = T("zeta", [128, 1]); gc = T("gc", [128, 1]); invf = T("invf", [128, 1])
        c.emit("pool", lambda e: e.iota(io_p.t[0:64, :], pattern=[[0, 1]], base=63, channel_multiplier=-1), [io_p.v()], [])
        c.copy(zeta.v(zeta.t[0:64, :]), io_p.v(io_p.t[0:64, :]))
        c.act(zeta.v(zeta.t[0:64, :]), zeta.v(zeta.t[0:64, :]), AF.Exp, scale=lg.v(lg.t[0:64, :]))
        c.act(gc.v(), lg.v(), AF.Exp, scale=64.0)
        for h0 in (0, 64):
            c.emit("pool", lambda e, h0=h0: e.iota(io_p.t[h0:h0 + 64, :], pattern=[[0, 1]], base=0, channel_multiplier=1), [io_p.v()], [])
        c.copy(invf.v(), io_p.v())
        c.act(invf.v(), invf.v(), AF.Exp, scale=-math.log(10000.0) / 64.0)
        sgn = T("sgn", [128, 1])
        c.memset(sgn.v(sgn.t[0:64, :]), -1.0)
        c.memset(sgn.v(sgn.t[64:128, :]), 1.0)
        posi = T("posi", [128, 512], I32); ang = T("ang", [128, 512]); sn = T("sn", [128, 512]); cs_ = T("cs", [128, 512])
        tf = T("tf", [128, 512]); ti = T("ti", [128, 512], I32)
        QT = T("rQT", [128, 512]); QTs = T("rQTs", [128, 512]); KT = T("rKT", [128, 512]); KTs = T("rKTs", [128, 512])
        QtT = T("rQtT", [128, 512]); Kc = T("rKc", [64, 8, 128]); V = T("rV", [64, 8, 256]); O = T("rO", [64, 8, 256])
        G = T("rG", [64, 8, 256]); ST = T("rST", [64, 8, 64]); s8 = T("rs8", [64, 8]); s8b = T("rs8b", [64, 8]); sqr = T("rsq", [64, 8, 256])
        S = {"bufs": [T("rS0", [128, 256]), T("rS1", [128, 256])], "i": 0}
        c.memset(S["bufs"][0].v(), 0.0)
        for seg in range(self.nseg):
            sl = slice(seg * 512, (seg + 1) * 512)
            c.dma("sp", posi.v(), dview(I["pos"], I["pos"].t[:, sl]))
            for (tl, nm) in ((QT, "r_qT"), (QTs, "r_qTs"), (KT, "r_kT"), (KTs, "r_kTs")):
                c.dma("sp", tl.v(), dview(I[nm], I[nm].t[:, sl]))
            c.dma("sp", V.v(), self.tokview(I["r_v"], seg, 256))
            c.dma("sp", G.v(), self.tokview(I["r_g"], seg, 256))
            c.copy(ang.v(), posi.v())
            c.ts(ang.v(), ang.v(), invf.v(), None, ALU.mult)
            self.sincos_tables(ang.v(), sn.v(), cs_.v(), tf.v(), ti.v())
            c.ts(sn.v(), sn.v(), sgn.v(), None, ALU.mult)
            for (A, As) in ((QT, QTs), (KT, KTs)):
                c.tt(A.v(), A.v(), cs_.v(), ALU.mult)
                c.tt(As.v(), As.v(), sn.v(), ALU.mult)
                c.tt(A.v(), A.v(), As.v(), ALU.add)
            c.tt(QtT.v(QtT.t[:].rearrange("p (a b) -> p a b", a=8)), QT.v(QT.t[:].rearrange("p (a b) -> p a b", a=8)),
                 xi.v(xi.t[:, :].unsqueeze(1).to_broadcast([128, 8, 64])), ALU.mult)
            c.ts(QT.v(), QT.v(), 128 ** -0.5, None, ALU.mult)
            self.scores(KT, QT, DT, ST)
            for j in range(8):
                pp = self.pb[5 + (j % 2)]
                c.transpose(pp.v(pp.t[0:64, 0:128]), KT.v(KT.t[:, j * 64:(j + 1) * 64]), self.ident.v())
                c.ts(Kc.v(Kc.t[:, j, :], key=j), pp.v(pp.t[0:64, 0:128]), zeta.v(zeta.t[0:64, :]), None, ALU.mult)
            self.chunk_loop(seg, QtT, ST, V, Kc, lambda g: gc.v(), S, 256, O)
            self.post_rms(O, s8, s8b, ng, G, "silu", 256, True, sqr)
            st = c.dma("sp", View(self.yd.t[sl, :].rearrange("(a p) d -> p a d", p=64), [Dep()]), O.all())
            self.out_stamps.append(st)


class Even(MixBase):
    def __init__(self, c, L, which="both"):
        super().__init__(c, L)
        n = self.n
        D = c.dram
        self.i = {}
        if which in ("both", "g"):
            self.i.update(
                g_qT=D("g_qT", [2, 128, L], kind="ExternalInput"), g_kT=D("g_kT", [2, 128, L], kind="ExternalInput"),
                g_vT=D("g_vT", [2, 128, L], kind="ExternalInput"), g_z=D("g_z", [2, L, 128], kind="ExternalInput"),
                g_cw=D("g_cw", [2, 128, 3, 4], kind="ExternalInput"), g_gN=D("g_gN", [2, n, 2, 64], kind="ExternalInput"),
                g_par=D("g_par", [2, 128, 2], kind="ExternalInput"), g_ng=D("g_ng", [64, 128], kind="ExternalInput"))
            self.yb = D("yb", [2, L, 128], kind="ExternalOutput")
        if which in ("both", "s"):
            self.i.update(
                s_uT=D("s_uT", [256, L], kind="ExternalInput"), s_par=D("s_par", [8, 128, 3], kind="ExternalInput"),
                s_B=D("s_B", [8, 128, 2, 16], kind="ExternalInput"), s_C=D("s_C", [8, 128, 2, 16], kind="ExternalInput"),
                s_d=D("s_d", [8, 32, 1], kind="ExternalInput"))
            self.ya = D("yaT", [256, L], kind="ExternalOutput")
        self.strict01 = c.sb("strict01", [64, 64], F32)
        c.emit("pool", lambda e: e.affine_select(out=self.strict01.t[:], in_=self.ones.t[0:64, 0:64], pattern=[[1, 64]],
                                                 compare_op=ALU.is_gt, fill=0.0, base=0, channel_multiplier=-1),
               [self.strict01.v()], [self.ones.v()])

    def gdn_alloc(self):
        T = self.mk_
        n = self.n
        t = {}
        for nm in ("cw",):
            t[nm] = T(nm, [128, 3, 4])
        t["par"] = T("gpar", [128, 2]); t["gN"] = T("ggN", [n, 2, 64]); t["ng"] = T("gng", [64, 128])
        for nm in ("beta", "sp", "gg", "gcum"):
            t[nm] = T("g" + nm, [n, 64])
        t["ea"] = T("gea", [128, 1]); t["glast"] = T("gglast", [n, 1])
        for nm in ("Rgl", "Rdec"):
            t[nm] = T("g" + nm, [128, n])
        for nm in ("gcumC", "ngcumC", "betaC", "egC", "bgC", "kdC"):
            t[nm] = T("g" + nm, [64, n])
        t["E"] = T("gE", [n, 8, 64])
        for nm in ("Xq", "Xk", "Xv"):
            t[nm] = T("g" + nm, [128, 515])
        for nm in ("QT", "KT", "VT", "KbT", "QdT", "tmp", "gR", "bR", "WT"):
            t[nm] = T("g" + nm, [128, 512])
        for nm in ("DT", "DTsn", "ST", "X8", "Z0", "Z1", "Y0", "Y1", "P"):
            t[nm] = T("g" + nm, [64, 8, 64])
        for nm in ("Kbg", "Kd", "Vb", "U", "Vn", "O", "G", "sq"):
            t[nm] = T("g" + nm, [64, 8, 128])
        t["s8"] = T("gs8", [64, 8]); t["s8b"] = T("gs8b", [64, 8])
        t["S"] = [T("gS0", [128, 128]), T("gS1", [128, 128])]
        return t

    def gdn_head(self, hh, t):
        c, n, L = self.c, self.n, self.L
        I = self.i

        def dv(nm, ap):
            return dview(I[nm], ap)
        c.dma("sp", t["cw"].v(), dv("g_cw", I["g_cw"].t[hh]))
        c.dma("sp", t["par"].v(), dv("g_par", I["g_par"].t[hh]))
        c.dma("sp", t["gN"].v(), dv("g_gN", I["g_gN"].t[hh]))
        c.dma("sp", t["ng"].v(), I["g_ng"].v())
        par, gN = t["par"], t["gN"]
        beta, sp, gg, gcum = t["beta"], t["sp"], t["gg"], t["gcum"]
        c.act(beta.v(), gN.v(gN.t[:, 0, :]), AF.Sigmoid)
        c.act(sp.v(), gN.v(gN.t[:, 1, :]), AF.Exp, bias=par.v(par.t[0:n, 1:2]))
        c.act(sp.v(), sp.v(), AF.Ln, bias=1.0)
        c.act(t["ea"].v(), par.v(par.t[:, 0:1]), AF.Exp)
        c.ts(gg.v(), sp.v(), t["ea"].v(t["ea"].t[0:n, :]), -1.0, ALU.mult, ALU.mult)
        c.scan(gcum.v(), self.ones.v(self.ones.t[0:n, 0:64]), gg.v(), 0.0, ALU.mult, ALU.add)
        c.copy(t["glast"].v(), gcum.v(gcum.t[:, 63:64]))
        self.rep_chunk(t["glast"].v(), t["Rgl"].v(), 5)
        c.act(t["Rdec"].v(), t["Rgl"].v(), AF.Exp)
        self.to_col(gcum.v(), t["gcumC"].v(), 5)
        self.to_col(beta.v(), t["betaC"].v(), 6)
        c.ts(t["ngcumC"].v(), t["gcumC"].v(), -1.0, None, ALU.mult)
        c.act(t["egC"].v(), t["gcumC"].v(), AF.Exp)
        c.tt(t["bgC"].v(), t["betaC"].v(), t["egC"].v(), ALU.mult)
        c.tt(t["kdC"].v(), t["Rgl"].v(t["Rgl"].t[0:64, :]), t["gcumC"].v(), ALU.subtract)
        c.act(t["kdC"].v(), t["kdC"].v(), AF.Exp)
        S = {"bufs": t["S"], "i": 0}
        c.memset(t["S"][0].v(), 0.0)
        QT, KT, VT, KbT, QdT, tmp, gR, bR, WT = (t[k] for k in ("QT", "KT", "VT", "KbT", "QdT", "tmp", "gR", "bR", "WT"))
        DT, DTsn, ST, X8, P = (t[k] for k in ("DT", "DTsn", "ST", "X8", "P"))
        Kbg, Kd, Vb, U, Vn, O, G = (t[k] for k in ("Kbg", "Kd", "Vb", "U", "Vn", "O", "G"))
        for seg in range(self.nseg):
            sl = slice(seg * 512, (seg + 1) * 512)
            cs = slice(seg * 8, seg * 8 + 8)
            for ci, (X, src, dst) in enumerate(((t["Xq"], "g_qT", QT), (t["Xk"], "g_kT", KT), (t["Xv"], "g_vT", VT))):
                if seg == 0:
                    c.memset(X.v(X.t[:, 0:3]), 0.0)
                    c.dma("sp", X.v(X.t[:, 3:515]), dv(src, I[src].t[hh, :, 0:512]))
                else:
                    c.dma("sp", X.v(), dv(src, I[src].t[hh, :, seg * 512 - 3:seg * 512 + 512]))
                cw = t["cw"]
                c.ts(dst.v(), X.v(X.t[:, 0:512]), cw.v(cw.t[:, ci, 0:1]), None, ALU.mult)
                for j in range(1, 4):
                    c.stt(dst.v(), X.v(X.t[:, j:j + 512]), cw.v(cw.t[:, ci, j:j + 1]), dst.v(), ALU.mult, ALU.add)
                c.act(dst.v(), dst.v(), AF.Silu)
                if ci < 2:
                    c.tt(tmp.v(), dst.v(), dst.v(), ALU.mult)
                    pp = self.pb[5 + ci]
                    c.matmul(pp.v(), self.ones.v(), tmp.v())
                    c.act(tmp.v(), pp.v(), AF.Ln, bias=1e-6)
                    c.act(tmp.v(), tmp.v(), AF.Exp, scale=-0.5)
                    if ci == 0:
                        c.stt(dst.v(), dst.v(), 128 ** -0.5, tmp.v(), ALU.mult, ALU.mult)
                    else:
                        c.tt(dst.v(), dst.v(), tmp.v(), ALU.mult)
            c.dma("sp", G.v(), dv("g_z", I["g_z"].t[hh, sl, :].rearrange("(a p) d -> p a d", p=64)))
            self.to_row_seg(gcum, seg, gR.v(), 5, t["E"])
            self.to_row_seg(beta, seg, bR.v(), 6, t["E"])
            c.tt(KbT.v(), KT.v(), bR.v(), ALU.mult)
            c.act(tmp.v(), gR.v(), AF.Exp)
            c.tt(QdT.v(), QT.v(), tmp.v(), ALU.mult)
            c.tt(DT.v(), gR.v(gR.t[0:64, :].rearrange("p (a b) -> p a b", a=8)),
                 self.negmask.v(self.negmask.t[:, :].unsqueeze(1).to_broadcast([64, 8, 64])), ALU.add)
            for j in range(8):
                g = seg * 8 + j
                c.act(DT.v(DT.t[:, j, :]), DT.v(DT.t[:, j, :]), AF.Exp, bias=t["ngcumC"].v(t["ngcumC"].t[:, g:g + 1]))
            c.stt(DTsn.v(), DT.v(), -1.0, self.strict01.v(self.strict01.t[:, :].unsqueeze(1).to_broadcast([64, 8, 64])), ALU.mult, ALU.mult)
            self.scores(KT, QT, DT, ST)
            self.scores(KT, KbT, DTsn, X8)
            for j in range(8):
                g = seg * 8 + j
                pp = self.pb[5 + (j % 2)]
                c.transpose(pp.v(pp.t[0:64, 0:128]), KT.v(KT.t[:, j * 64:(j + 1) * 64]), self.ident.v())
                c.transpose(pp.v(pp.t[0:64, 128:256]), VT.v(VT.t[:, j * 64:(j + 1) * 64]), self.ident.v())
                c.ts(Kbg.v(Kbg.t[:, j, :], key=j), pp.v(pp.t[0:64, 0:128]), t["bgC"].v(t["bgC"].t[:, g:g + 1]), None, ALU.mult)
                c.ts(Kd.v(Kd.t[:, j, :], key=j), pp.v(pp.t[0:64, 0:128]), t["kdC"].v(t["kdC"].t[:, g:g + 1]), None, ALU.mult)
                c.ts(Vb.v(Vb.t[:, j, :], key=j), pp.v(pp.t[0:64, 128:256]), t["betaC"].v(t["betaC"].t[:, g:g + 1]), None, ALU.mult)
            Y = [X8, t["Y0"], t["Y1"]]
            Z = [t["Z0"], t["Z1"]]
            pz = self.pb[7]
            for j in range(8):
                c.transpose(pz.v(pz.t[0:64, j * 64:(j + 1) * 64]), X8.v(X8.t[:, j, :]), self.ident.v(self.ident.t[0:64, 0:64]))
            c.copy(Z[0].v(), pz.v(pz.t[0:64, :].rearrange("p (a b) -> p a b", a=8)), eng="act")
            c.tt(P.v(), X8.v(), self.ident.v(self.ident.t[0:64, 0:64].unsqueeze(1).to_broadcast([64, 8, 64])), ALU.add)
            ycur, zcur = X8, Z[0]
            for lvl in range(5):
                ynew = t["Y0"] if lvl % 2 == 0 else t["Y1"]
                znew = Z[(lvl + 1) % 2]
                pa, pbk, pc = self.pb[5], self.pb[6], self.pb[7]
                for j in range(8):
                    c.matmul(pa.v(pa.t[0:64, j * 64:(j + 1) * 64]), zcur.v(zcur.t[:, j, :]), ycur.v(ycur.t[:, j, :]))
                    c.matmul(pbk.v(pbk.t[0:64, j * 64:(j + 1) * 64]), ycur.v(ycur.t[:, j, :]), zcur.v(zcur.t[:, j, :]))
                c.copy(ynew.v(), pa.v(pa.t[0:64, :].rearrange("p (a b) -> p a b", a=8)), eng="act")
                c.copy(znew.v(), pbk.v(pbk.t[0:64, :].rearrange("p (a b) -> p a b", a=8)))
                for j in range(8):
                    c.matmul(pc.v(pc.t[0:64, j * 64:(j + 1) * 64]), znew.v(znew.t[:, j, :]), P.v(P.t[:, j, :]))
                c.tt(P.v(), P.v(), pc.v(pc.t[0:64, :].rearrange("p (a b) -> p a b", a=8)), ALU.add)
                ycur, zcur = ynew, znew
            pw = self.pb[5]
            for j in range(8):
                c.matmul(pw.v(pw.t[:, j * 64:(j + 1) * 64]), Kbg.v(Kbg.t[:, j, :], key=j), P.v(P.t[:, j, :]))
            c.copy(WT.v(), pw.v(), eng="act")
            for half in range(2):
                pu = self.pb[6 + half]
                for jj in range(4):
                    j = half * 4 + jj
                    c.matmul(pu.v(pu.t[0:64, jj * 128:(jj + 1) * 128]), P.v(P.t[:, j, :]), Vb.v(Vb.t[:, j, :], key=j))
                c.copy(U.v(U.t[:, half * 4:half * 4 + 4, :]), pu.v(pu.t[0:64, :].rearrange("p (a b) -> p a b", a=4)))

            def pre_hook(j, g, cur):
                p0 = self.pb[0]
                c.matmul(p0.v(p0.t[0:64, 0:128]), WT.v(WT.t[:, j * 64:(j + 1) * 64]), cur.v())
                c.tt(Vn.v(Vn.t[:, j, :], key=j), U.v(U.t[:, j, :]), p0.v(p0.t[0:64, 0:128]), ALU.subtract)
                return Vn.v(Vn.t[:, j, :], key=j)
            self.chunk_loop(seg, QdT, ST, Vn, Kd, lambda g: t["Rdec"].v(t["Rdec"].t[:, g:g + 1]), S, 128, O, pre_hook=pre_hook)
            self.post_rms(O, t["s8"], t["s8b"], t["ng"], G, "silu", 128, False, t["sq"])
            st = c.dma("sp", View(self.yb.t[hh, sl, :].rearrange("(a p) d -> p a d", p=64), [Dep()]), O.v())
            self.out_stamps.append(st)

    def gdn(self):
        t = self.gdn_alloc()
        for hh in range(2):
            self.gdn_head(hh, t)

    def s5(self):
        c, L = self.c, self.L
        I = self.i
        T = self.mk_
        TC = 512
        nch = L // TC
        par = T("spar", [128, 3]); Bm = T("sB", [128, 2, 16]); Cm = T("sC", [128, 2, 16]); dsk = T("sd", [32, 1])
        step = T("sstep", [128, 1]); rr = T("srr", [128, 1]); th = T("sth", [128, 1])
        sn1 = T("ssn1", [128, 1]); cs1 = T("scs1", [128, 1]); f1 = T("sf1", [128, 1]); i1 = T("si1", [128, 1], I32)
        lr = T("slr", [128, 1]); li = T("sli", [128, 1]); den = T("sden", [128, 1]); fr = T("sfr", [128, 1]); fi = T("sfi", [128, 1])
        a1 = T("sa1", [128, 1]); a2 = T("sa2", [128, 1])
        bb = T("sbb", [128, 2, 16]); b1 = T("sb1", [128, 16]); b2 = T("sb2", [128, 16])
        BBr = T("sBBr", [128, 32]); BBi = T("sBBi", [128, 32]); CCr = T("sCCr", [128, 32]); nCCi = T("snCCi", [128, 32])
        BrT = T("sBrT", [32, 128]); BiT = T("sBiT", [32, 128])
        io = T("sio", [128, TC], I32); tp1 = T("stp1", [128, TC])
        ang = T("sang", [128, TC]); sn = T("ssn", [128, TC]); cs_ = T("scs", [128, TC]); tf = T("stf", [128, TC]); ti = T("sti", [128, TC], I32)
        Rt = T("sRt", [128, TC]); t1 = T("st1", [128, TC]); t2 = T("st2", [128, TC]); br = T("sbr", [128, TC]); bi = T("sbi", [128, TC])
        zr = T("szr", [128, TC]); zi = T("szi", [128, TC])
        xr = [T("sxr0", [128, TC]), T("sxr1", [128, TC])]; xi = [T("sxi0", [128, TC]), T("sxi1", [128, TC])]
        uT = T("suT", [32, TC]); ystg = [T("sy0", [32, TC]), T("sy1", [32, TC])]
        x0 = T("sx0", [128, 1])
        c.memset(x0.v(), 0.0)
        c.emit("pool", lambda e: e.iota(io.t[:], pattern=[[1, TC]], base=1, channel_multiplier=0), [io.v()], [])
        c.copy(tp1.v(), io.v())
        cnt = 0
        for tl in range(8):
            c.dma("sp", par.v(), dview(I["s_par"], I["s_par"].t[tl]))
            c.dma("sp", Bm.v(), dview(I["s_B"], I["s_B"].t[tl]))
            c.dma("sp", Cm.v(), dview(I["s_C"], I["s_C"].t[tl]))
            c.dma("sp", dsk.v(), dview(I["s_d"], I["s_d"].t[tl]))
            ar, ai = par.v(par.t[:, 0:1]), par.v(par.t[:, 1:2])
            c.act(step.v(), par.v(par.t[:, 2:3]), AF.Exp)
            c.tt(rr.v(), ar, step.v(), ALU.mult)
            c.act(rr.v(), rr.v(), AF.Exp)
            c.tt(th.v(), ai, step.v(), ALU.mult)
            self.sincos_tables(th.v(), sn1.v(), cs1.v(), f1.v(), i1.v())
            c.tt(lr.v(), rr.v(), cs1.v(), ALU.mult)
            c.tt(li.v(), rr.v(), sn1.v(), ALU.mult)
            c.tt(den.v(), ar, ar, ALU.mult)
            c.tt(a1.v(), ai, ai, ALU.mult)
            c.tt(den.v(), den.v(), a1.v(), ALU.add)
            c.recip(den.v(), den.v())
            c.ts(a1.v(), lr.v(), -1.0, None, ALU.add)
            c.tt(fr.v(), a1.v(), ar, ALU.mult)
            c.tt(a2.v(), li.v(), ai, ALU.mult)
            c.tt(fr.v(), fr.v(), a2.v(), ALU.add)
            c.tt(fr.v(), fr.v(), den.v(), ALU.mult)
            c.tt(fi.v(), li.v(), ar, ALU.mult)
            c.tt(a2.v(), a1.v(), ai, ALU.mult)
            c.tt(fi.v(), fi.v(), a2.v(), ALU.subtract)
            c.tt(fi.v(), fi.v(), den.v(), ALU.mult)
            c.ts(b1.v(), Bm.v(Bm.t[:, 0, :]), fr.v(), None, ALU.mult)
            c.ts(b2.v(), Bm.v(Bm.t[:, 1, :]), fi.v(), None, ALU.mult)
            c.tt(bb.v(bb.t[:, 0, :]), b1.v(), b2.v(), ALU.subtract)
            c.ts(b1.v(), Bm.v(Bm.t[:, 1, :]), fr.v(), None, ALU.mult)
            c.ts(b2.v(), Bm.v(Bm.t[:, 0, :]), fi.v(), None, ALU.mult)
            c.tt(bb.v(bb.t[:, 1, :]), b1.v(), b2.v(), ALU.add)
            for (dstb, src, k, sc) in ((BBr, bb, 0, 1.0), (BBi, bb, 1, 1.0), (CCr, Cm, 0, 1.0), (nCCi, Cm, 1, -1.0)):
                c.memset(dstb.v(), 0.0)
                for gi in range(2):
                    c.ts(dstb.v(dstb.t[gi * 64:(gi + 1) * 64, gi * 16:(gi + 1) * 16]), src.v(src.t[gi * 64:(gi + 1) * 64, k, :]), sc, None, ALU.mult)
            for (dstT, srcb, bank) in ((BrT, BBr, 5), (BiT, BBi, 6)):
                pp = self.pb[bank]
                c.transpose(pp.v(pp.t[0:32, 0:128]), srcb.v(), self.ident.v())
                c.copy(dstT.v(), pp.v(pp.t[0:32, 0:128]), eng="act")
            c.ts(ang.v(), tp1.v(), th.v(), None, ALU.mult)
            self.sincos_tables(ang.v(), sn.v(), cs_.v(), tf.v(), ti.v())
            c.ts(Rt.v(), self.ones.v(self.ones.t[:, 0:1].to_broadcast([128, TC])), rr.v(), None, ALU.mult)
            prev_r, prev_i = x0.v(), x0.v()
            for ch in range(nch):
                sl = slice(ch * TC, (ch + 1) * TC)
                c.dma("sp", uT.v(), dview(I["s_uT"], I["s_uT"].t[tl * 32:(tl + 1) * 32, sl]))
                pr, pi = self.pb[1 + (cnt % 2) * 2], self.pb[2 + (cnt % 2) * 2]
                c.matmul(pr.v(), BrT.v(), uT.v())
                c.matmul(pi.v(), BiT.v(), uT.v())
                c.tt(t1.v(), pr.v(), cs_.v(), ALU.mult)
                c.tt(t2.v(), pi.v(), sn.v(), ALU.mult)
                c.tt(br.v(), t1.v(), t2.v(), ALU.add)
                c.tt(t1.v(), pi.v(), cs_.v(), ALU.mult)
                c.tt(t2.v(), pr.v(), sn.v(), ALU.mult)
                c.tt(bi.v(), t1.v(), t2.v(), ALU.subtract)
                c.scan(zr.v(), Rt.v(), br.v(), prev_r, ALU.mult, ALU.add)
                c.scan(zi.v(), Rt.v(), bi.v(), prev_i, ALU.mult, ALU.add)
                xrc, xic = xr[cnt % 2], xi[cnt % 2]
                c.tt(t1.v(), zr.v(), cs_.v(), ALU.mult)
                c.tt(t2.v(), zi.v(), sn.v(), ALU.mult)
                c.tt(xrc.v(), t1.v(), t2.v(), ALU.subtract)
                c.tt(t1.v(), zi.v(), cs_.v(), ALU.mult)
                c.tt(t2.v(), zr.v(), sn.v(), ALU.mult)
                c.tt(xic.v(), t1.v(), t2.v(), ALU.add)
                prev_r, prev_i = xrc.v(xrc.t[:, TC - 1:TC]), xic.v(xic.t[:, TC - 1:TC])
                py = self.pb[5 + (cnt % 2)]
                c.matmul(py.v(py.t[0:32, :]), CCr.v(), xrc.v(), start=True, stop=False)
                c.matmul(py.v(py.t[0:32, :]), nCCi.v(), xic.v(), start=False, stop=True)
                ys = ystg[cnt % 2]
                c.stt(ys.v(), uT.v(), dsk.v(), py.v(py.t[0:32, :]), ALU.mult, ALU.add)
                st = c.dma("sp", View(self.ya.t[tl * 32:(tl + 1) * 32, sl], [Dep()]), ys.v())
                self.out_stamps.append(st)
                cnt += 1

Even.post_rms = Odd.post_rms


INTERLEAVE = True


class FusedMix(MixBase):
    def __init__(self, c, L, consts):
        self.c = c
        self.L = L
        self.n = L // 64
        self.nseg = L // 512
        for k, v in consts.items():
            setattr(self, k, v)
        self.out_stamps = []
        self.uid = consts["uidbox"]

    def name(self, s):
        self.uid[0] += 1
        return "%s_%d" % (s, self.uid[0])

    def mk(self, s, shape, dt=F32):
        return self.c.sb(self.name(s), shape, dt)

    def setup_io(self, G, nrows, idx, SA, SB, GA, GB, groups, tag):
        self.G, self.nrows, self.idx = G, nrows, idx
        self.SA, self.SB, self.GA, self.GB, self.groups = SA, SB, GA, GB, groups
        self.skeys = {}
        self.gtag = tag
        self.Grow = G.t.rearrange("k a r t -> (k a r) t")
        self.Gblk = G.t.rearrange("k a r (j b) -> (k a r j) b", b=64)

    def ld(self, dst, col, parts=128, blocks=None):
        self.c.gather(dst, self.G, self.Grow, self.idx.v(self.idx.t[0:parts, col:col + 1]), blocks)

    def ld_gN(self, dst, col, blocks=None):
        self.c.gather(dst, self.G, self.Gblk, self.idx.v(self.idx.t[0:self.n, col:col + 1]), blocks)

    def st(self, src, seg, row0, rows):
        part = "A" if row0 < 256 else "B"
        S = self.SA if part == "A" else self.SB
        r0 = row0 % 256
        key = (seg, r0)
        self.skeys.setdefault((part, seg // 2), []).append(key)
        return self.c.dma("sp", S.v(S.t[seg // 2, seg % 2, r0:r0 + rows, :], key=key), src)

    def send(self, part, pr):
        S, G = (self.SA, self.GA) if part == "A" else (self.SB, self.GB)
        self.c.allgather(S.t[pr].rearrange("a r t -> (a r) t"), G.t[pr].rearrange("s a r t -> (s a r) t"),
                         [S.v(S.t[pr], key=self.skeys[(part, pr)])], [G.v(G.t[pr], key=pr)], self.groups,
                         grp=self.gtag + part, grp_total=4)

    def step(self, gen, k):
        if gen is not None and INTERLEAVE:
            done = 0
            while done < k:
                r = next(gen, "end")
                if r == "end":
                    return
                if r == 1:
                    done += 1

    def tok_from_feat(self, FT, out, w0=0, scale=None):
        c = self.c
        for half in range(2):
            pp = self.pb[5 + half]
            for jj in range(4):
                j = half * 4 + jj
                c.transpose(pp.v(pp.t[0:64, jj * 128:(jj + 1) * 128]), FT.v(FT.t[:, j * 64:(j + 1) * 64]), self.ident.v())
            c.copy(out.v(out.t[:, half * 4:half * 4 + 4, w0:w0 + 128]), pp.v(pp.t[0:64, :].rearrange("p (a b) -> p a b", a=4)), eng="act")

    def feat_from_tok(self, TK, w0, out, bank):
        c = self.c
        pp = self.pb[bank]
        for j in range(8):
            c.transpose(pp.v(pp.t[:, j * 64:(j + 1) * 64]), TK.v(TK.t[:, j, w0:w0 + 128]), self.ident.v(self.ident.t[0:64, 0:64]))
        c.copy(out.v(), pp.v(), eng="act")


class OddF(FusedMix):
    COLS_PER_SEG = 14

    def mlstm(self, ext):
        c, n, L = self.c, self.n, self.L
        T = self.mk
        par = T("mpar", [128, 2])
        c.dma("sp", par.v(), ext["m_par"].v())
        gN = T("gN", [n, 2, 64])
        gcol = self.nseg * self.COLS_PER_SEG
        self.ld_gN(gN.v(gN.t[:, 0, :]), gcol, [12])
        self.ld_gN(gN.v(gN.t[:, 1, :]), gcol + 1, [12])
        ng = T("mng", [64, 256])
        c.dma("sp", ng.v(), ext["m_ng"].v())
        nbf = T("nbf", [128, 1])
        c.ts(nbf.v(), par.v(par.t[:, 1:2]), -1.0, None, ALU.mult)
        ig = T("ig", [n, 64]); sp = T("sp", [n, 64]); csp = T("csp", [n, 64]); bcum = T("bcum", [n, 64])
        colB = T("colB", [n, 64]); cmx = T("cmx", [n, 64]); imax = T("imax", [n, 64]); ul = T("ul", [n, 64])
        umax = T("umax", [n, 1]); blast = T("blast", [n, 1])
        c.ts(ig.v(), gN.v(gN.t[:, 0, :]), par.v(par.t[0:n, 0:1]), None, ALU.add)
        c.act(sp.v(), gN.v(gN.t[:, 1, :]), AF.Exp, bias=nbf.v(nbf.t[0:n, :]), scale=-1.0)
        c.act(sp.v(), sp.v(), AF.Ln, bias=1.0)
        c.scan(csp.v(), self.ones.v(self.ones.t[0:n, 0:64]), sp.v(), 0.0, ALU.mult, ALU.add)
        c.ts(bcum.v(), csp.v(), -1.0, None, ALU.mult)
        c.tt(colB.v(), ig.v(), csp.v(), ALU.add)
        c.scan(cmx.v(), self.zeros.v(self.zeros.t[0:n, :]), colB.v(), NEG, ALU.add, ALU.max)
        c.tt(imax.v(), bcum.v(), cmx.v(), ALU.add)
        c.copy(blast.v(), bcum.v(bcum.t[:, 63:64]))
        c.ts(ul.v(), colB.v(), blast.v(), None, ALU.add)
        c.reduce(umax.v(), ul.v(), ALU.max)
        Rb = T("Rb", [128, n]); Ru = T("Ru", [128, n]); Mn = T("Mn", [128, n]); Mo = T("Mo", [128, n]); Rc = T("Rc", [128, n])
        self.rep_chunk(blast.v(), Rb.v(), 5)
        self.rep_chunk(umax.v(), Ru.v(), 6)
        c.scan(Mn.v(), Rb.v(), Ru.v(), 0.0, ALU.add, ALU.max)
        c.memset(Mo.v(), 0.0)
        c.copy(Mo.v(Mo.t[:, 1:n]), Mn.v(Mn.t[:, 0:n - 1]))
        c.tt(Rc.v(), Rb.v(), Mo.v(), ALU.add)
        c.tt(Rc.v(), Rc.v(), Mn.v(), ALU.subtract)
        c.act(Rc.v(), Rc.v(), AF.Exp)
        bcumC = T("bcumC", [64, n]); colBC = T("colBC", [64, n]); imaxC = T("imaxC", [64, n]); ulC = T("ulC", [64, n])
        self.to_col(bcum.v(), bcumC.v(), 5)
        self.to_col(colB.v(), colBC.v(), 6)
        self.to_col(imax.v(), imaxC.v(), 5)
        self.to_col(ul.v(), ulC.v(), 6)
        msC = T("msC", [64, n]); enC = T("enC", [64, n]); kwsC = T("kwsC", [64, n])
        c.tt(msC.v(), bcumC.v(), Mo.v(Mo.t[0:64, :]), ALU.add)
        c.tt(msC.v(), msC.v(), imaxC.v(), ALU.max)
        c.act(enC.v(), msC.v(), AF.Exp, scale=-1.0)
        c.tt(kwsC.v(), ulC.v(), Mn.v(Mn.t[0:64, :]), ALU.subtract)
        c.act(kwsC.v(), kwsC.v(), AF.Exp)
        E = T("E", [n, 8, 64])
        bR = T("bR", [128, 512]); iR = T("iR", [128, 512]); msR = T("msR", [128, 512])
        QT = T("mQT", [128, 512]); QtT = T("mQtT", [128, 512]); KT = T("mKT", [128, 512]); FT = [T("mFT0", [128, 512]), T("mFT1", [128, 512])]
        Kt = T("mKt", [64, 8, 128]); Kc = T("mKc", [64, 8, 128]); V = T("mV", [64, 8, 258]); O = T("mO", [64, 8, 258])
        G = T("mG", [64, 8, 256]); DT = T("mDT", [64, 8, 64]); ST = T("mST", [64, 8, 64])
        H = T("mH", [64, 8, 256]); s8 = T("ms8", [64, 8]); s8b = T("ms8b", [64, 8]); sqm = T("msq", [64, 8, 256])
        OF = [T("mOF0", [128, 512]), T("mOF1", [128, 512])]
        S = {"bufs": [T("mS0", [128, 258]), T("mS1", [128, 258])], "i": 0}
        c.memset(S["bufs"][0].v(), 0.0)
        for seg in range(self.nseg):
            cs = slice(seg * 8, seg * 8 + 8)
            c0 = seg * self.COLS_PER_SEG
            self.ld(QT.v(), c0 + 0, blocks=[0, 1])
            self.ld(KT.v(), c0 + 1, blocks=[2, 3])
            self.tok_from_feat(KT, Kt)
            for hb in range(2):
                f = FT[hb]
                self.ld(f.v(), c0 + 2 + hb, blocks=[4, 5, 6, 7])
                self.tok_from_feat(f, V, w0=hb * 128)
            c.memset(V.v(V.t[:, :, 256:257]), 1.0)
            c.memset(V.v(V.t[:, :, 257:258]), 0.0)
            for hb in range(2):
                f = FT[hb]
                self.ld(f.v(), c0 + 4 + hb, blocks=[8, 9, 10, 11])
                self.tok_from_feat(f, G, w0=hb * 128)
            self.to_row_seg(bcum, seg, bR.v(), 5, E)
            self.to_row_seg(imax, seg, iR.v(), 6, E)
            mo_b = Mo.v(Mo.t[:, cs].unsqueeze(2).to_broadcast([128, 8, 64]))
            b3 = bR.v(bR.t[:].rearrange("p (a b) -> p a b", a=8))
            ms3 = msR.v(msR.t[:].rearrange("p (a b) -> p a b", a=8))
            c.tt(ms3, b3, mo_b, ALU.add)
            c.tt(iR.v(), msR.v(), iR.v(), ALU.max)
            c.tt(msR.v(), msR.v(), iR.v(), ALU.subtract)
            c.act(msR.v(), msR.v(), AF.Exp)
            c.tt(bR.v(), bR.v(), iR.v(), ALU.subtract)
            c.ts(QT.v(), QT.v(), 128 ** -0.5, None, ALU.mult)
            c.tt(QtT.v(), QT.v(), msR.v(), ALU.mult)
            c.tt(DT.v(), bR.v(bR.t[0:64, :].rearrange("p (a b) -> p a b", a=8)),
                 self.negmask.v(self.negmask.t[:, :].unsqueeze(1).to_broadcast([64, 8, 64])), ALU.add)
            for j in range(8):
                c.act(DT.v(DT.t[:, j, :]), DT.v(DT.t[:, j, :]), AF.Exp, bias=colBC.v(colBC.t[:, seg * 8 + j:seg * 8 + j + 1]))
            self.scores(KT, QT, DT, ST)
            c.tt(Kc.v(), Kt.v(), kwsC.v(kwsC.t[:, cs].unsqueeze(2).to_broadcast([64, 8, 128])), ALU.mult)
            self.chunk_loop(seg, QtT, ST, V, Kc, lambda g: Rc.v(Rc.t[:, g:g + 1]), S, 258, O)
            c.ts(s8.v(), O.v(O.t[:, :, 256]), -1.0, None, ALU.mult)
            c.tt(s8.v(), s8.v(), O.v(O.t[:, :, 256]), ALU.max)
            c.tt(s8.v(), s8.v(), enC.v(enC.t[:, cs]), ALU.max)
            c.recip(s8.v(), s8.v())
            c.tt(H.v(), O.v(O.t[:, :, 0:256]), s8.v(s8.t[:, :].unsqueeze(2).to_broadcast([64, 8, 256])), ALU.mult)
            self.post_rms(H, s8, s8b, ng, G, AF.Sigmoid, 256, False, sqm)
            for hb in range(2):
                self.feat_from_tok(H, hb * 128, OF[hb], 5 + hb)
                self.out_stamps.append(self.st(OF[hb].v(), seg, hb * 128, 128))
            if seg % 2 == 1:
                self.send("A", seg // 2)

    def retention(self, ext):
        c, n, L = self.c, self.n, self.L
        T = self.mk
        hid = T("hid", [128, 1]); lg = T("lg", [128, 1])
        c.dma("sp", hid.v(), ext["hidx"].v())
        ng = T("rng", [64, 256])
        c.dma("sp", ng.v(), ext["r_ng"].v())
        c.act(lg.v(), hid.v(), AF.Exp, scale=-math.log(2.0))
        c.ts(lg.v(), lg.v(), 2.0 ** -5, None, ALU.mult)
        c.act(lg.v(), lg.v(), AF.Ln, scale=-1.0, bias=1.0)
        io_cs = T("io_cs", [64, 64], I32); f_cs = T("f_cs", [64, 64])
        c.emit("pool", lambda e: e.iota(io_cs.t[:], pattern=[[1, 64]], base=0, channel_multiplier=-1), [io_cs.v()], [])
        c.copy(f_cs.v(), io_cs.v())
        DT1 = T("DT1", [64, 64])
        c.act(DT1.v(), f_cs.v(), AF.Exp, scale=lg.v(lg.t[0:64, :]))
        c.emit("pool", lambda e: e.affine_select(out=DT1.t[:], in_=DT1.t[:], pattern=[[1, 64]], compare_op=ALU.is_ge,
                                                 fill=0.0, base=0, channel_multiplier=-1), [DT1.v()], [DT1.v()])
        DT = T("rDT", [64, 8, 64])
        c.copy(DT.v(), DT1.v(DT1.t[:, :].unsqueeze(1).to_broadcast([64, 8, 64])))
        io_r = T("io_r", [128, 64], I32); xi = T("xi", [128, 64])
        c.emit("pool", lambda e: e.iota(io_r.t[:], pattern=[[1, 64]], base=1, channel_multiplier=0), [io_r.v()], [])
        c.copy(xi.v(), io_r.v())
        c.act(xi.v(), xi.v(), AF.Exp, scale=lg.v())
        c.ts(xi.v(), xi.v(), 128 ** -0.5, None, ALU.mult)
        io_p = T("io_p", [128, 1], I32); zeta = T("zeta", [128, 1]); gc = T("gc", [128, 1]); invf = T("invf", [128, 1])
        c.emit("pool", lambda e: e.iota(io_p.t[0:64, :], pattern=[[0, 1]], base=63, channel_multiplier=-1), [io_p.v()], [])
        c.copy(zeta.v(zeta.t[0:64, :]), io_p.v(io_p.t[0:64, :]))
        c.act(zeta.v(zeta.t[0:64, :]), zeta.v(zeta.t[0:64, :]), AF.Exp, scale=lg.v(lg.t[0:64, :]))
        c.act(gc.v(), lg.v(), AF.Exp, scale=64.0)
        for h0 in (0, 64):
            c.emit("pool", lambda e, h0=h0: e.iota(io_p.t[h0:h0 + 64, :], pattern=[[0, 1]], base=0, channel_multiplier=1), [io_p.v()], [])
        c.copy(invf.v(), io_p.v())
        c.act(invf.v(), invf.v(), AF.Exp, scale=-math.log(10000.0) / 64.0)
        sgn = T("sgn", [128, 1])
        c.memset(sgn.v(sgn.t[0:64, :]), -1.0)
        c.memset(sgn.v(sgn.t[64:128, :]), 1.0)
        posi = T("posi", [128, 512], I32); ang = T("ang", [128, 512]); sn = T("sn", [128, 512]); cs_ = T("cs", [128, 512])
        tf = T("tf", [128, 512]); ti = T("ti", [128, 512], I32)
        QT = T("rQT", [128, 512]); QTs = T("rQTs", [128, 512]); KT = T("rKT", [128, 512]); KTs = T("rKTs", [128, 512])
        QtT = T("rQtT", [128, 512]); Kc = T("rKc", [64, 8, 128]); V = T("rV", [64, 8, 256]); O = T("rO", [64, 8, 256])
        G = T("rG", [64, 8, 256]); ST = T("rST", [64, 8, 64]); s8 = T("rs8", [64, 8]); s8b = T("rs8b", [64, 8]); sqr = T("rsq", [64, 8, 256])
        FT = [T("rFT0", [128, 512]), T("rFT1", [128, 512])]
        OF = [T("rOF0", [128, 512]), T("rOF1", [128, 512])]
        S = {"bufs": [T("rS0", [128, 256]), T("rS1", [128, 256])], "i": 0}
        c.memset(S["bufs"][0].v(), 0.0)
        pos_d = ext["pos"]
        for seg in range(self.nseg):
            sl = slice(seg * 512, (seg + 1) * 512)
            c0 = seg * self.COLS_PER_SEG + 6
            c.dma("sp", posi.v(), dview(pos_d, pos_d.t[:, sl]))
            for i, tl in enumerate((QT, QTs, KT, KTs)):
                self.ld(tl.v(), c0 + i)
            for hb in range(2):
                self.ld(FT[hb].v(), c0 + 4 + hb)
                self.tok_from_feat(FT[hb], V, w0=hb * 128)
            for hb in range(2):
                self.ld(FT[hb].v(), c0 + 6 + hb)
                self.tok_from_feat(FT[hb], G, w0=hb * 128)
            c.copy(ang.v(), posi.v())
            c.ts(ang.v(), ang.v(), invf.v(), None, ALU.mult)
            self.sincos_tables(ang.v(), sn.v(), cs_.v(), tf.v(), ti.v())
            c.ts(sn.v(), sn.v(), sgn.v(), None, ALU.mult)
            for (A, As) in ((QT, QTs), (KT, KTs)):
                c.tt(A.v(), A.v(), cs_.v(), ALU.mult)
                c.tt(As.v(), As.v(), sn.v(), ALU.mult)
                c.tt(A.v(), A.v(), As.v(), ALU.add)
            c.tt(QtT.v(QtT.t[:].rearrange("p (a b) -> p a b", a=8)), QT.v(QT.t[:].rearrange("p (a b) -> p a b", a=8)),
                 xi.v(xi.t[:, :].unsqueeze(1).to_broadcast([128, 8, 64])), ALU.mult)
            c.ts(QT.v(), QT.v(), 128 ** -0.5, None, ALU.mult)
            self.scores(KT, QT, DT, ST)
            for j in range(8):
                pp = self.pb[5 + (j % 2)]
                c.transpose(pp.v(pp.t[0:64, 0:128]), KT.v(KT.t[:, j * 64:(j + 1) * 64]), self.ident.v())
                c.ts(Kc.v(Kc.t[:, j, :], key=j), pp.v(pp.t[0:64, 0:128]), zeta.v(zeta.t[0:64, :]), None, ALU.mult)
            self.chunk_loop(seg, QtT, ST, V, Kc, lambda g: gc.v(), S, 256, O)
            self.post_rms(O, s8, s8b, ng, G, "silu", 256, True, sqr)
            for hb in range(2):
                self.feat_from_tok(O, hb * 128, OF[hb], 5 + hb)
                self.out_stamps.append(self.st(OF[hb].v(), seg, 256 + hb * 128, 128))
            if seg % 2 == 1:
                self.send("B", seg // 2)


OddF.post_rms = Odd.post_rms


class EvenF(FusedMix):
    GCOLS = 8

    def gdn_alloc(self):
        T = self.mk
        n = self.n
        t = {}
        t["cw"] = T("cw", [128, 3, 4])
        t["par"] = T("gpar", [128, 2]); t["gN"] = T("ggN", [n, 2, 64]); t["ng"] = T("gng", [64, 128])
        for nm in ("beta", "sp", "gg", "gcum"):
            t[nm] = T("g" + nm, [n, 64])
        t["ea"] = T("gea", [128, 1]); t["glast"] = T("gglast", [n, 1])
        for nm in ("Rgl", "Rdec"):
            t[nm] = T("g" + nm, [128, n])
        for nm in ("gcumC", "ngcumC", "betaC", "egC", "bgC", "kdC"):
            t[nm] = T("g" + nm, [64, n])
        t["E"] = T("gE", [n, 8, 64])
        for nm in ("Xq", "Xk", "Xv"):
            t[nm] = T("g" + nm, [128, 515])
        for nm in ("QT", "KT", "VT", "KbT", "QdT", "tmp", "gR", "bR", "WT", "ZT", "OF"):
            t[nm] = T("g" + nm, [128, 512])
        for nm in ("DT", "DTsn", "ST", "X8", "Z0", "Z1", "Y0", "Y1", "P"):
            t[nm] = T("g" + nm, [64, 8, 64])
        for nm in ("Kbg", "Kd", "Vb", "U", "Vn", "O", "G", "sq"):
            t[nm] = T("g" + nm, [64, 8, 128])
        t["s8"] = T("gs8", [64, 8]); t["s8b"] = T("gs8b", [64, 8])
        t["S"] = [T("gS0", [128, 128]), T("gS1", [128, 128])]
        return t

    def gdn_head(self, hh, t, ext, gen=None):
        c, n, L = self.c, self.n, self.L
        c.dma("sp", t["cw"].v(), dview(ext["g_cw"], ext["g_cw"].t[hh]))
        c.dma("sp", t["par"].v(), dview(ext["g_par"], ext["g_par"].t[hh]))
        c.dma("sp", t["ng"].v(), ext["g_ng"].v())
        gcol = self.nseg * self.GCOLS + hh * 2
        par, gN = t["par"], t["gN"]
        self.ld_gN(gN.v(gN.t[:, 0, :]), gcol, [20])
        self.ld_gN(gN.v(gN.t[:, 1, :]), gcol + 1, [20])
        beta, sp, gg, gcum = t["beta"], t["sp"], t["gg"], t["gcum"]
        c.act(beta.v(), gN.v(gN.t[:, 0, :]), AF.Sigmoid)
        c.act(sp.v(), gN.v(gN.t[:, 1, :]), AF.Exp, bias=par.v(par.t[0:n, 1:2]))
        c.act(sp.v(), sp.v(), AF.Ln, bias=1.0)
        c.act(t["ea"].v(), par.v(par.t[:, 0:1]), AF.Exp)
        c.ts(gg.v(), sp.v(), t["ea"].v(t["ea"].t[0:n, :]), -1.0, ALU.mult, ALU.mult)
        c.scan(gcum.v(), self.ones.v(self.ones.t[0:n, 0:64]), gg.v(), 0.0, ALU.mult, ALU.add)
        c.copy(t["glast"].v(), gcum.v(gcum.t[:, 63:64]))
        self.rep_chunk(t["glast"].v(), t["Rgl"].v(), 5)
        c.act(t["Rdec"].v(), t["Rgl"].v(), AF.Exp)
        self.to_col(gcum.v(), t["gcumC"].v(), 5)
        self.to_col(beta.v(), t["betaC"].v(), 6)
        c.ts(t["ngcumC"].v(), t["gcumC"].v(), -1.0, None, ALU.mult)
        c.act(t["egC"].v(), t["gcumC"].v(), AF.Exp)
        c.tt(t["bgC"].v(), t["betaC"].v(), t["egC"].v(), ALU.mult)
        c.tt(t["kdC"].v(), t["Rgl"].v(t["Rgl"].t[0:64, :]), t["gcumC"].v(), ALU.subtract)
        c.act(t["kdC"].v(), t["kdC"].v(), AF.Exp)
        S = {"bufs": t["S"], "i": 0}
        c.memset(t["S"][0].v(), 0.0)
        QT, KT, VT, KbT, QdT, tmp, gR, bR, WT = (t[k] for k in ("QT", "KT", "VT", "KbT", "QdT", "tmp", "gR", "bR", "WT"))
        DT, DTsn, ST, X8, P = (t[k] for k in ("DT", "DTsn", "ST", "X8", "P"))
        Kbg, Kd, Vb, U, Vn, O, G = (t[k] for k in ("Kbg", "Kd", "Vb", "U", "Vn", "O", "G"))
        for seg in range(self.nseg):
            c0 = seg * self.GCOLS + hh * 4
            for ci, (X, dst) in enumerate(((t["Xq"], QT), (t["Xk"], KT), (t["Xv"], VT))):
                if seg == 0:
                    c.memset(X.v(X.t[:, 0:3]), 0.0)
                else:
                    c.copy(X.v(X.t[:, 0:3]), X.v(X.t[:, 512:515]))
                self.ld(X.v(X.t[:, 3:515]), c0 + ci, blocks=[4 + 4 * ci + i_ for i_ in range(4)])
                cw = t["cw"]
                c.ts(dst.v(), X.v(X.t[:, 0:512]), cw.v(cw.t[:, ci, 0:1]), None, ALU.mult)
                for j in range(1, 4):
                    c.stt(dst.v(), X.v(X.t[:, j:j + 512]), cw.v(cw.t[:, ci, j:j + 1]), dst.v(), ALU.mult, ALU.add)
                c.act(dst.v(), dst.v(), AF.Silu)
                if ci < 2:
                    c.tt(tmp.v(), dst.v(), dst.v(), ALU.mult)
                    pp = self.pb[5 + ci]
                    c.matmul(pp.v(), self.ones.v(), tmp.v())
                    c.act(tmp.v(), pp.v(), AF.Ln, bias=1e-6)
                    c.act(tmp.v(), tmp.v(), AF.Exp, scale=-0.5)
                    if ci == 0:
                        c.stt(dst.v(), dst.v(), 128 ** -0.5, tmp.v(), ALU.mult, ALU.mult)
                    else:
                        c.tt(dst.v(), dst.v(), tmp.v(), ALU.mult)
            self.ld(t["ZT"].v(), c0 + 3, blocks=[16, 17, 18, 19])
            self.tok_from_feat(t["ZT"], G)
            self.to_row_seg(gcum, seg, gR.v(), 5, t["E"])
            self.to_row_seg(beta, seg, bR.v(), 6, t["E"])
            c.tt(KbT.v(), KT.v(), bR.v(), ALU.mult)
            c.act(tmp.v(), gR.v(), AF.Exp)
            c.tt(QdT.v(), QT.v(), tmp.v(), ALU.mult)
            c.tt(DT.v(), gR.v(gR.t[0:64, :].rearrange("p (a b) -> p a b", a=8)),
                 self.negmask.v(self.negmask.t[:, :].unsqueeze(1).to_broadcast([64, 8, 64])), ALU.add)
            for j in range(8):
                g = seg * 8 + j
                c.act(DT.v(DT.t[:, j, :]), DT.v(DT.t[:, j, :]), AF.Exp, bias=t["ngcumC"].v(t["ngcumC"].t[:, g:g + 1]))
            c.stt(DTsn.v(), DT.v(), -1.0, self.strict01.v(self.strict01.t[:, :].unsqueeze(1).to_broadcast([64, 8, 64])), ALU.mult, ALU.mult)
            self.scores(KT, QT, DT, ST)
            self.scores(KT, KbT, DTsn, X8)
            for j in range(8):
                g = seg * 8 + j
                pp = self.pb[5 + (j % 2)]
                c.transpose(pp.v(pp.t[0:64, 0:128]), KT.v(KT.t[:, j * 64:(j + 1) * 64]), self.ident.v())
                c.transpose(pp.v(pp.t[0:64, 128:256]), VT.v(VT.t[:, j * 64:(j + 1) * 64]), self.ident.v())
                c.ts(Kbg.v(Kbg.t[:, j, :], key=j), pp.v(pp.t[0:64, 0:128]), t["bgC"].v(t["bgC"].t[:, g:g + 1]), None, ALU.mult)
                c.ts(Kd.v(Kd.t[:, j, :], key=j), pp.v(pp.t[0:64, 0:128]), t["kdC"].v(t["kdC"].t[:, g:g + 1]), None, ALU.mult)
                c.ts(Vb.v(Vb.t[:, j, :], key=j), pp.v(pp.t[0:64, 128:256]), t["betaC"].v(t["betaC"].t[:, g:g + 1]), None, ALU.mult)
            Z = [t["Z0"], t["Z1"]]
            pz = self.pb[7]
            for j in range(8):
                c.transpose(pz.v(pz.t[0:64, j * 64:(j + 1) * 64]), X8.v(X8.t[:, j, :]), self.ident.v(self.ident.t[0:64, 0:64]))
            c.copy(Z[0].v(), pz.v(pz.t[0:64, :].rearrange("p (a b) -> p a b", a=8)), eng="act")
            c.tt(P.v(), X8.v(), self.ident.v(self.ident.t[0:64, 0:64].unsqueeze(1).to_broadcast([64, 8, 64])), ALU.add)
            ycur, zcur = X8, Z[0]
            for lvl in range(5):
                ynew = t["Y0"] if lvl % 2 == 0 else t["Y1"]
                znew = Z[(lvl + 1) % 2]
                pa, pbk, pc = self.pb[5], self.pb[6], self.pb[7]
                for j in range(8):
                    c.matmul(pa.v(pa.t[0:64, j * 64:(j + 1) * 64]), zcur.v(zcur.t[:, j, :]), ycur.v(ycur.t[:, j, :]))
                    c.matmul(pbk.v(pbk.t[0:64, j * 64:(j + 1) * 64]), ycur.v(ycur.t[:, j, :]), zcur.v(zcur.t[:, j, :]))
                c.copy(ynew.v(), pa.v(pa.t[0:64, :].rearrange("p (a b) -> p a b", a=8)), eng="act")
                c.copy(znew.v(), pbk.v(pbk.t[0:64, :].rearrange("p (a b) -> p a b", a=8)))
                for j in range(8):
                    c.matmul(pc.v(pc.t[0:64, j * 64:(j + 1) * 64]), znew.v(znew.t[:, j, :]), P.v(P.t[:, j, :]))
                c.tt(P.v(), P.v(), pc.v(pc.t[0:64, :].rearrange("p (a b) -> p a b", a=8)), ALU.add)
                ycur, zcur = ynew, znew
            self.step(gen, 2)
            pw = self.pb[5]
            for j in range(8):
                c.matmul(pw.v(pw.t[:, j * 64:(j + 1) * 64]), Kbg.v(Kbg.t[:, j, :], key=j), P.v(P.t[:, j, :]))
            c.copy(WT.v(), pw.v(), eng="act")
            for half in range(2):
                pu = self.pb[6 + half]
                for jj in range(4):
                    j = half * 4 + jj
                    c.matmul(pu.v(pu.t[0:64, jj * 128:(jj + 1) * 128]), P.v(P.t[:, j, :]), Vb.v(Vb.t[:, j, :], key=j))
                c.copy(U.v(U.t[:, half * 4:half * 4 + 4, :]), pu.v(pu.t[0:64, :].rearrange("p (a b) -> p a b", a=4)))

            def pre_hook(j, g, cur):
                p0 = self.pb[0]
                c.matmul(p0.v(p0.t[0:64, 0:128]), WT.v(WT.t[:, j * 64:(j + 1) * 64]), cur.v())
                c.tt(Vn.v(Vn.t[:, j, :], key=j), U.v(U.t[:, j, :]), p0.v(p0.t[0:64, 0:128]), ALU.subtract)
                return Vn.v(Vn.t[:, j, :], key=j)
            self.chunk_loop(seg, QdT, ST, Vn, Kd, lambda g: t["Rdec"].v(t["Rdec"].t[:, g:g + 1]), S, 128, O, pre_hook=pre_hook,
                            post_chunk=None)
            self.post_rms(O, t["s8"], t["s8b"], t["ng"], G, "silu", 128, False, t["sq"])
            self.feat_from_tok(O, 0, t["OF"], 5)
            self.out_stamps.append(self.st(t["OF"].v(), seg, 256 + hh * 128, 128))
            if hh == 1 and seg % 2 == 1:
                self.send("B", seg // 2)
            self.step(gen, 2)

    def gdn(self, ext, with_s5=True):
        t = self.gdn_alloc()
        gen = self.s5(ext) if with_s5 else None
        for hh in range(2):
            self.gdn_head(hh, t, ext, gen)
        if gen is not None:
            for _ in gen:
                pass

    def s5(self, ext):
        c, L = self.c, self.L
        T = self.mk
        TC = 512
        nch = L // TC
        scol = self.nseg * self.GCOLS + 4
        par = T("spar", [128, 3]); Bm = T("sB", [128, 2, 16]); Cm = T("sC", [128, 2, 16]); dsk = T("sd", [32, 1])
        step = T("sstep", [128, 1]); rr = T("srr", [128, 1]); th = T("sth", [128, 1])
        sn1 = T("ssn1", [128, 1]); cs1 = T("scs1", [128, 1]); f1 = T("sf1", [128, 1]); i1 = T("si1", [128, 1], I32)
        lr = T("slr", [128, 1]); li = T("sli", [128, 1]); den = T("sden", [128, 1]); fr = T("sfr", [128, 1]); fi = T("sfi", [128, 1])
        a1 = T("sa1", [128, 1]); a2 = T("sa2", [128, 1])
        bb = T("sbb", [128, 2, 16]); b1 = T("sb1", [128, 16]); b2 = T("sb2", [128, 16])
        BBr = T("sBBr", [128, 32]); BBi = T("sBBi", [128, 32]); CCr = T("sCCr", [128, 32]); nCCi = T("snCCi", [128, 32])
        BrT = T("sBrT", [32, 128]); BiT = T("sBiT", [32, 128])
        io = T("sio", [128, TC], I32); tp1 = T("stp1", [128, TC])
        ang = T("sang", [128, TC]); sn = T("ssn", [128, TC]); cs_ = T("scs", [128, TC]); tf = T("stf", [128, TC]); ti = T("sti", [128, TC], I32)
        Rt = T("sRt", [128, TC]); t1 = T("st1", [128, TC]); t2 = T("st2", [128, TC]); br = T("sbr", [128, TC]); bi = T("sbi", [128, TC])
        zr = T("szr", [128, TC]); zi = T("szi", [128, TC])
        xr = [T("sxr0", [128, TC]), T("sxr1", [128, TC])]; xi = [T("sxi0", [128, TC]), T("sxi1", [128, TC])]
        uT = [T("suT0", [32, TC]), T("suT1", [32, TC])]; ystg = [T("sy0", [32, TC]), T("sy1", [32, TC])]
        x0 = T("sx0", [128, 1])
        c.memset(x0.v(), 0.0)
        c.emit("pool", lambda e: e.iota(io.t[:], pattern=[[1, TC]], base=1, channel_multiplier=0), [io.v()], [])
        c.copy(tp1.v(), io.v())
        cnt = 0
        for tl in range(8):
            c.dma("sp", par.v(), dview(ext["s_par"], ext["s_par"].t[tl]))
            c.dma("sp", Bm.v(), dview(ext["s_B"], ext["s_B"].t[tl]))
            c.dma("sp", Cm.v(), dview(ext["s_C"], ext["s_C"].t[tl]))
            c.dma("sp", dsk.v(), dview(ext["s_d"], ext["s_d"].t[tl]))
            ar, ai = par.v(par.t[:, 0:1]), par.v(par.t[:, 1:2])
            c.act(step.v(), par.v(par.t[:, 2:3]), AF.Exp)
            c.tt(rr.v(), ar, step.v(), ALU.mult)
            c.act(rr.v(), rr.v(), AF.Exp)
            c.tt(th.v(), ai, step.v(), ALU.mult)
            self.sincos_tables(th.v(), sn1.v(), cs1.v(), f1.v(), i1.v())
            yield
            c.tt(lr.v(), rr.v(), cs1.v(), ALU.mult)
            c.tt(li.v(), rr.v(), sn1.v(), ALU.mult)
            c.tt(den.v(), ar, ar, ALU.mult)
            c.tt(a1.v(), ai, ai, ALU.mult)
            c.tt(den.v(), den.v(), a1.v(), ALU.add)
            c.recip(den.v(), den.v())
            c.ts(a1.v(), lr.v(), -1.0, None, ALU.add)
            c.tt(fr.v(), a1.v(), ar, ALU.mult)
            c.tt(a2.v(), li.v(), ai, ALU.mult)
            c.tt(fr.v(), fr.v(), a2.v(), ALU.add)
            c.tt(fr.v(), fr.v(), den.v(), ALU.mult)
            c.tt(fi.v(), li.v(), ar, ALU.mult)
            c.tt(a2.v(), a1.v(), ai, ALU.mult)
            c.tt(fi.v(), fi.v(), a2.v(), ALU.subtract)
            c.tt(fi.v(), fi.v(), den.v(), ALU.mult)
            yield
            c.ts(b1.v(), Bm.v(Bm.t[:, 0, :]), fr.v(), None, ALU.mult)
            c.ts(b2.v(), Bm.v(Bm.t[:, 1, :]), fi.v(), None, ALU.mult)
            c.tt(bb.v(bb.t[:, 0, :]), b1.v(), b2.v(), ALU.subtract)
            c.ts(b1.v(), Bm.v(Bm.t[:, 1, :]), fr.v(), None, ALU.mult)
            c.ts(b2.v(), Bm.v(Bm.t[:, 0, :]), fi.v(), None, ALU.mult)
            c.tt(bb.v(bb.t[:, 1, :]), b1.v(), b2.v(), ALU.add)
            for (dstb, src, k, sc) in ((BBr, bb, 0, 1.0), (BBi, bb, 1, 1.0), (CCr, Cm, 0, 1.0), (nCCi, Cm, 1, -1.0)):
                c.memset(dstb.v(), 0.0)
                for gi in range(2):
                    c.ts(dstb.v(dstb.t[gi * 64:(gi + 1) * 64, gi * 16:(gi + 1) * 16]), src.v(src.t[gi * 64:(gi + 1) * 64, k, :]), sc, None, ALU.mult)
            for (dstT, srcb, bank) in ((BrT, BBr, 5), (BiT, BBi, 6)):
                pp = self.pb[bank]
                c.transpose(pp.v(pp.t[0:32, 0:128]), srcb.v(), self.ident.v())
                c.copy(dstT.v(), pp.v(pp.t[0:32, 0:128]), eng="act")
            yield
            c.ts(ang.v(), tp1.v(), th.v(), None, ALU.mult)
            self.sincos_tables(ang.v(), sn.v(), cs_.v(), tf.v(), ti.v())
            yield
            c.ts(Rt.v(), self.ones.v(self.ones.t[:, 0:1].to_broadcast([128, TC])), rr.v(), None, ALU.mult)
            prev_r, prev_i = x0.v(), x0.v()
            for ch in range(nch):
                u = uT[cnt % 2]
                self.ld(u.v(), scol + tl * self.nseg + ch, parts=32, blocks=[0, 1, 2, 3])
                pr, pi = self.pb[1 + (cnt % 2) * 2], self.pb[2 + (cnt % 2) * 2]
                c.matmul(pr.v(), BrT.v(), u.v())
                c.matmul(pi.v(), BiT.v(), u.v())
                yield
                c.tt(t1.v(), pr.v(), cs_.v(), ALU.mult)
                c.tt(t2.v(), pi.v(), sn.v(), ALU.mult)
                yield
                c.tt(br.v(), t1.v(), t2.v(), ALU.add)
                c.tt(t1.v(), pi.v(), cs_.v(), ALU.mult)
                yield
                c.tt(t2.v(), pr.v(), sn.v(), ALU.mult)
                c.tt(bi.v(), t1.v(), t2.v(), ALU.subtract)
                yield
                c.scan(zr.v(), Rt.v(), br.v(), prev_r, ALU.mult, ALU.add)
                yield
                c.scan(zi.v(), Rt.v(), bi.v(), prev_i, ALU.mult, ALU.add)
                yield
                xrc, xic = xr[cnt % 2], xi[cnt % 2]
                c.tt(t1.v(), zr.v(), cs_.v(), ALU.mult)
                c.tt(t2.v(), zi.v(), sn.v(), ALU.mult)
                yield
                c.tt(xrc.v(), t1.v(), t2.v(), ALU.subtract)
                c.tt(t1.v(), zi.v(), cs_.v(), ALU.mult)
                yield
                c.tt(t2.v(), zr.v(), sn.v(), ALU.mult)
                c.tt(xic.v(), t1.v(), t2.v(), ALU.add)
                yield
                prev_r, prev_i = xrc.v(xrc.t[:, TC - 1:TC]), xic.v(xic.t[:, TC - 1:TC])
                py = self.pb[5 + (cnt % 2)]
                c.matmul(py.v(py.t[0:32, :]), CCr.v(), xrc.v(), start=True, stop=False)
                c.matmul(py.v(py.t[0:32, :]), nCCi.v(), xic.v(), start=False, stop=True)
                ys = ystg[cnt % 2]
                c.stt(ys.v(), u.v(), dsk.v(), py.v(py.t[0:32, :]), ALU.mult, ALU.add)
                self.out_stamps.append(self.st(ys.v(), ch, tl * 32, 32))
                cnt += 1
                if tl == 7 and ch % 2 == 1:
                    self.send("A", ch // 2)
                yield 1


EvenF.post_rms = Odd.post_rms


GROUPS = [[0, 1, 2, 3], [4, 5, 6, 7]]
EVEN_IN = 5136
ODD_IN = 6152
L = 4096
NE = 8 * 8 + 4 + 64
NO = 8 * 14 + 2


class TokF(Tok):
    def __init__(self, c, P, yf=False):
        self.c = c
        self.xT = P["xT"]
        self.norms = P["norms"]
        self.ones = P["onesb"]
        self.pb = P["pb"]
        self.hT = c.sb("hT", [128, 16, NT], BF16)
        self.gT = c.sb("gT", [128, 12, NT], BF16)
        self.rstd = c.sb("rstd", [128, NT], F32)
        self.sq = [c.sb("sq%d" % i, [128, NT], BF16) for i in range(2)]
        self.wa = [c.sb("wa%d" % i, [128, 16, 256], BF16) for i in range(2)]
        self.wb = [c.sb("wb%d" % i, [128, 16, 256], BF16) for i in range(2)]
        self.w2 = [c.sb("w2_%d" % i, [128, 12, 256], BF16) for i in range(2)]
        self.tmp = [c.sb("tmp%d" % i, [128, 512], F32) for i in range(2)]
        self.stage = [c.sb("stg%d" % i, [128, NT], F32) for i in range(2)]
        self.cnt = {"wa": 0, "wb": 0, "w2": 0, "tmp": 0, "sq": 0, "stage": 0}
        self.tasks = []
        self.out_stamps = []
        if yf:
            self.yf = c.sb("yf", [128, 8, 512], F32)
            self.t1 = c.sb("t1g", [128, 512], F32)

    def proj_send(self, ni, w_d, N, S, G, grp_of, order=None):
        c = self.c
        self.rmsnorm(ni)
        nblk = (N + 255) // 256
        for b in (order if order is not None else range(nblk)):
            ncols = min(256, N - b * 256)

            def load(b=b, ncols=ncols):
                ta = self.rot("wa")
                c.dma("pool", ta.v(ta.t[:, :, 0:ncols]), self.wview(w_d, b * 256, ncols))
                return ta

            def compute(ta, b=b, ncols=ncols):
                for jj in range((ncols + 127) // 128):
                    m = min(128, ncols - jj * 128)
                    n0 = b * 256 + jj * 128
                    sg = self.rot("stage")
                    for th in range(2):
                        pp = self.pb[(jj % 2) * 2 + th]
                        for k in range(16):
                            c.matmul(pp.v(pp.t[0:m, :]), ta.v(ta.t[:, k, jj * 128:jj * 128 + m]),
                                     self.hT.v(self.hT.t[:, k, th * 512:(th + 1) * 512], key=k),
                                     start=(k == 0), stop=(k == 15))
                        c.copy(sg.v(sg.t[0:m, th * 512:(th + 1) * 512]), pp.v(pp.t[0:m, :]), eng="act")
                        c.dma("sp", S.v(S.t[b, th, jj * 128:jj * 128 + m, :], key=(b, th, jj)), sg.v(sg.t[0:m, th * 512:(th + 1) * 512]))
                keys = [(b, th, jj) for th in range(2) for jj in range((ncols + 127) // 128)]
                gname, gtot = grp_of(b)
                c.allgather(S.t[b].rearrange("a r t -> (a r) t"), G.t[b].rearrange("a r t -> (a r) t"),
                            [S.v(S.t[b], key=keys)], [G.v(G.t[b], key=b)], GROUPS, grp=gname, grp_total=gtot)
            self.add_task(load, compute)
        self.flush()

    def mix_gather(self, GA, GB, idx, w_out_d, even=None):
        c = self.c
        k0 = 8 if even is not None else 0
        for k in range(k0, 16):
            G = GA if k < 8 else GB
            Grow = G.t.rearrange("p s a r t -> (p s a r) t")
            sg = self.rot("stage")
            for half in range(2):
                c.gather(sg.v(sg.t[:, half * 512:(half + 1) * 512]), G, Grow, idx.v(idx.t[:, k * 2 + half:k * 2 + half + 1]))
            c.copy(self.hT.v(self.hT.t[:, k, :], key=k), sg.v(), eng=("act" if k % 2 else "dve"))
        if even is not None:
            self.s5_glu_g(GA, GA.t.rearrange("p s a r t -> (p s a r) t"), idx, even)
        src = []
        for k in range(16):
            if even is not None and k < 8:
                src.append((self.gT, k))
            else:
                src.append((self.hT, k))
        for b in range(8):
            def load(b=b):
                ta = self.rot("wa")
                c.dma("pool", ta.v(), self.wview(w_out_d, b * 256, 256))
                return ta

            def compute(ta, b=b):
                for jj in range(2):
                    i = b * 2 + jj
                    for th in range(2):
                        pp = self.pb[(i % 2) * 2 + th]
                        for k in range(16):
                            buf, kk = src[k]
                            key = kk if buf is self.hT else [(kk, 0), (kk, 1)]
                            c.matmul(pp.v(), ta.v(ta.t[:, k, jj * 128:(jj + 1) * 128]),
                                     buf.v(buf.t[:, kk, th * 512:(th + 1) * 512], key=key),
                                     start=(k == 0), stop=(k == 15))
                        xv = self.xT.v(self.xT.t[:, i, th * 512:(th + 1) * 512], key=i)
                        c.stt(xv, pp.v(), 1.0, xv, ALU.mult, ALU.add)
            self.add_task(load, compute)
        self.flush()

    def s5_glu_g(self, G, Grow, idx, even):
        c = self.c
        yf, t1 = self.yf, self.t1
        bglu = even["bglu"]
        wglu_d = even["wglu"]
        for th in range(2):
            sl = slice(th * 512, (th + 1) * 512)
            for k in range(8):
                yk = yf.v(yf.t[:, k, :], key=k)
                c.gather(yk, G, Grow, idx.v(idx.t[:, k * 2 + th:k * 2 + th + 1]))
                c.tt(t1.v(), yk, yk, ALU.mult)
                c.ts(t1.v(), t1.v(), 0.044715, 1.0, ALU.mult, ALU.add)
                c.tt(t1.v(), t1.v(), yk, ALU.mult)
                c.act(t1.v(), t1.v(), AF.Sigmoid, scale=2.0 * 0.7978845608028654)
                c.tt(yk, yk, t1.v(), ALU.mult)
                c.copy(self.hT.v(self.hT.t[:, k, sl], key=k), yk, eng="act")
            for b in range(4):
                ta = self.rot("wa")
                c.dma("pool", ta.v(ta.t[:, 0:8, :]), self.wview(wglu_d, b * 256, 256, kchunks=8))
                for jj in range(2):
                    n = b * 2 + jj
                    pp = self.pb[4 + (n % 2)]
                    for k in range(8):
                        c.matmul(pp.v(), ta.v(ta.t[:, k, jj * 128:(jj + 1) * 128]), self.hT.v(self.hT.t[:, k, sl], key=k),
                                 start=(k == 0), stop=(k == 7))
                    t = self.rot("tmp")
                    c.act(t.v(), pp.v(), AF.Sigmoid, bias=bglu.v(bglu.t[:, n:n + 1]))
                    c.tt(self.gT.v(self.gT.t[:, n, sl], key=(n, th)), t.v(), yf.v(yf.t[:, n, :], key=n), ALU.mult)


def build(debug=False, upto=9):
    c = Ctx()
    X = lambda name, shape, dt=F32: c.dram(name, shape, dt, kind="ExternalInput")
    xin = X("xT_in", [D, NT])
    norms_d = X("norms_in", [128, 7, 16])
    W = {}
    for tag in "abcd":
        if tag == "a" or (tag in "bc" and upto >= 3) or (tag == "d" and upto >= 5):
            W[tag] = (X("w1_" + tag, [D, DFF]), X("w3_" + tag, [D, DFF]), X("w2_" + tag, [DFF, D]))
    w_in_e = X("w_in_e", [D, EVEN_IN])
    bglu_d = X("bglu_in", [128, 8])
    if upto >= 3:
        w_out_e = X("w_out_e", [D, D]); w_in_o = X("w_in_o", [D, ODD_IN]); w_glu = X("w_glu", [1024, 1024])
    if upto >= 5:
        w_out_o = X("w_out_o", [D, D])
    idxE_d = X("idxE", [128, NE], I32); idxO_d = X("idxO", [128, NO], I32)
    idxM0_d = X("idxM0", [128, 32], I32); idxM1_d = X("idxM1", [128, 32], I32)
    ext = dict(
        g_cw=X("g_cw", [2, 128, 3, 4]), g_par=X("g_par", [2, 128, 2]), g_ng=X("g_ng", [64, 128]),
        s_par=X("s_par", [8, 128, 3]), s_B=X("s_B", [8, 128, 2, 16]), s_C=X("s_C", [8, 128, 2, 16]), s_d=X("s_d", [8, 32, 1]),
        m_par=X("m_par", [128, 2]), m_ng=X("m_ng", [64, 256]), pos=X("pos", [128, L], I32), hidx=X("hidx", [128, 1]),
        r_ng=X("r_ng", [64, 256]))
    outT = c.dram("outT", [D, NT], F32, kind="ExternalOutput")
    NB0 = (EVEN_IN + 255) // 256
    NB2 = (ODD_IN + 255) // 256
    S0 = c.dram("S0", [NB0, 2, 256, 512]); G0 = c.dram("G0", [NB0, 8, 256, 512])
    S1a = c.dram("S1a", [4, 2, 256, 512]); G1a = c.dram("G1a", [4, 4, 2, 256, 512])
    S1b = c.dram("S1b", [4, 2, 256, 512]); G1b = c.dram("G1b", [4, 4, 2, 256, 512])
    S2 = c.dram("S2", [NB2, 2, 256, 512]); G2 = c.dram("G2", [NB2, 8, 256, 512])
    S3a = c.dram("S3a", [4, 2, 256, 512]); G3a = c.dram("G3a", [4, 4, 2, 256, 512])
    S3b = c.dram("S3b", [4, 2, 256, 512]); G3b = c.dram("G3b", [4, 4, 2, 256, 512])
    P = dict(xT=c.sb("xT", [128, 16, NT], F32), norms=c.sb("norms", [128, 7, 16], F32), onesb=c.sb("onesb", [128, 128], BF16),
             pb=[c.ps("pb%d" % i, [128, 512], F32) for i in range(8)])
    ones = c.sb("ones", [128, 128], F32); ident = c.sb("ident", [128, 128], F32); negmask = c.sb("negmask", [64, 64], F32)
    zeros = c.sb("zeros", [128, 64], F32); strict01 = c.sb("strict01", [64, 64], F32)
    idxE = c.sb("idxE_t", [128, NE], I32); idxO = c.sb("idxO_t", [128, NO], I32)
    idxM0 = c.sb("idxM0_t", [128, 32], I32); idxM1 = c.sb("idxM1_t", [128, 32], I32)
    bglu = c.sb("bglu", [128, 8], F32)
    c.memset(P["onesb"].v(), 1.0)
    c.memset(ones.v(), 1.0)
    c.memset(zeros.v(), 0.0)
    c.emit("pool", lambda e: e.affine_select(out=ident.t[:], in_=ones.t[:], pattern=[[1, 128]], compare_op=ALU.is_equal,
                                             fill=0.0, base=0, channel_multiplier=-1), [ident.v()], [ones.v()])
    c.emit("pool", lambda e: e.affine_select(out=negmask.t[:], in_=zeros.t[0:64, :], pattern=[[1, 64]], compare_op=ALU.is_ge,
                                             fill=NEG, base=0, channel_multiplier=-1), [negmask.v()], [zeros.v()])
    c.emit("pool", lambda e: e.affine_select(out=strict01.t[:], in_=ones.t[0:64, 0:64], pattern=[[1, 64]], compare_op=ALU.is_gt,
                                             fill=0.0, base=0, channel_multiplier=-1), [strict01.v()], [ones.v()])
    for t, d in ((idxE, idxE_d), (idxO, idxO_d), (idxM0, idxM0_d), (idxM1, idxM1_d), (bglu, bglu_d), (P["norms"], norms_d)):
        c.dma("sp", t.v(), d.v())
    consts = dict(ones=ones, ident=ident, negmask=negmask, zeros=zeros, strict01=strict01, pb=P["pb"], uidbox=[0])
    dbg_stamps = []

    def dump(name, buf):
        if debug:
            shape = list(buf.t.shape)
            d = c.dram("dbg_" + name, shape, F32, kind="ExternalOutput")
            dbg_stamps.append(c.dma("sp", View(d.t, [Dep()]), buf.v()))

    def ag(S, G):
        for seg in range(8):
            c.allgather(S.t[seg], G.t[seg].rearrange("a r t -> (a r) t"), [S.v(S.t[seg])], [G.v(G.t[seg], key=seg)], GROUPS)

    with c.scope():
        T = TokF(c, P)
        T.load_x(xin)
        T.ffn(0, *W["a"])
        T.proj_send(4, w_in_e, EVEN_IN, S0, G0, lambda b: ("e1", 17) if b >= 4 else ("e2", 4), order=[20] + list(range(4, 20)) + [0, 1, 2, 3])
    dump("S0", S0)
    if upto >= 2:
        with c.scope():
            M = EvenF(c, L, consts)
            M.setup_io(G0, EVEN_IN, idxE, S1a, S1b, G1a, G1b, GROUPS, "me")
            M.gdn(ext, with_s5=True)
    if upto >= 3:
        with c.scope():
            T = TokF(c, P, yf=True)
            T.mix_gather(G1a, G1b, idxM0, w_out_e, even=dict(bglu=bglu, wglu=w_glu))
            T.ffn(1, *W["b"])
            T.ffn(2, *W["c"])
            T.proj_send(5, w_in_o, ODD_IN, S2, G2, lambda b: ("o1", 13) if b <= 12 else ("o2", 12))
            if debug:
                xd = c.dram("dbg_x4T", [D, NT], F32, kind="ExternalOutput")
                T.store_x(xd)
                dbg_stamps.extend(T.out_stamps)
        dump("S2", S2)
    if upto >= 4:
        with c.scope():
            M = OddF(c, L, consts)
            M.setup_io(G2, ODD_IN, idxO, S3a, S3b, G3a, G3b, GROUPS, "mo")
            M.mlstm(ext)
        with c.scope():
            M = OddF(c, L, consts)
            M.setup_io(G2, ODD_IN, idxO, S3a, S3b, G3a, G3b, GROUPS, "mo")
            M.retention(ext)
    with c.scope():
        T = TokF(c, P)
        if upto >= 5:
            T.mix_gather(G3a, G3b, idxM1, w_out_o)
            T.ffn(3, *W["d"])
        T.final_norm(6, outT)
        fin = T.out_stamps
    c.finish(fin + dbg_stamps)
    c.close()
    return c


def _nl(g):
    return np.ascontiguousarray(np.asarray(g, np.float32).reshape(16, 128).T)


def host_inputs(inp):
    A = np.asarray
    f32 = np.float32
    C_ = np.ascontiguousarray
    x = A(inp["x"]).astype(f32, copy=False).reshape(8, NT, D)
    fn, mn, fin = A(inp["ffn_norm"]), A(inp["mix_norm"]), A(inp["final_norm"])
    norms = C_(np.stack([_nl(fn[0, 0]), _nl(fn[0, 1]), _nl(fn[1, 0]), _nl(fn[1, 1]), _nl(mn[0]), _nl(mn[1]), _nl(fin)], axis=1))
    W1, W3, W2 = A(inp["ffn_w1"]), A(inp["ffn_w3"]), A(inp["ffn_w2"])
    shared = {"norms_in": norms, "w_in_e": A(inp["even_w_in"])[0], "w_out_e": A(inp["even_w_out"])[0],
              "w_in_o": A(inp["odd_w_in"])[0], "w_out_o": A(inp["odd_w_out"])[0], "w_glu": A(inp["s5_w_glu"])[0],
              "bglu_in": C_(A(inp["s5_b_glu"])[0].astype(f32).reshape(8, 128).T),
              "g_ng": C_(np.broadcast_to(A(inp["gdn_norm"])[0][None], (64, 128)).astype(f32)),
              "m_ng": C_(np.broadcast_to(A(inp["mlstm_norm"])[0][None], (64, 256)).astype(f32)),
              "r_ng": C_(np.broadcast_to(A(inp["ret_norm"])[0][None], (64, 256)).astype(f32))}
    for tag, (l, i) in zip("abcd", ((0, 0), (0, 1), (1, 0), (1, 1))):
        shared["w1_" + tag] = W1[l, i]; shared["w3_" + tag] = W3[l, i]; shared["w2_" + tag] = W2[l, i]
    cw = A(inp["gdn_conv_w"])[0]
    P5 = {k: A(inp["s5_" + k])[0] for k in ("a_re", "a_im", "log_step", "b_re", "b_im", "c_re", "c_im", "d")}
    pos = A(inp["positions"]).astype(np.int32)
    gb = A(inp["mlstm_gate_bias"])[0]
    p = np.arange(128)

    def fr(seg, r):
        r = np.asarray(r)
        return ((r // 256) * 8 + seg) * 256 + r % 256
    maps = []
    for core in range(8):
        b, r = core // 4, core % 4
        m = dict(shared)
        m["xT_in"] = C_(x[core].T)
        hg = r
        heads = [2 * hg, 2 * hg + 1]
        m["g_cw"] = C_(np.stack([np.stack([cw[:, off + h * 128: off + (h + 1) * 128].T for off in (0, 1024, 2048)], axis=1) for h in heads]).astype(f32))
        m["g_par"] = C_(np.stack([np.broadcast_to(np.array([A(inp["gdn_a_log"])[0, h], A(inp["gdn_dt_bias"])[0, h]], f32)[None], (128, 2)) for h in heads]))
        s_par = np.zeros((8, 128, 3), f32); s_B = np.zeros((8, 128, 2, 16), f32); s_Cm = np.zeros((8, 128, 2, 16), f32); s_d = np.zeros((8, 32, 1), f32)
        for tl in range(8):
            for gi in range(2):
                g = hg * 16 + tl * 2 + gi
                sl = slice(gi * 64, (gi + 1) * 64)
                s_par[tl, sl, 0] = P5["a_re"][g]; s_par[tl, sl, 1] = P5["a_im"][g]; s_par[tl, sl, 2] = P5["log_step"][g]
                s_B[tl, sl, 0] = P5["b_re"][g]; s_B[tl, sl, 1] = P5["b_im"][g]
                s_Cm[tl, sl, 0] = P5["c_re"][g].T; s_Cm[tl, sl, 1] = P5["c_im"][g].T
                s_d[tl, gi * 16:(gi + 1) * 16, 0] = P5["d"][g]
        m.update(s_par=s_par, s_B=s_B, s_C=s_Cm, s_d=s_d)
        h = r
        m["m_par"] = C_(np.broadcast_to(gb[:, h][None, :], (128, 2)).astype(f32))
        m["pos"] = C_(np.broadcast_to(pos[b][None], (128, L)))
        m["hidx"] = np.full((128, 1), h, f32)
        iE = np.zeros((128, NE), np.int64)
        for seg in range(8):
            for hh in range(2):
                hd = 2 * hg + hh
                for ci, base in enumerate((1024, 2048, 3072, 4096)):
                    iE[:, seg * 8 + hh * 4 + ci] = fr(seg, base + hd * 128 + p)
        cidx = np.arange(64)
        for hh in range(2):
            hd = 2 * hg + hh
            for gi, rbase in enumerate((5120, 5128)):
                iE[:64, 64 + hh * 2 + gi] = fr(cidx // 8, rbase + hd) * 8 + cidx % 8
        for tl in range(8):
            for ch in range(8):
                iE[:32, 68 + tl * 8 + ch] = fr(ch, hg * 256 + tl * 32 + np.arange(32))
        m["idxE"] = C_(iE.astype(np.int32))
        iO = np.zeros((128, NO), np.int64)
        sw = (p + 64) % 128
        for seg in range(8):
            rows = [h * 128 + p, 512 + h * 128 + p, 1024 + h * 256 + p, 1024 + h * 256 + 128 + p, 2048 + h * 256 + p, 2048 + h * 256 + 128 + p,
                    3080 + h * 128 + p, 3080 + h * 128 + sw, 3592 + h * 128 + p, 3592 + h * 128 + sw,
                    4104 + h * 256 + p, 4104 + h * 256 + 128 + p, 5128 + h * 256 + p, 5128 + h * 256 + 128 + p]
            for k, rr in enumerate(rows):
                iO[:, seg * 14 + k] = fr(seg, rr)
        for gi, rbase in enumerate((3072, 3076)):
            iO[:64, 112 + gi] = fr(cidx // 8, rbase + h) * 8 + cidx % 8
        m["idxO"] = C_(iO.astype(np.int32))
        q = r
        iM0 = np.zeros((128, 32), np.int64)
        for k in range(16):
            f_ = (k % 8) * 128 + p
            for half in range(2):
                iM0[:, k * 2 + half] = ((q * 4 + f_ // 256) * 2 + half) * 256 + f_ % 256
        iM1 = iM0
        m["idxM0"] = C_(iM0.astype(np.int32)); m["idxM1"] = C_(iM1.astype(np.int32))
        maps.append(m)
    return maps


def kernel(**inputs):
    c = build(debug=False)
    maps = host_inputs(inputs)
    res = run_bass_kernel_spmd(c.nc, maps, core_ids=list(range(8)))
    out = np.stack([res.results[i]["outT"].T for i in range(8)]).reshape(2, L, D)
    return np.ascontiguousarray(out.astype(np.float32, copy=False))
```

```python
import contextlib
import numpy as np
import concourse.bass as bass
import concourse.mybir as mybir
from concourse.bass_utils import run_bass_kernel_spmd

F32 = mybir.dt.float32
BF16 = mybir.dt.bfloat16
I32 = mybir.dt.int32
AF = mybir.ActivationFunctionType
ALU = mybir.AluOpType
AX = mybir.AxisListType


class Dep:
    __slots__ = ("w", "r")

    def __init__(self):
        self.w = None
        self.r = {}


class View:
    __slots__ = ("ap", "deps")

    def __init__(self, ap, deps):
        self.ap = ap
        self.deps = deps


class Buf:
    def __init__(self, t):
        self.t = t
        self.base = Dep()
        self.parts = {}

    def dep(self, key=None):
        if key is None:
            return self.base
        d = self.parts.get(key)
        if d is None:
            d = self.parts[key] = Dep()
            d.w = self.base.w
            d.r = dict(self.base.r)
        return d

    def v(self, ap=None, key=None):
        if ap is None:
            ap = self.t[:]
        if key is None:
            return View(ap, [self.base] + list(self.parts.values()))
        if isinstance(key, list):
            return View(ap, [self.dep(k) for k in key])
        return View(ap, [self.dep(key)])

    def all(self, ap=None):
        if ap is None:
            ap = self.t[:]
        return View(ap, [self.base] + list(self.parts.values()))


class Eng:
    def __init__(self, name, handle, sem, self_sync):
        self.name = name
        self.h = handle
        self.sem = sem
        self.count = 0
        self.known = {}
        self.self_sync = self_sync
        self.own = {id(sem)}


SEM_ROT = 1800


class Ctx:
    def __init__(self, n_dma_sems=14):
        self.nc = bass.Bass("TRN2", target_bir_lowering=False)
        nc = self.nc
        self.es = contextlib.ExitStack()
        self.sems = {}
        self.E = {}
        for name, h, ss in (("pe", nc.tensor, False), ("act", nc.scalar, True), ("dve", nc.vector, True),
                            ("pool", nc.gpsimd, True), ("sp", nc.sync, True)):
            s = self.es.enter_context(nc.semaphore("s_" + name))
            self.E[name] = Eng(name, h, s, ss)
            self.sems[id(s)] = s
        self.rings = {}
        for q in ("sp", "pool"):
            ring = []
            for i in range(n_dma_sems):
                s = self.es.enter_context(nc.semaphore("d_%s%d" % (q, i)))
                self.sems[id(s)] = s
                ring.append([s, 0])
            self.rings[q] = [ring, 0]
        self.out_stamps = []
        self.cc_stamps = []
        self.cc_groups = {}
        self.cc_count = 0
        self.sem_es = self.es
        self.n_wait = 0
        self.n_ins = 0

    def sb(self, name, shape, dtype=F32):
        self.nalloc = getattr(self, "nalloc", 0) + 1
        return Buf(self.es.enter_context(self.nc.sbuf_tensor("%s_a%d" % (name, self.nalloc), list(shape), dtype)))

    def ps(self, name, shape, dtype=F32):
        return Buf(self.es.enter_context(self.nc.psum_tensor(name, list(shape), dtype)))

    def dram(self, name, shape, dtype=F32, kind="Internal"):
        if kind == "Internal":
            t = self.nc.dram_tensor(name, list(shape), dtype)
        else:
            t = self.nc.dram_tensor(name, list(shape), dtype, kind=kind)
        b = Buf(t.ap())
        b.is_out = kind == "ExternalOutput"
        return b

    def _waits(self, E, reads, writes, extra=()):
        need = {}

        def add(st):
            if st is None:
                return
            sid, val = st
            if need.get(sid, 0) < val:
                need[sid] = val

        for v in reads:
            for d in v.deps:
                add(d.w)
        for v in writes:
            for d in v.deps:
                add(d.w)
                for sid, val in d.r.items():
                    add((sid, val))
        for st in extra:
            add(st)
        for sid, val in need.items():
            if sid in E.own and not E.self_sync:
                continue
            if E.known.get(sid, 0) < val:
                E.h.wait_ge(self.sems[sid], val)
                E.known[sid] = val
                self.n_wait += 1

    def _commit(self, stamp, reads, writes):
        sid, val = stamp
        for v in reads:
            for d in v.deps:
                if d.r.get(sid, 0) < val:
                    d.r[sid] = val
        for v in writes:
            for d in v.deps:
                d.w = stamp
                d.r = {}

    def emit(self, eng, fn, writes, reads):
        E = self.E[eng]
        self._waits(E, reads, writes)
        if E.count >= SEM_ROT:
            ns = self.sem_es.enter_context(self.nc.semaphore("s_%s_%d" % (eng, self.n_ins)))
            self.sems[id(ns)] = ns
            E.prev = (id(E.sem), E.count)
            E.sem = ns
            E.count = 0
            E.own.add(id(ns))
        ins = fn(E.h)
        E.count += 1
        ins.then_inc(E.sem, 1)
        self.n_ins += 1
        self._commit((id(E.sem), E.count), reads, writes)
        return ins

    def barrier(self):
        stamps = []
        for e in self.E.values():
            if e.count:
                stamps.append((id(e.sem), e.count))
            elif getattr(e, "prev", None):
                stamps.append(e.prev)
        for ring, _ in self.rings.values():
            for sem, total in ring:
                if total:
                    stamps.append((id(sem), total))
        for name, e in self.E.items():
            for sid, val in stamps:
                if sid in e.own and sid != id(e.sem):
                    continue
                if sid == id(e.sem) and not e.self_sync:
                    continue
                if e.known.get(sid, 0) < val:
                    e.h.wait_ge(self.sems[sid], val)
                    e.known[sid] = val
                    self.n_wait += 1

    @contextlib.contextmanager
    def scope(self):
        outer = self.es
        inner = contextlib.ExitStack()
        self.es = inner
        try:
            yield
        finally:
            self.barrier()
            self.es = outer
            inner.close()

    def allgather(self, snd_ap, rcv_ap, snd_views, rcv_views, groups, grp="g", grp_total=None):
        E = self.E["pool"]
        self._waits(E, snd_views, rcv_views)
        st = self.cc_groups.get(grp)
        if st is None:
            s = self.sem_es.enter_context(self.nc.semaphore("cc_" + grp))
            self.sems[id(s)] = s
            st = self.cc_groups[grp] = [s, 0, grp_total]
        ins = E.h.collective_compute("AllGather", ALU.bypass, replica_groups=groups, ins=[snd_ap.opt()], outs=[rcv_ap.opt()])
        ins.then_inc(st[0], 1)
        st[1] += 1
        assert st[1] <= st[2]
        self.cc_count += 1
        self.n_ins += 1
        stamp = (id(st[0]), st[2])
        self._commit(stamp, snd_views, rcv_views)
        return stamp

    def gather(self, out, G, g_ap, idx, blocks=None):
        q = "pool"
        E = self.E[q]
        ringinfo = self.rings[q]
        ring, i = ringinfo
        slot = ring[i]
        ringinfo[1] = (i + 1) % len(ring)
        sem, total = slot
        extra = [(id(sem), total)] if total else []
        gv = G.v() if blocks is None else G.v(key=list(blocks))
        self._waits(E, [gv, idx], [out], extra)
        ins = E.h.indirect_dma_start(out=out.ap, out_offset=None, in_=g_ap, in_offset=bass.IndirectOffsetOnAxis(ap=idx.ap, axis=0))
        slot[1] = total + 16
        ins.then_inc(sem, 16)
        self.n_ins += 1
        stamp = (id(sem), total + 16)
        self._commit(stamp, [gv, idx], [out])
        return stamp

    def dma(self, q, out, in_, **kw):
        E = self.E[q]
        ringinfo = self.rings[q]
        ring, idx = ringinfo
        slot = ring[idx]
        ringinfo[1] = (idx + 1) % len(ring)
        sem, total = slot
        extra = [(id(sem), total)] if total else []
        self._waits(E, [in_], [out], extra)
        ins = E.h.dma_start(out=out.ap, in_=in_.ap, **kw)
        slot[1] = total + 16
        ins.then_inc(sem, 16)
        self.n_ins += 1
        stamp = (id(sem), total + 16)
        self._commit(stamp, [in_], [out])
        return stamp

    def finish(self, stamps, eng="sp"):
        E = self.E[eng]
        self._waits(E, [], [], extra=stamps)

    def matmul(self, out, lhsT, rhs, start=True, stop=True, **kw):
        return self.emit("pe", lambda e: e.matmul(out.ap, lhsT=lhsT.ap, rhs=rhs.ap, start=start, stop=stop, **kw),
                         [out], [lhsT, rhs] + ([] if start else [out]))

    def transpose(self, out, in_, ident):
        return self.emit("pe", lambda e: e.transpose(out.ap, in_.ap, ident.ap), [out], [in_, ident])

    def act(self, out, in_, func, bias=None, scale=1.0, accum_out=None, eng="act"):
        reads = [in_]
        kw = {}
        if isinstance(bias, View):
            reads.append(bias)
            kw["bias"] = bias.ap
        elif bias is not None:
            kw["bias"] = bias
        if isinstance(scale, View):
            reads.append(scale)
            kw["scale"] = scale.ap
        else:
            kw["scale"] = scale
        writes = [out]
        if accum_out is not None:
            writes.append(accum_out)
            kw["accum_out"] = accum_out.ap
        return self.emit(eng, lambda e: e.activation(out=out.ap, in_=in_.ap, func=func, **kw), writes, reads)

    def tt(self, out, in0, in1, op, eng="dve"):
        return self.emit(eng, lambda e: e.tensor_tensor(out=out.ap, in0=in0.ap, in1=in1.ap, op=op), [out], [in0, in1])

    def ts(self, out, in0, s1, s2=None, op0=ALU.mult, op1=None, accum_out=None, eng="dve"):
        reads = [in0]
        a1 = s1
        if isinstance(s1, View):
            reads.append(s1)
            a1 = s1.ap
        a2 = s2
        if isinstance(s2, View):
            reads.append(s2)
            a2 = s2.ap
        kw = {}
        if op1 is not None:
            kw["op1"] = op1
        writes = [out]
        if accum_out is not None:
            writes.append(accum_out)
            kw["accum_out"] = accum_out.ap
        return self.emit(eng, lambda e: e.tensor_scalar(out=out.ap, in0=in0.ap, scalar1=a1, scalar2=a2, op0=op0, **kw),
                         writes, reads)

    def stt(self, out, in0, scalar, in1, op0, op1, eng="dve"):
        reads = [in0, in1]
        a = scalar
        if isinstance(scalar, View):
            reads.append(scalar)
            a = scalar.ap
        return self.emit(eng, lambda e: e.scalar_tensor_tensor(out=out.ap, in0=in0.ap, scalar=a, in1=in1.ap, op0=op0, op1=op1),
                         [out], reads)

    def copy(self, out, in_, eng="dve"):
        if eng == "act":
            return self.emit("act", lambda e: e.copy(out=out.ap, in_=in_.ap), [out], [in_])
        return self.emit(eng, lambda e: e.tensor_copy(out=out.ap, in_=in_.ap), [out], [in_])

    def memset(self, out, val, eng="dve"):
        return self.emit(eng, lambda e: e.memset(out.ap, val), [out], [])

    def scan(self, out, d0, d1, initial, op0, op1, eng="dve"):
        reads = [d0, d1]
        a = initial
        if isinstance(initial, View):
            reads.append(initial)
            a = initial.ap
        return self.emit(eng, lambda e: e.tensor_tensor_scan(out=out.ap, data0=d0.ap, data1=d1.ap, initial=a, op0=op0, op1=op1),
                         [out], reads)

    def reduce(self, out, in_, op, axis=AX.X, eng="dve", **kw):
        return self.emit(eng, lambda e: e.tensor_reduce(out=out.ap, in_=in_.ap, axis=axis, op=op, **kw), [out], [in_])

    def recip(self, out, in_):
        return self.emit("dve", lambda e: e.reciprocal(out=out.ap, in_=in_.ap), [out], [in_])

    def close(self):
        self.es.close()


NT = 1024
D = 2048
DFF = 5632
EPS = 1e-6


class Tok:
    def __init__(self, c, n_norms, yf=False):
        self.c = c
        self.xT = c.sb("xT", [128, 16, NT], F32)
        self.hT = c.sb("hT", [128, 16, NT], BF16)
        self.gT = c.sb("gT", [128, 12, NT], BF16)
        self.rstd = c.sb("rstd", [128, NT], F32)
        self.sq = [c.sb("sq%d" % i, [128, NT], BF16) for i in range(2)]
        self.wa = [c.sb("wa%d" % i, [128, 16, 256], BF16) for i in range(2)]
        self.wb = [c.sb("wb%d" % i, [128, 16, 256], BF16) for i in range(2)]
        self.w2 = [c.sb("w2_%d" % i, [128, 12, 256], BF16) for i in range(2)]
        self.tmp = [c.sb("tmp%d" % i, [128, 512], F32) for i in range(2)]
        self.stage = [c.sb("stg%d" % i, [128, NT], F32) for i in range(2)]
        self.ones = c.sb("ones", [128, 128], BF16)
        self.norms = c.sb("norms", [128, n_norms, 16], F32)
        self.pb = [c.ps("pb%d" % i, [128, 512], F32) for i in range(8)]
        self.cnt = {"wa": 0, "wb": 0, "w2": 0, "tmp": 0, "sq": 0, "stage": 0}
        c.memset(self.ones.v(), 1.0)
        self.tasks = []
        self.out_stamps = []

    def rot(self, kind):
        lst = getattr(self, kind)
        i = self.cnt[kind]
        self.cnt[kind] = i + 1
        return lst[i % len(lst)]

    def add_task(self, load, compute):
        self.tasks.append((load, compute))

    def flush(self):
        ts = self.tasks
        self.tasks = []
        if not ts:
            return
        state = [None] * len(ts)
        state[0] = ts[0][0]()
        for i in range(len(ts)):
            if i + 1 < len(ts):
                state[i + 1] = ts[i + 1][0]()
            ts[i][1](state[i])

    def load_x(self, xT_d):
        c = self.c
        v = xT_d.t.rearrange("(c p) t -> p c t", p=128)
        for k in range(16):
            c.dma("sp", self.xT.v(self.xT.t[:, k, :], key=k), View(v[:, k, :], [xT_d.dep()]))

    def load_norms(self, norms_d):
        self.c.dma("sp", self.norms.v(), norms_d.v())

    def rmsnorm(self, ni, to_bf=True, out_fn=None):
        c = self.c
        xT = self.xT
        pss = [self.pb[0], self.pb[1]]
        for k in range(16):
            s = self.rot("sq")
            c.act(s.v(), xT.v(xT.t[:, k, :], key=k), AF.Square)
            for th in range(2):
                c.matmul(pss[th].v(), self.ones.v(), s.v(s.t[:, th * 512:(th + 1) * 512]), start=(k == 0), stop=(k == 15))
        for th in range(2):
            sl = slice(th * 512, (th + 1) * 512)
            c.act(self.rstd.v(self.rstd.t[:, sl], key=th), pss[th].v(), AF.Ln, bias=EPS, scale=1.0 / D)
            c.act(self.rstd.v(self.rstd.t[:, sl], key=th), self.rstd.v(self.rstd.t[:, sl], key=th), AF.Exp, scale=-0.5)
        for k in range(16):
            if to_bf:
                out = self.hT.v(self.hT.t[:, k, :], key=k)
            else:
                out = out_fn(k)
            c.stt(out, xT.v(xT.t[:, k, :], key=k), self.norms.v(self.norms.t[:, ni, k:k + 1]),
                  self.rstd.v(self.rstd.t[:], key=[0, 1]), ALU.mult, ALU.mult)
            if not to_bf:
                out_fn(k, done=True)

    def wview(self, w_d, c0, ncols, kchunks=16):
        v = w_d.t.rearrange("(c p) f -> p c f", p=128)
        return View(v[:, 0:kchunks, c0:c0 + ncols], [w_d.dep()])

    def ffn(self, ni, w1_d, w3_d, w2_d, splits=(6, 6, 5, 5)):
        c = self.c
        self.rmsnorm(ni)
        b0 = 0
        par = [0]
        for nb in splits:
            nj = nb * 2
            for b in range(b0, b0 + nb):
                def load(b=b):
                    ta = self.rot("wa")
                    tb = self.rot("wb")
                    c.dma("pool", ta.v(), self.wview(w1_d, b * 256, 256))
                    c.dma("pool", tb.v(), self.wview(w3_d, b * 256, 256))
                    return ta, tb

                def compute(st, b=b, b0=b0):
                    ta, tb = st
                    for jj in range(2):
                        jloc = (b - b0) * 2 + jj
                        p = par[0]
                        par[0] ^= 1
                        for th in range(2):
                            pa = self.pb[p * 4 + th * 2]
                            pbk = self.pb[p * 4 + th * 2 + 1]
                            for (wt, pp) in ((ta, pa), (tb, pbk)):
                                for k in range(16):
                                    c.matmul(pp.v(), wt.v(wt.t[:, k, jj * 128:(jj + 1) * 128]),
                                             self.hT.v(self.hT.t[:, k, th * 512:(th + 1) * 512], key=k),
                                             start=(k == 0), stop=(k == 15))
                            t = self.rot("tmp")
                            c.act(t.v(), pa.v(), AF.Silu)
                            c.tt(self.gT.v(self.gT.t[:, jloc, th * 512:(th + 1) * 512], key=(jloc, th)), t.v(), pbk.v(), ALU.mult)
                self.add_task(load, compute)
            for ib in range(8):
                def load(ib=ib, b0=b0, nj=nj):
                    t2 = self.rot("w2")
                    v = w2_d.t.rearrange("(j p) d -> p j d", p=128)
                    c.dma("pool", t2.v(t2.t[:, 0:nj, :]), View(v[:, b0 * 2:b0 * 2 + nj, ib * 256:(ib + 1) * 256], [w2_d.dep()]))
                    return t2

                def compute(t2, ib=ib, nj=nj):
                    for dd in range(2):
                        i = ib * 2 + dd
                        for th in range(2):
                            pp = self.pb[(i % 2) * 4 + th]
                            for j in range(nj):
                                c.matmul(pp.v(), t2.v(t2.t[:, j, dd * 128:(dd + 1) * 128]),
                                         self.gT.v(self.gT.t[:, j, th * 512:(th + 1) * 512], key=(j, th)),
                                         start=(j == 0), stop=(j == nj - 1))
                            xv = self.xT.v(self.xT.t[:, i, th * 512:(th + 1) * 512], key=i)
                            c.stt(xv, pp.v(), 0.5, xv, ALU.mult, ALU.add)
                self.add_task(load, compute)
            b0 += nb
        self.flush()

    def proj_out(self, ni, w_d, N, out_d):
        c = self.c
        self.rmsnorm(ni)
        nblk = (N + 255) // 256
        for b in range(nblk):
            ncols = min(256, N - b * 256)

            def load(b=b, ncols=ncols):
                ta = self.rot("wa")
                c.dma("pool", ta.v(ta.t[:, :, 0:ncols]), self.wview(w_d, b * 256, ncols))
                return ta

            def compute(ta, b=b, ncols=ncols):
                for jj in range((ncols + 127) // 128):
                    m = min(128, ncols - jj * 128)
                    n0 = b * 256 + jj * 128
                    sg = self.rot("stage")
                    for th in range(2):
                        pp = self.pb[(jj % 2) * 2 + th]
                        for k in range(16):
                            c.matmul(pp.v(pp.t[0:m, :]), ta.v(ta.t[:, k, jj * 128:jj * 128 + m]),
                                     self.hT.v(self.hT.t[:, k, th * 512:(th + 1) * 512], key=k),
                                     start=(k == 0), stop=(k == 15))
                        c.copy(sg.v(sg.t[0:m, th * 512:(th + 1) * 512]), pp.v(pp.t[0:m, :]), eng="act")
                    st = c.dma("sp", View(out_d.t[n0:n0 + m, :], [Dep()]), sg.v(sg.t[0:m, :]))
                    self.out_stamps.append(st)
            self.add_task(load, compute)
        self.flush()

    def store_x(self, out_d):
        c = self.c
        v = out_d.t.rearrange("(c p) t -> p c t", p=128)
        for k in range(16):
            st = c.dma("sp", View(v[:, k, :], [Dep()]), self.xT.v(self.xT.t[:, k, :], key=k))
            self.out_stamps.append(st)

    def final_norm(self, ni, out_d):
        c = self.c
        v = out_d.t.rearrange("(c p) t -> p c t", p=128)
        cur = {}

        def out_fn(k, done=False):
            if not done:
                cur[k] = self.rot("stage")
                return cur[k].v()
            st = c.dma("sp", View(v[:, k, :], [Dep()]), cur[k].v())
            self.out_stamps.append(st)
        self.rmsnorm(ni, to_bf=False, out_fn=out_fn)

    def mix_out(self, yT_d, w_out_d, even=None):
        c = self.c
        yv = yT_d.t.rearrange("(c p) t -> p c t", p=128)
        k0 = 0
        if even is not None:
            k0 = 8
        for k in range(k0, 16):
            c.dma("pool", self.hT.v(self.hT.t[:, k, :], key=k), View(yv[:, k, :], [yT_d.dep()]))
        if even is not None:
            self.s5_glu(yv, yT_d, even)
        src = []
        for k in range(16):
            if even is not None and k < 8:
                src.append((self.gT, k))
            else:
                src.append((self.hT, k))
        for b in range(8):
            def load(b=b):
                ta = self.rot("wa")
                c.dma("pool", ta.v(), self.wview(w_out_d, b * 256, 256))
                return ta

            def compute(ta, b=b):
                for jj in range(2):
                    i = b * 2 + jj
                    for th in range(2):
                        pp = self.pb[(i % 2) * 2 + th]
                        for k in range(16):
                            buf, kk = src[k]
                            key = kk if buf is self.hT else [(kk, 0), (kk, 1)]
                            c.matmul(pp.v(), ta.v(ta.t[:, k, jj * 128:(jj + 1) * 128]),
                                     buf.v(buf.t[:, kk, th * 512:(th + 1) * 512], key=key),
                                     start=(k == 0), stop=(k == 15))
                        xv = self.xT.v(self.xT.t[:, i, th * 512:(th + 1) * 512], key=i)
                        c.stt(xv, pp.v(), 1.0, xv, ALU.mult, ALU.add)
            self.add_task(load, compute)
        self.flush()

    def s5_glu(self, yv, yT_d, even):
        c = self.c
        yf = even["yf"]
        t1 = even["t1"]
        bglu = even["bglu"]
        wglu_d = even["wglu"]
        for th in range(2):
            sl = slice(th * 512, (th + 1) * 512)
            for k in range(8):
                yk = yf.v(yf.t[:, k, :], key=k)
                c.dma("sp", yk, View(yv[:, k, sl], [yT_d.dep()]))
                c.tt(t1.v(), yk, yk, ALU.mult)
                c.ts(t1.v(), t1.v(), 0.044715, 1.0, ALU.mult, ALU.add)
                c.tt(t1.v(), t1.v(), yk, ALU.mult)
                c.act(t1.v(), t1.v(), AF.Sigmoid, scale=2.0 * 0.7978845608028654)
                c.tt(yk, yk, t1.v(), ALU.mult)
                c.copy(self.hT.v(self.hT.t[:, k, sl], key=k), yk, eng="act")
            for b in range(4):
                ta = self.rot("wa")
                c.dma("pool", ta.v(ta.t[:, 0:8, :]), self.wview(wglu_d, b * 256, 256, kchunks=8))
                for jj in range(2):
                    n = b * 2 + jj
                    pp = self.pb[4 + (n % 2)]
                    for k in range(8):
                        c.matmul(pp.v(), ta.v(ta.t[:, k, jj * 128:(jj + 1) * 128]), self.hT.v(self.hT.t[:, k, sl], key=k),
                                 start=(k == 0), stop=(k == 7))
                    t = self.rot("tmp")
                    c.act(t.v(), pp.v(), AF.Sigmoid, bias=bglu.v(bglu.t[:, n:n + 1]))
                    c.tt(self.gT.v(self.gT.t[:, n, sl], key=(n, th)), t.v(), yf.v(yf.t[:, n, :], key=n), ALU.mult)

import math

NEG = -1.0e30


class MixBase:
    def __init__(self, c, L):
        self.c = c
        self.L = L
        self.n = L // 64
        self.nseg = L // 512
        self.ones = c.sb("ones", [128, 128], F32)
        self.ident = c.sb("ident", [128, 128], F32)
        self.negmask = c.sb("negmask", [64, 64], F32)
        self.zeros = c.sb("zeros", [128, 64], F32)
        c.memset(self.ones.v(), 1.0)
        c.memset(self.zeros.v(), 0.0)
        c.emit("pool", lambda e: e.affine_select(out=self.ident.t[:], in_=self.ones.t[:], pattern=[[1, 128]],
                                                 compare_op=ALU.is_equal, fill=0.0, base=0, channel_multiplier=-1),
               [self.ident.v()], [self.ones.v()])
        c.emit("pool", lambda e: e.affine_select(out=self.negmask.t[:], in_=self.zeros.t[0:64, :], pattern=[[1, 64]],
                                                 compare_op=ALU.is_ge, fill=NEG, base=0, channel_multiplier=-1),
               [self.negmask.v()], [self.zeros.v()])
        self.pb = [c.ps("pb%d" % i, [128, 512], F32) for i in range(8)]
        self.out_stamps = []
        self.uid = 0

    def dbg(self, nm, buf, shape):
        d = self.c.dram("dbg_" + nm, list(shape), kind="ExternalOutput")
        st = self.c.dma("sp", View(d.t, [Dep()]), buf.all())
        self.out_stamps.append(st)

    def name(self, s):
        self.uid += 1
        return "%s_%d" % (s, self.uid)

    def mk_(self, s, shape, dt=F32):
        return self.c.sb(self.name(s), shape, dt)

    def rep_chunk(self, colN, out, bank):
        c, n = self.c, self.n
        t = self.mk_("rep", [n, 128])
        c.ts(t.v(), self.ones.v(self.ones.t[0:n, :]), colN, None, ALU.mult)
        pp = self.pb[bank]
        c.matmul(pp.v(pp.t[:, 0:n]), t.v(), self.ident.v(self.ident.t[0:n, 0:n]))
        c.copy(out, pp.v(pp.t[:, 0:n]), eng="act")

    def to_col(self, XN, out, bank):
        c, n = self.c, self.n
        pp = self.pb[bank]
        c.transpose(pp.v(pp.t[0:64, 0:n]), XN, self.ident.v(self.ident.t[0:n, 0:n]))
        c.copy(out, pp.v(pp.t[0:64, 0:n]), eng="act")

    def to_row_seg(self, XN_buf, seg, out, bank, E):
        c, n = self.c, self.n
        j0 = seg * 8
        c.tt(E.v(), self.ident.v(self.ident.t[0:n, j0:j0 + 8].unsqueeze(2).to_broadcast([n, 8, 64])),
             XN_buf.v(XN_buf.t[:, :].unsqueeze(1).to_broadcast([n, 8, 64])), ALU.mult)
        pp = self.pb[bank]
        c.matmul(pp.v(), self.ones.v(self.ones.t[0:n, :]), E.v(E.t[:].rearrange("p a b -> p (a b)")))
        c.copy(out, pp.v(), eng="act")

    def chunk_loop(self, seg, QtT, ST, V, Kc, decay, S, dvx, O, pre_hook=None, post_chunk=None):
        c = self.c
        for j in range(8):
            g = seg * 8 + j
            cur = S["bufs"][S["i"] % 2]
            nxt = S["bufs"][(S["i"] + 1) % 2]
            S["i"] += 1
            Vj = V.v(V.t[:, j, 0:dvx], key=j)
            if pre_hook is not None:
                Vj = pre_hook(j, g, cur)
            po = self.pb[1 + (g % 2)]
            pov = po.v(po.t[0:64, 0:dvx])
            c.matmul(pov, QtT.v(QtT.t[:, j * 64:(j + 1) * 64]), cur.v(cur.t[:, 0:dvx]), start=True, stop=False)
            c.matmul(pov, ST.v(ST.t[:, j, :], key=j), Vj, start=False, stop=True)
            c.copy(O.v(O.t[:, j, 0:dvx], key=j), pov, eng="act")
            pu = self.pb[3 + (g % 2)]
            puv = pu.v(pu.t[:, 0:dvx])
            c.matmul(puv, Kc.v(Kc.t[:, j, :], key=j), Vj)
            c.stt(nxt.v(nxt.t[:, 0:dvx]), cur.v(cur.t[:, 0:dvx]), decay(g), puv, ALU.mult, ALU.add)
            if post_chunk is not None:
                post_chunk()

    def scores(self, KT, QT, DT, ST, bank=0):
        c = self.c
        pp = self.pb[bank]
        for j in range(8):
            c.matmul(pp.v(pp.t[0:64, j * 64:(j + 1) * 64], key=j), KT.v(KT.t[:, j * 64:(j + 1) * 64]), QT.v(QT.t[:, j * 64:(j + 1) * 64]))
        c.tt(ST.all(), pp.all(pp.t[0:64, :].rearrange("p (a b) -> p a b", a=8)), DT.all(), ALU.mult)

    def sincos_tables(self, ang, sin_out, cos_out, tmp_f, tmp_i):
        c = self.c
        for (off, out) in ((0.5, sin_out), (0.75, cos_out)):
            c.ts(tmp_f, ang, 1.0 / (2 * math.pi), off, ALU.mult, ALU.add)
            c.copy(tmp_i, tmp_f)
            c.copy(out, tmp_i)
            c.tt(tmp_f, tmp_f, out, ALU.subtract)
            c.act(out, tmp_f, AF.Sign)
            c.ts(tmp_f, tmp_f, 2 * math.pi, None, ALU.mult)
            c.stt(tmp_f, out, -math.pi, tmp_f, ALU.mult, ALU.add)
            c.ts(tmp_f, tmp_f, math.pi, -math.pi, ALU.min, ALU.max)
            c.act(out, tmp_f, AF.Sin)


def dview(d, ap):
    return View(ap, [d.dep()])


class Odd(MixBase):
    def __init__(self, c, L):
        super().__init__(c, L)
        n = self.n
        D = c.dram
        self.i = dict(
            m_qT=D("m_qT", [128, L], kind="ExternalInput"), m_kT=D("m_kT", [128, L], kind="ExternalInput"),
            m_k=D("m_k", [L, 128], kind="ExternalInput"), m_v=D("m_v", [L, 256], kind="ExternalInput"),
            m_o=D("m_o", [L, 256], kind="ExternalInput"), m_gN=D("m_gN", [n, 2, 64], kind="ExternalInput"),
            m_par=D("m_par", [128, 2], kind="ExternalInput"), m_ng=D("m_ng", [64, 256], kind="ExternalInput"),
            r_qT=D("r_qT", [128, L], kind="ExternalInput"), r_qTs=D("r_qTs", [128, L], kind="ExternalInput"),
            r_kT=D("r_kT", [128, L], kind="ExternalInput"), r_kTs=D("r_kTs", [128, L], kind="ExternalInput"),
            r_v=D("r_v", [L, 256], kind="ExternalInput"), r_g=D("r_g", [L, 256], kind="ExternalInput"),
            pos=D("pos", [128, L], I32, kind="ExternalInput"), hidx=D("hidx", [128, 1], kind="ExternalInput"),
            r_ng=D("r_ng", [64, 256], kind="ExternalInput"),
        )
        self.yc = D("yc", [L, 256], kind="ExternalOutput")
        self.yd = D("yd", [L, 256], kind="ExternalOutput")

    def tokview(self, d, seg, w):
        return dview(d, d.t[seg * 512:(seg + 1) * 512, :].rearrange("(a p) d -> p a d", p=64)[:, :, 0:w])

    def mlstm(self):
        c, n, L = self.c, self.n, self.L
        I = self.i
        T = self.mk_
        par = T("mpar", [128, 2])
        c.dma("sp", par.v(), I["m_par"].v())
        gN = T("gN", [n, 2, 64])
        c.dma("sp", gN.v(), I["m_gN"].v())
        ng = T("mng", [64, 256])
        c.dma("sp", ng.v(), I["m_ng"].v())
        nbf = T("nbf", [128, 1])
        c.ts(nbf.v(), par.v(par.t[:, 1:2]), -1.0, None, ALU.mult)
        ig = T("ig", [n, 64]); sp = T("sp", [n, 64]); csp = T("csp", [n, 64]); bcum = T("bcum", [n, 64])
        colB = T("colB", [n, 64]); cmx = T("cmx", [n, 64]); imax = T("imax", [n, 64]); ul = T("ul", [n, 64])
        umax = T("umax", [n, 1]); blast = T("blast", [n, 1])
        c.ts(ig.v(), gN.v(gN.t[:, 0, :]), par.v(par.t[0:n, 0:1]), None, ALU.add)
        c.act(sp.v(), gN.v(gN.t[:, 1, :]), AF.Exp, bias=nbf.v(nbf.t[0:n, :]), scale=-1.0)
        c.act(sp.v(), sp.v(), AF.Ln, bias=1.0)
        c.scan(csp.v(), self.ones.v(self.ones.t[0:n, 0:64]), sp.v(), 0.0, ALU.mult, ALU.add)
        c.ts(bcum.v(), csp.v(), -1.0, None, ALU.mult)
        c.tt(colB.v(), ig.v(), csp.v(), ALU.add)
        c.scan(cmx.v(), self.zeros.v(self.zeros.t[0:n, :]), colB.v(), NEG, ALU.add, ALU.max)
        c.tt(imax.v(), bcum.v(), cmx.v(), ALU.add)
        c.copy(blast.v(), bcum.v(bcum.t[:, 63:64]))
        c.ts(ul.v(), colB.v(), blast.v(), None, ALU.add)
        c.reduce(umax.v(), ul.v(), ALU.max)
        Rb = T("Rb", [128, n]); Ru = T("Ru", [128, n]); Mn = T("Mn", [128, n]); Mo = T("Mo", [128, n]); Rc = T("Rc", [128, n])
        self.rep_chunk(blast.v(), Rb.v(), 5)
        self.rep_chunk(umax.v(), Ru.v(), 6)
        c.scan(Mn.v(), Rb.v(), Ru.v(), 0.0, ALU.add, ALU.max)
        c.memset(Mo.v(), 0.0)
        if n > 1:
            c.copy(Mo.v(Mo.t[:, 1:n]), Mn.v(Mn.t[:, 0:n - 1]))
        c.tt(Rc.v(), Rb.v(), Mo.v(), ALU.add)
        c.tt(Rc.v(), Rc.v(), Mn.v(), ALU.subtract)
        c.act(Rc.v(), Rc.v(), AF.Exp)
        bcumC = T("bcumC", [64, n]); colBC = T("colBC", [64, n]); imaxC = T("imaxC", [64, n]); ulC = T("ulC", [64, n])
        self.to_col(bcum.v(), bcumC.v(), 5)
        self.to_col(colB.v(), colBC.v(), 6)
        self.to_col(imax.v(), imaxC.v(), 5)
        self.to_col(ul.v(), ulC.v(), 6)
        msC = T("msC", [64, n]); enC = T("enC", [64, n]); kwsC = T("kwsC", [64, n])
        c.tt(msC.v(), bcumC.v(), Mo.v(Mo.t[0:64, :]), ALU.add)
        c.tt(msC.v(), msC.v(), imaxC.v(), ALU.max)
        c.act(enC.v(), msC.v(), AF.Exp, scale=-1.0)
        c.tt(kwsC.v(), ulC.v(), Mn.v(Mn.t[0:64, :]), ALU.subtract)
        c.act(kwsC.v(), kwsC.v(), AF.Exp)
        if getattr(self, "debug", False):
            self.dbg("bcum", bcum, [n, 64]); self.dbg("Mn", Mn, [128, n]); self.dbg("Rc", Rc, [128, n])
            self.dbg("msC", msC, [64, n]); self.dbg("kwsC", kwsC, [64, n]); self.dbg("imax", imax, [n, 64])
        E = T("E", [n, 8, 64])
        bR = T("bR", [128, 512]); iR = T("iR", [128, 512]); msR = T("msR", [128, 512])
        QT = T("mQT", [128, 512]); QtT = T("mQtT", [128, 512]); KT = T("mKT", [128, 512])
        Kt = T("mKt", [64, 8, 128]); Kc = T("mKc", [64, 8, 128]); V = T("mV", [64, 8, 258]); O = T("mO", [64, 8, 258])
        G = T("mG", [64, 8, 256]); DT = T("mDT", [64, 8, 64]); ST = T("mST", [64, 8, 64])
        H = T("mH", [64, 8, 256]); s8 = T("ms8", [64, 8]); s8b = T("ms8b", [64, 8]); sqm = T("msq", [64, 8, 256])
        S = {"bufs": [T("mS0", [128, 258]), T("mS1", [128, 258])], "i": 0}
        c.memset(S["bufs"][0].v(), 0.0)
        for seg in range(self.nseg):
            sl = slice(seg * 512, (seg + 1) * 512)
            cs = slice(seg * 8, seg * 8 + 8)
            c.dma("sp", QT.v(), dview(I["m_qT"], I["m_qT"].t[:, sl]))
            c.dma("sp", KT.v(), dview(I["m_kT"], I["m_kT"].t[:, sl]))
            c.dma("sp", Kt.v(), self.tokview(I["m_k"], seg, 128))
            c.dma("sp", V.v(V.t[:, :, 0:256]), self.tokview(I["m_v"], seg, 256))
            c.memset(V.v(V.t[:, :, 256:257]), 1.0)
            c.memset(V.v(V.t[:, :, 257:258]), 0.0)
            c.dma("sp", G.v(), self.tokview(I["m_o"], seg, 256))
            self.to_row_seg(bcum, seg, bR.v(), 5, E)
            self.to_row_seg(imax, seg, iR.v(), 6, E)
            mo_b = Mo.v(Mo.t[:, cs].unsqueeze(2).to_broadcast([128, 8, 64]))
            b3 = bR.v(bR.t[:].rearrange("p (a b) -> p a b", a=8))
            ms3 = msR.v(msR.t[:].rearrange("p (a b) -> p a b", a=8))
            i3 = iR.v(iR.t[:].rearrange("p (a b) -> p a b", a=8))
            c.tt(i3, i3, i3, ALU.max) if False else None
            c.tt(ms3, b3, mo_b, ALU.add)
            c.tt(iR.v(), msR.v(), iR.v(), ALU.max)
            c.tt(msR.v(), msR.v(), iR.v(), ALU.subtract)
            c.act(msR.v(), msR.v(), AF.Exp)
            c.tt(bR.v(), bR.v(), iR.v(), ALU.subtract)
            c.ts(QT.v(), QT.v(), 128 ** -0.5, None, ALU.mult)
            c.tt(QtT.v(), QT.v(), msR.v(), ALU.mult)
            c.tt(DT.v(), bR.v(bR.t[0:64, :].rearrange("p (a b) -> p a b", a=8)),
                 self.negmask.v(self.negmask.t[:, :].unsqueeze(1).to_broadcast([64, 8, 64])), ALU.add)
            for j in range(8):
                c.act(DT.v(DT.t[:, j, :]), DT.v(DT.t[:, j, :]), AF.Exp, bias=colBC.v(colBC.t[:, seg * 8 + j:seg * 8 + j + 1]))
            self.scores(KT, QT, DT, ST)
            c.tt(Kc.v(), Kt.v(), kwsC.v(kwsC.t[:, cs].unsqueeze(2).to_broadcast([64, 8, 128])), ALU.mult)
            if getattr(self, "debug", False) and seg == 0:
                self.dbg("rowA", bR, [128, 512]); self.dbg("iw", msR, [128, 512]); self.dbg("DT", DT, [64, 8, 64]); self.dbg("ST", ST, [64, 8, 64])
            self.chunk_loop(seg, QtT, ST, V, Kc, lambda g: Rc.v(Rc.t[:, g:g + 1]), S, 258, O)
            if getattr(self, "debug", False) and seg == 0:
                self.dbg("O", O, [64, 8, 258])
            c.ts(s8.v(), O.all(O.t[:, :, 256]), -1.0, None, ALU.mult)
            c.tt(s8.v(), s8.v(), O.all(O.t[:, :, 256]), ALU.max)
            c.tt(s8.v(), s8.v(), enC.v(enC.t[:, cs]), ALU.max)
            c.recip(s8.v(), s8.v())
            c.tt(H.v(), O.all(O.t[:, :, 0:256]), s8.v(s8.t[:, :].unsqueeze(2).to_broadcast([64, 8, 256])), ALU.mult)
            self.post_rms(H, s8, s8b, ng, G, AF.Sigmoid, 256, False, sqm)
            st = c.dma("sp", View(self.yc.t[sl, :].rearrange("(a p) d -> p a d", p=64), [Dep()]), H.v())
            self.out_stamps.append(st)

    def post_rms(self, H, s8, s8b, ng, G, gate_fn, dv, center, sq):
        c = self.c
        if center:
            c.reduce(s8.v(), H.v(), ALU.add)
            c.ts(s8.v(), s8.v(), -1.0 / dv, None, ALU.mult)
            c.tt(H.v(), H.v(), s8.v(s8.t[:, :].unsqueeze(2).to_broadcast([64, 8, dv])), ALU.add)
        c.tt(sq.v(), H.v(), H.v(), ALU.mult)
        c.reduce(s8b.v(), sq.v(), ALU.add)
        c.act(s8b.v(), s8b.v(), AF.Ln, bias=1e-6, scale=1.0 / dv)
        c.act(s8b.v(), s8b.v(), AF.Exp, scale=-0.5)
        c.tt(H.v(), H.v(), s8b.v(s8b.t[:, :].unsqueeze(2).to_broadcast([64, 8, dv])), ALU.mult)
        c.tt(H.v(), H.v(), ng.v(ng.t[:, :].unsqueeze(1).to_broadcast([64, 8, dv])), ALU.mult)
        if G is None:
            return
        if gate_fn == "silu":
            c.act(sq.v(), G.v(), AF.Silu)
        else:
            c.act(sq.v(), G.v(), gate_fn)
        c.tt(H.v(), H.v(), sq.v(), ALU.mult)

    def retention(self):
        c, n, L = self.c, self.n, self.L
        I = self.i
        T = self.mk_
        hid = T("hid", [128, 1]); lg = T("lg", [128, 1])
        c.dma("sp", hid.v(), I["hidx"].v())
        ng = T("rng", [64, 256])
        c.dma("sp", ng.v(), I["r_ng"].v())
        c.act(lg.v(), hid.v(), AF.Exp, scale=-math.log(2.0), bias=None)
        c.ts(lg.v(), lg.v(), 2.0 ** -5, None, ALU.mult)
        c.act(lg.v(), lg.v(), AF.Ln, scale=-1.0, bias=1.0)
        io_cs = T("io_cs", [64, 64], I32); f_cs = T("f_cs", [64, 64])
        c.emit("pool", lambda e: e.iota(io_cs.t[:], pattern=[[1, 64]], base=0, channel_multiplier=-1), [io_cs.v()], [])
        c.copy(f_cs.v(), io_cs.v())
        DT1 = T("DT1", [64, 64])
        c.act(DT1.v(), f_cs.v(), AF.Exp, scale=lg.v(lg.t[0:64, :]))
        c.emit("pool", lambda e: e.affine_select(out=DT1.t[:], in_=DT1.t[:], pattern=[[1, 64]], compare_op=ALU.is_ge,
                                                 fill=0.0, base=0, channel_multiplier=-1), [DT1.v()], [DT1.v()])
        DT = T("rDT", [64, 8, 64])
        c.copy(DT.v(), DT1.v(DT1.t[:, :].unsqueeze(1).to_broadcast([64, 8, 64])))
        io_r = T("io_r", [128, 64], I32); xi = T("xi", [128, 64])
        c.emit("pool", lambda e: e.iota(io_r.t[:], pattern=[[1, 64]], base=1, channel_multiplier=0), [io_r.v()], [])
        c.copy(xi.v(), io_r.v())
        c.act(xi.v(), xi.v(), AF.Exp, scale=lg.v())
        c.ts(xi.v(), xi.v(), 128 ** -0.5, None, ALU.mult)
        io_p = T("io_p", [128, 1], I32); zeta = T("zeta", [128, 1]); gc = T("gc", [128, 1]); invf = T("invf", [128, 1])
        c.emit("pool", lambda e: e.iota(io_p.t[0:64, :], pattern=[[0, 1]], base=63, channel_multiplier=-1), [io_p.v()], [])
        c.copy(zeta.v(zeta.t[0:64, :]), io_p.v(io_p.t[0:64, :]))
        c.act(zeta.v(zeta.t[0:64, :]), zeta.v(zeta.t[0:64, :]), AF.Exp, scale=lg.v(lg.t[0:64, :]))
        c.act(gc.v(), lg.v(), AF.Exp, scale=64.0)
        for h0 in (0, 64):
            c.emit("pool", lambda e, h0=h0: e.iota(io_p.t[h0:h0 + 64, :], pattern=[[0, 1]], base=0, channel_multiplier=1), [io_p.v()], [])
        c.copy(invf.v(), io_p.v())
        c.act(invf.v(), invf.v(), AF.Exp, scale=-math.log(10000.0) / 64.0)
        sgn = T("sgn", [128, 1])
        c.memset(sgn.v(sgn.t[0:64, :]), -1.0)
        c.memset(sgn.v(sgn.t[64:128, :]), 1.0)
        posi = T("posi", [128, 512], I32); ang = T("ang", [128, 512]); sn = T("sn", [128, 512]); cs_ = T("cs", [128, 512])
        tf = T("tf", [128, 512]); ti = T("ti", [128, 512], I32)
        QT = T("rQT", [128, 512]); QTs = T("rQTs", [128, 512]); KT = T("rKT", [128, 512]); KTs = T("rKTs", [128, 512])
        QtT = T("rQtT", [128, 512]); Kc = T("rKc", [64, 8, 128]); V = T("rV", [64, 8, 256]); O = T("rO", [64, 8, 256])
        G = T("rG", [64, 8, 256]); ST = T("rST", [64, 8, 64]); s8 = T("rs8", [64, 8]); s8b = T("rs8b", [64, 8]); sqr = T("rsq", [64, 8, 256])
        S = {"bufs": [T("rS0", [128, 256]), T("rS1", [128, 256])], "i": 0}
        c.memset(S["bufs"][0].v(), 0.0)
        for seg in range(self.nseg):
            sl = slice(seg * 512, (seg + 1) * 512)
            c.dma("sp", posi.v(), dview(I["pos"], I["pos"].t[:, sl]))
            for (tl, nm) in ((QT, "r_qT"), (QTs, "r_qTs"), (KT, "r_kT"), (KTs, "r_kTs")):
                c.dma("sp", tl.v(), dview(I[nm], I[nm].t[:, sl]))
            c.dma("sp", V.v(), self.tokview(I["r_v"], seg, 256))
            c.dma("sp", G.v(), self.tokview(I["r_g"], seg, 256))
            c.copy(ang.v(), posi.v())
            c.ts(ang.v(), ang.v(), invf.v(), None, ALU.mult)
            self.sincos_tables(ang.v(), sn.v(), cs_.v(), tf.v(), ti.v())
            c.ts(sn.v(), sn.v(), sgn.v(), None, ALU.mult)
            for (A, As) in ((QT, QTs), (KT, KTs)):
                c.tt(A.v(), A.v(), cs_.v(), ALU.mult)
                c.tt(As.v(), As.v(), sn.v(), ALU.mult)
                c.tt(A.v(), A.v(), As.v(), ALU.add)
            c.tt(QtT.v(QtT.t[:].rearrange("p (a b) -> p a b", a=8)), QT.v(QT.t[:].rearrange("p (a b) -> p a b", a=8)),
                 xi.v(xi.t[:, :].unsqueeze(1).to_broadcast([128, 8, 64])), ALU.mult)
            c.ts(QT.v(), QT.v(), 128 ** -0.5, None, ALU.mult)
            self.scores(KT, QT, DT, ST)
            for j in range(8):
                pp = self.pb[5 + (j % 2)]
                c.transpose(pp.v(pp.t[0:64, 0:128]), KT.v(KT.t[:, j * 64:(j + 1) * 64]), self.ident.v())
                c.ts(Kc.v(Kc.t[:, j, :], key=j), pp.v(pp.t[0:64, 0:128]), zeta.v(zeta.t[0:64, :]), None, ALU.mult)
            self.chunk_loop(seg, QtT, ST, V, Kc, lambda g: gc.v(), S, 256, O)
            self.post_rms(O, s8, s8b, ng, G, "silu", 256, True, sqr)
            st = c.dma("sp", View(self.yd.t[sl, :].rearrange("(a p) d -> p a d", p=64), [Dep()]), O.all())
            self.out_stamps.append(st)


class Even(MixBase):
    def __init__(self, c, L, which="both"):
        super().__init__(c, L)
        n = self.n
        D = c.dram
        self.i = {}
        if which in ("both", "g"):
            self.i.update(
                g_qT=D("g_qT", [2, 128, L], kind="ExternalInput"), g_kT=D("g_kT", [2, 128, L], kind="ExternalInput"),
                g_vT=D("g_vT", [2, 128, L], kind="ExternalInput"), g_z=D("g_z", [2, L, 128], kind="ExternalInput"),
                g_cw=D("g_cw", [2, 128, 3, 4], kind="ExternalInput"), g_gN=D("g_gN", [2, n, 2, 64], kind="ExternalInput"),
                g_par=D("g_par", [2, 128, 2], kind="ExternalInput"), g_ng=D("g_ng", [64, 128], kind="ExternalInput"))
            self.yb = D("yb", [2, L, 128], kind="ExternalOutput")
        if which in ("both", "s"):
            self.i.update(
                s_uT=D("s_uT", [256, L], kind="ExternalInput"), s_par=D("s_par", [8, 128, 3], kind="ExternalInput"),
                s_B=D("s_B", [8, 128, 2, 16], kind="ExternalInput"), s_C=D("s_C", [8, 128, 2, 16], kind="ExternalInput"),
                s_d=D("s_d", [8, 32, 1], kind="ExternalInput"))
            self.ya = D("yaT", [256, L], kind="ExternalOutput")
        self.strict01 = c.sb("strict01", [64, 64], F32)
        c.emit("pool", lambda e: e.affine_select(out=self.strict01.t[:], in_=self.ones.t[0:64, 0:64], pattern=[[1, 64]],
                                                 compare_op=ALU.is_gt, fill=0.0, base=0, channel_multiplier=-1),
               [self.strict01.v()], [self.ones.v()])

    def gdn_alloc(self):
        T = self.mk_
        n = self.n
        t = {}
        for nm in ("cw",):
            t[nm] = T(nm, [128, 3, 4])
        t["par"] = T("gpar", [128, 2]); t["gN"] = T("ggN", [n, 2, 64]); t["ng"] = T("gng", [64, 128])
        for nm in ("beta", "sp", "gg", "gcum"):
            t[nm] = T("g" + nm, [n, 64])
        t["ea"] = T("gea", [128, 1]); t["glast"] = T("gglast", [n, 1])
        for nm in ("Rgl", "Rdec"):
            t[nm] = T("g" + nm, [128, n])
        for nm in ("gcumC", "ngcumC", "betaC", "egC", "bgC", "kdC"):
            t[nm] = T("g" + nm, [64, n])
        t["E"] = T("gE", [n, 8, 64])
        for nm in ("Xq", "Xk", "Xv"):
            t[nm] = T("g" + nm, [128, 515])
        for nm in ("QT", "KT", "VT", "KbT", "QdT", "tmp", "gR", "bR", "WT"):
            t[nm] = T("g" + nm, [128, 512])
        for nm in ("DT", "DTsn", "ST", "X8", "Z0", "Z1", "Y0", "Y1", "P"):
            t[nm] = T("g" + nm, [64, 8, 64])
        for nm in ("Kbg", "Kd", "Vb", "U", "Vn", "O", "G", "sq"):
            t[nm] = T("g" + nm, [64, 8, 128])
        t["s8"] = T("gs8", [64, 8]); t["s8b"] = T("gs8b", [64, 8])
        t["S"] = [T("gS0", [128, 128]), T("gS1", [128, 128])]
        return t

    def gdn_head(self, hh, t):
        c, n, L = self.c, self.n, self.L
        I = self.i

        def dv(nm, ap):
            return dview(I[nm], ap)
        c.dma("sp", t["cw"].v(), dv("g_cw", I["g_cw"].t[hh]))
        c.dma("sp", t["par"].v(), dv("g_par", I["g_par"].t[hh]))
        c.dma("sp", t["gN"].v(), dv("g_gN", I["g_gN"].t[hh]))
        c.dma("sp", t["ng"].v(), I["g_ng"].v())
        par, gN = t["par"], t["gN"]
        beta, sp, gg, gcum = t["beta"], t["sp"], t["gg"], t["gcum"]
        c.act(beta.v(), gN.v(gN.t[:, 0, :]), AF.Sigmoid)
        c.act(sp.v(), gN.v(gN.t[:, 1, :]), AF.Exp, bias=par.v(par.t[0:n, 1:2]))
        c.act(sp.v(), sp.v(), AF.Ln, bias=1.0)
        c.act(t["ea"].v(), par.v(par.t[:, 0:1]), AF.Exp)
        c.ts(gg.v(), sp.v(), t["ea"].v(t["ea"].t[0:n, :]), -1.0, ALU.mult, ALU.mult)
        c.scan(gcum.v(), self.ones.v(self.ones.t[0:n, 0:64]), gg.v(), 0.0, ALU.mult, ALU.add)
        c.copy(t["glast"].v(), gcum.v(gcum.t[:, 63:64]))
        self.rep_chunk(t["glast"].v(), t["Rgl"].v(), 5)
        c.act(t["Rdec"].v(), t["Rgl"].v(), AF.Exp)
        self.to_col(gcum.v(), t["gcumC"].v(), 5)
        self.to_col(beta.v(), t["betaC"].v(), 6)
        c.ts(t["ngcumC"].v(), t["gcumC"].v(), -1.0, None, ALU.mult)
        c.act(t["egC"].v(), t["gcumC"].v(), AF.Exp)
        c.tt(t["bgC"].v(), t["betaC"].v(), t["egC"].v(), ALU.mult)
        c.tt(t["kdC"].v(), t["Rgl"].v(t["Rgl"].t[0:64, :]), t["gcumC"].v(), ALU.subtract)
        c.act(t["kdC"].v(), t["kdC"].v(), AF.Exp)
        S = {"bufs": t["S"], "i": 0}
        c.memset(t["S"][0].v(), 0.0)
        QT, KT, VT, KbT, QdT, tmp, gR, bR, WT = (t[k] for k in ("QT", "KT", "VT", "KbT", "QdT", "tmp", "gR", "bR", "WT"))
        DT, DTsn, ST, X8, P = (t[k] for k in ("DT", "DTsn", "ST", "X8", "P"))
        Kbg, Kd, Vb, U, Vn, O, G = (t[k] for k in ("Kbg", "Kd", "Vb", "U", "Vn", "O", "G"))
        for seg in range(self.nseg):
            sl = slice(seg * 512, (seg + 1) * 512)
            cs = slice(seg * 8, seg * 8 + 8)
            for ci, (X, src, dst) in enumerate(((t["Xq"], "g_qT", QT), (t["Xk"], "g_kT", KT), (t["Xv"], "g_vT", VT))):
                if seg == 0:
                    c.memset(X.v(X.t[:, 0:3]), 0.0)
                    c.dma("sp", X.v(X.t[:, 3:515]), dv(src, I[src].t[hh, :, 0:512]))
                else:
                    c.dma("sp", X.v(), dv(src, I[src].t[hh, :, seg * 512 - 3:seg * 512 + 512]))
                cw = t["cw"]
                c.ts(dst.v(), X.v(X.t[:, 0:512]), cw.v(cw.t[:, ci, 0:1]), None, ALU.mult)
                for j in range(1, 4):
                    c.stt(dst.v(), X.v(X.t[:, j:j + 512]), cw.v(cw.t[:, ci, j:j + 1]), dst.v(), ALU.mult, ALU.add)
                c.act(dst.v(), dst.v(), AF.Silu)
                if ci < 2:
                    c.tt(tmp.v(), dst.v(), dst.v(), ALU.mult)
                    pp = self.pb[5 + ci]
                    c.matmul(pp.v(), self.ones.v(), tmp.v())
                    c.act(tmp.v(), pp.v(), AF.Ln, bias=1e-6)
                    c.act(tmp.v(), tmp.v(), AF.Exp, scale=-0.5)
                    if ci == 0:
                        c.stt(dst.v(), dst.v(), 128 ** -0.5, tmp.v(), ALU.mult, ALU.mult)
                    else:
                        c.tt(dst.v(), dst.v(), tmp.v(), ALU.mult)
            c.dma("sp", G.v(), dv("g_z", I["g_z"].t[hh, sl, :].rearrange("(a p) d -> p a d", p=64)))
            self.to_row_seg(gcum, seg, gR.v(), 5, t["E"])
            self.to_row_seg(beta, seg, bR.v(), 6, t["E"])
            c.tt(KbT.v(), KT.v(), bR.v(), ALU.mult)
            c.act(tmp.v(), gR.v(), AF.Exp)
            c.tt(QdT.v(), QT.v(), tmp.v(), ALU.mult)
            c.tt(DT.v(), gR.v(gR.t[0:64, :].rearrange("p (a b) -> p a b", a=8)),
                 self.negmask.v(self.negmask.t[:, :].unsqueeze(1).to_broadcast([64, 8, 64])), ALU.add)
            for j in range(8):
                g = seg * 8 + j
                c.act(DT.v(DT.t[:, j, :]), DT.v(DT.t[:, j, :]), AF.Exp, bias=t["ngcumC"].v(t["ngcumC"].t[:, g:g + 1]))
            c.stt(DTsn.v(), DT.v(), -1.0, self.strict01.v(self.strict01.t[:, :].unsqueeze(1).to_broadcast([64, 8, 64])), ALU.mult, ALU.mult)
            self.scores(KT, QT, DT, ST)
            self.scores(KT, KbT, DTsn, X8)
            for j in range(8):
                g = seg * 8 + j
                pp = self.pb[5 + (j % 2)]
                c.transpose(pp.v(pp.t[0:64, 0:128]), KT.v(KT.t[:, j * 64:(j + 1) * 64]), self.ident.v())
                c.transpose(pp.v(pp.t[0:64, 128:256]), VT.v(VT.t[:, j * 64:(j + 1) * 64]), self.ident.v())
                c.ts(Kbg.v(Kbg.t[:, j, :], key=j), pp.v(pp.t[0:64, 0:128]), t["bgC"].v(t["bgC"].t[:, g:g + 1]), None, ALU.mult)
                c.ts(Kd.v(Kd.t[:, j, :], key=j), pp.v(pp.t[0:64, 0:128]), t["kdC"].v(t["kdC"].t[:, g:g + 1]), None, ALU.mult)
                c.ts(Vb.v(Vb.t[:, j, :], key=j), pp.v(pp.t[0:64, 128:256]), t["betaC"].v(t["betaC"].t[:, g:g + 1]), None, ALU.mult)
            Y = [X8, t["Y0"], t["Y1"]]
            Z = [t["Z0"], t["Z1"]]
            pz = self.pb[7]
            for j in range(8):
                c.transpose(pz.v(pz.t[0:64, j * 64:(j + 1) * 64]), X8.v(X8.t[:, j, :]), self.ident.v(self.ident.t[0:64, 0:64]))
            c.copy(Z[0].v(), pz.v(pz.t[0:64, :].rearrange("p (a b) -> p a b", a=8)), eng="act")
            c.tt(P.v(), X8.v(), self.ident.v(self.ident.t[0:64, 0:64].unsqueeze(1).to_broadcast([64, 8, 64])), ALU.add)
            ycur, zcur = X8, Z[0]
            for lvl in range(5):
                ynew = t["Y0"] if lvl % 2 == 0 else t["Y1"]
                znew = Z[(lvl + 1) % 2]
                pa, pbk, pc = self.pb[5], self.pb[6], self.pb[7]
                for j in range(8):
                    c.matmul(pa.v(pa.t[0:64, j * 64:(j + 1) * 64]), zcur.v(zcur.t[:, j, :]), ycur.v(ycur.t[:, j, :]))
                    c.matmul(pbk.v(pbk.t[0:64, j * 64:(j + 1) * 64]), ycur.v(ycur.t[:, j, :]), zcur.v(zcur.t[:, j, :]))
                c.copy(ynew.v(), pa.v(pa.t[0:64, :].rearrange("p (a b) -> p a b", a=8)), eng="act")
                c.copy(znew.v(), pbk.v(pbk.t[0:64, :].rearrange("p (a b) -> p a b", a=8)))
                for j in range(8):
                    c.matmul(pc.v(pc.t[0:64, j * 64:(j + 1) * 64]), znew.v(znew.t[:, j, :]), P.v(P.t[:, j, :]))
                c.tt(P.v(), P.v(), pc.v(pc.t[0:64, :].rearrange("p (a b) -> p a b", a=8)), ALU.add)
                ycur, zcur = ynew, znew
            pw = self.pb[5]
            for j in range(8):
                c.matmul(pw.v(pw.t[:, j * 64:(j + 1) * 64]), Kbg.v(Kbg.t[:, j, :], key=j), P.v(P.t[:, j, :]))
            c.copy(WT.v(), pw.v(), eng="act")
            for half in range(2):
                pu = self.pb[6 + half]
                for jj in range(4):
                    j = half * 4 + jj
                    c.matmul(pu.v(pu.t[0:64, jj * 128:(jj + 1) * 128]), P.v(P.t[:, j, :]), Vb.v(Vb.t[:, j, :], key=j))
                c.copy(U.v(U.t[:, half * 4:half * 4 + 4, :]), pu.v(pu.t[0:64, :].rearrange("p (a b) -> p a b", a=4)))

            def pre_hook(j, g, cur):
                p0 = self.pb[0]
                c.matmul(p0.v(p0.t[0:64, 0:128]), WT.v(WT.t[:, j * 64:(j + 1) * 64]), cur.v())
                c.tt(Vn.v(Vn.t[:, j, :], key=j), U.v(U.t[:, j, :]), p0.v(p0.t[0:64, 0:128]), ALU.subtract)
                return Vn.v(Vn.t[:, j, :], key=j)
            self.chunk_loop(seg, QdT, ST, Vn, Kd, lambda g: t["Rdec"].v(t["Rdec"].t[:, g:g + 1]), S, 128, O, pre_hook=pre_hook)
            self.post_rms(O, t["s8"], t["s8b"], t["ng"], G, "silu", 128, False, t["sq"])
            st = c.dma("sp", View(self.yb.t[hh, sl, :].rearrange("(a p) d -> p a d", p=64), [Dep()]), O.v())
            self.out_stamps.append(st)

    def gdn(self):
        t = self.gdn_alloc()
        for hh in range(2):
            self.gdn_head(hh, t)

    def s5(self):
        c, L = self.c, self.L
        I = self.i
        T = self.mk_
        TC = 512
        nch = L // TC
        par = T("spar", [128, 3]); Bm = T("sB", [128, 2, 16]); Cm = T("sC", [128, 2, 16]); dsk = T("sd", [32, 1])
        step = T("sstep", [128, 1]); rr = T("srr", [128, 1]); th = T("sth", [128, 1])
        sn1 = T("ssn1", [128, 1]); cs1 = T("scs1", [128, 1]); f1 = T("sf1", [128, 1]); i1 = T("si1", [128, 1], I32)
        lr = T("slr", [128, 1]); li = T("sli", [128, 1]); den = T("sden", [128, 1]); fr = T("sfr", [128, 1]); fi = T("sfi", [128, 1])
        a1 = T("sa1", [128, 1]); a2 = T("sa2", [128, 1])
        bb = T("sbb", [128, 2, 16]); b1 = T("sb1", [128, 16]); b2 = T("sb2", [128, 16])
        BBr = T("sBBr", [128, 32]); BBi = T("sBBi", [128, 32]); CCr = T("sCCr", [128, 32]); nCCi = T("snCCi", [128, 32])
        BrT = T("sBrT", [32, 128]); BiT = T("sBiT", [32, 128])
        io = T("sio", [128, TC], I32); tp1 = T("stp1", [128, TC])
        ang = T("sang", [128, TC]); sn = T("ssn", [128, TC]); cs_ = T("scs", [128, TC]); tf = T("stf", [128, TC]); ti = T("sti", [128, TC], I32)
        Rt = T("sRt", [128, TC]); t1 = T("st1", [128, TC]); t2 = T("st2", [128, TC]); br = T("sbr", [128, TC]); bi = T("sbi", [128, TC])
        zr = T("szr", [128, TC]); zi = T("szi", [128, TC])
        xr = [T("sxr0", [128, TC]), T("sxr1", [128, TC])]; xi = [T("sxi0", [128, TC]), T("sxi1", [128, TC])]
        uT = T("suT", [32, TC]); ystg = [T("sy0", [32, TC]), T("sy1", [32, TC])]
        x0 = T("sx0", [128, 1])
        c.memset(x0.v(), 0.0)
        c.emit("pool", lambda e: e.iota(io.t[:], pattern=[[1, TC]], base=1, channel_multiplier=0), [io.v()], [])
        c.copy(tp1.v(), io.v())
        cnt = 0
        for tl in range(8):
            c.dma("sp", par.v(), dview(I["s_par"], I["s_par"].t[tl]))
            c.dma("sp", Bm.v(), dview(I["s_B"], I["s_B"].t[tl]))
            c.dma("sp", Cm.v(), dview(I["s_C"], I["s_C"].t[tl]))
            c.dma("sp", dsk.v(), dview(I["s_d"], I["s_d"].t[tl]))
            ar, ai = par.v(par.t[:, 0:1]), par.v(par.t[:, 1:2])
            c.act(step.v(), par.v(par.t[:, 2:3]), AF.Exp)
            c.tt(rr.v(), ar, step.v(), ALU.mult)
            c.act(rr.v(), rr.v(), AF.Exp)
            c.tt(th.v(), ai, step.v(), ALU.mult)
            self.sincos_tables(th.v(), sn1.v(), cs1.v(), f1.v(), i1.v())
            c.tt(lr.v(), rr.v(), cs1.v(), ALU.mult)
            c.tt(li.v(), rr.v(), sn1.v(), ALU.mult)
            c.tt(den.v(), ar, ar, ALU.mult)
            c.tt(a1.v(), ai, ai, ALU.mult)
            c.tt(den.v(), den.v(), a1.v(), ALU.add)
            c.recip(den.v(), den.v())
            c.ts(a1.v(), lr.v(), -1.0, None, ALU.add)
            c.tt(fr.v(), a1.v(), ar, ALU.mult)
            c.tt(a2.v(), li.v(), ai, ALU.mult)
            c.tt(fr.v(), fr.v(), a2.v(), ALU.add)
            c.tt(fr.v(), fr.v(), den.v(), ALU.mult)
            c.tt(fi.v(), li.v(), ar, ALU.mult)
            c.tt(a2.v(), a1.v(), ai, ALU.mult)
            c.tt(fi.v(), fi.v(), a2.v(), ALU.subtract)
            c.tt(fi.v(), fi.v(), den.v(), ALU.mult)
            c.ts(b1.v(), Bm.v(Bm.t[:, 0, :]), fr.v(), None, ALU.mult)
            c.ts(b2.v(), Bm.v(Bm.t[:, 1, :]), fi.v(), None, ALU.mult)
            c.tt(bb.v(bb.t[:, 0, :]), b1.v(), b2.v(), ALU.subtract)
            c.ts(b1.v(), Bm.v(Bm.t[:, 1, :]), fr.v(), None, ALU.mult)
            c.ts(b2.v(), Bm.v(Bm.t[:, 0, :]), fi.v(), None, ALU.mult)
            c.tt(bb.v(bb.t[:, 1, :]), b1.v(), b2.v(), ALU.add)
            for (dstb, src, k, sc) in ((BBr, bb, 0, 1.0), (BBi, bb, 1, 1.0), (CCr, Cm, 0, 1.0), (nCCi, Cm, 1, -1.0)):
                c.memset(dstb.v(), 0.0)
                for gi in range(2):
                    c.ts(dstb.v(dstb.t[gi * 64:(gi + 1) * 64, gi * 16:(gi + 1) * 16]), src.v(src.t[gi * 64:(gi + 1) * 64, k, :]), sc, None, ALU.mult)
            for (dstT, srcb, bank) in ((BrT, BBr, 5), (BiT, BBi, 6)):
                pp = self.pb[bank]
                c.transpose(pp.v(pp.t[0:32, 0:128]), srcb.v(), self.ident.v())
                c.copy(dstT.v(), pp.v(pp.t[0:32, 0:128]), eng="act")
            c.ts(ang.v(), tp1.v(), th.v(), None, ALU.mult)
            self.sincos_tables(ang.v(), sn.v(), cs_.v(), tf.v(), ti.v())
            c.ts(Rt.v(), self.ones.v(self.ones.t[:, 0:1].to_broadcast([128, TC])), rr.v(), None, ALU.mult)
            prev_r, prev_i = x0.v(), x0.v()
            for ch in range(nch):
                sl = slice(ch * TC, (ch + 1) * TC)
                c.dma("sp", uT.v(), dview(I["s_uT"], I["s_uT"].t[tl * 32:(tl + 1) * 32, sl]))
                pr, pi = self.pb[1 + (cnt % 2) * 2], self.pb[2 + (cnt % 2) * 2]
                c.matmul(pr.v(), BrT.v(), uT.v())
                c.matmul(pi.v(), BiT.v(), uT.v())
                c.tt(t1.v(), pr.v(), cs_.v(), ALU.mult)
                c.tt(t2.v(), pi.v(), sn.v(), ALU.mult)
                c.tt(br.v(), t1.v(), t2.v(), ALU.add)
                c.tt(t1.v(), pi.v(), cs_.v(), ALU.mult)
                c.tt(t2.v(), pr.v(), sn.v(), ALU.mult)
                c.tt(bi.v(), t1.v(), t2.v(), ALU.subtract)
                c.scan(zr.v(), Rt.v(), br.v(), prev_r, ALU.mult, ALU.add)
                c.scan(zi.v(), Rt.v(), bi.v(), prev_i, ALU.mult, ALU.add)
                xrc, xic = xr[cnt % 2], xi[cnt % 2]
                c.tt(t1.v(), zr.v(), cs_.v(), ALU.mult)
                c.tt(t2.v(), zi.v(), sn.v(), ALU.mult)
                c.tt(xrc.v(), t1.v(), t2.v(), ALU.subtract)
                c.tt(t1.v(), zi.v(), cs_.v(), ALU.mult)
                c.tt(t2.v(), zr.v(), sn.v(), ALU.mult)
                c.tt(xic.v(), t1.v(), t2.v(), ALU.add)
                prev_r, prev_i = xrc.v(xrc.t[:, TC - 1:TC]), xic.v(xic.t[:, TC - 1:TC])
                py = self.pb[5 + (cnt % 2)]
                c.matmul(py.v(py.t[0:32, :]), CCr.v(), xrc.v(), start=True, stop=False)
                c.matmul(py.v(py.t[0:32, :]), nCCi.v(), xic.v(), start=False, stop=True)
                ys = ystg[cnt % 2]
                c.stt(ys.v(), uT.v(), dsk.v(), py.v(py.t[0:32, :]), ALU.mult, ALU.add)
                st = c.dma("sp", View(self.ya.t[tl * 32:(tl + 1) * 32, sl], [Dep()]), ys.v())
                self.out_stamps.append(st)
                cnt += 1

Even.post_rms = Odd.post_rms


INTERLEAVE = True


class FusedMix(MixBase):
    def __init__(self, c, L, consts):
        self.c = c
        self.L = L
        self.n = L // 64
        self.nseg = L // 512
        for k, v in consts.items():
            setattr(self, k, v)
        self.out_stamps = []
        self.uid = consts["uidbox"]

    def name(self, s):
        self.uid[0] += 1
        return "%s_%d" % (s, self.uid[0])

    def mk(self, s, shape, dt=F32):
        return self.c.sb(self.name(s), shape, dt)

    def setup_io(self, G, nrows, idx, SA, SB, GA, GB, groups, tag):
        self.G, self.nrows, self.idx = G, nrows, idx
        self.SA, self.SB, self.GA, self.GB, self.groups = SA, SB, GA, GB, groups
        self.skeys = {}
        self.gtag = tag
        self.Grow = G.t.rearrange("k a r t -> (k a r) t")
        self.Gblk = G.t.rearrange("k a r (j b) -> (k a r j) b", b=64)

    def ld(self, dst, col, parts=128, blocks=None):
        self.c.gather(dst, self.G, self.Grow, self.idx.v(self.idx.t[0:parts, col:col + 1]), blocks)

    def ld_gN(self, dst, col, blocks=None):
        self.c.gather(dst, self.G, self.Gblk, self.idx.v(self.idx.t[0:self.n, col:col + 1]), blocks)

    def st(self, src, seg, row0, rows):
        part = "A" if row0 < 256 else "B"
        S = self.SA if part == "A" else self.SB
        r0 = row0 % 256
        key = (seg, r0)
        self.skeys.setdefault((part, seg // 2), []).append(key)
        return self.c.dma("sp", S.v(S.t[seg // 2, seg % 2, r0:r0 + rows, :], key=key), src)

    def send(self, part, pr):
        S, G = (self.SA, self.GA) if part == "A" else (self.SB, self.GB)
        self.c.allgather(S.t[pr].rearrange("a r t -> (a r) t"), G.t[pr].rearrange("s a r t -> (s a r) t"),
                         [S.v(S.t[pr], key=self.skeys[(part, pr)])], [G.v(G.t[pr], key=pr)], self.groups,
                         grp=self.gtag + part, grp_total=4)

    def step(self, gen, k):
        if gen is not None and INTERLEAVE:
            done = 0
            while done < k:
                r = next(gen, "end")
                if r == "end":
                    return
                if r == 1:
                    done += 1

    def tok_from_feat(self, FT, out, w0=0, scale=None):
        c = self.c
        for half in range(2):
            pp = self.pb[5 + half]
            for jj in range(4):
                j = half * 4 + jj
                c.transpose(pp.v(pp.t[0:64, jj * 128:(jj + 1) * 128]), FT.v(FT.t[:, j * 64:(j + 1) * 64]), self.ident.v())
            c.copy(out.v(out.t[:, half * 4:half * 4 + 4, w0:w0 + 128]), pp.v(pp.t[0:64, :].rearrange("p (a b) -> p a b", a=4)), eng="act")

    def feat_from_tok(self, TK, w0, out, bank):
        c = self.c
        pp = self.pb[bank]
        for j in range(8):
            c.transpose(pp.v(pp.t[:, j * 64:(j + 1) * 64]), TK.v(TK.t[:, j, w0:w0 + 128]), self.ident.v(self.ident.t[0:64, 0:64]))
        c.copy(out.v(), pp.v(), eng="act")


class OddF(FusedMix):
    COLS_PER_SEG = 14

    def mlstm(self, ext):
        c, n, L = self.c, self.n, self.L
        T = self.mk
        par = T("mpar", [128, 2])
        c.dma("sp", par.v(), ext["m_par"].v())
        gN = T("gN", [n, 2, 64])
        gcol = self.nseg * self.COLS_PER_SEG
        self.ld_gN(gN.v(gN.t[:, 0, :]), gcol, [12])
        self.ld_gN(gN.v(gN.t[:, 1, :]), gcol + 1, [12])
        ng = T("mng", [64, 256])
        c.dma("sp", ng.v(), ext["m_ng"].v())
        nbf = T("nbf", [128, 1])
        c.ts(nbf.v(), par.v(par.t[:, 1:2]), -1.0, None, ALU.mult)
        ig = T("ig", [n, 64]); sp = T("sp", [n, 64]); csp = T("csp", [n, 64]); bcum = T("bcum", [n, 64])
        colB = T("colB", [n, 64]); cmx = T("cmx", [n, 64]); imax = T("imax", [n, 64]); ul = T("ul", [n, 64])
        umax = T("umax", [n, 1]); blast = T("blast", [n, 1])
        c.ts(ig.v(), gN.v(gN.t[:, 0, :]), par.v(par.t[0:n, 0:1]), None, ALU.add)
        c.act(sp.v(), gN.v(gN.t[:, 1, :]), AF.Exp, bias=nbf.v(nbf.t[0:n, :]), scale=-1.0)
        c.act(sp.v(), sp.v(), AF.Ln, bias=1.0)
        c.scan(csp.v(), self.ones.v(self.ones.t[0:n, 0:64]), sp.v(), 0.0, ALU.mult, ALU.add)
        c.ts(bcum.v(), csp.v(), -1.0, None, ALU.mult)
        c.tt(colB.v(), ig.v(), csp.v(), ALU.add)
        c.scan(cmx.v(), self.zeros.v(self.zeros.t[0:n, :]), colB.v(), NEG, ALU.add, ALU.max)
        c.tt(imax.v(), bcum.v(), cmx.v(), ALU.add)
        c.copy(blast.v(), bcum.v(bcum.t[:, 63:64]))
        c.ts(ul.v(), colB.v(), blast.v(), None, ALU.add)
        c.reduce(umax.v(), ul.v(), ALU.max)
        Rb = T("Rb", [128, n]); Ru = T("Ru", [128, n]); Mn = T("Mn", [128, n]); Mo = T("Mo", [128, n]); Rc = T("Rc", [128, n])
        self.rep_chunk(blast.v(), Rb.v(), 5)
        self.rep_chunk(umax.v(), Ru.v(), 6)
        c.scan(Mn.v(), Rb.v(), Ru.v(), 0.0, ALU.add, ALU.max)
        c.memset(Mo.v(), 0.0)
        c.copy(Mo.v(Mo.t[:, 1:n]), Mn.v(Mn.t[:, 0:n - 1]))
        c.tt(Rc.v(), Rb.v(), Mo.v(), ALU.add)
        c.tt(Rc.v(), Rc.v(), Mn.v(), ALU.subtract)
        c.act(Rc.v(), Rc.v(), AF.Exp)
        bcumC = T("bcumC", [64, n]); colBC = T("colBC", [64, n]); imaxC = T("imaxC", [64, n]); ulC = T("ulC", [64, n])
        self.to_col(bcum.v(), bcumC.v(), 5)
        self.to_col(colB.v(), colBC.v(), 6)
        self.to_col(imax.v(), imaxC.v(), 5)
        self.to_col(ul.v(), ulC.v(), 6)
        msC = T("msC", [64, n]); enC = T("enC", [64, n]); kwsC = T("kwsC", [64, n])
        c.tt(msC.v(), bcumC.v(), Mo.v(Mo.t[0:64, :]), ALU.add)
        c.tt(msC.v(), msC.v(), imaxC.v(), ALU.max)
        c.act(enC.v(), msC.v(), AF.Exp, scale=-1.0)
        c.tt(kwsC.v(), ulC.v(), Mn.v(Mn.t[0:64, :]), ALU.subtract)
        c.act(kwsC.v(), kwsC.v(), AF.Exp)
        E = T("E", [n, 8, 64])
        bR = T("bR", [128, 512]); iR = T("iR", [128, 512]); msR = T("msR", [128, 512])
        QT = T("mQT", [128, 512]); QtT = T("mQtT", [128, 512]); KT = T("mKT", [128, 512]); FT = [T("mFT0", [128, 512]), T("mFT1", [128, 512])]
        Kt = T("mKt", [64, 8, 128]); Kc = T("mKc", [64, 8, 128]); V = T("mV", [64, 8, 258]); O = T("mO", [64, 8, 258])
        G = T("mG", [64, 8, 256]); DT = T("mDT", [64, 8, 64]); ST = T("mST", [64, 8, 64])
        H = T("mH", [64, 8, 256]); s8 = T("ms8", [64, 8]); s8b = T("ms8b", [64, 8]); sqm = T("msq", [64, 8, 256])
        OF = [T("mOF0", [128, 512]), T("mOF1", [128, 512])]
        S = {"bufs": [T("mS0", [128, 258]), T("mS1", [128, 258])], "i": 0}
        c.memset(S["bufs"][0].v(), 0.0)
        for seg in range(self.nseg):
            cs = slice(seg * 8, seg * 8 + 8)
            c0 = seg * self.COLS_PER_SEG
            self.ld(QT.v(), c0 + 0, blocks=[0, 1])
            self.ld(KT.v(), c0 + 1, blocks=[2, 3])
            self.tok_from_feat(KT, Kt)
            for hb in range(2):
                f = FT[hb]
                self.ld(f.v(), c0 + 2 + hb, blocks=[4, 5, 6, 7])
                self.tok_from_feat(f, V, w0=hb * 128)
            c.memset(V.v(V.t[:, :, 256:257]), 1.0)
            c.memset(V.v(V.t[:, :, 257:258]), 0.0)
            self.to_row_seg(bcum, seg, bR.v(), 5, E)
            self.to_row_seg(imax, seg, iR.v(), 6, E)
            mo_b = Mo.v(Mo.t[:, cs].unsqueeze(2).to_broadcast([128, 8, 64]))
            b3 = bR.v(bR.t[:].rearrange("p (a b) -> p a b", a=8))
            ms3 = msR.v(msR.t[:].rearrange("p (a b) -> p a b", a=8))
            c.tt(ms3, b3, mo_b, ALU.add)
            c.tt(iR.v(), msR.v(), iR.v(), ALU.max)
            c.tt(msR.v(), msR.v(), iR.v(), ALU.subtract)
            c.act(msR.v(), msR.v(), AF.Exp)
            c.tt(bR.v(), bR.v(), iR.v(), ALU.subtract)
            c.ts(QT.v(), QT.v(), 128 ** -0.5, None, ALU.mult)
            c.tt(QtT.v(), QT.v(), msR.v(), ALU.mult)
            c.tt(DT.v(), bR.v(bR.t[0:64, :].rearrange("p (a b) -> p a b", a=8)),
                 self.negmask.v(self.negmask.t[:, :].unsqueeze(1).to_broadcast([64, 8, 64])), ALU.add)
            for j in range(8):
                c.act(DT.v(DT.t[:, j, :]), DT.v(DT.t[:, j, :]), AF.Exp, bias=colBC.v(colBC.t[:, seg * 8 + j:seg * 8 + j + 1]))
            self.scores(KT, QT, DT, ST)
            c.tt(Kc.v(), Kt.v(), kwsC.v(kwsC.t[:, cs].unsqueeze(2).to_broadcast([64, 8, 128])), ALU.mult)
            self.chunk_loop(seg, QtT, ST, V, Kc, lambda g: Rc.v(Rc.t[:, g:g + 1]), S, 258, O)
            c.ts(s8.v(), O.v(O.t[:, :, 256]), -1.0, None, ALU.mult)
            c.tt(s8.v(), s8.v(), O.v(O.t[:, :, 256]), ALU.max)
            c.tt(s8.v(), s8.v(), enC.v(enC.t[:, cs]), ALU.max)
            c.recip(s8.v(), s8.v())
            c.tt(H.v(), O.v(O.t[:, :, 0:256]), s8.v(s8.t[:, :].unsqueeze(2).to_broadcast([64, 8, 256])), ALU.mult)
            self.post_rms(H, s8, s8b, ng, None, AF.Sigmoid, 256, False, sqm)
            for hb in range(2):
                self.feat_from_tok(H, hb * 128, OF[hb], 5 + hb)
                self.out_stamps.append(self.st(OF[hb].v(), seg, hb * 128, 128))
            if seg % 2 == 1:
                self.send("A", seg // 2)

    def retention(self, ext):
        c, n, L = self.c, self.n, self.L
        T = self.mk
        hid = T("hid", [128, 1]); lg = T("lg", [128, 1])
        c.dma("sp", hid.v(), ext["hidx"].v())
        ng = T("rng", [64, 256])
        c.dma("sp", ng.v(), ext["r_ng"].v())
        c.act(lg.v(), hid.v(), AF.Exp, scale=-math.log(2.0))
        c.ts(lg.v(), lg.v(), 2.0 ** -5, None, ALU.mult)
        c.act(lg.v(), lg.v(), AF.Ln, scale=-1.0, bias=1.0)
        io_cs = T("io_cs", [64, 64], I32); f_cs = T("f_cs", [64, 64])
        c.emit("pool", lambda e: e.iota(io_cs.t[:], pattern=[[1, 64]], base=0, channel_multiplier=-1), [io_cs.v()], [])
        c.copy(f_cs.v(), io_cs.v())
        DT1 = T("DT1", [64, 64])
        c.act(DT1.v(), f_cs.v(), AF.Exp, scale=lg.v(lg.t[0:64, :]))
        c.emit("pool", lambda e: e.affine_select(out=DT1.t[:], in_=DT1.t[:], pattern=[[1, 64]], compare_op=ALU.is_ge,
                                                 fill=0.0, base=0, channel_multiplier=-1), [DT1.v()], [DT1.v()])
        DT = T("rDT", [64, 8, 64])
        c.copy(DT.v(), DT1.v(DT1.t[:, :].unsqueeze(1).to_broadcast([64, 8, 64])))
        io_r = T("io_r", [128, 64], I32); xi = T("xi", [128, 64])
        c.emit("pool", lambda e: e.iota(io_r.t[:], pattern=[[1, 64]], base=1, channel_multiplier=0), [io_r.v()], [])
        c.copy(xi.v(), io_r.v())
        c.act(xi.v(), xi.v(), AF.Exp, scale=lg.v())
        c.ts(xi.v(), xi.v(), 128 ** -0.5, None, ALU.mult)
        io_p = T("io_p", [128, 1], I32); zeta = T("zeta", [128, 1]); gc = T("gc", [128, 1]); invf = T("invf", [128, 1])
        c.emit("pool", lambda e: e.iota(io_p.t[0:64, :], pattern=[[0, 1]], base=63, channel_multiplier=-1), [io_p.v()], [])
        c.copy(zeta.v(zeta.t[0:64, :]), io_p.v(io_p.t[0:64, :]))
        c.act(zeta.v(zeta.t[0:64, :]), zeta.v(zeta.t[0:64, :]), AF.Exp, scale=lg.v(lg.t[0:64, :]))
        c.act(gc.v(), lg.v(), AF.Exp, scale=64.0)
        for h0 in (0, 64):
            c.emit("pool", lambda e, h0=h0: e.iota(io_p.t[h0:h0 + 64, :], pattern=[[0, 1]], base=0, channel_multiplier=1), [io_p.v()], [])
        c.copy(invf.v(), io_p.v())
        c.act(invf.v(), invf.v(), AF.Exp, scale=-math.log(10000.0) / 64.0)
        sgn = T("sgn", [128, 1])
        c.memset(sgn.v(sgn.t[0:64, :]), -1.0)
        c.memset(sgn.v(sgn.t[64:128, :]), 1.0)
        posi = T("posi", [128, 512], I32); ang = T("ang", [128, 512]); sn = T("sn", [128, 512]); cs_ = T("cs", [128, 512])
        tf = T("tf", [128, 512]); ti = T("ti", [128, 512], I32)
        QT = T("rQT", [128, 512]); QTs = T("rQTs", [128, 512]); KT = T("rKT", [128, 512]); KTs = T("rKTs", [128, 512])
        QtT = T("rQtT", [128, 512]); Kc = T("rKc", [64, 8, 128]); V = T("rV", [64, 8, 256]); O = T("rO", [64, 8, 256])
        G = T("rG", [64, 8, 256]); ST = T("rST", [64, 8, 64]); s8 = T("rs8", [64, 8]); s8b = T("rs8b", [64, 8]); sqr = T("rsq", [64, 8, 256])
        FT = [T("rFT0", [128, 512]), T("rFT1", [128, 512])]
        OF = [T("rOF0", [128, 512]), T("rOF1", [128, 512])]
        S = {"bufs": [T("rS0", [128, 256]), T("rS1", [128, 256])], "i": 0}
        c.memset(S["bufs"][0].v(), 0.0)
        pos_d = ext["pos"]
        for seg in range(self.nseg):
            sl = slice(seg * 512, (seg + 1) * 512)
            c0 = seg * self.COLS_PER_SEG + 6
            c.dma("sp", posi.v(), dview(pos_d, pos_d.t[:, sl]))
            for i, tl in enumerate((QT, QTs, KT, KTs)):
                self.ld(tl.v(), c0 + i)
            for hb in range(2):
                self.ld(FT[hb].v(), c0 + 4 + hb)
                self.tok_from_feat(FT[hb], V, w0=hb * 128)
            c.copy(ang.v(), posi.v())
            c.ts(ang.v(), ang.v(), invf.v(), None, ALU.mult)
            self.sincos_tables(ang.v(), sn.v(), cs_.v(), tf.v(), ti.v())
            c.ts(sn.v(), sn.v(), sgn.v(), None, ALU.mult)
            for (A, As) in ((QT, QTs), (KT, KTs)):
                c.tt(A.v(), A.v(), cs_.v(), ALU.mult)
                c.tt(As.v(), As.v(), sn.v(), ALU.mult)
                c.tt(A.v(), A.v(), As.v(), ALU.add)
            c.tt(QtT.v(QtT.t[:].rearrange("p (a b) -> p a b", a=8)), QT.v(QT.t[:].rearrange("p (a b) -> p a b", a=8)),
                 xi.v(xi.t[:, :].unsqueeze(1).to_broadcast([128, 8, 64])), ALU.mult)
            c.ts(QT.v(), QT.v(), 128 ** -0.5, None, ALU.mult)
            self.scores(KT, QT, DT, ST)
            for j in range(8):
                pp = self.pb[5 + (j % 2)]
                c.transpose(pp.v(pp.t[0:64, 0:128]), KT.v(KT.t[:, j * 64:(j + 1) * 64]), self.ident.v())
                c.ts(Kc.v(Kc.t[:, j, :], key=j), pp.v(pp.t[0:64, 0:128]), zeta.v(zeta.t[0:64, :]), None, ALU.mult)
            self.chunk_loop(seg, QtT, ST, V, Kc, lambda g: gc.v(), S, 256, O)
            self.post_rms(O, s8, s8b, ng, None, "silu", 256, True, sqr)
            for hb in range(2):
                self.feat_from_tok(O, hb * 128, OF[hb], 5 + hb)
                self.out_stamps.append(self.st(OF[hb].v(), seg, 256 + hb * 128, 128))
            if seg % 2 == 1:
                self.send("B", seg // 2)


OddF.post_rms = Odd.post_rms


class EvenF(FusedMix):
    GCOLS = 8

    def gdn_alloc(self):
        T = self.mk
        n = self.n
        t = {}
        t["cw"] = T("cw", [128, 3, 4])
        t["par"] = T("gpar", [128, 2]); t["gN"] = T("ggN", [n, 2, 64]); t["ng"] = T("gng", [64, 128])
        for nm in ("beta", "sp", "gg", "gcum"):
            t[nm] = T("g" + nm, [n, 64])
        t["ea"] = T("gea", [128, 1]); t["glast"] = T("gglast", [n, 1])
        for nm in ("Rgl", "Rdec"):
            t[nm] = T("g" + nm, [128, n])
        for nm in ("gcumC", "ngcumC", "betaC", "egC", "bgC", "kdC"):
            t[nm] = T("g" + nm, [64, n])
        t["E"] = T("gE", [n, 8, 64])
        for nm in ("Xq", "Xk", "Xv"):
            t[nm] = T("g" + nm, [128, 515])
        for nm in ("QT", "KT", "VT", "KbT", "QdT", "tmp", "gR", "bR", "WT", "ZT", "OF"):
            t[nm] = T("g" + nm, [128, 512])
        for nm in ("DT", "DTsn", "ST", "X8", "Z0", "Z1", "Y0", "Y1", "P"):
            t[nm] = T("g" + nm, [64, 8, 64])
        for nm in ("Kbg", "Kd", "Vb", "U", "Vn", "O", "G", "sq"):
            t[nm] = T("g" + nm, [64, 8, 128])
        t["s8"] = T("gs8", [64, 8]); t["s8b"] = T("gs8b", [64, 8])
        t["S"] = [T("gS0", [128, 128]), T("gS1", [128, 128])]
        return t

    def gdn_head(self, hh, t, ext, gen=None):
        c, n, L = self.c, self.n, self.L
        c.dma("sp", t["cw"].v(), dview(ext["g_cw"], ext["g_cw"].t[hh]))
        c.dma("sp", t["par"].v(), dview(ext["g_par"], ext["g_par"].t[hh]))
        c.dma("sp", t["ng"].v(), ext["g_ng"].v())
        gcol = self.nseg * self.GCOLS + hh * 2
        par, gN = t["par"], t["gN"]
        self.ld_gN(gN.v(gN.t[:, 0, :]), gcol, [20])
        self.ld_gN(gN.v(gN.t[:, 1, :]), gcol + 1, [20])
        beta, sp, gg, gcum = t["beta"], t["sp"], t["gg"], t["gcum"]
        c.act(beta.v(), gN.v(gN.t[:, 0, :]), AF.Sigmoid)
        c.act(sp.v(), gN.v(gN.t[:, 1, :]), AF.Exp, bias=par.v(par.t[0:n, 1:2]))
        c.act(sp.v(), sp.v(), AF.Ln, bias=1.0)
        c.act(t["ea"].v(), par.v(par.t[:, 0:1]), AF.Exp)
        c.ts(gg.v(), sp.v(), t["ea"].v(t["ea"].t[0:n, :]), -1.0, ALU.mult, ALU.mult)
        c.scan(gcum.v(), self.ones.v(self.ones.t[0:n, 0:64]), gg.v(), 0.0, ALU.mult, ALU.add)
        c.copy(t["glast"].v(), gcum.v(gcum.t[:, 63:64]))
        self.rep_chunk(t["glast"].v(), t["Rgl"].v(), 5)
        c.act(t["Rdec"].v(), t["Rgl"].v(), AF.Exp)
        self.to_col(gcum.v(), t["gcumC"].v(), 5)
        self.to_col(beta.v(), t["betaC"].v(), 6)
        c.ts(t["ngcumC"].v(), t["gcumC"].v(), -1.0, None, ALU.mult)
        c.act(t["egC"].v(), t["gcumC"].v(), AF.Exp)
        c.tt(t["bgC"].v(), t["betaC"].v(), t["egC"].v(), ALU.mult)
        c.tt(t["kdC"].v(), t["Rgl"].v(t["Rgl"].t[0:64, :]), t["gcumC"].v(), ALU.subtract)
        c.act(t["kdC"].v(), t["kdC"].v(), AF.Exp)
        S = {"bufs": t["S"], "i": 0}
        c.memset(t["S"][0].v(), 0.0)
        QT, KT, VT, KbT, QdT, tmp, gR, bR, WT = (t[k] for k in ("QT", "KT", "VT", "KbT", "QdT", "tmp", "gR", "bR", "WT"))
        DT, DTsn, ST, X8, P = (t[k] for k in ("DT", "DTsn", "ST", "X8", "P"))
        Kbg, Kd, Vb, U, Vn, O, G = (t[k] for k in ("Kbg", "Kd", "Vb", "U", "Vn", "O", "G"))
        for seg in range(self.nseg):
            c0 = seg * self.GCOLS + hh * 4
            for ci, (X, dst) in enumerate(((t["Xq"], QT), (t["Xk"], KT), (t["Xv"], VT))):
                if seg == 0:
                    c.memset(X.v(X.t[:, 0:3]), 0.0)
                else:
                    c.copy(X.v(X.t[:, 0:3]), X.v(X.t[:, 512:515]))
                self.ld(X.v(X.t[:, 3:515]), c0 + ci, blocks=[4 + 4 * ci + i_ for i_ in range(4)])
                cw = t["cw"]
                c.ts(dst.v(), X.v(X.t[:, 0:512]), cw.v(cw.t[:, ci, 0:1]), None, ALU.mult)
                for j in range(1, 4):
                    c.stt(dst.v(), X.v(X.t[:, j:j + 512]), cw.v(cw.t[:, ci, j:j + 1]), dst.v(), ALU.mult, ALU.add)
                c.act(dst.v(), dst.v(), AF.Silu)
                if ci < 2:
                    c.tt(tmp.v(), dst.v(), dst.v(), ALU.mult)
                    pp = self.pb[5 + ci]
                    c.matmul(pp.v(), self.ones.v(), tmp.v())
                    c.act(tmp.v(), pp.v(), AF.Ln, bias=1e-6)
                    c.act(tmp.v(), tmp.v(), AF.Exp, scale=-0.5)
                    if ci == 0:
                        c.stt(dst.v(), dst.v(), 128 ** -0.5, tmp.v(), ALU.mult, ALU.mult)
                    else:
                        c.tt(dst.v(), dst.v(), tmp.v(), ALU.mult)
            self.to_row_seg(gcum, seg, gR.v(), 5, t["E"])
            self.to_row_seg(beta, seg, bR.v(), 6, t["E"])
            c.tt(KbT.v(), KT.v(), bR.v(), ALU.mult)
            c.act(tmp.v(), gR.v(), AF.Exp)
            c.tt(QdT.v(), QT.v(), tmp.v(), ALU.mult)
            c.tt(DT.v(), gR.v(gR.t[0:64, :].rearrange("p (a b) -> p a b", a=8)),
                 self.negmask.v(self.negmask.t[:, :].unsqueeze(1).to_broadcast([64, 8, 64])), ALU.add)
            for j in range(8):
                g = seg * 8 + j
                c.act(DT.v(DT.t[:, j, :]), DT.v(DT.t[:, j, :]), AF.Exp, bias=t["ngcumC"].v(t["ngcumC"].t[:, g:g + 1]))
            c.stt(DTsn.v(), DT.v(), -1.0, self.strict01.v(self.strict01.t[:, :].unsqueeze(1).to_broadcast([64, 8, 64])), ALU.mult, ALU.mult)
            self.scores(KT, QT, DT, ST)
            self.scores(KT, KbT, DTsn, X8)
            for j in range(8):
                g = seg * 8 + j
                pp = self.pb[5 + (j % 2)]
                c.transpose(pp.v(pp.t[0:64, 0:128]), KT.v(KT.t[:, j * 64:(j + 1) * 64]), self.ident.v())
                c.transpose(pp.v(pp.t[0:64, 128:256]), VT.v(VT.t[:, j * 64:(j + 1) * 64]), self.ident.v())
                c.ts(Kbg.v(Kbg.t[:, j, :], key=j), pp.v(pp.t[0:64, 0:128]), t["bgC"].v(t["bgC"].t[:, g:g + 1]), None, ALU.mult)
                c.ts(Kd.v(Kd.t[:, j, :], key=j), pp.v(pp.t[0:64, 0:128]), t["kdC"].v(t["kdC"].t[:, g:g + 1]), None, ALU.mult)
                c.ts(Vb.v(Vb.t[:, j, :], key=j), pp.v(pp.t[0:64, 128:256]), t["betaC"].v(t["betaC"].t[:, g:g + 1]), None, ALU.mult)
            Z = [t["Z0"], t["Z1"]]
            pz = self.pb[7]
            for j in range(8):
                c.transpose(pz.v(pz.t[0:64, j * 64:(j + 1) * 64]), X8.v(X8.t[:, j, :]), self.ident.v(self.ident.t[0:64, 0:64]))
            c.copy(Z[0].v(), pz.v(pz.t[0:64, :].rearrange("p (a b) -> p a b", a=8)), eng="act")
            c.tt(P.v(), X8.v(), self.ident.v(self.ident.t[0:64, 0:64].unsqueeze(1).to_broadcast([64, 8, 64])), ALU.add)
            ycur, zcur = X8, Z[0]
            for lvl in range(5):
                ynew = t["Y0"] if lvl % 2 == 0 else t["Y1"]
                znew = Z[(lvl + 1) % 2]
                pa, pbk, pc = self.pb[5], self.pb[6], self.pb[7]
                for j in range(8):
                    c.matmul(pa.v(pa.t[0:64, j * 64:(j + 1) * 64]), zcur.v(zcur.t[:, j, :]), ycur.v(ycur.t[:, j, :]))
                    c.matmul(pbk.v(pbk.t[0:64, j * 64:(j + 1) * 64]), ycur.v(ycur.t[:, j, :]), zcur.v(zcur.t[:, j, :]))
                c.copy(ynew.v(), pa.v(pa.t[0:64, :].rearrange("p (a b) -> p a b", a=8)), eng="act")
                c.copy(znew.v(), pbk.v(pbk.t[0:64, :].rearrange("p (a b) -> p a b", a=8)))
                for j in range(8):
                    c.matmul(pc.v(pc.t[0:64, j * 64:(j + 1) * 64]), znew.v(znew.t[:, j, :]), P.v(P.t[:, j, :]))
                c.tt(P.v(), P.v(), pc.v(pc.t[0:64, :].rearrange("p (a b) -> p a b", a=8)), ALU.add)
                ycur, zcur = ynew, znew
            self.step(gen, 2)
            pw = self.pb[5]
            for j in range(8):
                c.matmul(pw.v(pw.t[:, j * 64:(j + 1) * 64]), Kbg.v(Kbg.t[:, j, :], key=j), P.v(P.t[:, j, :]))
            c.copy(WT.v(), pw.v(), eng="act")
            for half in range(2):
                pu = self.pb[6 + half]
                for jj in range(4):
                    j = half * 4 + jj
                    c.matmul(pu.v(pu.t[0:64, jj * 128:(jj + 1) * 128]), P.v(P.t[:, j, :]), Vb.v(Vb.t[:, j, :], key=j))
                c.copy(U.v(U.t[:, half * 4:half * 4 + 4, :]), pu.v(pu.t[0:64, :].rearrange("p (a b) -> p a b", a=4)))

            def pre_hook(j, g, cur):
                p0 = self.pb[0]
                c.matmul(p0.v(p0.t[0:64, 0:128]), WT.v(WT.t[:, j * 64:(j + 1) * 64]), cur.v())
                c.tt(Vn.v(Vn.t[:, j, :], key=j), U.v(U.t[:, j, :]), p0.v(p0.t[0:64, 0:128]), ALU.subtract)
                return Vn.v(Vn.t[:, j, :], key=j)
            self.chunk_loop(seg, QdT, ST, Vn, Kd, lambda g: t["Rdec"].v(t["Rdec"].t[:, g:g + 1]), S, 128, O, pre_hook=pre_hook,
                            post_chunk=None)
            self.post_rms(O, t["s8"], t["s8b"], t["ng"], None, "silu", 128, False, t["sq"])
            self.feat_from_tok(O, 0, t["OF"], 5)
            self.out_stamps.append(self.st(t["OF"].v(), seg, 256 + hh * 128, 128))
            if hh == 1 and seg % 2 == 1:
                self.send("B", seg // 2)
            self.step(gen, 2)

    def gdn(self, ext, with_s5=True):
        t = self.gdn_alloc()
        gen = self.s5(ext) if with_s5 else None
        for hh in range(2):
            self.gdn_head(hh, t, ext, gen)
        if gen is not None:
            for _ in gen:
                pass

    def s5(self, ext):
        c, L = self.c, self.L
        T = self.mk
        TC = 512
        nch = L // TC
        scol = self.nseg * self.GCOLS + 4
        par = T("spar", [128, 3]); Bm = T("sB", [128, 2, 16]); Cm = T("sC", [128, 2, 16]); dsk = T("sd", [32, 1])
        step = T("sstep", [128, 1]); rr = T("srr", [128, 1]); th = T("sth", [128, 1])
        sn1 = T("ssn1", [128, 1]); cs1 = T("scs1", [128, 1]); f1 = T("sf1", [128, 1]); i1 = T("si1", [128, 1], I32)
        lr = T("slr", [128, 1]); li = T("sli", [128, 1]); den = T("sden", [128, 1]); fr = T("sfr", [128, 1]); fi = T("sfi", [128, 1])
        a1 = T("sa1", [128, 1]); a2 = T("sa2", [128, 1])
        bb = T("sbb", [128, 2, 16]); b1 = T("sb1", [128, 16]); b2 = T("sb2", [128, 16])
        BBr = T("sBBr", [128, 32]); BBi = T("sBBi", [128, 32]); CCr = T("sCCr", [128, 32]); nCCi = T("snCCi", [128, 32])
        BrT = T("sBrT", [32, 128]); BiT = T("sBiT", [32, 128])
        io = T("sio", [128, TC], I32); tp1 = T("stp1", [128, TC])
        ang = T("sang", [128, TC]); sn = T("ssn", [128, TC]); cs_ = T("scs", [128, TC]); tf = T("stf", [128, TC]); ti = T("sti", [128, TC], I32)
        Rt = T("sRt", [128, TC]); t1 = T("st1", [128, TC]); t2 = T("st2", [128, TC]); br = T("sbr", [128, TC]); bi = T("sbi", [128, TC])
        zr = T("szr", [128, TC]); zi = T("szi", [128, TC])
        xr = [T("sxr0", [128, TC]), T("sxr1", [128, TC])]; xi = [T("sxi0", [128, TC]), T("sxi1", [128, TC])]
        uT = [T("suT0", [32, TC]), T("suT1", [32, TC])]; ystg = [T("sy0", [32, TC]), T("sy1", [32, TC])]
        x0 = T("sx0", [128, 1])
        c.memset(x0.v(), 0.0)
        c.emit("pool", lambda e: e.iota(io.t[:], pattern=[[1, TC]], base=1, channel_multiplier=0), [io.v()], [])
        c.copy(tp1.v(), io.v())
        cnt = 0
        for tl in range(8):
            c.dma("sp", par.v(), dview(ext["s_par"], ext["s_par"].t[tl]))
            c.dma("sp", Bm.v(), dview(ext["s_B"], ext["s_B"].t[tl]))
            c.dma("sp", Cm.v(), dview(ext["s_C"], ext["s_C"].t[tl]))
            c.dma("sp", dsk.v(), dview(ext["s_d"], ext["s_d"].t[tl]))
            ar, ai = par.v(par.t[:, 0:1]), par.v(par.t[:, 1:2])
            c.act(step.v(), par.v(par.t[:, 2:3]), AF.Exp)
            c.tt(rr.v(), ar, step.v(), ALU.mult)
            c.act(rr.v(), rr.v(), AF.Exp)
            c.tt(th.v(), ai, step.v(), ALU.mult)
            self.sincos_tables(th.v(), sn1.v(), cs1.v(), f1.v(), i1.v())
            yield
            c.tt(lr.v(), rr.v(), cs1.v(), ALU.mult)
            c.tt(li.v(), rr.v(), sn1.v(), ALU.mult)
            c.tt(den.v(), ar, ar, ALU.mult)
            c.tt(a1.v(), ai, ai, ALU.mult)
            c.tt(den.v(), den.v(), a1.v(), ALU.add)
            c.recip(den.v(), den.v())
            c.ts(a1.v(), lr.v(), -1.0, None, ALU.add)
            c.tt(fr.v(), a1.v(), ar, ALU.mult)
            c.tt(a2.v(), li.v(), ai, ALU.mult)
            c.tt(fr.v(), fr.v(), a2.v(), ALU.add)
            c.tt(fr.v(), fr.v(), den.v(), ALU.mult)
            c.tt(fi.v(), li.v(), ar, ALU.mult)
            c.tt(a2.v(), a1.v(), ai, ALU.mult)
            c.tt(fi.v(), fi.v(), a2.v(), ALU.subtract)
            c.tt(fi.v(), fi.v(), den.v(), ALU.mult)
            yield
            c.ts(b1.v(), Bm.v(Bm.t[:, 0, :]), fr.v(), None, ALU.mult)
            c.ts(b2.v(), Bm.v(Bm.t[:, 1, :]), fi.v(), None, ALU.mult)
            c.tt(bb.v(bb.t[:, 0, :]), b1.v(), b2.v(), ALU.subtract)
            c.ts(b1.v(), Bm.v(Bm.t[:, 1, :]), fr.v(), None, ALU.mult)
            c.ts(b2.v(), Bm.v(Bm.t[:, 0, :]), fi.v(), None, ALU.mult)
            c.tt(bb.v(bb.t[:, 1, :]), b1.v(), b2.v(), ALU.add)
            for (dstb, src, k, sc) in ((BBr, bb, 0, 1.0), (BBi, bb, 1, 1.0), (CCr, Cm, 0, 1.0), (nCCi, Cm, 1, -1.0)):
                c.memset(dstb.v(), 0.0)
                for gi in range(2):
                    c.ts(dstb.v(dstb.t[gi * 64:(gi + 1) * 64, gi * 16:(gi + 1) * 16]), src.v(src.t[gi * 64:(gi + 1) * 64, k, :]), sc, None, ALU.mult)
            for (dstT, srcb, bank) in ((BrT, BBr, 5), (BiT, BBi, 6)):
                pp = self.pb[bank]
                c.transpose(pp.v(pp.t[0:32, 0:128]), srcb.v(), self.ident.v())
                c.copy(dstT.v(), pp.v(pp.t[0:32, 0:128]), eng="act")
            yield
            c.ts(ang.v(), tp1.v(), th.v(), None, ALU.mult)
            self.sincos_tables(ang.v(), sn.v(), cs_.v(), tf.v(), ti.v())
            yield
            c.ts(Rt.v(), self.ones.v(self.ones.t[:, 0:1].to_broadcast([128, TC])), rr.v(), None, ALU.mult)
            prev_r, prev_i = x0.v(), x0.v()
            for ch in range(nch):
                u = uT[cnt % 2]
                self.ld(u.v(), scol + tl * self.nseg + ch, parts=32, blocks=[0, 1, 2, 3])
                pr, pi = self.pb[1 + (cnt % 2) * 2], self.pb[2 + (cnt % 2) * 2]
                c.matmul(pr.v(), BrT.v(), u.v())
                c.matmul(pi.v(), BiT.v(), u.v())
                yield
                c.tt(t1.v(), pr.v(), cs_.v(), ALU.mult)
                c.tt(t2.v(), pi.v(), sn.v(), ALU.mult)
                yield
                c.tt(br.v(), t1.v(), t2.v(), ALU.add)
                c.tt(t1.v(), pi.v(), cs_.v(), ALU.mult)
                yield
                c.tt(t2.v(), pr.v(), sn.v(), ALU.mult)
                c.tt(bi.v(), t1.v(), t2.v(), ALU.subtract)
                yield
                c.scan(zr.v(), Rt.v(), br.v(), prev_r, ALU.mult, ALU.add)
                yield
                c.scan(zi.v(), Rt.v(), bi.v(), prev_i, ALU.mult, ALU.add)
                yield
                xrc, xic = xr[cnt % 2], xi[cnt % 2]
                c.tt(t1.v(), zr.v(), cs_.v(), ALU.mult)
                c.tt(t2.v(), zi.v(), sn.v(), ALU.mult)
                yield
                c.tt(xrc.v(), t1.v(), t2.v(), ALU.subtract)
                c.tt(t1.v(), zi.v(), cs_.v(), ALU.mult)
                yield
                c.tt(t2.v(), zr.v(), sn.v(), ALU.mult)
                c.tt(xic.v(), t1.v(), t2.v(), ALU.add)
                yield
                prev_r, prev_i = xrc.v(xrc.t[:, TC - 1:TC]), xic.v(xic.t[:, TC - 1:TC])
                py = self.pb[5 + (cnt % 2)]
                c.matmul(py.v(py.t[0:32, :]), CCr.v(), xrc.v(), start=True, stop=False)
                c.matmul(py.v(py.t[0:32, :]), nCCi.v(), xic.v(), start=False, stop=True)
                ys = ystg[cnt % 2]
                c.stt(ys.v(), u.v(), dsk.v(), py.v(py.t[0:32, :]), ALU.mult, ALU.add)
                self.out_stamps.append(self.st(ys.v(), ch, tl * 32, 32))
                cnt += 1
                if tl == 7 and ch % 2 == 1:
                    self.send("A", ch // 2)
                yield 1


EvenF.post_rms = Odd.post_rms


GROUPS = [[0, 1, 2, 3], [4, 5, 6, 7]]
EVEN_IN = 5136
ODD_IN = 6152
L = 4096
NE = 8 * 8 + 4 + 64
NO = 8 * 14 + 2


class TokF(Tok):
    def __init__(self, c, P, yf=False):
        self.c = c
        self.xT = P["xT"]
        self.norms = P["norms"]
        self.ones = P["onesb"]
        self.pb = P["pb"]
        self.hT = c.sb("hT", [128, 16, NT], BF16)
        self.gT = c.sb("gT", [128, 12, NT], BF16)
        self.rstd = c.sb("rstd", [128, NT], F32)
        self.sq = [c.sb("sq%d" % i, [128, NT], BF16) for i in range(2)]
        self.wa = [c.sb("wa%d" % i, [128, 16, 256], BF16) for i in range(2)]
        self.wb = [c.sb("wb%d" % i, [128, 16, 256], BF16) for i in range(2)]
        self.w2 = [c.sb("w2_%d" % i, [128, 12, 256], BF16) for i in range(2)]
        self.tmp = [c.sb("tmp%d" % i, [128, 512], F32) for i in range(2)]
        self.stage = [c.sb("stg%d" % i, [128, NT], F32) for i in range(2)]
        self.cnt = {"wa": 0, "wb": 0, "w2": 0, "tmp": 0, "sq": 0, "stage": 0}
        self.tasks = []
        self.out_stamps = []
        self.t1 = c.sb("t1g", [128, 512], F32)
        if yf:
            self.yf = c.sb("yf", [128, 8, 512], F32)

    def proj_send(self, ni, w_d, N, S, G, grp_of, order=None, skip=()):
        c = self.c
        self.rmsnorm(ni)
        nblk = (N + 255) // 256
        for b in (order if order is not None else range(nblk)):
            ncols = min(256, N - b * 256)

            def load(b=b, ncols=ncols):
                ta = self.rot("wa")
                c.dma("pool", ta.v(ta.t[:, :, 0:ncols]), self.wview(w_d, b * 256, ncols))
                return ta

            def compute(ta, b=b, ncols=ncols):
                for jj in range((ncols + 127) // 128):
                    m = min(128, ncols - jj * 128)
                    n0 = b * 256 + jj * 128
                    sg = self.rot("stage")
                    for th in range(2):
                        pp = self.pb[(jj % 2) * 2 + th]
                        for k in range(16):
                            c.matmul(pp.v(pp.t[0:m, :]), ta.v(ta.t[:, k, jj * 128:jj * 128 + m]),
                                     self.hT.v(self.hT.t[:, k, th * 512:(th + 1) * 512], key=k),
                                     start=(k == 0), stop=(k == 15))
                        c.copy(sg.v(sg.t[0:m, th * 512:(th + 1) * 512]), pp.v(pp.t[0:m, :]), eng="act")
                        c.dma("sp", S.v(S.t[b, th, jj * 128:jj * 128 + m, :], key=(b, th, jj)), sg.v(sg.t[0:m, th * 512:(th + 1) * 512]))
                keys = [(b, th, jj) for th in range(2) for jj in range((ncols + 127) // 128)]
                if b in skip:
                    return
                gname, gtot = grp_of(b)
                c.allgather(S.t[b].rearrange("a r t -> (a r) t"), G.t[b].rearrange("a r t -> (a r) t"),
                            [S.v(S.t[b], key=keys)], [G.v(G.t[b], key=b)], GROUPS, grp=gname, grp_total=gtot)
            self.add_task(load, compute)
        self.flush()

    def mix_gather(self, GA, GB, idx, w_out_d, even=None, Sloc=None, gates=None):
        c = self.c
        k0 = 8 if even is not None else 0
        for k in range(k0, 16):
            G = GA if k < 8 else GB
            Grow = G.t.rearrange("p s a r t -> (p s a r) t")
            sg = self.rot("stage")
            for half in range(2):
                c.gather(sg.v(sg.t[:, half * 512:(half + 1) * 512]), G, Grow, idx.v(idx.t[:, k * 2 + half:k * 2 + half + 1]))
            r0, fn = gates(k)
            for half in range(2):
                r = r0
                while r < r0 + 128:
                    b, rl = r // 256, r % 256
                    m = min(r0 + 128 - r, 256 - rl)
                    c.dma("sp", self.t1.v(self.t1.t[r - r0:r - r0 + m, :]), Sloc.v(Sloc.t[b, half, rl:rl + m, :]))
                    r += m
                c.act(self.t1.v(), self.t1.v(), fn)
                hv = sg.v(sg.t[:, half * 512:(half + 1) * 512])
                c.tt(hv, hv, self.t1.v(), ALU.mult)
            c.copy(self.hT.v(self.hT.t[:, k, :], key=k), sg.v(), eng=("act" if k % 2 else "dve"))
        if even is not None:
            self.s5_glu_g(GA, GA.t.rearrange("p s a r t -> (p s a r) t"), idx, even)
        src = []
        for k in range(16):
            if even is not None and k < 8:
                src.append((self.gT, k))
            else:
                src.append((self.hT, k))
        for b in range(8):
            def load(b=b):
                ta = self.rot("wa")
                c.dma("pool", ta.v(), self.wview(w_out_d, b * 256, 256))
                return ta

            def compute(ta, b=b):
                for jj in range(2):
                    i = b * 2 + jj
                    for th in range(2):
                        pp = self.pb[(i % 2) * 2 + th]
                        for k in range(16):
                            buf, kk = src[k]
                            key = kk if buf is self.hT else [(kk, 0), (kk, 1)]
                            c.matmul(pp.v(), ta.v(ta.t[:, k, jj * 128:(jj + 1) * 128]),
                                     buf.v(buf.t[:, kk, th * 512:(th + 1) * 512], key=key),
                                     start=(k == 0), stop=(k == 15))
                        xv = self.xT.v(self.xT.t[:, i, th * 512:(th + 1) * 512], key=i)
                        c.stt(xv, pp.v(), 1.0, xv, ALU.mult, ALU.add)
            self.add_task(load, compute)
        self.flush()

    def s5_glu_g(self, G, Grow, idx, even):
        c = self.c
        yf, t1 = self.yf, self.t1
        bglu = even["bglu"]
        wglu_d = even["wglu"]
        for th in range(2):
            sl = slice(th * 512, (th + 1) * 512)
            for k in range(8):
                yk = yf.v(yf.t[:, k, :], key=k)
                c.gather(yk, G, Grow, idx.v(idx.t[:, k * 2 + th:k * 2 + th + 1]))
                c.tt(t1.v(), yk, yk, ALU.mult)
                c.ts(t1.v(), t1.v(), 0.044715, 1.0, ALU.mult, ALU.add)
                c.tt(t1.v(), t1.v(), yk, ALU.mult)
                c.act(t1.v(), t1.v(), AF.Sigmoid, scale=2.0 * 0.7978845608028654)
                c.tt(yk, yk, t1.v(), ALU.mult)
                c.copy(self.hT.v(self.hT.t[:, k, sl], key=k), yk, eng="act")
            for b in range(4):
                ta = self.rot("wa")
                c.dma("pool", ta.v(ta.t[:, 0:8, :]), self.wview(wglu_d, b * 256, 256, kchunks=8))
                for jj in range(2):
                    n = b * 2 + jj
                    pp = self.pb[4 + (n % 2)]
                    for k in range(8):
                        c.matmul(pp.v(), ta.v(ta.t[:, k, jj * 128:(jj + 1) * 128]), self.hT.v(self.hT.t[:, k, sl], key=k),
                                 start=(k == 0), stop=(k == 7))
                    t = self.rot("tmp")
                    c.act(t.v(), pp.v(), AF.Sigmoid, bias=bglu.v(bglu.t[:, n:n + 1]))
                    c.tt(self.gT.v(self.gT.t[:, n, sl], key=(n, th)), t.v(), yf.v(yf.t[:, n, :], key=n), ALU.mult)


def build(debug=False, upto=9):
    c = Ctx()
    X = lambda name, shape, dt=F32: c.dram(name, shape, dt, kind="ExternalInput")
    xin = X("xT_in", [D, NT])
    norms_d = X("norms_in", [128, 7, 16])
    W = {}
    for tag in "abcd":
        if tag == "a" or (tag in "bc" and upto >= 3) or (tag == "d" and upto >= 5):
            W[tag] = (X("w1_" + tag, [D, DFF]), X("w3_" + tag, [D, DFF]), X("w2_" + tag, [DFF, D]))
    w_in_e = X("w_in_e", [D, EVEN_IN])
    bglu_d = X("bglu_in", [128, 8])
    if upto >= 3:
        w_out_e = X("w_out_e", [D, D]); w_in_o = X("w_in_o", [D, ODD_IN]); w_glu = X("w_glu", [1024, 1024])
    if upto >= 5:
        w_out_o = X("w_out_o", [D, D])
    idxE_d = X("idxE", [128, NE], I32); idxO_d = X("idxO", [128, NO], I32)
    idxM0_d = X("idxM0", [128, 32], I32); idxM1_d = X("idxM1", [128, 32], I32)
    ext = dict(
        g_cw=X("g_cw", [2, 128, 3, 4]), g_par=X("g_par", [2, 128, 2]), g_ng=X("g_ng", [64, 128]),
        s_par=X("s_par", [8, 128, 3]), s_B=X("s_B", [8, 128, 2, 16]), s_C=X("s_C", [8, 128, 2, 16]), s_d=X("s_d", [8, 32, 1]),
        m_par=X("m_par", [128, 2]), m_ng=X("m_ng", [64, 256]), pos=X("pos", [128, L], I32), hidx=X("hidx", [128, 1]),
        r_ng=X("r_ng", [64, 256]))
    outT = c.dram("outT", [D, NT], F32, kind="ExternalOutput")
    NB0 = (EVEN_IN + 255) // 256
    NB2 = (ODD_IN + 255) // 256
    S0 = c.dram("S0", [NB0, 2, 256, 512]); G0 = c.dram("G0", [NB0, 8, 256, 512])
    S1a = c.dram("S1a", [4, 2, 256, 512]); G1a = c.dram("G1a", [4, 4, 2, 256, 512])
    S1b = c.dram("S1b", [4, 2, 256, 512]); G1b = c.dram("G1b", [4, 4, 2, 256, 512])
    S2 = c.dram("S2", [NB2, 2, 256, 512]); G2 = c.dram("G2", [NB2, 8, 256, 512])
    S3a = c.dram("S3a", [4, 2, 256, 512]); G3a = c.dram("G3a", [4, 4, 2, 256, 512])
    S3b = c.dram("S3b", [4, 2, 256, 512]); G3b = c.dram("G3b", [4, 4, 2, 256, 512])
    P = dict(xT=c.sb("xT", [128, 16, NT], F32), norms=c.sb("norms", [128, 7, 16], F32), onesb=c.sb("onesb", [128, 128], BF16),
             pb=[c.ps("pb%d" % i, [128, 512], F32) for i in range(8)])
    ones = c.sb("ones", [128, 128], F32); ident = c.sb("ident", [128, 128], F32); negmask = c.sb("negmask", [64, 64], F32)
    zeros = c.sb("zeros", [128, 64], F32); strict01 = c.sb("strict01", [64, 64], F32)
    idxE = c.sb("idxE_t", [128, NE], I32); idxO = c.sb("idxO_t", [128, NO], I32)
    idxM0 = c.sb("idxM0_t", [128, 32], I32); idxM1 = c.sb("idxM1_t", [128, 32], I32)
    bglu = c.sb("bglu", [128, 8], F32)
    c.memset(P["onesb"].v(), 1.0)
    c.memset(ones.v(), 1.0)
    c.memset(zeros.v(), 0.0)
    c.emit("pool", lambda e: e.affine_select(out=ident.t[:], in_=ones.t[:], pattern=[[1, 128]], compare_op=ALU.is_equal,
                                             fill=0.0, base=0, channel_multiplier=-1), [ident.v()], [ones.v()])
    c.emit("pool", lambda e: e.affine_select(out=negmask.t[:], in_=zeros.t[0:64, :], pattern=[[1, 64]], compare_op=ALU.is_ge,
                                             fill=NEG, base=0, channel_multiplier=-1), [negmask.v()], [zeros.v()])
    c.emit("pool", lambda e: e.affine_select(out=strict01.t[:], in_=ones.t[0:64, 0:64], pattern=[[1, 64]], compare_op=ALU.is_gt,
                                             fill=0.0, base=0, channel_multiplier=-1), [strict01.v()], [ones.v()])
    for t, d in ((idxE, idxE_d), (idxO, idxO_d), (idxM0, idxM0_d), (idxM1, idxM1_d), (bglu, bglu_d), (P["norms"], norms_d)):
        c.dma("sp", t.v(), d.v())
    consts = dict(ones=ones, ident=ident, negmask=negmask, zeros=zeros, strict01=strict01, pb=P["pb"], uidbox=[0])
    dbg_stamps = []

    def dump(name, buf):
        if debug:
            shape = list(buf.t.shape)
            d = c.dram("dbg_" + name, shape, F32, kind="ExternalOutput")
            dbg_stamps.append(c.dma("sp", View(d.t, [Dep()]), buf.v()))

    def ag(S, G):
        for seg in range(8):
            c.allgather(S.t[seg], G.t[seg].rearrange("a r t -> (a r) t"), [S.v(S.t[seg])], [G.v(G.t[seg], key=seg)], GROUPS)

    with c.scope():
        T = TokF(c, P)
        T.load_x(xin)
        T.ffn(0, *W["a"])
        T.proj_send(4, w_in_e, EVEN_IN, S0, G0, lambda b: ("e1", 13) if b >= 4 else ("e2", 4), order=[20] + list(range(4, 20)) + [0, 1, 2, 3], skip=(16, 17, 18, 19))
    dump("S0", S0)
    if upto >= 2:
        with c.scope():
            M = EvenF(c, L, consts)
            M.setup_io(G0, EVEN_IN, idxE, S1a, S1b, G1a, G1b, GROUPS, "me")
            M.gdn(ext, with_s5=True)
    if upto >= 3:
        with c.scope():
            T = TokF(c, P, yf=True)
            T.mix_gather(G1a, G1b, idxM0, w_out_e, even=dict(bglu=bglu, wglu=w_glu), Sloc=S0,
                         gates=lambda k: (4096 + (k - 8) * 128, AF.Silu))
            T.ffn(1, *W["b"])
            T.ffn(2, *W["c"])
            T.proj_send(5, w_in_o, ODD_IN, S2, G2, lambda b: ("o1", 9) if b <= 12 else ("o2", 8), skip=(8, 9, 10, 11, 21, 22, 23, 24))
            if debug:
                xd = c.dram("dbg_x4T", [D, NT], F32, kind="ExternalOutput")
                T.store_x(xd)
                dbg_stamps.extend(T.out_stamps)
        dump("S2", S2)
    if upto >= 4:
        with c.scope():
            M = OddF(c, L, consts)
            M.setup_io(G2, ODD_IN, idxO, S3a, S3b, G3a, G3b, GROUPS, "mo")
            M.mlstm(ext)
        with c.scope():
            M = OddF(c, L, consts)
            M.setup_io(G2, ODD_IN, idxO, S3a, S3b, G3a, G3b, GROUPS, "mo")
            M.retention(ext)
    with c.scope():
        T = TokF(c, P)
        if upto >= 5:
            T.mix_gather(G3a, G3b, idxM1, w_out_o, Sloc=S2,
                         gates=lambda k: (2048 + k * 128, AF.Sigmoid) if k < 8 else (5128 + (k - 8) * 128, AF.Silu))
            T.ffn(3, *W["d"])
        T.final_norm(6, outT)
        fin = T.out_stamps
    c.finish(fin + dbg_stamps)
    c.close()
    return c


def _nl(g):
    return np.ascontiguousarray(np.asarray(g, np.float32).reshape(16, 128).T)


def host_inputs(inp):
    A = np.asarray
    f32 = np.float32
    C_ = np.ascontiguousarray
    x = A(inp["x"]).astype(f32, copy=False).reshape(8, NT, D)
    fn, mn, fin = A(inp["ffn_norm"]), A(inp["mix_norm"]), A(inp["final_norm"])
    norms = C_(np.stack([_nl(fn[0, 0]), _nl(fn[0, 1]), _nl(fn[1, 0]), _nl(fn[1, 1]), _nl(mn[0]), _nl(mn[1]), _nl(fin)], axis=1))
    W1, W3, W2 = A(inp["ffn_w1"]), A(inp["ffn_w3"]), A(inp["ffn_w2"])
    shared = {"norms_in": norms, "w_in_e": A(inp["even_w_in"])[0], "w_out_e": A(inp["even_w_out"])[0],
              "w_in_o": A(inp["odd_w_in"])[0], "w_out_o": A(inp["odd_w_out"])[0], "w_glu": A(inp["s5_w_glu"])[0],
              "bglu_in": C_(A(inp["s5_b_glu"])[0].astype(f32).reshape(8, 128).T),
              "g_ng": C_(np.broadcast_to(A(inp["gdn_norm"])[0][None], (64, 128)).astype(f32)),
              "m_ng": C_(np.broadcast_to(A(inp["mlstm_norm"])[0][None], (64, 256)).astype(f32)),
              "r_ng": C_(np.broadcast_to(A(inp["ret_norm"])[0][None], (64, 256)).astype(f32))}
    for tag, (l, i) in zip("abcd", ((0, 0), (0, 1), (1, 0), (1, 1))):
        shared["w1_" + tag] = W1[l, i]; shared["w3_" + tag] = W3[l, i]; shared["w2_" + tag] = W2[l, i]
    cw = A(inp["gdn_conv_w"])[0]
    P5 = {k: A(inp["s5_" + k])[0] for k in ("a_re", "a_im", "log_step", "b_re", "b_im", "c_re", "c_im", "d")}
    pos = A(inp["positions"]).astype(np.int32)
    gb = A(inp["mlstm_gate_bias"])[0]
    p = np.arange(128)

    def fr(seg, r):
        r = np.asarray(r)
        return ((r // 256) * 8 + seg) * 256 + r % 256
    maps = []
    for core in range(8):
        b, r = core // 4, core % 4
        m = dict(shared)
        m["xT_in"] = C_(x[core].T)
        hg = r
        heads = [2 * hg, 2 * hg + 1]
        m["g_cw"] = C_(np.stack([np.stack([cw[:, off + h * 128: off + (h + 1) * 128].T for off in (0, 1024, 2048)], axis=1) for h in heads]).astype(f32))
        m["g_par"] = C_(np.stack([np.broadcast_to(np.array([A(inp["gdn_a_log"])[0, h], A(inp["gdn_dt_bias"])[0, h]], f32)[None], (128, 2)) for h in heads]))
        s_par = np.zeros((8, 128, 3), f32); s_B = np.zeros((8, 128, 2, 16), f32); s_Cm = np.zeros((8, 128, 2, 16), f32); s_d = np.zeros((8, 32, 1), f32)
        for tl in range(8):
            for gi in range(2):
                g = hg * 16 + tl * 2 + gi
                sl = slice(gi * 64, (gi + 1) * 64)
                s_par[tl, sl, 0] = P5["a_re"][g]; s_par[tl, sl, 1] = P5["a_im"][g]; s_par[tl, sl, 2] = P5["log_step"][g]
                s_B[tl, sl, 0] = P5["b_re"][g]; s_B[tl, sl, 1] = P5["b_im"][g]
                s_Cm[tl, sl, 0] = P5["c_re"][g].T; s_Cm[tl, sl, 1] = P5["c_im"][g].T
                s_d[tl, gi * 16:(gi + 1) * 16, 0] = P5["d"][g]
        m.update(s_par=s_par, s_B=s_B, s_C=s_Cm, s_d=s_d)
        h = r
        m["m_par"] = C_(np.broadcast_to(gb[:, h][None, :], (128, 2)).astype(f32))
        m["pos"] = C_(np.broadcast_to(pos[b][None], (128, L)))
        m["hidx"] = np.full((128, 1), h, f32)
        iE = np.zeros((128, NE), np.int64)
        for seg in range(8):
            for hh in range(2):
                hd = 2 * hg + hh
                for ci, base in enumerate((1024, 2048, 3072, 4096)):
                    iE[:, seg * 8 + hh * 4 + ci] = fr(seg, base + hd * 128 + p)
        cidx = np.arange(64)
        for hh in range(2):
            hd = 2 * hg + hh
            for gi, rbase in enumerate((5120, 5128)):
                iE[:64, 64 + hh * 2 + gi] = fr(cidx // 8, rbase + hd) * 8 + cidx % 8
        for tl in range(8):
            for ch in range(8):
                iE[:32, 68 + tl * 8 + ch] = fr(ch, hg * 256 + tl * 32 + np.arange(32))
        m["idxE"] = C_(iE.astype(np.int32))
        iO = np.zeros((128, NO), np.int64)
        sw = (p + 64) % 128
        for seg in range(8):
            rows = [h * 128 + p, 512 + h * 128 + p, 1024 + h * 256 + p, 1024 + h * 256 + 128 + p, 2048 + h * 256 + p, 2048 + h * 256 + 128 + p,
                    3080 + h * 128 + p, 3080 + h * 128 + sw, 3592 + h * 128 + p, 3592 + h * 128 + sw,
                    4104 + h * 256 + p, 4104 + h * 256 + 128 + p, 5128 + h * 256 + p, 5128 + h * 256 + 128 + p]
            for k, rr in enumerate(rows):
                iO[:, seg * 14 + k] = fr(seg, rr)
        for gi, rbase in enumerate((3072, 3076)):
            iO[:64, 112 + gi] = fr(cidx // 8, rbase + h) * 8 + cidx % 8
        m["idxO"] = C_(iO.astype(np.int32))
        q = r
        iM0 = np.zeros((128, 32), np.int64)
        for k in range(16):
            f_ = (k % 8) * 128 + p
            for half in range(2):
                iM0[:, k * 2 + half] = ((q * 4 + f_ // 256) * 2 + half) * 256 + f_ % 256
        iM1 = iM0
        m["idxM0"] = C_(iM0.astype(np.int32)); m["idxM1"] = C_(iM1.astype(np.int32))
        maps.append(m)
    return maps


def kernel(**inputs):
    c = build(debug=False)
    maps = host_inputs(inputs)
    res = run_bass_kernel_spmd(c.nc, maps, core_ids=list(range(8)))
    out = np.stack([res.results[i]["outT"].T for i in range(8)]).reshape(2, L, D)
    return np.ascontiguousarray(out.astype(np.float32, copy=False))
```
